# Optimizing a Trainium2 kernel written in Bass

```python
import math, functools
import jax, jax.numpy as jnp
from jax import lax
import numpy as np

D_MODEL = 1024
BATCH = 4
SEQ = 4096
DEPTH = 2
DEC_BATCH = 32
DEC_SEQ = 1
PAST_LEN = 8192
PAGE_SIZE = 128

N_AB = (DEPTH + 1) // 2
N_C = DEPTH // 2

D_INNER = 2 * D_MODEL
SSM_HEAD_DIM = 64
SSM_HEADS = D_INNER // SSM_HEAD_DIM
SSM_GROUPS = 4
SSM_STATE = 128
SSM_CONV = 4
SSM_CHUNK = 128
CONV_DIM = D_INNER + 2 * SSM_GROUPS * SSM_STATE

NSA_HEADS = 16
NSA_KV = 2
NSA_HPG = NSA_HEADS // NSA_KV
NSA_HEAD_DIM = 64
CMP_LEN = 32
CMP_STRIDE = 16
CMP_HIDDEN = 128
SLC_LEN = 64
SLC_TOPN = 16
WINDOW = 512
Q_BLOCK = 128
ROPE_THETA = 10000.0
FORCE_SCORE = 1e4

D_CONF = D_MODEL
CONF_CONV = 31

D_FF = 2816
FFN_CONV = 3

EPS = 1e-6

PROJ_SPLITS = (D_INNER, CONV_DIM, SSM_HEADS, NSA_HEADS * NSA_HEAD_DIM,
               2 * NSA_KV * NSA_HEAD_DIM, 2 * NSA_KV * NSA_HEAD_DIM, 2 * NSA_KV * NSA_HEAD_DIM,
               3 * NSA_HEADS)
D_PROJ = sum(PROJ_SPLITS)
D_MIX0 = D_INNER + NSA_HEADS * NSA_HEAD_DIM

kernel_name = 'hybrid_ssd_nsa_conformer_convffn_step'


def rms_norm(x, g):
    xf = x.astype(jnp.float32)
    y = xf * lax.rsqrt(jnp.mean(xf * xf, axis=-1, keepdims=True) + EPS)
    return (y * g.astype(jnp.float32)).astype(x.dtype)


def layer_norm(x, g, b):
    xf = x.astype(jnp.float32)
    mu = jnp.mean(xf, axis=-1, keepdims=True)
    var = jnp.mean(jnp.square(xf - mu), axis=-1, keepdims=True)
    y = (xf - mu) * lax.rsqrt(var + EPS)
    return (y * g.astype(jnp.float32) + b.astype(jnp.float32)).astype(x.dtype)


def rope(x, pos):
    half = x.shape[-1] // 2
    inv = ROPE_THETA ** (-jnp.arange(half, dtype=jnp.float32) / half)
    ang = pos.astype(jnp.float32)[:, None] * inv[None, :]
    cos = jnp.cos(ang)[None, :, None, :]
    sin = jnp.sin(ang)[None, :, None, :]
    xf = x.astype(jnp.float32)
    x1, x2 = xf[..., :half], xf[..., half:]
    return jnp.concatenate([x1 * cos - x2 * sin, x2 * cos + x1 * sin], axis=-1).astype(x.dtype)


def masked_softmax(s, mask):
    s = jnp.where(mask, s.astype(jnp.float32), -jnp.inf)
    m = jnp.max(s, axis=-1, keepdims=True)
    m = jnp.where(jnp.isfinite(m), m, 0.0)
    e = jnp.exp(s - m)
    d = jnp.sum(e, axis=-1, keepdims=True)
    return e / jnp.where(d > 0, d, 1.0)


def causal_dwconv(xh, w, b):
    c = w.shape[-1]
    y = lax.conv_general_dilated(xh, w[:, None, :].astype(xh.dtype), window_strides=(1,),
                                 padding='VALID', dimension_numbers=('NWC', 'WIO', 'NWC'),
                                 feature_group_count=c)
    return y + b.astype(xh.dtype)


def ssd_scan(x, dt, a, bm, cm, h0, chunk):
    f32 = jnp.float32
    bsz, l, h, p = x.shape
    g, n = bm.shape[2], bm.shape[3]
    hg = h // g
    q = min(chunk, l)
    pad = (-l) % q
    x, dt, bm, cm = (v.astype(f32) for v in (x, dt, bm, cm))
    if pad:
        padt = lambda v: jnp.pad(v, [(0, 0), (0, pad)] + [(0, 0)] * (v.ndim - 2))
        x, dt, bm, cm = padt(x), padt(dt), padt(bm), padt(cm)
    c = (l + pad) // q
    x = x.reshape(bsz, c, q, g, hg, p)
    dt = dt.reshape(bsz, c, q, g, hg)
    bm = bm.reshape(bsz, c, q, g, n)
    cm = cm.reshape(bsz, c, q, g, n)
    cum = jnp.cumsum(dt * a.astype(f32).reshape(g, hg), axis=2)
    seg = cum[:, :, :, None] - cum[:, :, None, :]
    causal = jnp.tril(jnp.ones((q, q), bool))[:, :, None, None]
    decay = jnp.exp(jnp.where(causal, seg, -jnp.inf))
    cb = jnp.einsum('bcign,bcjgn->bcijg', cm, bm)
    w = cb[..., None] * decay * dt[:, :, None]
    y_diag = jnp.einsum('bcijgh,bcjghp->bcighp', w, x)
    to_end = jnp.exp(cum[:, :, -1:] - cum) * dt
    states = jnp.einsum('bcjgn,bcjgh,bcjghp->bcghpn', bm, to_end, x)
    chunk_decay = jnp.exp(cum[:, :, -1])

    def step(hc, inp):
        st, dec = inp
        return dec[..., None, None] * hc + st, hc

    h_T, h_start = lax.scan(step, h0.astype(f32).reshape(bsz, g, hg, p, n),
                            (jnp.moveaxis(states, 1, 0), jnp.moveaxis(chunk_decay, 1, 0)))
    h_start = jnp.moveaxis(h_start, 0, 1)
    y_off = jnp.einsum('bcign,bcghpn->bcighp', cm, h_start) * jnp.exp(cum)[..., None]
    y = (y_diag + y_off).reshape(bsz, c * q, h, p)[:, :l]
    return y, h_T.reshape(bsz, h, p, n)


def mamba_branch(z, xbc, dt_raw, conv_hist, h0, conv_w, conv_b, dt_bias, a_log, d_skip, ssm_norm):
    bsz, t = z.shape[:2]
    xh = jnp.concatenate([conv_hist.astype(xbc.dtype), xbc], axis=1)
    new_hist = xh[:, xh.shape[1] - (SSM_CONV - 1):]
    xbc = jax.nn.silu(causal_dwconv(xh, conv_w, conv_b))
    xs, bm, cm = jnp.split(xbc, [D_INNER, D_INNER + SSM_GROUPS * SSM_STATE], axis=-1)
    xs = xs.reshape(bsz, t, SSM_HEADS, SSM_HEAD_DIM)
    bm = bm.reshape(bsz, t, SSM_GROUPS, SSM_STATE)
    cm = cm.reshape(bsz, t, SSM_GROUPS, SSM_STATE)
    dt = jax.nn.softplus(dt_raw.astype(jnp.float32) + dt_bias.astype(jnp.float32))
    a = -jnp.exp(a_log.astype(jnp.float32))
    y, h_T = ssd_scan(xs, dt, a, bm, cm, h0, SSM_CHUNK)
    y = y.astype(xs.dtype) + d_skip[:, None] * xs
    y = y.reshape(bsz, t, D_INNER) * jax.nn.silu(z)
    y = rms_norm(y.reshape(bsz, t, SSM_GROUPS, D_INNER // SSM_GROUPS),
                 ssm_norm.reshape(SSM_GROUPS, D_INNER // SSM_GROUPS)).reshape(bsz, t, D_INNER)
    return y, new_hist, h_T.astype(h0.dtype)


def compress_rows(rows, pe, w1, b1, w2, b2):
    bsz, t = rows.shape[:2]
    r = CMP_LEN // CMP_STRIDE
    nch = t // CMP_STRIDE
    n_cmp = nch - r + 1
    ch = rows[:, :nch * CMP_STRIDE].reshape(bsz, nch, CMP_STRIDE, NSA_KV, NSA_HEAD_DIM)
    blk = jnp.concatenate([ch[:, k:k + n_cmp] for k in range(r)], axis=2)
    blk = blk + pe[None, None, :, None, :]
    flat = blk.transpose(0, 1, 3, 2, 4).reshape(bsz, n_cmp, NSA_KV, CMP_LEN * NSA_HEAD_DIM)
    hid = jax.nn.silu(flat @ w1 + b1)
    return hid @ w2 + b2


def compress_kv(kv, cmp_w):
    pe, w1, b1, w2, b2 = cmp_w
    kc = compress_rows(kv[:, :, 0], pe[0], w1[0], b1[0], w2[0], b2[0])
    vc = compress_rows(kv[:, :, 1], pe[1], w1[1], b1[1], w2[1], b2[1])
    cmp_end = jnp.arange(kc.shape[1]) * CMP_STRIDE + CMP_LEN - 1
    return kc, vc, cmp_end


def slc_blocks_of(kv):
    bsz, t = kv.shape[:2]
    n_slc = -(-t // SLC_LEN)
    kv = jnp.pad(kv, [(0, 0), (0, n_slc * SLC_LEN - t), (0, 0), (0, 0), (0, 0)])
    return kv.reshape(bsz, n_slc, SLC_LEN, 2, NSA_KV, NSA_HEAD_DIM).transpose(0, 4, 1, 2, 3, 5)


def overlap_matrix(n_cmp, n_slc):
    i = np.arange(n_cmp)[:, None]
    j = np.arange(n_slc)[None, :]
    ov = (i * CMP_STRIDE < (j + 1) * SLC_LEN) & (i * CMP_STRIDE + CMP_LEN > j * SLC_LEN)
    return jnp.asarray(ov.astype(np.float32))


def nsa_core(q, qpos, gates, kc, vc, cmp_end, slc_blocks, win_kv, win_pos):
    f32 = jnp.float32
    bsz, tq = q.shape[:2]
    qg = (q * NSA_HEAD_DIM ** -0.5).reshape(bsz, tq, NSA_KV, NSA_HPG, NSA_HEAD_DIM)
    s_c = jnp.einsum('btghd,bngd->btghn', qg, kc, preferred_element_type=f32)
    m_c = (cmp_end[None, :] <= qpos[:, None])[None, :, None, None, :]
    p_c = masked_softmax(s_c, m_c)
    o_c = jnp.einsum('btghn,bngd->btghd', p_c, vc.astype(f32))
    n_slc = slc_blocks.shape[2]
    imp = jnp.einsum('btghn,nj->btgj', p_c, overlap_matrix(kc.shape[1], n_slc))
    j = jnp.arange(n_slc)[None, :]
    qblk = (qpos // SLC_LEN)[:, None]
    valid = j * SLC_LEN <= qpos[:, None]
    forced = (j == 0) | (j == qblk) | (j == qblk - 1)
    score = jnp.where(valid[None, :, None, :], imp + FORCE_SCORE * forced[None, :, None, :], -jnp.inf)
    topn = min(SLC_TOPN, n_slc)
    _, idx = lax.top_k(score, topn)
    idx_t = idx.transpose(0, 2, 1, 3)
    bi = jnp.arange(bsz)[:, None, None, None]
    gi = jnp.arange(NSA_KV)[None, :, None, None]
    sel = slc_blocks[bi, gi, idx_t]
    s_s = jnp.einsum('btghd,bgtnld->btghnl', qg, sel[..., 0, :], preferred_element_type=f32)
    kpos = idx_t[..., None] * SLC_LEN + jnp.arange(SLC_LEN)
    m_s = (kpos <= qpos[None, None, :, None, None]).transpose(0, 2, 1, 3, 4)[:, :, :, None]
    p_s = masked_softmax(s_s.reshape(bsz, tq, NSA_KV, NSA_HPG, topn * SLC_LEN),
                         m_s.reshape(bsz, tq, NSA_KV, 1, topn * SLC_LEN)).reshape(s_s.shape)
    o_s = jnp.einsum('btghnl,bgtnld->btghd', p_s, sel[..., 1, :].astype(f32))
    s_w = jnp.einsum('btghd,bsgd->btghs', qg, win_kv[:, :, 0], preferred_element_type=f32)
    dpos = qpos[:, None] - win_pos[None, :]
    m_w = ((dpos >= 0) & (dpos < WINDOW) & (win_pos[None, :] >= 0))[None, :, None, None, :]
    p_w = masked_softmax(s_w, m_w)
    o_w = jnp.einsum('btghs,bsgd->btghd', p_w, win_kv[:, :, 1].astype(f32))
    g = jax.nn.sigmoid(gates.astype(f32)).reshape(bsz, tq, NSA_KV, NSA_HPG, 3)
    o = g[..., 0:1] * o_c + g[..., 1:2] * o_s + g[..., 2:3] * o_w
    return o.reshape(bsz, tq, NSA_HEADS * NSA_HEAD_DIM).astype(q.dtype)


def nsa_prompt(q, gates, kvc, kvs, kvw, cmp_w):
    bsz, s = q.shape[:2]
    kc, vc, cmp_end = compress_kv(kvc, cmp_w)
    slc_blocks = slc_blocks_of(kvs)
    win_pad = jnp.concatenate([jnp.zeros((bsz, WINDOW) + kvw.shape[2:], kvw.dtype), kvw], axis=1)
    nb = s // Q_BLOCK
    qb = q.reshape(bsz, nb, Q_BLOCK, NSA_HEADS, NSA_HEAD_DIM).swapaxes(0, 1)
    gb = gates.reshape(bsz, nb, Q_BLOCK, NSA_HEADS, 3).swapaxes(0, 1)

    def one_block(args):
        qi, gi, i = args
        qpos = i * Q_BLOCK + jnp.arange(Q_BLOCK)
        wkv = lax.dynamic_slice_in_dim(win_pad, i * Q_BLOCK, Q_BLOCK + WINDOW, axis=1)
        wpos = i * Q_BLOCK - WINDOW + jnp.arange(Q_BLOCK + WINDOW)
        return nsa_core(qi, qpos, gi, kc, vc, cmp_end, slc_blocks, wkv, wpos)

    out = lax.map(one_block, (qb, gb, jnp.arange(nb)))
    return out.swapaxes(0, 1).reshape(bsz, s, NSA_HEADS * NSA_HEAD_DIM)


def nsa_sample(q, gates, kvc, kvs, kvw, cache_cmp, cache_slc, cache_win, page_table, cmp_w):
    db, t = q.shape[:2]
    past_len = page_table.shape[1] * cache_cmp.shape[1]

    def gather(cache):
        return cache[page_table].reshape((db, past_len) + cache.shape[2:])

    rows_cmp = jnp.concatenate([gather(cache_cmp).astype(kvc.dtype), kvc], axis=1)
    rows_slc = jnp.concatenate([gather(cache_slc).astype(kvs.dtype), kvs], axis=1)
    kc, vc, cmp_end = compress_kv(rows_cmp, cmp_w)
    slc_blocks = slc_blocks_of(rows_slc)
    wkv = jnp.concatenate([cache_win.astype(kvw.dtype), kvw], axis=1)
    w_rows = cache_win.shape[1]
    wpos = past_len - w_rows + jnp.arange(w_rows + t)
    qpos = past_len + jnp.arange(t)
    return nsa_core(q, qpos, gates, kc, vc, cmp_end, slc_blocks, wkv, wpos)


def ab_project(xn, pos, w_in):
    bsz, t = xn.shape[:2]
    proj = xn @ w_in
    z, xbc, dt_raw, q, kvc, kvs, kvw, gates = jnp.split(
        proj, [int(v) for v in np.cumsum(PROJ_SPLITS)[:-1]], axis=-1)
    q = rope(q.reshape(bsz, t, NSA_HEADS, NSA_HEAD_DIM), pos)

    def kv_form(u):
        u = u.reshape(bsz, t, 2, NSA_KV, NSA_HEAD_DIM)
        return jnp.stack([rope(u[:, :, 0], pos), u[:, :, 1]], axis=2)

    return z, xbc, dt_raw, q, kv_form(kvc), kv_form(kvs), kv_form(kvw), gates.reshape(bsz, t, NSA_HEADS, 3)


def mixer_ab(xn, pos, conv_hist, h0, nsa_fn, w_in, ssm_w, w_out):
    z, xbc, dt_raw, q, kvc, kvs, kvw, gates = ab_project(xn, pos, w_in)
    y_ssm, new_hist, h_T = mamba_branch(z, xbc, dt_raw, conv_hist, h0, *ssm_w)
    y_nsa = nsa_fn(q, gates, kvc, kvs, kvw)
    out = jnp.concatenate([y_ssm, y_nsa.astype(y_ssm.dtype)], axis=-1) @ w_out
    return out, new_hist, h_T, kvc, kvs, kvw


def conformer_conv(xn, hist, w_pw1, b_pw1, dw_w, dw_b, ln_g, ln_b, w_pw2, b_pw2):
    u = xn @ w_pw1 + b_pw1
    a, gt = jnp.split(u, 2, axis=-1)
    a = a * jax.nn.sigmoid(gt)
    ah = jnp.concatenate([hist.astype(a.dtype), a], axis=1)
    new_hist = ah[:, ah.shape[1] - (CONF_CONV - 1):]
    c = layer_norm(causal_dwconv(ah, dw_w, dw_b), ln_g, ln_b)
    return jax.nn.silu(c) @ w_pw2 + b_pw2, new_hist


def conv_ffn(xn, hist, w_up, conv_w, conv_b, w_down):
    u = xn @ w_up
    uh = jnp.concatenate([hist.astype(u.dtype), u], axis=1)
    new_hist = uh[:, uh.shape[1] - (FFN_CONV - 1):]
    gate, val = jnp.split(causal_dwconv(uh, conv_w, conv_b), 2, axis=-1)
    return (jax.nn.gelu(gate, approximate=True) * val) @ w_down, new_hist


def setup_inputs(seed: int = 0) -> dict:
    key = jax.random.key(seed)
    ks = iter(jax.random.split(key, 64))
    nrm = lambda shape, scale: jax.random.normal(next(ks), shape, jnp.float32) * scale
    n_pages = PAST_LEN // PAGE_SIZE
    n_phys = (DEC_BATCH * n_pages * 5) // 4
    w_rows = min(WINDOW, PAST_LEN)
    page_table = jax.random.permutation(next(ks), n_phys)[:DEC_BATCH * n_pages].reshape(
        DEC_BATCH, n_pages).astype(jnp.int32)
    dt0 = jnp.exp(jax.random.uniform(next(ks), (N_AB, SSM_HEADS), jnp.float32,
                                     math.log(1e-3), math.log(1e-1)))
    dt_bias = dt0 + jnp.log(-jnp.expm1(-dt0))
    a_log = jnp.log(jax.random.uniform(next(ks), (N_AB, SSM_HEADS), jnp.float32, 1.0, 16.0))
    kvshape = (2, NSA_KV, NSA_HEAD_DIM)
    return {
        'x_prompt': nrm((BATCH, SEQ, D_MODEL), 1.0),
        'x_sample': nrm((DEC_BATCH, DEC_SEQ, D_MODEL), 1.0),
        'state_ssm': nrm((N_AB, DEC_BATCH, SSM_HEADS, SSM_HEAD_DIM, SSM_STATE), 0.3),
        'state_ssm_conv': nrm((N_AB, DEC_BATCH, SSM_CONV - 1, CONV_DIM), 1.0),
        'cache_cmp': nrm((N_AB, n_phys, PAGE_SIZE) + kvshape, 1.0),
        'cache_slc': nrm((N_AB, n_phys, PAGE_SIZE) + kvshape, 1.0),
        'cache_win': nrm((N_AB, DEC_BATCH, w_rows) + kvshape, 1.0),
        'state_conf_conv': nrm((N_C, DEC_BATCH, CONF_CONV - 1, D_CONF), 0.5),
        'state_ffn_conv': nrm((DEPTH, DEC_BATCH, FFN_CONV - 1, 2 * D_FF), 1.0),
        'page_table': page_table,
        'norm_g': 1.0 + nrm((DEPTH, 4, D_MODEL), 0.01),
        'ab_w_in': nrm((N_AB, D_MODEL, D_PROJ), D_MODEL ** -0.5),
        'ab_conv_w': nrm((N_AB, SSM_CONV, CONV_DIM), SSM_CONV ** -0.5),
        'ab_conv_b': nrm((N_AB, CONV_DIM), 0.01),
        'ab_dt_bias': dt_bias,
        'ab_a_log': a_log,
        'ab_d_skip': 1.0 + nrm((N_AB, SSM_HEADS), 0.01),
        'ab_ssm_norm': 1.0 + nrm((N_AB, D_INNER), 0.01),
        'ab_cmp_pe': nrm((N_AB, 2, CMP_LEN, NSA_HEAD_DIM), 0.02),
        'ab_cmp_w1': nrm((N_AB, 2, CMP_LEN * NSA_HEAD_DIM, CMP_HIDDEN), (CMP_LEN * NSA_HEAD_DIM) ** -0.5),
        'ab_cmp_b1': nrm((N_AB, 2, CMP_HIDDEN), 0.01),
        'ab_cmp_w2': nrm((N_AB, 2, CMP_HIDDEN, NSA_HEAD_DIM), CMP_HIDDEN ** -0.5),
        'ab_cmp_b2': nrm((N_AB, 2, NSA_HEAD_DIM), 0.01),
        'ab_w_out': nrm((N_AB, D_MIX0, D_MODEL), D_MIX0 ** -0.5),
        'c_w_pw1': nrm((N_C, D_MODEL, 2 * D_CONF), D_MODEL ** -0.5),
        'c_b_pw1': nrm((N_C, 2 * D_CONF), 0.01),
        'c_dw_w': nrm((N_C, CONF_CONV, D_CONF), CONF_CONV ** -0.5),
        'c_dw_b': nrm((N_C, D_CONF), 0.01),
        'c_ln_g': 1.0 + nrm((N_C, D_CONF), 0.01),
        'c_ln_b': nrm((N_C, D_CONF), 0.01),
        'c_w_pw2': nrm((N_C, D_CONF, D_MODEL), D_CONF ** -0.5),
        'c_b_pw2': nrm((N_C, D_MODEL), 0.01),
        'ffn_w_up': nrm((DEPTH, D_MODEL, 2 * D_FF), D_MODEL ** -0.5),
        'ffn_conv_w': nrm((DEPTH, FFN_CONV, 2 * D_FF), FFN_CONV ** -0.5),
        'ffn_conv_b': nrm((DEPTH, 2 * D_FF), 0.01),
        'ffn_w_down': nrm((DEPTH, D_FF, D_MODEL), D_FF ** -0.5),
    }


def reference(x_prompt, x_sample, state_ssm, state_ssm_conv, cache_cmp, cache_slc, cache_win,
              state_conf_conv, state_ffn_conv, page_table, norm_g, ab_w_in, ab_conv_w, ab_conv_b,
              ab_dt_bias, ab_a_log, ab_d_skip, ab_ssm_norm, ab_cmp_pe, ab_cmp_w1, ab_cmp_b1,
              ab_cmp_w2, ab_cmp_b2, ab_w_out, c_w_pw1, c_b_pw1, c_dw_w, c_dw_b, c_ln_g, c_ln_b,
              c_w_pw2, c_b_pw2, ffn_w_up, ffn_conv_w, ffn_conv_b, ffn_w_down):
    bsz, s = x_prompt.shape[:2]
    db, t = x_sample.shape[:2]
    past_len = page_table.shape[1] * cache_cmp.shape[2]
    pos_p = jnp.arange(s, dtype=jnp.int32)
    pos_s = past_len + jnp.arange(t, dtype=jnp.int32)
    hp, hs = x_prompt, x_sample
    (ssm_p, ssm_s, sconv_p, sconv_s, cmp_p, cmp_s, slc_p, slc_s,
     win_p, win_s, conf_p, conf_s, ffn_p, ffn_s) = ([] for _ in range(14))
    for layer in range(DEPTH):
        g = norm_g[layer]
        if layer % 2 == 0:
            a = layer // 2
            cmp_w = (ab_cmp_pe[a], ab_cmp_w1[a], ab_cmp_b1[a], ab_cmp_w2[a], ab_cmp_b2[a])
            ssm_w = (ab_conv_w[a], ab_conv_b[a], ab_dt_bias[a], ab_a_log[a], ab_d_skip[a], ab_ssm_norm[a])
            mix, hist, h_T, kvc, kvs, kvw = mixer_ab(
                rms_norm(hp, g[0]), pos_p,
                jnp.zeros((bsz, SSM_CONV - 1, CONV_DIM), hp.dtype),
                jnp.zeros((bsz, SSM_HEADS, SSM_HEAD_DIM, SSM_STATE), hp.dtype),
                functools.partial(nsa_prompt, cmp_w=cmp_w), ab_w_in[a], ssm_w, ab_w_out[a])
            hp = hp + rms_norm(mix, g[1])
            ssm_p.append(h_T); sconv_p.append(hist); cmp_p.append(kvc); slc_p.append(kvs)
            win_p.append(kvw[:, s - min(WINDOW, s):])
            mix, hist, h_T, kvc, kvs, kvw = mixer_ab(
                rms_norm(hs, g[0]), pos_s, state_ssm_conv[a], state_ssm[a],
                functools.partial(nsa_sample, cache_cmp=cache_cmp[a], cache_slc=cache_slc[a],
                                  cache_win=cache_win[a], page_table=page_table, cmp_w=cmp_w),
                ab_w_in[a], ssm_w, ab_w_out[a])
            hs = hs + rms_norm(mix, g[1])
            ssm_s.append(h_T); sconv_s.append(hist); cmp_s.append(kvc); slc_s.append(kvs); win_s.append(kvw)
        else:
            c = layer // 2
            cw = (c_w_pw1[c], c_b_pw1[c], c_dw_w[c], c_dw_b[c], c_ln_g[c], c_ln_b[c], c_w_pw2[c], c_b_pw2[c])
            mix, hist = conformer_conv(rms_norm(hp, g[0]),
                                       jnp.zeros((bsz, CONF_CONV - 1, D_CONF), hp.dtype), *cw)
            hp = hp + rms_norm(mix, g[1])
            conf_p.append(hist)
            mix, hist = conformer_conv(rms_norm(hs, g[0]), state_conf_conv[c], *cw)
            hs = hs + rms_norm(mix, g[1])
            conf_s.append(hist)
        fw = (ffn_w_up[layer], ffn_conv_w[layer], ffn_conv_b[layer], ffn_w_down[layer])
        f, hist = conv_ffn(rms_norm(hp, g[2]), jnp.zeros((bsz, FFN_CONV - 1, 2 * D_FF), hp.dtype), *fw)
        hp = hp + rms_norm(f, g[3])
        ffn_p.append(hist)
        f, hist = conv_ffn(rms_norm(hs, g[2]), state_ffn_conv[layer], *fw)
        hs = hs + rms_norm(f, g[3])
        ffn_s.append(hist)
    return (hp, hs, jnp.stack(ssm_p), jnp.stack(ssm_s), jnp.stack(sconv_p), jnp.stack(sconv_s),
            jnp.stack(cmp_p), jnp.stack(cmp_s), jnp.stack(slc_p), jnp.stack(slc_s),
            jnp.stack(win_p), jnp.stack(win_s), jnp.stack(conf_p), jnp.stack(conf_s),
            jnp.stack(ffn_p), jnp.stack(ffn_s))
```

```python
import numpy as np
from contextlib import ExitStack
import concourse.bass as bass
import concourse.mybir as mybir
from concourse.bass_utils import run_bass_kernel_spmd

F32 = mybir.dt.float32
BF16 = mybir.dt.bfloat16
I32 = mybir.dt.int32
AF = mybir.ActivationFunctionType
ALU = mybir.AluOpType
AX = mybir.AxisListType
EPS = 1e-6


class Buf:
    __slots__ = ("name", "writer", "readers", "dsem", "dcount")

    def __init__(self, name):
        self.name = name
        self.writer = None
        self.readers = {}
        self.dsem = None
        self.dcount = None


class Sched:
    ENG = ("tensor", "vector", "scalar", "gpsimd", "sync")

    def __init__(self, nc, stack):
        self.nc = nc
        self.stack = stack
        self.eng = {"tensor": nc.tensor, "vector": nc.vector, "scalar": nc.scalar,
                    "gpsimd": nc.gpsimd, "sync": nc.sync}
        self.sem = {}
        self.cnt = {}
        self.semobj = {}
        for e in self.ENG:
            self.sem[e] = stack.enter_context(nc.semaphore("s_" + e))
            self.semobj[e] = self.sem[e]
            self.cnt[e] = 0
        self.waited = {e: {} for e in self.ENG}
        self.ndsem = 0
        self.n_wait = 0
        self.n_ops = 0
        self.dcnt = {}
        self.free_sems = {False: [], True: []}
        self.phase_sems = []

    def _dsem(self, buf, sw=False):
        attr = "dcount" if sw else "dsem"
        if not getattr(buf, attr):
            free = self.free_sems[sw]
            if free:
                name = free.pop(0)
            else:
                name = "d%d" % self.ndsem
                self.ndsem += 1
                self.semobj[name] = self.stack.enter_context(self.nc.semaphore(name))
            self.phase_sems.append((sw, name))
            setattr(buf, attr, name)
        return getattr(buf, attr)

    def recycle(self):
        for sw, name in self.phase_sems:
            self.free_sems[sw].append(name)
        self.phase_sems = []

    def share_sem(self, bufs):
        name = self._dsem(bufs[0])
        for b in bufs[1:]:
            b.dsem = name

    def _wait(self, e, tok):
        if tok is None:
            return
        key, val = tok
        if key in self.dcnt:
            val = self.dcnt[key]
        if self.waited[e].get(key, 0) >= val:
            return
        if key == e:
            if e == "tensor" or val <= self.cnt[e] - 2:
                return
        self.waited[e][key] = val
        self.eng[e].wait_ge(self.semobj[key], val)
        self.n_wait += 1

    def _deps(self, e, reads, writes):
        for b in reads:
            self._wait(e, b.writer)
        for b in writes:
            self._wait(e, b.writer)
            for t in list(b.readers.items()):
                self._wait(e, t)

    def op(self, e, fn, reads=(), writes=()):
        self._deps(e, reads, writes)
        ins = fn(self.eng[e])
        self.cnt[e] += 1
        self.n_ops += 1
        ins.then_inc(self.sem[e], 1)
        tok = (e, self.cnt[e])
        for b in reads:
            b.readers[e] = self.cnt[e]
        for b in writes:
            b.writer = tok
            b.readers = {}
        return ins

    def dma(self, q, out, in_, reads=(), writes=(), sembuf=None, **kw):
        self._deps(q, reads, writes)
        sb = sembuf if sembuf is not None else (writes[0] if writes else reads[0])
        name = self._dsem(sb, sw=(q == "gpsimd"))
        ins = self.eng[q].dma_start(out=out, in_=in_, **kw)
        self.n_ops += 1
        self.dcnt[name] = self.dcnt.get(name, 0) + 16
        ins.then_inc(self.semobj[name], 16)
        tok = (name, self.dcnt[name])
        for b in reads:
            b.readers[name] = self.dcnt[name]
        for b in writes:
            b.writer = tok
            b.readers = {}
        return ins

    def dma_custom(self, q, fn, reads=(), writes=(), sembuf=None):
        self._deps(q, reads, writes)
        sb = sembuf if sembuf is not None else (writes[0] if writes else reads[0])
        name = self._dsem(sb, sw=(q == "gpsimd"))
        ins = fn(self.eng[q])
        self.n_ops += 1
        self.dcnt[name] = self.dcnt.get(name, 0) + 16
        ins.then_inc(self.semobj[name], 16)
        tok = (name, self.dcnt[name])
        for b in reads:
            b.readers[name] = self.dcnt[name]
        for b in writes:
            b.writer = tok
            b.readers = {}
        return ins

    def barrier(self):
        snap = dict(self.cnt)
        dsn = dict(self.dcnt)
        for e in self.ENG:
            for f in self.ENG:
                if snap[f] > 0 and self.waited[e].get(f, 0) < snap[f]:
                    self.waited[e][f] = snap[f]
                    self.eng[e].wait_ge(self.semobj[f], snap[f])
                    self.n_wait += 1
            for name, val in dsn.items():
                self._wait(e, (name, val))

    def finish(self):
        for name, val in self.dcnt.items():
            self._wait("sync", (name, val))
        for e in self.ENG:
            if e != "sync" and self.cnt[e] > 0:
                self._wait("sync", (e, self.cnt[e]))


class Cfg:
    def __init__(self, D=1024, SEQ=4096, NH=16, TOPN=16, WINDOW=512, DFF=2816, DEC_PER_CORE=4,
                 PAST=8192, NPHYS=2560):
        self.D = D
        self.KC = D // 128
        self.SEQ = SEQ
        self.HALF = SEQ // 2
        self.NT = SEQ // 128
        self.NTH = self.NT // 2
        self.TF0 = self.NTH - 1
        self.DI = 2 * D
        self.HS = self.DI // 64
        self.G = 4
        self.HPGS = self.HS // 4
        self.NS = 128
        self.CD = self.DI + 2 * 4 * 128
        self.NH = NH
        self.KV = 2
        self.HPG = NH // 2
        self.DH = 64
        self.TOPN = TOPN
        self.WINDOW = WINDOW
        self.DFF = DFF
        self.DPROJ = self.DI + self.CD + self.HS + NH * 64 + 3 * 256 + 3 * NH
        self.DMIX = self.DI + NH * 64
        self.o_z = 0
        self.o_xbc = self.DI
        self.o_dt = self.DI + self.CD
        self.o_q = self.o_dt + self.HS
        self.o_kv = self.o_q + NH * 64
        self.o_g = self.o_kv + 768
        self.NSLC = SEQ // 64
        self.NCMP = SEQ // 16 - 1
        self.DEC = DEC_PER_CORE
        self.PAST = PAST
        self.NPHYS = NPHYS
        self.groups = []
        t = 0
        while t < self.TF0:
            n = min(5, self.TF0 - t)
            self.groups.append((list(range(t, t + n)), False))
            t += n
        first = True
        while t < self.NT:
            n = min(5 if first else 4, self.NT - t)
            self.groups.append((list(range(t, t + n)), True))
            t += n
            first = False


FULL = Cfg()


class Ctx:
    def __init__(self, nc, st, cfg):
        self.nc, self.st, self.cfg = nc, st, cfg
        self.S = Sched(nc, st)
        self.sb_bytes = 0
        self.ps = []
        for i in range(8):
            t = st.enter_context(nc.psum_tensor("ps%d" % i, [128, 512], F32))
            self.ps.append((t, Buf("ps%d" % i)))
        self.ps_rr = {}

    def sb(self, name, shape, dt=F32):
        t = self.st.enter_context(self.nc.sbuf_tensor("sb_" + name, list(shape), dt))
        n = 1
        for s in shape[1:]:
            n *= s
        self.sb_bytes += n * (4 if dt in (F32, I32) else 2)
        return t, Buf(name)

    def psum(self, pool):
        k = self.ps_rr.get(pool, 0)
        self.ps_rr[pool] = k + 1
        return self.ps[pool[k % len(pool)]]

    def dram(self, name, shape, dt=F32, kind="Internal"):
        return self.nc.dram_tensor(name, list(shape), dt, kind=kind).ap()


PS_MM = (0, 1, 2, 3)
PS_TR = (4, 5)
PS_X = (6, 7)


def build(cfg, dbg=()):
    nc = bass.Bass("TRN2", target_bir_lowering=False)
    st = ExitStack()
    with st:
        C = Ctx(nc, st, cfg)
        _emit(C, cfg, dbg)
    return nc


def _emit(C, cfg, dbg):
    nc, S = C.nc, C.S
    D, KC, NT, DI, HS, CD, NH, HPG = cfg.D, cfg.KC, cfg.NT, cfg.DI, cfg.HS, cfg.CD, cfg.NH, cfg.HPG
    HPGS = cfg.HPGS
    NCT = CD // 128
    NXT = DI // 128
    TF0, NTH = cfg.TF0, cfg.NTH
    GW = HPGS * 64
    NSLC, NBT = cfg.NSLC, (cfg.NCMP + 127) // 128
    NF = NT - TF0
    WT = cfg.WINDOW // 128
    CW = 65 + NSLC
    DMIX, DFF = cfg.DMIX, cfg.DFF
    NFT = 2 * DFF // 128
    NHT = DFF // 128

    def V(fn, r=(), w=()):
        return S.op("vector", fn, r, w)

    def A(fn, r=(), w=()):
        return S.op("scalar", fn, r, w)

    def P(fn, r=(), w=()):
        return S.op("gpsimd", fn, r, w)

    def T(fn, r=(), w=()):
        return S.op("tensor", fn, r, w)

    def bc3(ap2, n):
        return ap2.unsqueeze(2).to_broadcast([128, ap2.shape[1], n])

    din = lambda n, s, dt=F32: nc.dram_tensor(n, list(s), dt, kind="ExternalInput").ap()
    dout = lambda n, s, dt=F32: nc.dram_tensor(n, list(s), dt, kind="ExternalOutput").ap()
    dscr = lambda n, s, dt=F32: nc.dram_tensor(n, list(s), dt, kind="Internal").ap()
    xin = din("xin", [NT * 128, D])
    flag_d = din("flag", [128, 1])
    normg = din("normg", [8, D])
    w_in = din("w_in", [D, cfg.DPROJ])
    convw_d = din("convwT", [128, NCT, 4])
    convb_d = din("convbT", [128, NCT])
    dtb_d = din("dt_bias", [1, HS])
    alog_d = din("a_log", [1, HS])
    dskip_d = din("d_skip", [1, HS])
    ssmn_d = din("ssm_norm", [1, DI])
    cos_d = din("rcos", [NT * 128, 32])
    sin_d = din("rsin", [NT * 128, 32])
    idb_d = din("identb", [128, 128], BF16)
    idf_d = din("identf", [128, 128])
    utri_d = din("utri", [128, 128])
    negm_d = din("negm", [128, 512], BF16)
    cw1_d = din("cmp_w1", [2, 128, 32 * 128])
    cpe_d = din("cmp_peT", [2, 128, 32])
    cb1_d = din("cmp_b1T", [128, 2])
    cw2k_d = din("cmp_w2k", [2, 128, 128])
    cw2v_d = din("cmp_w2v", [128, 64])
    cb2k_d = din("cmp_b2k", [128, 1])
    cb2v_d = din("cmp_b2v", [1, 64])
    ovm_d = din("ovm", [NBT * 128, NSLC])
    maskc_d = din("maskc", [NBT, 128, NF * 128], BF16)
    fbias_d = din("fbias", [NF * 128, NSLC])
    exm_d = din("exm", [NSLC, NT * 128], BF16)
    caus_d = din("causT", [128, 128], BF16)
    band_d = din("bandT", [128, 128], BF16)

    w_out = din("w_out", [DMIX, D])
    w_up = din("w_up", [2, D, 2 * DFF])
    w_down = din("w_down", [2, DFF, D])
    fcw_d = din("ffn_cwT", [2, 128, NFT, 3])
    fcb_d = din("ffn_cbT", [2, 128, NFT])
    w_pw1 = din("w_pw1", [D, 2 * D])
    bpw1_d = din("b_pw1T", [128, 2 * KC])
    dww_d = din("dw_wT", [128, KC, 31])
    dwb_d = din("dw_bT", [128, KC])
    lng_d = din("ln_gT", [128, KC])
    lnb_d = din("ln_bT", [128, KC])
    w_pw2 = din("w_pw2", [D, D])
    bpw2_d = din("b_pw2", [1, D])
    conf_p = dout("conf_p", [30, D])
    ffn_p = dout("ffn_p", [2, 2, 2 * DFF])
    y_p = dout("y_p", [cfg.HALF, D])
    ssm_p = dout("ssm_p", [DI, 128])
    sconv_p = dout("sconv_p", [3, CD])
    cmp_p = dout("cmp_p", [cfg.HALF, 256])
    slc_p = dout("slc_p", [cfg.HALF, 256])
    NWT = min(cfg.WINDOW, cfg.HALF) // 128
    win_p = dout("win_p", [NWT * 128, 256])
    dbg_o = {}
    for name, shape in dbg:
        dbg_o[name] = dout(name, shape)

    XS_d = dscr("XS_d", [NT * 128, DI], BF16)
    BTm_d = dscr("BTm_d", [NT * 128, 512], BF16)
    BT_d = dscr("BT_d", [4, 128, NT * 128], BF16)
    CT_d = dscr("CT_d", [4, 128, NT * 128], BF16)
    dtr_d = dscr("dtr_d", [NT * 128, HS])
    zs_d = dscr("zs_d", [NF * 128, DI], BF16)
    qb_d = dscr("qb_d", [NF * 128, NH * 64], BF16)
    gs_d = dscr("gs_d", [NF * 128, 3 * NH])
    kT_d = dscr("kT_d", [3, 128, NT * 128], BF16)
    vT_d = dscr("vT_d", [128, NT * 128], BF16)
    v_d = dscr("v_d", [2, NT * 128, 128], BF16)
    ycat_d = dscr("ycat_d", [NF * 128, DMIX], BF16)
    bXS_d, bBTm_d, bBT_d, bCT_d, bdtr_d, bzs_d, bqb_d, bgs_d, bkT_d, bvT_d, bv_d, bycat_d = [Buf("scr%d" % i) for i in range(12)]

    identb, b_identb = C.sb("identb", [128, 128], BF16)
    identf, b_identf = C.sb("identf", [128, 128], F32)
    onesf, b_onesf = C.sb("onesf", [128, 128], F32)
    flag, b_flag = C.sb("flag", [128, 1], F32)
    gtab, b_gtab = C.sb("gtab", [128, D], F32)
    bconst = Buf("consts")
    for (dst, src) in ((identb, idb_d), (identf, idf_d), (flag, flag_d)):
        S.dma("sync", dst[:], src, writes=[bconst])
    b_identb = b_identf = b_flag = bconst
    P(lambda e: e.memset(onesf[:], 1.0), [], [b_onesf])

    def load_gain(j):
        S.dma("sync", gtab[:], normg[j:j + 1, :].partition_broadcast(128), writes=[b_gtab])

    class Phase:
        def __enter__(self):
            self.old = C.st
            self.es = ExitStack()
            self.es.__enter__()
            C.st = self.es
            self.base = C.sb_bytes
            return self

        def __exit__(self, *a):
            S.barrier()
            S.recycle()
            print("phase sbuf KB", (C.sb_bytes - self.base) / 1024, "ops", S.n_ops, "sems", S.ndsem)
            C.sb_bytes = self.base
            C.st = self.old
            return self.es.__exit__(*a)

    def transpose_to(dst_fn, src_fn, n, b_src, b_dst, eng="scalar"):
        for c0 in range(0, n, 8):
            cnt = min(8, n - c0)
            pt, bpt = C.psum(PS_TR)
            ptb = pt[:].bitcast(BF16)
            for c in range(cnt):
                T(lambda e, c=c: e.transpose(ptb[:, c * 128:(c + 1) * 128], src_fn(c0 + c), identb[:]),
                  [b_src, b_identb], [bpt])
            src = ptb[:, 0:cnt * 128].rearrange("p (c t) -> p c t", c=cnt)
            if eng == "scalar":
                A(lambda e: e.copy(dst_fn(c0, cnt), src), [bpt], [b_dst])
            else:
                V(lambda e: e.tensor_copy(dst_fn(c0, cnt), src), [bpt], [b_dst])

    GA = 8
    groupsA = []
    t = 0
    while t < TF0:
        n = min(GA, TF0 - t)
        groupsA.append((list(range(t, t + n)), False))
        t += n
    while t < NT:
        n = min(GA, NT - t)
        if NT - t - n == 1:
            n += 1
        groupsA.append((list(range(t, t + n)), True))
        t += n
    GAM = max(len(g[0]) for g in groupsA)
    with Phase():
        convw, b_convw = C.sb("convw", [128, NCT, 4], F32)
        convb, b_convb = C.sb("convb", [128, NCT], F32)
        for (dst, src) in ((convw, convw_d), (convb, convb_d)):
            S.dma("sync", dst[:], src, writes=[bconst])
        b_convw = b_convb = bconst
        xt = [C.sb("xt%d" % i, [128, D], F32) for i in range(2)]
        junk, b_junk = C.sb("junk", [128, D], F32)
        xnb = [C.sb("xnb%d" % i, [128, D], BF16) for i in range(2)]
        stat, b_stat = C.sb("stat", [128, 16], F32)
        xnT, b_xnT = C.sb("xnT", [128, KC, GAM * 128], BF16)
        wbf = [C.sb("wbf%d" % i, [128, 8, 512], BF16) for i in range(3)]
        wrot = [0]
        cb = [C.sb("cb%d" % i, [128, 3 + GAM * 128], F32) for i in range(2)]
        acc = [C.sb("acc%d" % i, [128, GAM * 128], F32) for i in range(2)]
        hist, b_hist = C.sb("hist", [128, NCT, 3], F32)
        xcT = [C.sb("xcT%d" % i, [128, GAM * 128], BF16) for i in range(3)]
        xst = [C.sb("xst%d" % i, [128, GAM, 512], BF16) for i in range(2)]
        rcos, b_rcos = C.sb("rcos", [128, GAM, 32], F32)
        rsin, b_rsin = C.sb("rsin", [128, GAM, 32], F32)
        kvo = [C.sb("kvo%d" % i, [128, 256], F32) for i in range(3)]
        kvb = [C.sb("kvb%d" % i, [128, 256], BF16) for i in range(3)]
        kTs = [C.sb("kTs%d" % i, [128, 256], BF16) for i in range(3)]
        rtmp = [C.sb("rtmp%d" % i, [128, 512], F32) for i in range(2)]
        zst = [C.sb("zst%d" % i, [128, 512], BF16) for i in range(3)]
        dst4 = [C.sb("dst4%d" % i, [128, max(HS, 3 * NH)], F32) for i in range(3)]
        V(lambda e: e.memset(hist[:], 0.0), [], [b_hist])

        def load_w(W, r0, kcn, c0, ncols):
            k = wrot[0] % 3
            wrot[0] += 1
            wb, bwb = wbf[k]
            src = W[r0 * 128:(r0 + kcn) * 128, c0:c0 + ncols].rearrange("(kc p) c -> p kc c", p=128)
            S.dma("gpsimd", wb[:, :kcn, :ncols], src, writes=[bwb])
            return wb, bwb

        def rms_to_T(src_ap, b_src, i, k, dT, b_dT):
            xb, bxb = xnb[k % 2]
            A(lambda e: e.activation(junk[:, :D], src_ap, AF.Square, accum_out=stat[:, 0:1]), [b_src], [b_junk, b_stat])
            A(lambda e: e.activation(stat[:, 1:2], stat[:, 0:1], AF.Sqrt, scale=1.0 / D, bias=EPS), [b_stat], [b_stat])
            V(lambda e: e.reciprocal(stat[:, 2:3], stat[:, 1:2]), [b_stat], [b_stat])
            V(lambda e: e.scalar_tensor_tensor(xb[:], src_ap, stat[:, 2:3], gtab[:], op0=ALU.mult, op1=ALU.mult),
              [b_src, b_stat, b_gtab], [bxb])
            transpose_to(lambda c0, cnt: dT[:, c0:c0 + cnt, i * 128:(i + 1) * 128],
                         lambda c: xb[:, c * 128:(c + 1) * 128], KC, bxb, b_dT)

        def rope(dst, src, i, nh, b_src, b_dst):
            s3 = src.rearrange("p (g d) -> p g d", g=nh)
            d3 = dst.rearrange("p (g d) -> p g d", g=nh)
            x1, x2 = s3[:, :, 0:32], s3[:, :, 32:64]
            o1, o2 = d3[:, :, 0:32], d3[:, :, 32:64]
            cs = rcos[:, i, :].unsqueeze(1).to_broadcast([128, nh, 32])
            sn = rsin[:, i, :].unsqueeze(1).to_broadcast([128, nh, 32])
            ta = rtmp[0][0][:, 0:nh * 32].rearrange("p (g d) -> p g d", g=nh)
            tb = rtmp[1][0][:, 0:nh * 32].rearrange("p (g d) -> p g d", g=nh)
            bta, btb = rtmp[0][1], rtmp[1][1]
            V(lambda e: e.tensor_tensor(ta, x1, cs, ALU.mult), [b_src, b_rcos], [bta])
            V(lambda e: e.tensor_tensor(tb, x2, sn, ALU.mult), [b_src, b_rsin], [btb])
            V(lambda e: e.tensor_tensor(o1, ta, tb, ALU.subtract), [bta, btb], [b_dst])
            V(lambda e: e.tensor_tensor(ta, x2, cs, ALU.mult), [b_src, b_rcos], [bta])
            V(lambda e: e.tensor_tensor(tb, x1, sn, ALU.mult), [b_src, b_rsin], [btb])
            V(lambda e: e.tensor_tensor(o2, ta, tb, ALU.add), [bta, btb], [b_dst])

        load_gain(0)
        rr = [0]
        for gi, (tiles, full) in enumerate(groupsA):
            ng = len(tiles)
            NG = ng * 128
            t0 = tiles[0]
            chunks = [(c, min(512, NG - c)) for c in range(0, NG, 512)]
            for i, t in enumerate(tiles):
                src, bsrc = xt[i % 2][0][:], xt[i % 2][1]
                S.dma("sync", src, xin[t * 128:(t + 1) * 128, :], writes=[bsrc])
                rms_to_T(src, bsrc, i, i, xnT, b_xnT)
            S.dma("sync", rcos[:, :ng, :], cos_d[t0 * 128:(t0 + ng) * 128, :].rearrange("(i p) c -> p i c", p=128), writes=[b_rcos])
            S.dma("sync", rsin[:, :ng, :], sin_d[t0 * 128:(t0 + ng) * 128, :].rearrange("(i p) c -> p i c", p=128), writes=[b_rsin])
            for mb in range(0, NCT, 4):
                nm = min(4, NCT - mb)
                wb, bwb = load_w(w_in, 0, KC, cfg.o_xbc + mb * 128, nm * 128)
                st_, bst_ = xst[(mb // 4) % 2]
                for j in range(nm):
                    m = mb + j
                    cbt, bcb = cb[m % 2]
                    act, bacc = acc[m % 2]
                    for (c0, cn) in chunks:
                        pt, bpt = C.psum(PS_MM)
                        for kc in range(KC):
                            T(lambda e, kc=kc: e.matmul(pt[:, :cn], wb[:, kc, j * 128:(j + 1) * 128], xnT[:, kc, c0:c0 + cn],
                                                        start=(kc == 0), stop=(kc == KC - 1)), [bwb, b_xnT], [bpt])
                        A(lambda e: e.copy(cbt[:, 3 + c0:3 + c0 + cn], pt[:, :cn]), [bpt], [bcb])
                    P(lambda e: e.tensor_copy(cbt[:, 0:3], hist[:, m, :]), [b_hist], [bcb])
                    P(lambda e: e.tensor_copy(hist[:, m, :], cbt[:, NG:NG + 3]), [bcb], [b_hist])
                    V(lambda e: e.tensor_scalar(act[:, :NG], cbt[:, 0:NG], convw[:, m, 0:1], None, op0=ALU.mult),
                      [bcb, b_convw], [bacc])
                    for k in range(1, 4):
                        V(lambda e, k=k: e.scalar_tensor_tensor(act[:, :NG], cbt[:, k:k + NG], convw[:, m, k:k + 1], act[:, :NG],
                                                                op0=ALU.mult, op1=ALU.add), [bcb, b_convw, bacc], [bacc])
                    xc, bxc = xcT[m % 3]
                    A(lambda e: e.activation(xc[:, :NG], act[:, :NG], AF.Silu, bias=convb[:, m:m + 1]), [bacc, b_convb], [bxc])
                    if m < NXT + 4:
                        transpose_to(lambda c0, cnt: st_[:, c0:c0 + cnt, j * 128:(j + 1) * 128],
                                     lambda c: xc[:, c * 128:(c + 1) * 128], ng, bxc, bst_, eng="vector")
                    if m >= NXT:
                        g = (m - NXT) % 4
                        dstd, bd = (BT_d, bBT_d) if m < NXT + 4 else (CT_d, bCT_d)
                        S.dma("sync", dstd[g, :, t0 * 128:t0 * 128 + NG], xc[:, :NG], reads=[bxc], writes=[bd], sembuf=bxc)
                if mb < NXT:
                    S.dma("sync", XS_d[t0 * 128:t0 * 128 + NG, mb * 128:(mb + nm) * 128].rearrange("(i p) c -> p i c", p=128),
                          st_[:, :ng, :nm * 128], reads=[bst_], writes=[bXS_d], sembuf=bst_)
                elif mb < NXT + 4:
                    S.dma("sync", BTm_d[t0 * 128:t0 * 128 + NG, :].rearrange("(i p) c -> p i c", p=128),
                          st_[:, :ng, :512], reads=[bst_], writes=[bBTm_d], sembuf=bst_)
            blocks = [("dt", cfg.o_dt, HS)]
            for br in range(3):
                blocks.append(("kv%d" % br, cfg.o_kv + br * 256, 256))
            if full:
                for c0 in range(0, DI, 512):
                    blocks.append(("z", c0, min(512, DI - c0)))
                for c0 in range(0, NH * 64, 512):
                    blocks.append(("q", cfg.o_q + c0, min(512, NH * 64 - c0)))
                blocks.append(("g", cfg.o_g, 3 * NH))
            for (kind, c0, cn) in blocks:
                wb, bwb = load_w(w_in, 0, KC, c0, cn)
                for i, t in enumerate(tiles):
                    rows = slice(t * 128, (t + 1) * 128)
                    frows = slice((t - TF0) * 128, (t - TF0 + 1) * 128)
                    pt, bpt = C.psum(PS_MM)
                    for kc in range(KC):
                        T(lambda e, kc=kc: e.matmul(pt[:, :cn], xnT[:, kc, i * 128:(i + 1) * 128], wb[:, kc, :cn],
                                                    start=(kc == 0), stop=(kc == KC - 1)), [bwb, b_xnT], [bpt])
                    k3 = rr[0] % 3
                    rr[0] += 1
                    if kind == "dt":
                        d4, bd4 = dst4[k3]
                        A(lambda e: e.copy(d4[:, :HS], pt[:, :cn]), [bpt], [bd4])
                        S.dma("sync", dtr_d[rows, :], d4[:, :HS], reads=[bd4], writes=[bdtr_d], sembuf=bd4)
                    elif kind == "g":
                        d4, bd4 = dst4[k3]
                        A(lambda e: e.activation(d4[:, :3 * NH], pt[:, :cn], AF.Sigmoid), [bpt], [bd4])
                        S.dma("sync", gs_d[frows, :], d4[:, :3 * NH], reads=[bd4], writes=[bgs_d], sembuf=bd4)
                    elif kind == "z":
                        z_, bz_ = zst[k3]
                        A(lambda e: e.activation(z_[:, :cn], pt[:, :cn], AF.Silu), [bpt], [bz_])
                        S.dma("sync", zs_d[frows, c0:c0 + cn], z_[:, :cn], reads=[bz_], writes=[bzs_d], sembuf=bz_)
                    elif kind == "q":
                        z_, bz_ = zst[k3]
                        rope(z_[:, :cn], pt[:, :cn], i, cn // 64, bpt, bz_)
                        qd4 = qb_d[frows, :].rearrange("t (hh g d) -> t hh g d", g=2, d=64)
                        hs0 = (c0 - cfg.o_q) // 64
                        for g in range(2):
                            lo, hi = max(hs0, g * HPG), min(hs0 + cn // 64, (g + 1) * HPG)
                            if hi > lo:
                                S.dma("sync", qd4[:, lo - g * HPG:hi - g * HPG, g, :],
                                      z_[:, (lo - hs0) * 64:(hi - hs0) * 64].rearrange("p (h d) -> p h d", d=64),
                                      reads=[bz_], writes=[bqb_d], sembuf=bz_)
                    else:
                        br = int(kind[2])
                        kv, bkv = kvo[k3]
                        kb, bkb = kvb[k3]
                        kT_, bkT_ = kTs[k3]
                        rope(kv[:, 0:128], pt[:, 0:128], i, 2, bpt, bkv)
                        A(lambda e: e.copy(kv[:, 128:256], pt[:, 128:256]), [bpt], [bkv])
                        dsto = (cmp_p, slc_p, win_p)[br]
                        if br < 2 and t >= NTH:
                            r = (t - NTH) * 128
                            S.dma("sync", dsto[r:r + 128, :], kv[:], reads=[bkv], sembuf=bkv)
                        if br == 2 and t >= NT - NWT:
                            r = (t - (NT - NWT)) * 128
                            S.dma("sync", dsto[r:r + 128, :], kv[:], reads=[bkv], sembuf=bkv)
                        V(lambda e: e.tensor_copy(kb[:], kv[:]), [bkv], [bkb])
                        ntr = 2 if br == 0 else 1
                        pt2, bpt2 = C.psum(PS_TR)
                        ptb2 = pt2[:].bitcast(BF16)
                        for j2 in range(ntr):
                            T(lambda e, j2=j2: e.transpose(ptb2[:, j2 * 128:(j2 + 1) * 128], kb[:, j2 * 128:(j2 + 1) * 128], identb[:]),
                              [bkb, b_identb], [bpt2])
                        A(lambda e: e.copy(kT_[:, :ntr * 128], ptb2[:, :ntr * 128]), [bpt2], [bkT_])
                        S.dma("sync", kT_d[br, :, rows], kT_[:, 0:128], reads=[bkT_], writes=[bkT_d], sembuf=bkT_)
                        if br == 0:
                            S.dma("sync", vT_d[:, rows], kT_[:, 128:256], reads=[bkT_], writes=[bvT_d], sembuf=bkT_)
                        else:
                            S.dma("sync", v_d[br - 1, rows, :], kb[:, 128:256], reads=[bkb], writes=[bv_d], sembuf=bkb)
        sc, b_sc = C.sb("sc", [3, CD], F32)
        for m in range(NCT):
            pt, bpt = C.psum(PS_MM)
            T(lambda e: e.transpose(pt[:3, :128], hist[:, m, :], identf[:]), [b_hist, b_identf], [bpt])
            V(lambda e: e.tensor_copy(sc[:, m * 128:(m + 1) * 128], pt[:3, :128]), [bpt], [b_sc])
        S.dma("sync", sconv_p, sc[:], reads=[b_sc], sembuf=b_sc)

    if getattr(cfg, "stop", None) == "A":
        S.finish()
        return
    with Phase():
        utri, b_utri = C.sb("utri", [128, 128], F32)
        negm, b_negm = C.sb("negm", [128, 512], BF16)
        dtb, b_dtb = C.sb("dtb", [128, HS], F32)
        atab, b_atab = C.sb("atab", [128, HS], F32)
        dskip, b_dskip = C.sb("dskip", [128, HS], F32)
        ssmn, b_ssmn = C.sb("ssmn", [128, DI], F32)
        KTs, b_KTs = C.sb("KTs", [128, NT * 128], BF16)
        Vs, b_Vs = C.sb("Vs", [128, NT, 2, 65], BF16)
        KTw, b_KTw = C.sb("KTw", [128, NT * 128], BF16)
        Vw, b_Vw = C.sb("Vw", [128, NT, 2, 65], BF16)
        hidT, b_hidT = C.sb("hidT", [128, 2, 2, NBT * 128], BF16)
        kcT, b_kcT = C.sb("kcT", [128, NBT * 128], BF16)
        vca, b_vca = C.sb("vca", [128, NBT, 2, CW], BF16)
        cpe, b_cpe = C.sb("cpe", [128, 2, 32], BF16)
        cb1, b_cb1 = C.sb("cmpb1", [128, 2], F32)
        b1e, b_b1e = C.sb("b1e", [128, 2], F32)
        cw2k, b_cw2k = C.sb("cw2k", [128, 2, 128], BF16)
        cw2v, b_cw2v = C.sb("cw2v", [128, 64], BF16)
        cb2k, b_cb2k = C.sb("cb2k", [128, 1], F32)
        cb2v, b_cb2v = C.sb("cb2v", [128, 64], F32)
        ovt, b_ovt = C.sb("ovt", [128, NBT, NSLC], F32)
        exm, b_exm = C.sb("exm", [NSLC, NT * 128], BF16)
        causT, b_causT = C.sb("causT", [128, 128], BF16)
        bandT, b_bandT = C.sb("bandT", [128, 128], BF16)
        onesb, b_onesb = C.sb("onesb", [128, 128], BF16)
        bc2 = Buf("constsM")
        V(lambda e: e.memset(onesb[:], 1.0), [], [bc2])
        for (dst, src) in ((utri, utri_d), (negm, negm_d), (dtb, dtb_d.partition_broadcast(128)),
                           (atab, alog_d.partition_broadcast(128)), (dskip, dskip_d.partition_broadcast(128)),
                           (ssmn, ssmn_d.partition_broadcast(128)), (cb1, cb1_d), (cb2k, cb2k_d),
                           (cb2v, cb2v_d.partition_broadcast(128)), (ovt, ovm_d.rearrange("(b p) j -> p b j", p=128)),
                           (exm, exm_d), (causT, caus_d), (bandT, band_d),
                           (KTs, kT_d[1]), (KTw, kT_d[2])):
            S.dma("sync", dst[:], src, reads=[bkT_d], writes=[bc2])
        for (dst, src) in ((cpe, cpe_d.rearrange("k p l -> p k l")), (cw2k, cw2k_d.rearrange("g c m -> c g m")), (cw2v, cw2v_d)):
            S.dma("gpsimd", dst[:], src, writes=[bc2])
        b_utri = b_negm = b_dtb = b_atab = b_dskip = b_ssmn = b_cb1 = b_cb2k = b_cb2v = b_ovt = b_exm = b_causT = b_bandT = bc2
        b_KTs = b_KTw = b_cpe = b_cw2k = b_cw2v = bc2
        A(lambda e: e.activation(atab[:], atab[:], AF.Exp), [bc2], [bc2])
        V(lambda e: e.tensor_scalar(atab[:], atab[:], -1.0, None, op0=ALU.mult), [bc2], [bc2])
        for br, (Vx, b_Vx) in enumerate(((Vs, b_Vs), (Vw, b_Vw))):
            for g in range(2):
                S.dma("sync", Vx[:, :, g, 0:64], v_d[br, :, g * 64:(g + 1) * 64].rearrange("(t p) d -> p t d", p=128), reads=[bv_d], writes=[b_Vx])
            V(lambda e: e.memset(Vx[:, :, :, 64:65], 1.0), [b_Vx], [b_Vx])
            npre = TF0 + 1
            V(lambda e: e.tensor_scalar(Vx[:, 0:npre, :, :], Vx[:, 0:npre, :, :], flag[:, 0:1], None, op0=ALU.mult), [b_Vx, b_flag], [b_Vx])
        NCMP = cfg.NCMP
        V(lambda e: e.memset(hidT[:], 0.0), [], [b_hidT])
        V(lambda e: e.memset(kcT[:], 0.0), [], [b_kcT])
        V(lambda e: e.memset(vca[:], 0.0), [], [b_vca])
        V(lambda e: e.memset(vca[:, :, :, 64:65], 1.0), [], [b_vca])
        for g in range(2):
            V(lambda e, g=g: e.tensor_copy(vca[:, :, g, 65:CW], ovt[:]), [b_ovt], [b_vca])
        with ExitStack() as es2:
            old = C.st
            C.st = es2
            craw = [C.sb("craw%d" % i, [128, NT * 128], BF16) for i in range(2)]
            w1b = [C.sb("w1b%d" % i, [128, 32, 128], BF16) for i in range(2)]
            S.dma("sync", craw[0][0][:], kT_d[0], reads=[bkT_d], writes=[craw[0][1]])
            S.dma("sync", craw[1][0][:], vT_d, reads=[bvT_d], writes=[craw[1][1]])
            for kvi in range(2):
                w1v, bw1 = w1b[kvi]
                S.dma("gpsimd", w1v[:].rearrange("p l c -> p (l c)"), cw1_d[kvi], writes=[bw1])
                pb1, bpb1 = C.psum(PS_MM)
                for l in range(32):
                    T(lambda e, l=l: e.matmul(pb1[:, 0:1], w1v[0:64, l, :], cpe[0:64, kvi, l:l + 1], start=(l == 0), stop=(l == 31)),
                      [bw1, b_cpe], [bpb1])
                V(lambda e: e.tensor_tensor(b1e[:, kvi:kvi + 1], pb1[:, 0:1], cb1[:, kvi:kvi + 1], ALU.add), [bpb1, b_cb1], [b_b1e])
                cr, bcr = craw[kvi]
                for g in range(2):
                    gsl2 = slice(g * 64, (g + 1) * 64)
                    for b0 in range(0, NCMP, 512):
                        nb = min(512, NCMP - b0)
                        ph, bph = C.psum(PS_MM)
                        for l in range(32):
                            T(lambda e, l=l: e.matmul(ph[:, :nb], w1v[gsl2, l, :], cr[gsl2, 16 * b0 + l:16 * b0 + l + 16 * (nb - 1) + 1:16],
                                                      start=(l == 0), stop=(l == 31)), [bw1, bcr], [bph])
                        A(lambda e: e.activation(hidT[:, kvi, g, b0:b0 + nb], ph[:, :nb], AF.Silu, bias=b1e[:, kvi:kvi + 1]),
                          [bph, b_b1e], [b_hidT])
            for b0 in range(0, NCMP, 512):
                nb = min(512, NCMP - b0)
                pk, bpk = C.psum(PS_MM)
                for g in range(2):
                    T(lambda e, g=g: e.matmul(pk[:, :nb], cw2k[:, g, :], hidT[:, 0, g, b0:b0 + nb], start=(g == 0), stop=(g == 1)),
                      [b_cw2k, b_hidT], [bpk])
                A(lambda e: e.activation(kcT[:, b0:b0 + nb], pk[:, :nb], AF.Identity, bias=cb2k[:, 0:1]), [bpk, b_cb2k], [b_kcT])
            for bt in range(NBT):
                pv2, bpv2 = C.psum(PS_MM)
                for g in range(2):
                    T(lambda e, g=g: e.matmul(pv2[:, g * 64:(g + 1) * 64], hidT[:, 1, g, bt * 128:(bt + 1) * 128], cw2v[:], start=True, stop=True),
                      [b_hidT, b_cw2v], [bpv2])
                V(lambda e: e.tensor_tensor(vca[:, bt, :, 0:64], pv2[:, 0:128].rearrange("p (g d) -> p g d", g=2),
                                            cb2v[:].unsqueeze(1).to_broadcast([128, 2, 64]), ALU.add), [bpv2, b_cb2v], [b_vca])
            S.barrier()
            C.st = old
        XSt = [C.sb("XSt%d" % i, [128, DI], BF16) for i in range(2)]
        BTmt = [C.sb("BTmt%d" % i, [128, 512], BF16) for i in range(2)]
        BTt = [C.sb("BTt%d" % i, [128, 4, 128], BF16) for i in range(2)]
        CTt = [C.sb("CTt%d" % i, [128, 4, 128], BF16) for i in range(2)]
        dtrt = [C.sb("dtrt%d" % i, [128, HS], F32) for i in range(2)]
        zst_ = [C.sb("zst_%d" % i, [128, DI], BF16) for i in range(2)]
        qbt = [C.sb("qbt%d" % i, [128, NH * 64], BF16) for i in range(2)]
        gst = [C.sb("gst%d" % i, [128, 3 * NH], F32) for i in range(2)]
        mct = [C.sb("mct%d" % i, [128, NBT, 128], BF16) for i in range(2)]
        fbt = [C.sb("fbt%d" % i, [128, NSLC], F32) for i in range(2)]
        sm = {}
        for nm in ("dt", "ax", "ex", "dA", "cum", "ncum", "dec", "te", "ec", "coef", "cml"):
            sm[nm] = C.sb("sm_" + nm, [128, HS], F32)
        stat, b_stat = C.sb("statm", [128, 16], F32)
        junk, b_junk = C.sb("junkm", [128, GW], F32)
        Hs, b_Hs = C.sb("Hs", [128, DI], F32)
        Hb, b_Hb = C.sb("Hb", [128, DI], BF16)
        xw, b_xw = C.sb("xw", [128, DI], BF16)
        xdt = [C.sb("xdt%d" % i, [128, GW], BF16) for i in range(2)]
        Rg = [C.sb("Rg%d" % i, [128, HPGS * 128], F32) for i in range(2)]
        LTg = [C.sb("LTg%d" % i, [128, HPGS * 128], F32) for i in range(2)]
        Wg = [C.sb("Wg%d" % i, [128, HPGS * 128], BF16) for i in range(2)]
        yt1 = [C.sb("yt1_%d" % i, [128, GW], F32) for i in range(2)]
        yt2 = [C.sb("yt2_%d" % i, [128, GW], F32) for i in range(2)]
        yz, b_yz = C.sb("yz", [128, DI], F32)
        ycat = [C.sb("ycat%d" % i, [128, DMIX], BF16) for i in range(2)]
        qT, b_qT = C.sb("qT", [128, HPG, 128], BF16)
        Eb = [C.sb("Eb%d" % i, [128, HPG * 128], F32) for i in range(2)]
        PTb = [C.sb("PTb%d" % i, [128, HPG * 128], BF16) for i in range(2)]
        mdiag = [C.sb("mdiag%d" % i, [128, 128], F32) for i in range(2)]
        oacc, b_oacc = C.sb("oacc", [128, NH * 64], F32)
        otmp, b_otmp = C.sb("otmp", [128, HPG * 64], F32)
        nst, b_nst = C.sb("nst", [128, 4, HPG], F32)
        impg, b_impg = C.sb("impg", [128, NSLC], F32)
        wk1, b_wk1 = C.sb("wk1", [128, NSLC], F32)
        mx8, b_mx8 = C.sb("mx8", [128, 8], F32)
        selb, b_selb = C.sb("selb", [128, NSLC], BF16)
        selT, b_selT = C.sb("selT", [NSLC, 128], BF16)
        V(lambda e: e.memset(Hs[:], 0.0), [], [b_Hs])
        V(lambda e: e.memset(Hb[:], 0.0), [], [b_Hb])

        def f2(nm):
            return sm[nm][0][:]

        def b2(nm):
            return sm[nm][1]

        def load_tile(t):
            k = t % 2
            rows = slice(t * 128, (t + 1) * 128)
            S.dma("sync", XSt[k][0][:], XS_d[rows, :], reads=[bXS_d], writes=[XSt[k][1]])
            S.dma("sync", BTmt[k][0][:], BTm_d[rows, :], reads=[bBTm_d], writes=[BTmt[k][1]])
            S.dma("sync", dtrt[k][0][:], dtr_d[rows, :], reads=[bdtr_d], writes=[dtrt[k][1]])
            if t >= TF0:
                fr = slice((t - TF0) * 128, (t - TF0 + 1) * 128)
                S.dma("sync", BTt[k][0][:], BT_d[:, :, rows].rearrange("g n t -> n g t"), reads=[bBT_d], writes=[BTt[k][1]])
                S.dma("sync", CTt[k][0][:], CT_d[:, :, rows].rearrange("g n t -> n g t"), reads=[bCT_d], writes=[CTt[k][1]])
                S.dma("sync", zst_[k][0][:], zs_d[fr, :], reads=[bzs_d], writes=[zst_[k][1]])
                S.dma("sync", qbt[k][0][:], qb_d[fr, :], reads=[bqb_d], writes=[qbt[k][1]])
                S.dma("sync", gst[k][0][:], gs_d[fr, :], reads=[bgs_d], writes=[gst[k][1]])
                S.dma("sync", mct[k][0][:], maskc_d[:, :, fr].rearrange("b p q -> p b q"), writes=[mct[k][1]])
                S.dma("sync", fbt[k][0][:], fbias_d[fr, :], writes=[fbt[k][1]])

        load_tile(0)
        for t in range(NT if getattr(cfg, "stop", None) != "M0" else 0):
            if t + 1 < NT:
                load_tile(t + 1)
            k = t % 2
            full = t >= TF0 and getattr(cfg, "stop", None) != "M1"
            XS, b_XS = XSt[k]
            BTm, b_BTm = BTmt[k]
            BT, b_BT = BTt[k]
            CT, b_CT = CTt[k]
            dtr, b_dtr = dtrt[k]
            zs, b_zs = zst_[k]
            qb, b_qb = qbt[k]
            gs, b_gs = gst[k]
            maskc, b_maskc = mct[k]
            fbias, b_fbias = fbt[k]
            yc, b_yc = ycat[k]
            XS3 = XS[:].rearrange("p (h d) -> p h d", h=HS)
            V(lambda e: e.tensor_tensor(f2("dt"), dtr[:], dtb[:], ALU.add), [b_dtr, b_dtb], [b2("dt")])
            V(lambda e: e.tensor_scalar(f2("ax"), f2("dt"), -1.0, None, op0=ALU.mult), [b2("dt")], [b2("ax")])
            V(lambda e: e.tensor_tensor(f2("ax"), f2("ax"), f2("dt"), ALU.min), [b2("dt"), b2("ax")], [b2("ax")])
            A(lambda e: e.activation(f2("ex"), f2("ax"), AF.Exp), [b2("ax")], [b2("ex")])
            A(lambda e: e.activation(f2("ex"), f2("ex"), AF.Ln, bias=1.0), [b2("ex")], [b2("ex")])
            V(lambda e: e.tensor_scalar(f2("dt"), f2("dt"), 0.0, None, op0=ALU.max), [b2("dt")], [b2("dt")])
            V(lambda e: e.tensor_tensor(f2("dt"), f2("dt"), f2("ex"), ALU.add), [b2("dt"), b2("ex")], [b2("dt")])
            if t <= TF0:
                V(lambda e: e.tensor_scalar(f2("dt"), f2("dt"), flag[:, 0:1], None, op0=ALU.mult), [b2("dt"), b_flag], [b2("dt")])
            V(lambda e: e.tensor_tensor(f2("dA"), f2("dt"), atab[:], ALU.mult), [b2("dt"), b_atab], [b2("dA")])
            pt, bpt = C.psum(PS_MM)
            T(lambda e: e.matmul(pt[:, :HS], utri[:], f2("dA"), start=True, stop=True), [b_utri, b2("dA")], [bpt])
            V(lambda e: e.tensor_copy(f2("cum"), pt[:, :HS]), [bpt], [b2("cum")])
            A(lambda e: e.mul(f2("ncum"), pt[:, :HS], -1.0), [bpt], [b2("ncum")])
            V(lambda e: e.tensor_scalar(f2("cml"), f2("cum"), identf[:, 127:128], None, op0=ALU.mult), [b2("cum"), b_identf], [b2("cml")])
            pt, bpt = C.psum(PS_MM)
            T(lambda e: e.matmul(pt[:, :HS], onesf[:], f2("cml"), start=True, stop=True), [b_onesf, b2("cml")], [bpt])
            A(lambda e: e.activation(f2("dec"), pt[:, :HS], AF.Exp), [bpt], [b2("dec")])
            V(lambda e: e.tensor_tensor(f2("te"), pt[:, :HS], f2("cum"), ALU.subtract), [bpt, b2("cum")], [b2("te")])
            A(lambda e: e.activation(f2("te"), f2("te"), AF.Exp), [b2("te")], [b2("te")])
            A(lambda e: e.activation(f2("ec"), f2("cum"), AF.Exp), [b2("cum")], [b2("ec")])
            V(lambda e: e.tensor_tensor(f2("coef"), f2("dt"), f2("te"), ALU.mult), [b2("dt"), b2("te")], [b2("coef")])
            if full:
                for g in range(4):
                    hsl = slice(g * HPGS, (g + 1) * HPGS)
                    gsl = slice(g * GW, (g + 1) * GW)
                    R, bR = Rg[g % 2]
                    LT, bLT = LTg[g % 2]
                    W, bW = Wg[g % 2]
                    xd, bxd = xdt[g % 2]
                    R3 = R[:].rearrange("p (h i) -> p h i", h=HPGS)
                    V(lambda e: e.tensor_tensor(R3, identf[:].unsqueeze(1).to_broadcast([128, HPGS, 128]),
                                                bc3(sm["cum"][0][:, hsl], 128), ALU.mult), [b_identf, b2("cum")], [bR])
                    for c0 in range(0, HPGS * 128, 512):
                        cn = min(512, HPGS * 128 - c0)
                        pb, bpb = C.psum(PS_MM)
                        T(lambda e: e.matmul(pb[:, :cn], onesf[:], R[:, c0:c0 + cn], start=True, stop=False), [b_onesf, bR], [bpb])
                        T(lambda e: e.matmul(pb[:, :cn], identb[:], negm[:, :cn], start=False, stop=True), [b_identb, b_negm], [bpb])
                        for hh in range(cn // 128):
                            h = g * HPGS + c0 // 128 + hh
                            A(lambda e, hh=hh, h=h: e.activation(LT[:, c0 + hh * 128:c0 + (hh + 1) * 128], pb[:, hh * 128:(hh + 1) * 128],
                                                                 AF.Exp, bias=sm["ncum"][0][:, h:h + 1]), [bpb, b2("ncum")], [bLT])
                    pc, bpc = C.psum(PS_MM)
                    T(lambda e: e.matmul(pc[:, :128], BT[:, g, :], CT[:, g, :], start=True, stop=True), [b_BT, b_CT], [bpc])
                    V(lambda e: e.tensor_tensor(W[:].rearrange("p (h i) -> p h i", h=HPGS), LT[:].rearrange("p (h i) -> p h i", h=HPGS),
                                                pc[:, :128].unsqueeze(1).to_broadcast([128, HPGS, 128]), ALU.mult), [bLT, bpc], [bW])
                    P(lambda e: e.tensor_tensor(xd[:].rearrange("p (h d) -> p h d", h=HPGS), XS3[:, hsl, :],
                                                bc3(sm["dt"][0][:, hsl], 64), ALU.mult), [b_XS, b2("dt")], [bxd])
                    pyd, bpyd = C.ps[6]
                    pyo, bpyo = C.ps[7]
                    for hh in range(HPGS):
                        T(lambda e, hh=hh: e.matmul(pyd[:, hh * 64:(hh + 1) * 64], W[:, hh * 128:(hh + 1) * 128], xd[:, hh * 64:(hh + 1) * 64],
                                                    start=True, stop=True), [bW, bxd], [bpyd])
                    T(lambda e: e.matmul(pyo[:, :GW], CT[:, g, :], Hb[:, gsl], start=True, stop=True), [b_CT, b_Hb], [bpyo])
                    y1, by1 = yt1[g % 2]
                    y2, by2 = yt2[g % 2]
                    V(lambda e: e.tensor_tensor(y1[:].rearrange("p (h d) -> p h d", h=HPGS), pyo[:, :GW].rearrange("p (h d) -> p h d", h=HPGS),
                                                bc3(sm["ec"][0][:, hsl], 64), ALU.mult), [bpyo, b2("ec")], [by1])
                    V(lambda e: e.tensor_tensor(y1[:], y1[:], pyd[:, :GW], ALU.add), [by1, bpyd], [by1])
                    P(lambda e: e.tensor_tensor(y2[:].rearrange("p (h d) -> p h d", h=HPGS), XS3[:, hsl, :],
                                                bc3(dskip[:, hsl], 64), ALU.mult), [b_XS, b_dskip], [by2])
                    P(lambda e: e.tensor_tensor(y2[:], y2[:], y1[:], ALU.add), [by1, by2], [by2])
                    P(lambda e: e.tensor_tensor(yz[:, gsl], y2[:], zs[:, gsl], ALU.mult), [by2, b_zs], [b_yz])
                    A(lambda e: e.activation(junk[:, :GW], yz[:, gsl], AF.Square, accum_out=stat[:, 4 + g:5 + g]), [b_yz], [b_junk, b_stat])
                A(lambda e: e.activation(stat[:, 8:12], stat[:, 4:8], AF.Sqrt, scale=1.0 / GW, bias=EPS), [b_stat], [b_stat])
                V(lambda e: e.reciprocal(stat[:, 12:16], stat[:, 8:12]), [b_stat], [b_stat])
                for g in range(4):
                    gsl = slice(g * GW, (g + 1) * GW)
                    V(lambda e: e.scalar_tensor_tensor(yc[:, gsl], yz[:, gsl], stat[:, 12 + g:13 + g], ssmn[:, gsl],
                                                       op0=ALU.mult, op1=ALU.mult), [b_yz, b_stat, b_ssmn], [b_yc])
                nsa_on = getattr(cfg, "stop", None) != "M2"
                transpose_to(lambda c0, cnt: qT[:, c0:c0 + cnt, :], lambda c: qb[:, c * 128:(c + 1) * 128], HPG, b_qb, b_qT)
                qTf = qT[:].rearrange("p h q -> p (h q)")
                HB = [(a, min(a + 3, HPG)) for a in range(0, HPG, 3)]
                cbanks = (3, 6, 7)
                rot = [0]

                def s_scores(KT, bKT, kcols, g, need_f32):
                    gsl2 = slice(g * 64, (g + 1) * 64)
                    k2 = rot[0]
                    rot[0] += 1
                    E, bE = Eb[k2 % 2] if need_f32 else PTb[k2 % 2]
                    for h0 in range(0, HPG, 4):
                        nh = min(4, HPG - h0)
                        ps_, bps_ = C.psum((0, 1, 2)) if getattr(cfg, "stop2", None) != "pool" else C.psum(PS_TR)
                        T(lambda e: e.matmul(ps_[:, :nh * 128], KT[gsl2, kcols], qTf[gsl2, h0 * 128:(h0 + nh) * 128], start=True, stop=True),
                          [bKT, b_qT], [bps_])
                        if getattr(cfg, "stop2", None) == "noexp":
                            continue
                        if getattr(cfg, "stop2", None) == "last" and t != NT - 1:
                            continue
                        if getattr(cfg, "stop2", None) == "first" and t != TF0:
                            continue
                        if getattr(cfg, "stop2", None) == "second" and t != TF0 + 1:
                            continue
                        if getattr(cfg, "stop2", None) == "notfirst" and t == TF0:
                            continue
                        if getattr(cfg, "stop2", None) == "g0" and g != 0:
                            continue
                        if getattr(cfg, "stop2", None) == "g1" and g != 1:
                            continue
                        if getattr(cfg, "stop2", None) == "exp128":
                            for hq in range(nh):
                                A(lambda e, hq=hq: e.activation(E[:, (h0 + hq) * 128:(h0 + hq + 1) * 128], ps_[:, hq * 128:(hq + 1) * 128], AF.Exp, scale=0.125), [bps_], [bE])
                            continue
                        if getattr(cfg, "stop2", None) == "exp3":
                            A(lambda e: e.copy(E[:, h0 * 128:(h0 + nh) * 128], ps_[:, :nh * 128]), [bps_], [bE])
                            A(lambda e: e.activation(E[:, h0 * 128:(h0 + nh) * 128], E[:, h0 * 128:(h0 + nh) * 128], AF.Exp, scale=0.125), [bE], [bE])
                            continue
                        if getattr(cfg, "stop2", None) == "exp2":
                            V(lambda e: e.tensor_copy(E[:, h0 * 128:(h0 + nh) * 128], ps_[:, :nh * 128]), [bps_], [bE])
                            A(lambda e: e.activation(E[:, h0 * 128:(h0 + nh) * 128], E[:, h0 * 128:(h0 + nh) * 128], AF.Exp, scale=0.125), [bE], [bE])
                            continue
                        if getattr(cfg, "stop2", None) == "copy":
                            V(lambda e: e.tensor_copy(E[:, h0 * 128:(h0 + nh) * 128], ps_[:, :nh * 128]), [bps_], [bE])
                            continue
                        if getattr(cfg, "stop2", None) == "actpsum":
                            A(lambda e: e.activation(E[:, h0 * 128:(h0 + nh) * 128], ps_[:, :nh * 128], AF.Exp, scale=0.125), [bps_], [bE])
                            continue
                        Ef, bEf = Eb[k2 % 2]
                        V(lambda e: e.tensor_copy(Ef[:, h0 * 128:(h0 + nh) * 128], ps_[:, :nh * 128]), [bps_], [bEf])
                        A(lambda e: e.activation(E[:, h0 * 128:(h0 + nh) * 128], Ef[:, h0 * 128:(h0 + nh) * 128], AF.Exp, scale=0.125), [bEf], [bE] if bE is not bEf else [bEf])
                    return E, bE, k2

                def evac(banks, per, gate_idx, g, with_imp):
                    for (pb, bpb, h0, h1) in banks:
                        nh = h1 - h0
                        v = pb[:, :nh * per].rearrange("p (h c) -> p h c", h=nh)
                        V(lambda e: e.tensor_scalar(nst[:, 0, h0:h1], v[:, :, 64], 1e-30, None, op0=ALU.max), [bpb], [b_nst])
                    V(lambda e: e.reciprocal(nst[:, 1, :], nst[:, 0, :]), [b_nst], [b_nst])
                    g3 = gs[:].rearrange("p (h c) -> p h c", c=3)
                    V(lambda e: e.tensor_tensor(nst[:, 2, :], nst[:, 1, :], g3[:, g * HPG:(g + 1) * HPG, gate_idx], ALU.mult), [b_nst, b_gs], [b_nst])
                    for (pb, bpb, h0, h1) in banks:
                        nh = h1 - h0
                        v = pb[:, :nh * per].rearrange("p (h c) -> p h c", h=nh)
                        osl = oacc[:, (g * HPG + h0) * 64:(g * HPG + h1) * 64].rearrange("p (h d) -> p h d", h=nh)
                        if gate_idx == 0:
                            V(lambda e: e.tensor_tensor(osl, v[:, :, 0:64], bc3(nst[:, 2, h0:h1], 64), ALU.mult), [bpb, b_nst], [b_oacc])
                        else:
                            ot = otmp[:, h0 * 64:h1 * 64].rearrange("p (h d) -> p h d", h=nh)
                            V(lambda e: e.tensor_tensor(ot, v[:, :, 0:64], bc3(nst[:, 2, h0:h1], 64), ALU.mult), [bpb, b_nst], [b_otmp])
                            P(lambda e: e.tensor_tensor(osl, osl, ot, ALU.add), [b_otmp, b_oacc], [b_oacc])
                        if with_imp:
                            for hh in range(h0, h1):
                                if hh == 0:
                                    V(lambda e: e.tensor_scalar(impg[:], v[:, hh - h0, 65:CW], nst[:, 1, hh:hh + 1], None, op0=ALU.mult),
                                      [bpb, b_nst], [b_impg])
                                else:
                                    V(lambda e: e.scalar_tensor_tensor(impg[:], v[:, hh - h0, 65:CW], nst[:, 1, hh:hh + 1], impg[:],
                                                                       op0=ALU.mult, op1=ALU.add), [bpb, b_nst, b_impg], [b_impg])

                def pv(banks, PT, bPT, Vop, bV, first, last, per):
                    for (pb, bpb, h0, h1) in banks:
                        for hh in range(h0, h1):
                            T(lambda e, hh=hh: e.matmul(pb[:, (hh - h0) * per:(hh - h0 + 1) * per], PT[:, hh * 128:(hh + 1) * 128], Vop,
                                                        start=(first and hh == h0), stop=(last and hh == h1 - 1)), [bPT, bV], [bpb])

                def masked(E, bE, k2, msk, bmsk):
                    PT, bPT = PTb[(k2 + 1) % 2]
                    V(lambda e: e.tensor_tensor(PT[:].rearrange("p (h q) -> p h q", h=HPG), E[:].rearrange("p (h q) -> p h q", h=HPG),
                                                msk.unsqueeze(1).to_broadcast([128, HPG, 128]), ALU.mult), [bE, bmsk], [bPT])
                    return PT, bPT

                for g in (range(2) if nsa_on else ()):
                    nbt = min(NBT, (8 * t + 6) // 128 + 1)
                    banks = [(C.ps[cbanks[k_]][0], C.ps[cbanks[k_]][1], a, b) for k_, (a, b) in enumerate(HB)]
                    stp = getattr(cfg, "stop", None)
                    if stp == "C1":
                        continue
                    for bt in range(nbt):
                        E, bE, k2 = s_scores(kcT, b_kcT, slice(bt * 128, (bt + 1) * 128), g, True)
                        if stp == "C2":
                            if "dbgE" in dbg_o and t == getattr(cfg, "dump_t", NT - 1):
                                S.dma("sync", dbg_o["dbgE"][:, g * HPG * 128:(g + 1) * HPG * 128], E[:], reads=[bE], sembuf=bE)
                            continue
                        PT, bPT = masked(E, bE, k2, maskc[:, bt, :], b_maskc)
                        if stp == "C3":
                            continue
                        pv(banks, PT, bPT, vca[:, bt, g, :], b_vca, bt == 0, bt == nbt - 1, CW)
                    if stp in ("C2", "C3", "C4"):
                        continue
                    evac(banks, CW, 0, g, True)
                    if getattr(cfg, "stop", None) == "N1":
                        continue
                    V(lambda e: e.tensor_tensor(wk1[:], impg[:], fbias[:], ALU.add), [b_impg, b_fbias], [b_wk1])
                    for r_ in range((cfg.TOPN - 1) // 8):
                        V(lambda e: e.max(out=mx8[:], in_=wk1[:]), [b_wk1], [b_mx8])
                        V(lambda e: e.match_replace(out=wk1[:], in_to_replace=mx8[:], in_values=wk1[:], imm_value=-3.0e38), [b_mx8, b_wk1], [b_wk1])
                    V(lambda e: e.max(out=mx8[:], in_=wk1[:]), [b_wk1], [b_mx8])
                    kth = (cfg.TOPN - 1) % 8
                    V(lambda e: e.tensor_tensor(wk1[:], impg[:], fbias[:], ALU.add), [b_impg, b_fbias, b_mx8], [b_wk1])
                    V(lambda e: e.tensor_scalar(selb[:], wk1[:], mx8[:, kth:kth + 1], None, op0=ALU.is_ge), [b_wk1, b_mx8], [b_selb])
                    pt2, bpt2 = C.psum(PS_TR)
                    ptb2 = pt2[:].bitcast(BF16)
                    T(lambda e: e.transpose(ptb2[:NSLC, 0:128], selb[:], identb[:]), [b_selb, b_identb], [bpt2])
                    A(lambda e: e.copy(selT[:], ptb2[:NSLC, 0:128]), [bpt2], [b_selT])
                    if getattr(cfg, "stop", None) == "N2":
                        continue
                    HB2 = [(a, min(a + 4, HPG)) for a in range(0, HPG, 4)]
                    banks = [(C.ps[6 + k_][0], C.ps[6 + k_][1], a, b) for k_, (a, b) in enumerate(HB2)]
                    for kt in range(t + 1):
                        kc_ = slice(kt * 128, (kt + 1) * 128)
                        E, bE, k2 = s_scores(KTs, b_KTs, kc_, g, True)
                        pm, bpm = C.psum(PS_TR)
                        T(lambda e: e.matmul(pm[:, :128], exm[:, kc_], selT[:], start=True, stop=True), [b_exm, b_selT], [bpm])
                        if kt == t:
                            md, bmd = mdiag[kt % 2]
                            V(lambda e: e.tensor_tensor(md[:], pm[:, :128], causT[:], ALU.mult), [bpm, b_causT], [bmd])
                            PT, bPT = masked(E, bE, k2, md[:], bmd)
                        else:
                            PT, bPT = masked(E, bE, k2, pm[:, :128], bpm)
                        pv(banks, PT, bPT, Vs[:, kt, g, :], b_Vs, kt == 0, kt == t, 65)
                    evac(banks, 65, 1, g, False)
                    if getattr(cfg, "stop", None) == "N3":
                        continue
                    kts = [kt for kt in range(t - WT, t + 1) if kt >= 0]
                    for kt in kts:
                        kc_ = slice(kt * 128, (kt + 1) * 128)
                        E, bE, k2 = s_scores(KTw, b_KTw, kc_, g, True)
                        mk = causT if kt == t else (bandT if kt == t - WT else onesb)
                        PT, bPT = masked(E, bE, k2, mk[:], b_causT)
                        pv(banks, PT, bPT, Vw[:, kt, g, :], b_Vw, kt == kts[0], kt == kts[-1], 65)
                    evac(banks, 65, 2, g, False)
                if not nsa_on:
                    V(lambda e: e.memset(oacc[:], 0.0), [], [b_oacc])
                V(lambda e: e.tensor_copy(yc[:, DI:], oacc[:]), [b_oacc], [b_yc])
                S.dma("sync", ycat_d[(t - TF0) * 128:(t - TF0 + 1) * 128, :], yc[:], reads=[b_yc], writes=[bycat_d], sembuf=b_yc)
                if "ycat" in dbg_o and t >= NTH:
                    dbs, b_dbs = C.sb("dbs%d" % t, [128, DMIX], F32)
                    V(lambda e: e.tensor_copy(dbs[:], yc[:]), [b_yc], [b_dbs])
                    S.dma("sync", dbg_o["ycat"][(t - NTH) * 128:(t - NTH + 1) * 128, :], dbs[:], reads=[b_dbs], sembuf=b_dbs)
            V(lambda e: e.tensor_tensor(xw[:].rearrange("p (h d) -> p h d", h=HS), XS3, bc3(sm["coef"][0][:], 64), ALU.mult),
              [b_XS, b2("coef")], [b_xw])
            for g in range(4):
                hsl = slice(g * HPGS, (g + 1) * HPGS)
                gsl = slice(g * GW, (g + 1) * GW)
                pst, bpst = C.psum(PS_MM)
                T(lambda e: e.matmul(pst[:, :GW], BTm[:, g * 128:(g + 1) * 128], xw[:, gsl], start=True, stop=True), [b_BTm, b_xw], [bpst])
                H3 = Hs[:, gsl].rearrange("p (h d) -> p h d", h=HPGS)
                V(lambda e: e.tensor_tensor(H3, H3, bc3(sm["dec"][0][:, hsl], 64), ALU.mult), [b_Hs, b2("dec")], [b_Hs])
                V(lambda e: e.tensor_tensor(Hs[:, gsl], Hs[:, gsl], pst[:, :GW], ALU.add), [b_Hs, bpst], [b_Hs])
            P(lambda e: e.tensor_copy(Hb[:], Hs[:]), [b_Hs], [b_Hb])
        stg = [C.sb("stg%d" % i, [128, 128], F32) for i in range(2)]
        for c in range(DI // 128):
            sg, bsg = stg[c % 2]
            pt, bpt = C.psum(PS_MM)
            T(lambda e: e.transpose(pt[:, :128], Hs[:, c * 128:(c + 1) * 128], identf[:]), [b_Hs, b_identf], [bpt])
            V(lambda e: e.tensor_copy(sg[:], pt[:, :128]), [bpt], [bsg])
            S.dma("sync", ssm_p[c * 128:(c + 1) * 128, :], sg[:], reads=[bsg], sembuf=bsg)

    if getattr(cfg, "stop", None) == "M":
        S.finish()
        return
    GO = 5
    groupsO = []
    t = TF0
    while t < NT:
        n = min(GO, NT - t)
        if 0 < NT - t - n < 3:
            n = (NT - t + 1) // 2
        groupsO.append(list(range(t, t + n)))
        t += n
    GOM = max(len(g) for g in groupsO)
    NGM = GOM * 128
    KO = max(DMIX // 128, NHT)
    with Phase():
        hp, b_hp = C.sb("hp", [128, GOM, D], F32)
        xnT, b_xnT = C.sb("xnTo", [128, KC, NGM], BF16)
        arena, b_arena = C.sb("arena", [128, KO, NGM], BF16)
        wbo = [C.sb("wbo%d" % i, [128, KO, 512], BF16) for i in range(2)]
        wrot = [0]
        junk, b_junk = C.sb("junko", [128, D], F32)
        xnb = [C.sb("xnbo%d" % i, [128, D], BF16) for i in range(2)]
        stat, b_stat = C.sb("stato", [128, 16], F32)
        mixb = [C.sb("mixb%d" % i, [128, D], F32) for i in range(2)]
        yct = [C.sb("yct%d" % i, [128, DMIX], BF16) for i in range(1)]
        ub = [C.sb("ub%d" % i, [128, 2 + NGM], F32) for i in range(2)]
        ua = [C.sb("ua%d" % i, [128, NGM], F32) for i in range(2)]
        gg, b_gg = C.sb("gg", [128, NGM], F32)
        fhist = [C.sb("fhist%d" % l, [128, NFT, 2], F32) for l in range(2)]
        fcw, b_fcw = C.sb("fcw", [128, 2, NFT, 3], F32)
        fcb, b_fcb = C.sb("fcb", [128, 2, NFT], F32)
        chist, b_chist = C.sb("chist", [128, KC, 30], F32)
        ab_ = [C.sb("ab%d" % i, [128, 30 + NGM], F32) for i in range(2)]
        cT, b_cT = C.sb("cT", [128, KC, NGM], F32)
        sq, b_sq = C.sb("sq", [128, NGM], F32)
        mean, b_mean = C.sb("mean", [128, NGM], F32)
        rstd, b_rstd = C.sb("rstd", [128, NGM], F32)
        sg_, b_sg = C.sb("sgl", [128, NGM], F32)
        bpw1, b_bpw1 = C.sb("bpw1", [128, 2 * KC], F32)
        dww, b_dww = C.sb("dww", [128, KC, 31], F32)
        dwb, b_dwb = C.sb("dwb", [128, KC], F32)
        lng, b_lng = C.sb("lng", [128, KC], F32)
        lnb, b_lnb = C.sb("lnb", [128, KC], F32)
        bpw2, b_bpw2 = C.sb("bpw2", [128, D], F32)
        bco = Buf("constsO")
        for (dst, src) in ((fcw, fcw_d.rearrange("l p m k -> p l m k")), (fcb, fcb_d.rearrange("l p m -> p l m")),
                           (bpw1, bpw1_d), (dww, dww_d), (dwb, dwb_d), (lng, lng_d), (lnb, lnb_d),
                           (bpw2, bpw2_d.partition_broadcast(128))):
            S.dma("sync", dst[:], src, writes=[bco])
        b_fcw = b_fcb = b_bpw1 = b_dww = b_dwb = b_lng = b_lnb = b_bpw2 = bco
        for l in range(2):
            V(lambda e, l=l: e.memset(fhist[l][0][:], 0.0), [], [fhist[l][1]])
        V(lambda e: e.memset(chist[:], 0.0), [], [b_chist])

        def load_wo(W, r0, kcn, c0, ncols):
            k = wrot[0] % 2
            wrot[0] += 1
            wb, bwb = wbo[k]
            src = W[r0 * 128:(r0 + kcn) * 128, c0:c0 + ncols].rearrange("(kc p) c -> p kc c", p=128)
            S.dma("gpsimd", wb[:, :kcn, :ncols], src, writes=[bwb])
            return wb, bwb

        def rms_T(i, k):
            xb, bxb = xnb[k % 2]
            A(lambda e: e.activation(junk[:, :D], hp[:, i, :], AF.Square, accum_out=stat[:, 0:1]), [b_hp], [b_junk, b_stat])
            A(lambda e: e.activation(stat[:, 1:2], stat[:, 0:1], AF.Sqrt, scale=1.0 / D, bias=EPS), [b_stat], [b_stat])
            V(lambda e: e.reciprocal(stat[:, 2:3], stat[:, 1:2]), [b_stat], [b_stat])
            V(lambda e: e.scalar_tensor_tensor(xb[:], hp[:, i, :], stat[:, 2:3], gtab[:], op0=ALU.mult, op1=ALU.mult),
              [b_hp, b_stat, b_gtab], [bxb])
            transpose_to(lambda c0, cnt: xnT[:, c0:c0 + cnt, i * 128:(i + 1) * 128],
                         lambda c: xb[:, c * 128:(c + 1) * 128], KC, bxb, b_xnT)

        def post_add(i, mx, bmx):
            A(lambda e: e.activation(junk[:, :D], mx[:], AF.Square, accum_out=stat[:, 4:5]), [bmx], [b_junk, b_stat])
            A(lambda e: e.activation(stat[:, 5:6], stat[:, 4:5], AF.Sqrt, scale=1.0 / D, bias=EPS), [b_stat], [b_stat])
            V(lambda e: e.reciprocal(stat[:, 6:7], stat[:, 5:6]), [b_stat], [b_stat])
            V(lambda e: e.scalar_tensor_tensor(mx[:], mx[:], stat[:, 6:7], gtab[:], op0=ALU.mult, op1=ALU.mult),
              [bmx, b_stat, b_gtab], [bmx])
            P(lambda e: e.tensor_tensor(hp[:, i, :], hp[:, i, :], mx[:], ALU.add), [bmx, b_hp], [b_hp])

        def tok_major(W, kcn, actT, b_actT, ng, bias=None):
            wbs = [load_wo(W, 0, kcn, c0, min(512, D - c0)) for c0 in range(0, D, 512)]
            for i in range(ng):
                mx, bmx = mixb[i % 2]
                for ci, c0 in enumerate(range(0, D, 512)):
                    cn = min(512, D - c0)
                    wb, bwb = wbs[ci]
                    pt, bpt = C.psum(PS_MM)
                    for kc in range(kcn):
                        T(lambda e, kc=kc: e.matmul(pt[:, :cn], actT[:, kc, i * 128:(i + 1) * 128], wb[:, kc, :cn],
                                                    start=(kc == 0), stop=(kc == kcn - 1)), [bwb, b_actT], [bpt])
                    if bias is None:
                        A(lambda e: e.copy(mx[:, c0:c0 + cn], pt[:, :cn]), [bpt], [bmx])
                    else:
                        V(lambda e: e.tensor_tensor(mx[:, c0:c0 + cn], pt[:, :cn], bias[:, c0:c0 + cn], ALU.add), [bpt, b_bpw2], [bmx])
                yield i, mx, bmx

        def conv_ffn(layer, gi_, tiles, ng, NG, chunks):
            gsel = 4 * layer + 2
            load_gain(gsel)
            for i in range(ng):
                rms_T(i, i)
            fh, bfh = fhist[layer]
            for m in range(NHT):
                res = []
                for part in range(2):
                    mt = m + part * NHT
                    wb, bwb = load_wo(w_up[layer], 0, KC, mt * 128, 128)
                    u, bu = ub[part]
                    for (c0, cn) in chunks:
                        pt, bpt = C.psum(PS_MM)
                        for kc in range(KC):
                            T(lambda e, kc=kc: e.matmul(pt[:, :cn], wb[:, kc, 0:128], xnT[:, kc, c0:c0 + cn],
                                                        start=(kc == 0), stop=(kc == KC - 1)), [bwb, b_xnT], [bpt])
                        A(lambda e: e.copy(u[:, 2 + c0:2 + c0 + cn], pt[:, :cn]), [bpt], [bu])
                    if gi_ == 0:
                        V(lambda e: e.tensor_scalar(u[:, 2:130], u[:, 2:130], flag[:, 0:1], None, op0=ALU.mult), [bu, b_flag], [bu])
                    P(lambda e: e.tensor_copy(u[:, 0:2], fh[:, mt, :]), [bfh], [bu])
                    P(lambda e: e.tensor_copy(fh[:, mt, :], u[:, NG:NG + 2]), [bu], [bfh])
                    a_, ba_ = ua[part]
                    V(lambda e: e.tensor_scalar(a_[:, :NG], u[:, 0:NG], fcw[:, layer, mt, 0:1], None, op0=ALU.mult), [bu, b_fcw], [ba_])
                    for k in range(1, 3):
                        V(lambda e, k=k: e.scalar_tensor_tensor(a_[:, :NG], u[:, k:k + NG], fcw[:, layer, mt, k:k + 1], a_[:, :NG],
                                                                op0=ALU.mult, op1=ALU.add), [bu, b_fcw, ba_], [ba_])
                    res.append((a_, ba_, mt))
                (ag, bag, mg), (av, bav, mv) = res
                A(lambda e: e.activation(gg[:, :NG], ag[:, :NG], AF.Gelu_apprx_tanh, bias=fcb[:, layer, mg:mg + 1]), [bag, b_fcb], [b_gg])
                V(lambda e: e.scalar_tensor_tensor(arena[:, m, :NG], av[:, :NG], fcb[:, layer, mv:mv + 1], gg[:, :NG],
                                                   op0=ALU.add, op1=ALU.mult), [bav, b_fcb, b_gg], [b_arena])
            load_gain(gsel + 1)
            for i, mx, bmx in tok_major(w_down[layer], NHT, arena, b_arena, ng):
                post_add(i, mx, bmx)

        for gi_, tiles in enumerate(groupsO):
            ng = len(tiles)
            NG = ng * 128
            chunks = [(c, min(512, NG - c)) for c in range(0, NG, 512)]
            for i, t in enumerate(tiles):
                S.dma("sync", hp[:, i, :], xin[t * 128:(t + 1) * 128, :], writes=[b_hp])
                yc, byc = yct[0]
                S.dma("sync", yc[:], ycat_d[(t - TF0) * 128:(t - TF0 + 1) * 128, :], reads=[bycat_d], writes=[byc])
                transpose_to(lambda c0, cnt: arena[:, c0:c0 + cnt, i * 128:(i + 1) * 128],
                             lambda c: yc[:, c * 128:(c + 1) * 128], DMIX // 128, byc, b_arena)
            load_gain(1)
            for i, mx, bmx in tok_major(w_out, DMIX // 128, arena, b_arena, ng):
                post_add(i, mx, bmx)
            conv_ffn(0, gi_, tiles, ng, NG, chunks)
            load_gain(4)
            for i in range(ng):
                rms_T(i, i)
            for m in range(KC):
                wa, bwa = load_wo(w_pw1, 0, KC, m * 128, 128)
                wg, bwg = load_wo(w_pw1, 0, KC, (KC + m) * 128, 128)
                ab, bab = ab_[m % 2]
                for (c0, cn) in chunks:
                    pa, bpa = C.psum(PS_MM)
                    pg, bpg = C.psum(PS_MM)
                    for kc in range(KC):
                        T(lambda e, kc=kc: e.matmul(pa[:, :cn], wa[:, kc, 0:128], xnT[:, kc, c0:c0 + cn],
                                                    start=(kc == 0), stop=(kc == KC - 1)), [bwa, b_xnT], [bpa])
                    for kc in range(KC):
                        T(lambda e, kc=kc: e.matmul(pg[:, :cn], wg[:, kc, 0:128], xnT[:, kc, c0:c0 + cn],
                                                    start=(kc == 0), stop=(kc == KC - 1)), [bwg, b_xnT], [bpg])
                    A(lambda e: e.activation(sg_[:, :cn], pg[:, :cn], AF.Sigmoid, bias=bpw1[:, KC + m:KC + m + 1]), [bpg, b_bpw1], [b_sg])
                    V(lambda e: e.scalar_tensor_tensor(ab[:, 30 + c0:30 + c0 + cn], pa[:, :cn], bpw1[:, m:m + 1], sg_[:, :cn],
                                                       op0=ALU.add, op1=ALU.mult), [bpa, b_bpw1, b_sg], [bab])
                if gi_ == 0:
                    V(lambda e: e.tensor_scalar(ab[:, 30:158], ab[:, 30:158], flag[:, 0:1], None, op0=ALU.mult), [bab, b_flag], [bab])
                P(lambda e: e.tensor_copy(ab[:, 0:30], chist[:, m, :]), [b_chist], [bab])
                P(lambda e: e.tensor_copy(chist[:, m, :], ab[:, NG:NG + 30]), [bab], [b_chist])
                V(lambda e: e.tensor_scalar(cT[:, m, :NG], ab[:, 0:NG], dww[:, m, 0:1], None, op0=ALU.mult), [bab, b_dww], [b_cT])
                for k in range(1, 31):
                    V(lambda e, k=k: e.scalar_tensor_tensor(cT[:, m, :NG], ab[:, k:k + NG], dww[:, m, k:k + 1], cT[:, m, :NG],
                                                            op0=ALU.mult, op1=ALU.add), [bab, b_dww, b_cT], [b_cT])
                V(lambda e: e.tensor_scalar(cT[:, m, :NG], cT[:, m, :NG], dwb[:, m:m + 1], None, op0=ALU.add), [b_cT, b_dwb], [b_cT])
            for (c0, cn) in chunks:
                p1, bp1 = C.psum(PS_MM)
                p2, bp2 = C.psum(PS_MM)
                for m in range(KC):
                    T(lambda e, m=m: e.matmul(p1[:, :cn], onesf[:], cT[:, m, c0:c0 + cn], start=(m == 0), stop=(m == KC - 1)), [b_onesf, b_cT], [bp1])
                for m in range(KC):
                    A(lambda e, m=m: e.activation(sq[:, :cn], cT[:, m, c0:c0 + cn], AF.Square), [b_cT], [b_sq])
                    T(lambda e, m=m: e.matmul(p2[:, :cn], onesf[:], sq[:, :cn], start=(m == 0), stop=(m == KC - 1)), [b_onesf, b_sq], [bp2])
                A(lambda e: e.mul(mean[:, c0:c0 + cn], p1[:, :cn], 1.0 / D), [bp1], [b_mean])
                V(lambda e: e.tensor_tensor(sq[:, :cn], mean[:, c0:c0 + cn], mean[:, c0:c0 + cn], ALU.mult), [b_mean, b_sq], [b_sq])
                V(lambda e: e.scalar_tensor_tensor(rstd[:, c0:c0 + cn], p2[:, :cn], 1.0 / D, sq[:, :cn], op0=ALU.mult, op1=ALU.subtract),
                  [bp2, b_sq], [b_rstd])
                A(lambda e: e.activation(rstd[:, c0:c0 + cn], rstd[:, c0:c0 + cn], AF.Sqrt, bias=EPS), [b_rstd], [b_rstd])
                V(lambda e: e.reciprocal(rstd[:, c0:c0 + cn], rstd[:, c0:c0 + cn]), [b_rstd], [b_rstd])
            for m in range(KC):
                V(lambda e: e.tensor_tensor(cT[:, m, :NG], cT[:, m, :NG], mean[:, :NG], ALU.subtract), [b_cT, b_mean], [b_cT])
                V(lambda e: e.tensor_tensor(cT[:, m, :NG], cT[:, m, :NG], rstd[:, :NG], ALU.mult), [b_cT, b_rstd], [b_cT])
                A(lambda e: e.activation(arena[:, m, :NG], cT[:, m, :NG], AF.Silu, bias=lnb[:, m:m + 1], scale=lng[:, m:m + 1]),
                  [b_cT, b_lng, b_lnb], [b_arena])
            load_gain(5)
            for i, mx, bmx in tok_major(w_pw2, KC, arena, b_arena, ng, bias=bpw2):
                post_add(i, mx, bmx)
            conv_ffn(1, gi_, tiles, ng, NG, chunks)
            for i, t in enumerate(tiles):
                if t >= NTH:
                    S.dma("sync", y_p[(t - NTH) * 128:(t - NTH + 1) * 128, :], hp[:, i, :], reads=[b_hp], sembuf=b_hp)
        co, b_co = C.sb("co", [32, D], F32)
        for m in range(KC):
            pt, bpt = C.psum(PS_MM)
            T(lambda e: e.transpose(pt[:30, :128], chist[:, m, :], identf[:]), [b_chist, b_identf], [bpt])
            V(lambda e: e.tensor_copy(co[:30, m * 128:(m + 1) * 128], pt[:30, :128]), [bpt], [b_co])
        S.dma("sync", conf_p, co[:30, :], reads=[b_co], sembuf=b_co)
        fo = [C.sb("fo%d" % i, [2, 1024], F32) for i in range(2)]
        for l in range(2):
            for mb in range(0, NFT, 8):
                nmb = min(8, NFT - mb)
                f_, bf_ = fo[(mb // 8) % 2]
                for m in range(mb, mb + nmb):
                    pt, bpt = C.psum(PS_MM)
                    T(lambda e: e.transpose(pt[:2, :128], fhist[l][0][:, m, :], identf[:]), [fhist[l][1], b_identf], [bpt])
                    V(lambda e: e.tensor_copy(f_[:, (m - mb) * 128:(m - mb + 1) * 128], pt[:2, :128]), [bpt], [bf_])
                S.dma("sync", ffn_p[l, :, mb * 128:(mb + nmb) * 128], f_[:, :nmb * 128], reads=[bf_], sembuf=bf_)

    if getattr(cfg, "stop", None) == "O":
        S.finish()
        return
    _emit_sample(C, cfg, locals())
    S.finish()
    print("ops", S.n_ops, "waits", S.n_wait, "dma sems", S.ndsem)


def _emit_sample(C, cfg, L):
    nc, S = C.nc, C.S
    D, KC, DI, HS, CD, NH, HPG = cfg.D, cfg.KC, cfg.DI, cfg.HS, cfg.CD, cfg.NH, cfg.HPG
    HPGS, DMIX, DFF, DEC, PAST = cfg.HPGS, cfg.DMIX, cfg.DFF, cfg.DEC, cfg.PAST
    GW = HPGS * 64
    NPG = PAST // 128
    NCS = PAST // 16 - 1
    NBS = (NCS + 127) // 128
    NSL = PAST // 64
    CWs = 65 + NSL + 1
    NWT = min(cfg.WINDOW, PAST) // 128
    DPROJ = cfg.DPROJ
    V, A, P, T = L["V"], L["A"], L["P"], L["T"]
    din, dout, dscr = L["din"], L["dout"], L["dscr"]
    identb, identf, onesf, gtab = L["identb"], L["identf"], L["onesf"], L["gtab"]
    b_identb, b_identf, b_onesf, b_gtab = L["b_identb"], L["b_identf"], L["b_onesf"], L["b_gtab"]
    Phase, load_gain, normg = L["Phase"], L["load_gain"], L["normg"]
    w_in, w_out, w_up, w_down, w_pw1, w_pw2 = L["w_in"], L["w_out"], L["w_up"], L["w_down"], L["w_pw1"], L["w_pw2"]

    xs_in = din("xs_in", [DEC, D])
    st_ssm = din("st_ssm", [DEC, DI, 128])
    st_sconv = din("st_sconv", [DEC, 3 * CD])
    c_cmp = din("c_cmp", [cfg.NPHYS * 128, 256])
    c_slc = din("c_slc", [cfg.NPHYS * 128, 256])
    c_win = din("c_win", [DEC, NWT * 128, 256])
    ptab = din("ptab", [DEC, NPG], I32)
    st_conf = din("st_conf", [DEC, 30, D])
    st_ffn = din("st_ffn", [2, DEC, 2 * 2 * DFF])
    convw_r = din("convw_r", [4, CD])
    convb_r = din("convb_r", [1, CD])
    fcw_r = din("fcw_r", [2, 3, 2 * DFF])
    fcb_r = din("fcb_r", [2, 1, 2 * DFF])
    dww_r = din("dww_r", [31, D])
    dwb_r = din("dwb_r", [1, D])
    lng_r = din("lng_r", [1, D])
    lnb_r = din("lnb_r", [1, D])
    bpw1_r = din("bpw1_r", [1, 2 * D])
    bpw2_r = L["bpw2_d"]
    rope_s = din("rope_s", [2, 32])
    selc_d = din("selc", [DEC, 31, DEC])
    ovs_d = din("ovs", [NBS * 128, NSL + 1])
    fbs_d = din("fbs", [1, NSL + 1])
    dtb_d, alog_d, dskip_d, ssmn_d = L["dtb_d"], L["alog_d"], L["dskip_d"], L["ssmn_d"]
    cw1_d, cpe_d, cb1_d, cw2k_d, cw2v_d, cb2k_d, cb2v_d = (L[k] for k in ("cw1_d", "cpe_d", "cb1_d", "cw2k_d", "cw2v_d", "cb2k_d", "cb2v_d"))
    exm_d = L["exm_d"]

    y_s = dout("y_s", [DEC, D])
    ssm_s = dout("ssm_s", [DEC, DI, 128])
    sconv_s = dout("sconv_s", [DEC, 3 * CD])
    kv_s = dout("kv_s", [DEC, 768])
    conf_s = dout("conf_s", [DEC, 30, D])
    ffn_s = dout("ffn_s", [2, DEC, 2 * 2 * DFF])
    ycs_d = dscr("ycs_d", [DEC, DMIX])
    gss_d = dscr("gss_d", [DEC, 3 * NH])
    vn_d = dscr("vn_d", [DEC, 768], BF16)
    arow_d = dscr("arow_d", [DEC, D])
    bycs, bgss, bvn, barow = [Buf("sscr%d" % i) for i in range(4)]
    R_ = slice(0, DEC)

    def transposeF(dst, src, rows, cols, b_src, b_dst):
        pt, bpt = C.psum(PS_MM)
        T(lambda e: e.transpose(pt[:cols, :rows], src, identf[:rows, :rows]), [b_src, b_identf], [bpt])
        V(lambda e: e.tensor_copy(dst, pt[:cols, :rows]), [bpt], [b_dst])

    def rms_rows(dst_bf, src, b_src, b_dst, st_, b_st, jk, b_jk):
        A(lambda e: e.activation(jk[R_, :D], src, AF.Square, accum_out=st_[R_, 0:1]), [b_src], [b_jk, b_st])
        A(lambda e: e.activation(st_[R_, 1:2], st_[R_, 0:1], AF.Sqrt, scale=1.0 / D, bias=EPS), [b_st], [b_st])
        V(lambda e: e.reciprocal(st_[R_, 2:3], st_[R_, 1:2]), [b_st], [b_st])
        V(lambda e: e.scalar_tensor_tensor(dst_bf, src, st_[R_, 2:3], gtab[R_, :], op0=ALU.mult, op1=ALU.mult),
          [b_src, b_st, b_gtab], [b_dst])

    with Phase():
        hs, b_hs = C.sb("hs", [DEC, D], F32)
        stt, b_stt = C.sb("stt", [128, 16], F32)
        jk, b_jk = C.sb("jks", [DEC, max(D, DI)], F32)
        xb, b_xb = C.sb("xbs", [DEC, D], BF16)
        xT, b_xT = C.sb("xTs", [128, KC, DEC], BF16)
        raw, b_raw = C.sb("raws", [DEC, DPROJ], F32)
        wbs = [C.sb("wbs%d" % i, [128, KC, 512], BF16) for i in range(3)]
        hst = [C.sb("hsts%d" % i, [DEC, 3, 512], F32) for i in range(2)]
        cwr = [C.sb("cwr%d" % i, [DEC, 4, 512], F32) for i in range(2)]
        cbr = [C.sb("cbr%d" % i, [DEC, 512], F32) for i in range(2)]
        xc, b_xc = C.sb("xcs", [DEC, CD], F32)
        S.dma("sync", hs[:], xs_in, writes=[b_hs])
        load_gain(0)
        rms_rows(xb[:], hs[:], b_hs, b_xb, stt, b_stt, jk, b_jk)
        for kc in range(KC):
            pt, bpt = C.psum(PS_TR)
            ptb = pt[:].bitcast(BF16)
            T(lambda e: e.transpose(ptb[:, 0:DEC], xb[:, kc * 128:(kc + 1) * 128], identb[:DEC, :DEC]), [b_xb, b_identb], [bpt])
            A(lambda e: e.copy(xT[:, kc, :], ptb[:, 0:DEC]), [bpt], [b_xT])
        for bi, c0 in enumerate(range(0, DPROJ, 512)):
            cn = min(512, DPROJ - c0)
            wb, bwb = wbs[bi % 3]
            S.dma("gpsimd", wb[:, :, :cn], w_in[:, c0:c0 + cn].rearrange("(kc p) c -> p kc c", p=128), writes=[bwb])
            pt, bpt = C.psum(PS_MM)
            for kc in range(KC):
                T(lambda e, kc=kc: e.matmul(pt[:DEC, :cn], xT[:, kc, :], wb[:, kc, :cn], start=(kc == 0), stop=(kc == KC - 1)),
                  [bwb, b_xT], [bpt])
            A(lambda e: e.copy(raw[:, c0:c0 + cn], pt[:DEC, :cn]), [bpt], [b_raw])
        tmpc, b_tmpc = C.sb("tmpc", [DEC, 512], F32)
        for ci, c0 in enumerate(range(0, CD, 512)):
            cn = min(512, CD - c0)
            h_, bh_ = hst[ci % 2]
            w_, bw_ = cwr[ci % 2]
            b_, bb_ = cbr[ci % 2]
            for k in range(3):
                S.dma("sync", h_[:, k, :cn], st_sconv[:, k * CD + c0:k * CD + c0 + cn], writes=[bh_])
            for k in range(4):
                S.dma("sync", w_[:, k, :cn], convw_r[k:k + 1, c0:c0 + cn].partition_broadcast(DEC), writes=[bw_])
            S.dma("sync", b_[:, :cn], convb_r[:, c0:c0 + cn].partition_broadcast(DEC), writes=[bb_])
            xn_ = raw[:, cfg.o_xbc + c0:cfg.o_xbc + c0 + cn]
            V(lambda e: e.tensor_tensor(xc[:, c0:c0 + cn], xn_, w_[:, 3, :cn], ALU.mult), [b_raw, bw_], [b_xc])
            for k in range(3):
                V(lambda e, k=k: e.tensor_tensor(tmpc[:, :cn], h_[:, k, :cn], w_[:, k, :cn], ALU.mult), [bh_, bw_], [b_tmpc])
                V(lambda e: e.tensor_tensor(xc[:, c0:c0 + cn], xc[:, c0:c0 + cn], tmpc[:, :cn], ALU.add), [b_xc, b_tmpc], [b_xc])
            V(lambda e: e.tensor_tensor(xc[:, c0:c0 + cn], xc[:, c0:c0 + cn], b_[:, :cn], ALU.add), [b_xc, bb_], [b_xc])
            for k in range(1, 3):
                S.dma("sync", sconv_s[:, (k - 1) * CD + c0:(k - 1) * CD + c0 + cn], h_[:, k, :cn], reads=[bh_], sembuf=bh_)
        A(lambda e: e.activation(xc[:], xc[:], AF.Silu), [b_xc], [b_xc])
        S.dma("sync", sconv_s[:, 2 * CD:3 * CD], raw[:, cfg.o_xbc:cfg.o_xbc + CD], reads=[b_raw], sembuf=b_raw)
        tb = {}
        for nm, src in (("dtb", dtb_d), ("atab", alog_d), ("dskip", dskip_d)):
            tb[nm] = C.sb("s_" + nm, [DEC, HS], F32)
            S.dma("sync", tb[nm][0][:], src.partition_broadcast(DEC), writes=[tb[nm][1]])
        ssmn, b_ssmn = C.sb("s_ssmn", [DEC, DI], F32)
        S.dma("sync", ssmn[:], ssmn_d.partition_broadcast(DEC), writes=[b_ssmn])
        atab, b_atab = tb["atab"]
        A(lambda e: e.activation(atab[:], atab[:], AF.Exp), [b_atab], [b_atab])
        V(lambda e: e.tensor_scalar(atab[:], atab[:], -1.0, None, op0=ALU.mult), [b_atab], [b_atab])
        sm = {nm: C.sb("ss_" + nm, [DEC, HS], F32) for nm in ("dt", "ax", "ex", "dec")}
        f2 = lambda nm: sm[nm][0][:]
        b2 = lambda nm: sm[nm][1]
        V(lambda e: e.tensor_tensor(f2("dt"), raw[:, cfg.o_dt:cfg.o_dt + HS], tb["dtb"][0][:], ALU.add), [b_raw, tb["dtb"][1]], [b2("dt")])
        V(lambda e: e.tensor_scalar(f2("ax"), f2("dt"), -1.0, None, op0=ALU.mult), [b2("dt")], [b2("ax")])
        V(lambda e: e.tensor_tensor(f2("ax"), f2("ax"), f2("dt"), ALU.min), [b2("dt"), b2("ax")], [b2("ax")])
        A(lambda e: e.activation(f2("ex"), f2("ax"), AF.Exp), [b2("ax")], [b2("ex")])
        A(lambda e: e.activation(f2("ex"), f2("ex"), AF.Ln, bias=1.0), [b2("ex")], [b2("ex")])
        V(lambda e: e.tensor_scalar(f2("dt"), f2("dt"), 0.0, None, op0=ALU.max), [b2("dt")], [b2("dt")])
        V(lambda e: e.tensor_tensor(f2("dt"), f2("dt"), f2("ex"), ALU.add), [b2("dt"), b2("ex")], [b2("dt")])
        V(lambda e: e.tensor_tensor(f2("dec"), f2("dt"), atab[:], ALU.mult), [b2("dt"), b_atab], [b2("dec")])
        A(lambda e: e.activation(f2("dec"), f2("dec"), AF.Exp), [b2("dec")], [b2("dec")])
        xdr, b_xdr = C.sb("xdr", [DEC, DI], F32)
        der, b_der = C.sb("der", [DEC, DI], F32)
        xs3 = xc[:, 0:DI].rearrange("p (h d) -> p h d", h=HS)
        V(lambda e: e.tensor_tensor(xdr[:].rearrange("p (h d) -> p h d", h=HS), xs3, f2("dt").unsqueeze(2).to_broadcast([DEC, HS, 64]), ALU.mult),
          [b_xc, b2("dt")], [b_xdr])
        V(lambda e: e.tensor_copy(der[:].rearrange("p (h d) -> p h d", h=HS), f2("dec").unsqueeze(2).to_broadcast([DEC, HS, 64])), [b2("dec")], [b_der])
        NC16 = DI // 128
        xdc, b_xdc = C.sb("xdc", [128, NC16, DEC], F32)
        dcc, b_dcc = C.sb("dcc", [128, NC16, DEC], F32)
        ycol, b_ycol = C.sb("ycol", [128, NC16, DEC], F32)
        for c in range(NC16):
            transposeF(xdc[:, c, :], xdr[:, c * 128:(c + 1) * 128], DEC, 128, b_xdr, b_xdc)
            transposeF(dcc[:, c, :], der[:, c * 128:(c + 1) * 128], DEC, 128, b_der, b_dcc)
        stb = [C.sb("stb%d" % i, [128, NC16, 128], F32) for i in range(2)]
        Bb, b_Bb = C.sb("Bbs", [128, 512], F32)
        Cb, b_Cb = C.sb("Cbs", [128, 512], F32)
        tmp2, b_tmp2 = C.sb("tmp2s", [128, 128], F32)
        V(lambda e: e.memset(ycol[:], 0.0), [], [b_ycol])
        for s_ in range(DEC):
            st_, bst_ = stb[s_ % 2]
            S.dma("sync", st_[:], st_ssm[s_].rearrange("(c p) n -> p c n", p=128), writes=[bst_])
            selm, b_selm = C.sb("selm%d" % s_, [DEC, 128], F32)
            V(lambda e: e.tensor_copy(selm[:], identf[:DEC, s_:s_ + 1].to_broadcast([DEC, 128])), [b_identf], [b_selm])
            for (dst, bd, o) in ((Bb, b_Bb, DI), (Cb, b_Cb, DI + 512)):
                pt, bpt = C.psum(PS_MM)
                T(lambda e: e.matmul(pt[:, :512], selm[:], xc[:, o:o + 512], start=True, stop=True), [b_selm, b_xc], [bpt])
                V(lambda e: e.tensor_copy(dst[:], pt[:, :512]), [bpt], [bd])
            for c in range(NC16):
                g = (2 * c) // HPGS
                V(lambda e: e.tensor_scalar(st_[:, c, :], st_[:, c, :], dcc[:, c, s_:s_ + 1], None, op0=ALU.mult), [bst_, b_dcc], [bst_])
                V(lambda e: e.scalar_tensor_tensor(st_[:, c, :], Bb[:, g * 128:(g + 1) * 128], xdc[:, c, s_:s_ + 1], st_[:, c, :],
                                                   op0=ALU.mult, op1=ALU.add), [b_Bb, b_xdc, bst_], [bst_])
                V(lambda e: e.tensor_tensor(tmp2[:], st_[:, c, :], Cb[:, g * 128:(g + 1) * 128], ALU.mult), [bst_, b_Cb], [b_tmp2])
                V(lambda e: e.reduce_sum(ycol[:, c, s_:s_ + 1], tmp2[:], axis=AX.X), [b_tmp2], [b_ycol])
            S.dma("sync", ssm_s[s_].rearrange("(c p) n -> p c n", p=128), st_[:], reads=[bst_], sembuf=bst_)
        yr, b_yr = C.sb("yrs", [DEC, DI], F32)
        for c in range(NC16):
            transposeF(yr[:, c * 128:(c + 1) * 128], ycol[:, c, :], 128, DEC, b_ycol, b_yr)
        V(lambda e: e.tensor_tensor(xdr[:].rearrange("p (h d) -> p h d", h=HS), xs3, tb["dskip"][0][:].unsqueeze(2).to_broadcast([DEC, HS, 64]), ALU.mult),
          [b_xc, tb["dskip"][1]], [b_xdr])
        V(lambda e: e.tensor_tensor(yr[:], yr[:], xdr[:], ALU.add), [b_yr, b_xdr], [b_yr])
        A(lambda e: e.activation(der[:], raw[:, 0:DI], AF.Silu), [b_raw], [b_der])
        V(lambda e: e.tensor_tensor(yr[:], yr[:], der[:], ALU.mult), [b_yr, b_der], [b_yr])
        for g in range(4):
            A(lambda e, g=g: e.activation(jk[:, :GW], yr[:, g * GW:(g + 1) * GW], AF.Square, accum_out=stt[R_, 4 + g:5 + g]), [b_yr], [b_jk, b_stt])
        A(lambda e: e.activation(stt[R_, 8:12], stt[R_, 4:8], AF.Sqrt, scale=1.0 / GW, bias=EPS), [b_stt], [b_stt])
        V(lambda e: e.reciprocal(stt[R_, 12:16], stt[R_, 8:12]), [b_stt], [b_stt])
        for g in range(4):
            V(lambda e, g=g: e.scalar_tensor_tensor(yr[:, g * GW:(g + 1) * GW], yr[:, g * GW:(g + 1) * GW], stt[R_, 12 + g:13 + g], ssmn[:, g * GW:(g + 1) * GW],
                                                    op0=ALU.mult, op1=ALU.mult), [b_yr, b_stt, b_ssmn], [b_yr])
        S.dma("sync", ycs_d[:, 0:DI], yr[:], reads=[b_yr], writes=[bycs], sembuf=b_yr)
        rp, b_rp = C.sb("rps", [DEC, 2, 32], F32)
        S.dma("sync", rp[:].rearrange("p a c -> p (a c)"), rope_s.rearrange("a c -> (a c)").unsqueeze(0).partition_broadcast(DEC) if False else rope_s.rearrange("(o a) c -> o (a c)", o=1).partition_broadcast(DEC), writes=[b_rp])
        qk, b_qk = C.sb("qks", [DEC, NH * 64 + 768], F32)
        ta, b_ta = C.sb("tas", [DEC, (NH + 6) * 32], F32)
        tb2, b_tb2 = C.sb("tbs", [DEC, (NH + 6) * 32], F32)

        def rope_rows(dst, src, nh):
            s3 = src.rearrange("p (g d) -> p g d", g=nh)
            d3 = dst.rearrange("p (g d) -> p g d", g=nh)
            cs = rp[:, 0, :].unsqueeze(1).to_broadcast([DEC, nh, 32])
            sn = rp[:, 1, :].unsqueeze(1).to_broadcast([DEC, nh, 32])
            a3 = ta[:, :nh * 32].rearrange("p (g d) -> p g d", g=nh)
            b3 = tb2[:, :nh * 32].rearrange("p (g d) -> p g d", g=nh)
            V(lambda e: e.tensor_tensor(a3, s3[:, :, 0:32], cs, ALU.mult), [b_raw, b_rp], [b_ta])
            V(lambda e: e.tensor_tensor(b3, s3[:, :, 32:64], sn, ALU.mult), [b_raw, b_rp], [b_tb2])
            V(lambda e: e.tensor_tensor(d3[:, :, 0:32], a3, b3, ALU.subtract), [b_ta, b_tb2], [b_qk])
            V(lambda e: e.tensor_tensor(a3, s3[:, :, 32:64], cs, ALU.mult), [b_raw, b_rp], [b_ta])
            V(lambda e: e.tensor_tensor(b3, s3[:, :, 0:32], sn, ALU.mult), [b_raw, b_rp], [b_tb2])
            V(lambda e: e.tensor_tensor(d3[:, :, 32:64], a3, b3, ALU.add), [b_ta, b_tb2], [b_qk])

        rope_rows(qk[:, 0:NH * 64], raw[:, cfg.o_q:cfg.o_q + NH * 64], NH)
        for br in range(3):
            o = cfg.o_kv + br * 256
            rope_rows(qk[:, NH * 64 + br * 256:NH * 64 + br * 256 + 128], raw[:, o:o + 128], 2)
            V(lambda e: e.tensor_copy(qk[:, NH * 64 + br * 256 + 128:NH * 64 + (br + 1) * 256], raw[:, o + 128:o + 256]), [b_raw], [b_qk])
        S.dma("sync", kv_s, qk[:, NH * 64:], reads=[b_qk], sembuf=b_qk)
        kvb, b_kvb = C.sb("kvbs", [DEC, NH * 64 + 768], BF16)
        V(lambda e: e.tensor_copy(kvb[:], qk[:]), [b_qk], [b_kvb])
        S.dma("sync", vn_d, kvb[:, NH * 64:], reads=[b_kvb], writes=[bvn], sembuf=b_kvb)
        gsr, b_gsr = C.sb("gsr", [DEC, 3 * NH], F32)
        A(lambda e: e.activation(gsr[:], raw[:, cfg.o_g:cfg.o_g + 3 * NH], AF.Sigmoid), [b_raw], [b_gsr])
        S.dma("sync", gss_d, gsr[:], reads=[b_gsr], writes=[bgss], sembuf=b_gsr)
        qT_d = dscr("qTs_d", [128, HPG + 3, DEC], BF16)
        bqT = Buf("sscr_qT")
        qTa, b_qTa = C.sb("qTa", [128, HPG + 3, DEC], BF16)
        q4 = kvb[:, 0:NH * 64].rearrange("p (g h d) -> p g h d", g=2, h=HPG)
        qpair, b_qpair = C.sb("qpair", [DEC, 128], BF16)
        for hh in range(HPG + 3):
            if hh < HPG:
                for g in range(2):
                    V(lambda e, g=g: e.tensor_copy(qpair[:, g * 64:(g + 1) * 64], q4[:, g, hh, :]), [b_kvb], [b_qpair])
                src, bsrc = qpair[:], b_qpair
            else:
                o = NH * 64 + (hh - HPG) * 256
                src, bsrc = kvb[:, o:o + 128], b_kvb
            pt, bpt = C.psum(PS_TR)
            ptb = pt[:].bitcast(BF16)
            T(lambda e: e.transpose(ptb[:, 0:DEC], src, identb[:DEC, :DEC]), [bsrc, b_identb], [bpt])
            A(lambda e: e.copy(qTa[:, hh, :], ptb[:, 0:DEC]), [bpt], [b_qTa])
        S.dma("sync", qT_d, qTa[:], reads=[b_qTa], writes=[bqT], sembuf=b_qTa)

    exms_d = din("exms", [NSL, PAST], BF16)
    iota_d = din("iota_p", [128, NPG], F32)
    mcs_d = din("maskcs", [128, NBS])
    with Phase():
        crK, b_crK = C.sb("crK", [128, PAST], BF16)
        crV, b_crV = C.sb("crV", [128, PAST], BF16)
        KTs, b_KTs = C.sb("sKTs", [128, PAST + 128], BF16)
        Vs, b_Vs = C.sb("sVs", [128, NPG + 1, 2, 65], BF16)
        KTw, b_KTw = C.sb("sKTw", [128, (NWT + 1) * 128], BF16)
        Vw, b_Vw = C.sb("sVw", [128, NWT + 1, 2, 65], BF16)
        w1b = [C.sb("sw1b%d" % i, [128, 32, 128], BF16) for i in range(2)]
        hidT, b_hidT = C.sb("shidT", [128, 2, 2, NBS * 128], BF16)
        kcT, b_kcT = C.sb("skcT", [128, NBS * 128], BF16)
        vca, b_vca = C.sb("svca", [128, NBS, 2, CWs], BF16)
        ovt, b_ovt = C.sb("sovt", [128, NBS, NSL + 1], F32)
        exms, b_exms = C.sb("sexms", [NSL, PAST], BF16)
        mcs, b_mcs = C.sb("smcs", [128, NBS], F32)
        cpe, b_cpe = C.sb("scpe", [128, 2, 32], BF16)
        cb1, b_cb1 = C.sb("scb1", [128, 2], F32)
        b1e, b_b1e = C.sb("sb1e", [128, 2], F32)
        cw2k, b_cw2k = C.sb("scw2k", [128, 2, 128], BF16)
        cw2v, b_cw2v = C.sb("scw2v", [128, 64], BF16)
        cb2k, b_cb2k = C.sb("scb2k", [128, 1], F32)
        cb2v, b_cb2v = C.sb("scb2v", [128, 64], F32)
        qTb, b_qTb = C.sb("sqTb", [128, HPG + 3, DEC], BF16)
        fbs, b_fbs = C.sb("sfbs", [1, NSL + 1], F32)
        e0, b_e0 = C.sb("se0", [128, 1], F32)
        ne0, b_ne0 = C.sb("sne0", [128, 1], F32)
        one1, b_one1 = C.sb("sone1", [128, 1], F32)
        iot, b_iot = C.sb("siot", [128, NPG], F32)
        idxf, b_idxf = C.sb("sidxf", [128, NPG], F32)
        idx, b_idx = C.sb("sidx", [128, NPG], I32)
        bc3_ = Buf("constsS")
        for (dst, src) in ((ovt, ovs_d.rearrange("(b p) j -> p b j", p=128)), (exms, exms_d), (mcs, mcs_d), (cb1, cb1_d), (cb2k, cb2k_d),
                           (cb2v, cb2v_d.partition_broadcast(128)), (fbs, fbs_d), (qTb, qT_d)):
            S.dma("sync", dst[:], src, reads=[bqT], writes=[bc3_])
        for (dst, src) in ((cpe, cpe_d.rearrange("k p l -> p k l")), (cw2k, cw2k_d.rearrange("g c m -> c g m")), (cw2v, cw2v_d)):
            S.dma("gpsimd", dst[:], src, writes=[bc3_])
        for kvi in range(2):
            S.dma("gpsimd", w1b[kvi][0][:].rearrange("p l c -> p (l c)"), cw1_d[kvi], writes=[bc3_])
        b_ovt = b_exms = b_mcs = b_cb1 = b_cb2k = b_cb2v = b_fbs = b_qTb = b_cpe = b_cw2k = b_cw2v = bc3_
        V(lambda e: e.tensor_copy(e0[:], identf[:, 0:1]), [b_identf], [b_e0])
        V(lambda e: e.tensor_scalar(ne0[:], identf[:, 0:1], -1.0, 1.0, op0=ALU.mult, op1=ALU.add), [b_identf], [b_ne0])
        V(lambda e: e.memset(one1[:], 1.0), [], [b_one1])
        S.dma("sync", iot[:], iota_d, writes=[b_iot])
        for kvi in range(2):
            w1v = w1b[kvi][0]
            pb1, bpb1 = C.psum(PS_MM)
            for l in range(32):
                T(lambda e, l=l: e.matmul(pb1[:, 0:1], w1v[0:64, l, :], cpe[0:64, kvi, l:l + 1], start=(l == 0), stop=(l == 31)),
                  [bc3_], [bpb1])
            V(lambda e: e.tensor_tensor(b1e[:, kvi:kvi + 1], pb1[:, 0:1], cb1[:, kvi:kvi + 1], ALU.add), [bpb1, b_cb1], [b_b1e])
        V(lambda e: e.memset(vca[:], 0.0), [], [b_vca])
        V(lambda e: e.memset(vca[:, :, :, 64:65], 1.0), [], [b_vca])
        for g in range(2):
            V(lambda e, g=g: e.tensor_copy(vca[:, :, g, 65:CWs], ovt[:]), [b_ovt], [b_vca])
        V(lambda e: e.memset(hidT[:], 0.0), [], [b_hidT])
        V(lambda e: e.memset(kcT[:], 0.0), [], [b_kcT])
        pg = [C.sb("spg%d" % i, [128, 256], F32) for i in range(4)]
        pgb = [C.sb("spgb%d" % i, [128, 256], BF16) for i in range(4)]
        Es, b_Es = C.sb("sEs", [128, HPG], F32)
        PTs = [C.sb("sPTs%d" % i, [128, HPG], BF16) for i in range(2)]
        pmS, b_pmS = C.sb("spmS", [128, 1], F32)
        nst, b_nst = C.sb("snst", [HPG, 4], F32)
        gt, b_gt = C.sb("sgt", [HPG, 3], F32)
        oac = [C.sb("soac%d" % i, [HPG, 64], F32) for i in range(2)]
        otm, b_otm = C.sb("sotm", [HPG, 64], F32)
        impu, b_impu = C.sb("simpu", [HPG, NSL + 1], F32)
        wk1, b_wk1 = C.sb("swk1", [1, NSL + 1], F32)
        wk2, b_wk2 = C.sb("swk2", [1, NSL + 1], F32)
        mx8, b_mx8 = C.sb("smx8", [1, 8], F32)
        selr, b_selr = C.sb("sselr", [1, NSL], F32)
        selTb, b_selTb = C.sb("sselTb", [128, 1], BF16)
        rot = [0]

        def page_in(src_ap, idx_col, k, use_idx=True):
            p_, bp_ = pg[k % 4]
            pb_, bpb_ = pgb[k % 4]
            if use_idx:
                S.dma_custom("gpsimd", lambda e: e.indirect_dma_start(out=p_[:], out_offset=None, in_=src_ap,
                                                                      in_offset=bass.IndirectOffsetOnAxis(ap=idx_col, axis=0)), [b_idx], [bp_])
            else:
                S.dma("sync", p_[:], src_ap, writes=[bp_])
            P(lambda e: e.tensor_copy(pb_[:], p_[:]), [bp_], [bpb_])
            return pb_, bpb_

        def tr_to(dst, b_dst, src, b_src):
            pt, bpt = C.psum(PS_TR)
            ptb = pt[:].bitcast(BF16)
            T(lambda e: e.transpose(ptb[:, 0:128], src, identb[:]), [b_src, b_identb], [bpt])
            A(lambda e: e.copy(dst, ptb[:, 0:128]), [bpt], [b_dst])

        def branch(KT, bKT, Vv, bVv, tiles, mask_fn, g, s_, per, rhs_fn, pso, bpso):
            gsl = slice(g * 64, (g + 1) * 64)
            for n_, kt in enumerate(tiles):
                ps_, bps_ = C.psum((0, 1, 2))
                T(lambda e: e.matmul(ps_[:, :HPG], KT[gsl, kt * 128:(kt + 1) * 128], qTb[gsl, 0:HPG, s_], start=True, stop=True),
                  [bKT, b_qTb], [bps_])
                V(lambda e: e.tensor_copy(Es[:], ps_[:, :HPG]), [bps_], [b_Es])
                A(lambda e: e.activation(Es[:], Es[:], AF.Exp, scale=0.125), [b_Es], [b_Es])
                PT, bPT = PTs[rot[0] % 2]
                rot[0] += 1
                msk, bmsk = mask_fn(kt)
                V(lambda e: e.tensor_scalar(PT[:], Es[:], msk, None, op0=ALU.mult), [b_Es, bmsk], [bPT])
                T(lambda e: e.matmul(pso[:HPG, :per], PT[:], rhs_fn(kt), start=(n_ == 0), stop=(n_ == len(tiles) - 1)), [bPT, bVv], [bpso])

        def evac_s(pso, bpso, gi_, oa, boa, first):
            V(lambda e: e.tensor_scalar(nst[:, 0:1], pso[:HPG, 64:65], 1e-30, None, op0=ALU.max), [bpso], [b_nst])
            V(lambda e: e.reciprocal(nst[:, 1:2], nst[:, 0:1]), [b_nst], [b_nst])
            V(lambda e: e.tensor_tensor(nst[:, 2:3], nst[:, 1:2], gt[:, gi_:gi_ + 1], ALU.mult), [b_nst, b_gt], [b_nst])
            if first:
                V(lambda e: e.tensor_scalar(oa[:], pso[:HPG, 0:64], nst[:, 2:3], None, op0=ALU.mult), [bpso, b_nst], [boa])
            else:
                V(lambda e: e.scalar_tensor_tensor(oa[:], pso[:HPG, 0:64], nst[:, 2:3], oa[:], op0=ALU.mult, op1=ALU.add), [bpso, b_nst, boa], [boa])

        for s_ in range(DEC):
            S.dma("sync", idx[:], ptab[s_:s_ + 1, :].partition_broadcast(128), writes=[b_idx])
            V(lambda e: e.tensor_copy(idxf[:], idx[:]), [b_idx], [b_idxf])
            V(lambda e: e.scalar_tensor_tensor(idxf[:], idxf[:], 128.0, iot[:], op0=ALU.mult, op1=ALU.add), [b_idxf, b_iot], [b_idxf])
            V(lambda e: e.tensor_copy(idx[:], idxf[:]), [b_idxf], [b_idx])
            V(lambda e: e.memset(KTs[:, PAST:PAST + 128], 0.0), [], [b_KTs])
            V(lambda e: e.memset(KTw[:, NWT * 128:(NWT + 1) * 128], 0.0), [], [b_KTw])
            V(lambda e: e.memset(Vs[:, NPG, :, :], 0.0), [], [b_Vs])
            V(lambda e: e.memset(Vw[:, NWT, :, :], 0.0), [], [b_Vw])
            kk = 0
            for j in range(NPG):
                pb_, bpb_ = page_in(c_cmp, idx[:, j:j + 1], kk)
                kk += 1
                tr_to(crK[:, j * 128:(j + 1) * 128], b_crK, pb_[:, 0:128], bpb_)
                tr_to(crV[:, j * 128:(j + 1) * 128], b_crV, pb_[:, 128:256], bpb_)
                pb_, bpb_ = page_in(c_slc, idx[:, j:j + 1], kk)
                kk += 1
                tr_to(KTs[:, j * 128:(j + 1) * 128], b_KTs, pb_[:, 0:128], bpb_)
                V(lambda e: e.tensor_copy(Vs[:, j, :, 0:64], pb_[:, 128:256].rearrange("p (g d) -> p g d", g=2)), [bpb_], [b_Vs])
            V(lambda e: e.memset(Vs[:, 0:NPG, :, 64:65], 1.0), [b_Vs], [b_Vs])
            for j in range(NWT):
                pb_, bpb_ = page_in(c_win[s_, j * 128:(j + 1) * 128, :], None, kk, use_idx=False)
                kk += 1
                tr_to(KTw[:, j * 128:(j + 1) * 128], b_KTw, pb_[:, 0:128], bpb_)
                V(lambda e: e.tensor_copy(Vw[:, j, :, 0:64], pb_[:, 128:256].rearrange("p (g d) -> p g d", g=2)), [bpb_], [b_Vw])
            V(lambda e: e.memset(Vw[:, 0:NWT, :, 64:65], 1.0), [b_Vw], [b_Vw])
            V(lambda e: e.tensor_copy(KTs[:, PAST:PAST + 1], qTb[:, HPG + 1, s_:s_ + 1]), [b_qTb], [b_KTs])
            V(lambda e: e.tensor_copy(KTw[:, NWT * 128:NWT * 128 + 1], qTb[:, HPG + 2, s_:s_ + 1]), [b_qTb], [b_KTw])
            for g in range(2):
                S.dma("sync", Vs[0:1, NPG, g, 0:64], vn_d[s_:s_ + 1, 256 + 128 + g * 64:256 + 128 + (g + 1) * 64], reads=[bvn], writes=[b_Vs])
                S.dma("sync", Vw[0:1, NWT, g, 0:64], vn_d[s_:s_ + 1, 512 + 128 + g * 64:512 + 128 + (g + 1) * 64], reads=[bvn], writes=[b_Vw])
            V(lambda e: e.memset(Vs[0:1, NPG, :, 64:65], 1.0), [b_Vs], [b_Vs])
            V(lambda e: e.memset(Vw[0:1, NWT, :, 64:65], 1.0), [b_Vw], [b_Vw])
            for kvi, (cr, bcr) in enumerate(((crK, b_crK), (crV, b_crV))):
                w1v = w1b[kvi][0]
                for g in range(2):
                    gsl2 = slice(g * 64, (g + 1) * 64)
                    ph, bph = C.psum(PS_MM)
                    for l in range(32):
                        T(lambda e, l=l: e.matmul(ph[:, :NCS], w1v[gsl2, l, :], cr[gsl2, l:l + 16 * (NCS - 1) + 1:16],
                                                  start=(l == 0), stop=(l == 31)), [bc3_, bcr], [bph])
                    A(lambda e: e.activation(hidT[:, kvi, g, 0:NCS], ph[:, :NCS], AF.Silu, bias=b1e[:, kvi:kvi + 1]), [bph, b_b1e], [b_hidT])
            pk, bpk = C.psum(PS_MM)
            for g in range(2):
                T(lambda e, g=g: e.matmul(pk[:, :NCS], cw2k[:, g, :], hidT[:, 0, g, 0:NCS], start=(g == 0), stop=(g == 1)), [b_cw2k, b_hidT], [bpk])
            A(lambda e: e.activation(kcT[:, 0:NCS], pk[:, :NCS], AF.Identity, bias=cb2k[:, 0:1]), [bpk, b_cb2k], [b_kcT])
            for bt in range(NBS):
                pv2, bpv2 = C.psum(PS_MM)
                for g in range(2):
                    T(lambda e, g=g: e.matmul(pv2[:, g * 64:(g + 1) * 64], hidT[:, 1, g, bt * 128:(bt + 1) * 128], cw2v[:], start=True, stop=True),
                      [b_hidT, b_cw2v], [bpv2])
                V(lambda e: e.tensor_tensor(vca[:, bt, :, 0:64], pv2[:, 0:128].rearrange("p (g d) -> p g d", g=2),
                                            cb2v[:].unsqueeze(1).to_broadcast([128, 2, 64]), ALU.add), [bpv2, b_cb2v], [b_vca])
            for g in range(2):
                oa, boa = oac[g]
                S.dma("sync", gt[:], gss_d[s_, g * HPG * 3:(g + 1) * HPG * 3].rearrange("(h c) -> h c", c=3), reads=[bgss], writes=[b_gt])
                pso, bpso = C.ps[3]
                branch(kcT, b_kcT, vca, b_vca, list(range(NBS)), lambda kt: (mcs[:, kt:kt + 1], b_mcs), g, s_, CWs,
                       lambda kt: vca[:, kt, g, :], pso, bpso)
                evac_s(pso, bpso, 0, oa, boa, True)
                V(lambda e: e.tensor_copy(impu[:], pso[:HPG, 65:CWs]), [bpso], [b_impu])
                pi, bpi = C.psum(PS_MM)
                T(lambda e: e.matmul(pi[:1, :NSL + 1], nst[:, 1:2], impu[:], start=True, stop=True), [b_nst, b_impu], [bpi])
                V(lambda e: e.tensor_tensor(wk1[:], pi[:1, :NSL + 1], fbs[:], ALU.add), [bpi, b_fbs], [b_wk1])
                V(lambda e: e.tensor_copy(wk2[:], wk1[:]), [b_wk1], [b_wk2])
                for r_ in range((cfg.TOPN - 1) // 8):
                    V(lambda e: e.max(out=mx8[:], in_=wk2[:]), [b_wk2], [b_mx8])
                    V(lambda e: e.match_replace(out=wk2[:], in_to_replace=mx8[:], in_values=wk2[:], imm_value=-3.0e38), [b_mx8, b_wk2], [b_wk2])
                V(lambda e: e.max(out=mx8[:], in_=wk2[:]), [b_wk2], [b_mx8])
                kth = (cfg.TOPN - 1) % 8
                V(lambda e: e.tensor_scalar(selr[:], wk1[:, 0:NSL], mx8[:, kth:kth + 1], None, op0=ALU.is_ge), [b_wk1, b_mx8], [b_selr])
                pt_, bpt_ = C.psum(PS_MM)
                T(lambda e: e.transpose(pt_[:NSL, 0:1], selr[:], identf[0:1, 0:1]), [b_selr, b_identf], [bpt_])
                V(lambda e: e.tensor_copy(selTb[:NSL, :], pt_[:NSL, 0:1]), [bpt_], [b_selTb])
                pso, bpso = C.ps[6]

                def slc_mask(kt):
                    if kt == NPG:
                        return e0[:, 0:1], b_e0
                    pm, bpm = C.psum(PS_TR)
                    T(lambda e: e.matmul(pm[:, 0:1], exms[:, kt * 128:(kt + 1) * 128], selTb[:NSL, :], start=True, stop=True), [b_exms, b_selTb], [bpm])
                    V(lambda e: e.tensor_copy(pmS[:], pm[:, 0:1]), [bpm], [b_pmS])
                    return pmS[:, 0:1], b_pmS
                branch(KTs, b_KTs, Vs, b_Vs, list(range(NPG + 1)), slc_mask, g, s_, 65, lambda kt: Vs[:, kt, g, :], pso, bpso)
                evac_s(pso, bpso, 1, oa, boa, False)
                pso, bpso = C.ps[7]

                def win_mask(kt):
                    if kt == NWT:
                        return e0[:, 0:1], b_e0
                    if kt == 0 and NWT * 128 >= cfg.WINDOW:
                        return ne0[:, 0:1], b_ne0
                    return one1[:, 0:1], b_one1
                branch(KTw, b_KTw, Vw, b_Vw, list(range(NWT + 1)), win_mask, g, s_, 65, lambda kt: Vw[:, kt, g, :], pso, bpso)
                evac_s(pso, bpso, 2, oa, boa, False)
                S.dma("sync", ycs_d[s_, DI + g * HPG * 64:DI + (g + 1) * HPG * 64].rearrange("(h d) -> h d", d=64), oa[:],
                      reads=[boa], writes=[bycs], sembuf=boa)

    with Phase():
        hs, b_hs = C.sb("hs3", [DEC, D], F32)
        stt, b_stt = C.sb("stt3", [128, 16], F32)
        jk, b_jk = C.sb("jk3", [DEC, D], F32)
        xb, b_xb = C.sb("xb3", [DEC, D], BF16)
        xT, b_xT = C.sb("xT3", [128, KC, DEC], BF16)
        KO = max(DMIX // 128, DFF // 128)
        aT, b_aT = C.sb("aT3", [128, KO, DEC], BF16)
        wbs = [C.sb("wb3_%d" % i, [128, KO, 512], BF16) for i in range(2)]
        mix, b_mix = C.sb("mix3", [DEC, D], F32)
        yc, b_yc = C.sb("yc3", [DEC, DMIX], F32)
        ycb, b_ycb = C.sb("ycb3", [DEC, DMIX], BF16)
        ur, b_ur = C.sb("ur3", [DEC, 2 * DFF], F32)
        fhc = [C.sb("fh3_%d" % i, [DEC, 2, 512], F32) for i in range(2)]
        fw, b_fw = C.sb("fw3", [DEC, 3, 512], F32)
        fb_, b_fb = C.sb("fb3", [DEC, 512], F32)
        uc, b_uc = C.sb("uc3", [DEC, 2 * DFF], F32)
        t5, b_t5 = C.sb("t53", [DEC, 512], F32)
        hb, b_hb = C.sb("hb3", [DEC, DFF], BF16)
        rowt = {nm: C.sb("rw_" + nm, [DEC, w_], F32) for nm, w_ in (("bpw1", 2 * D), ("bpw2", D), ("dwb", D), ("lng", D), ("lnb", D))}
        for nm, src in (("bpw1", bpw1_r), ("bpw2", bpw2_r), ("dwb", dwb_r), ("lng", lng_r), ("lnb", lnb_r)):
            S.dma("sync", rowt[nm][0][:], src.partition_broadcast(DEC), writes=[rowt[nm][1]])
        wrot = [0]
        nrot = [0]

        def to_T(dst, b_dst, src_bf, b_src, n):
            for c in range(n):
                pt, bpt = C.psum(PS_TR)
                ptb = pt[:].bitcast(BF16)
                T(lambda e: e.transpose(ptb[:, 0:DEC], src_bf[:, c * 128:(c + 1) * 128], identb[:DEC, :DEC]), [b_src, b_identb], [bpt])
                A(lambda e: e.copy(dst[:, c, :], ptb[:, 0:DEC]), [bpt], [b_dst])

        def rows_mm(W, kcn, actT, b_actT, ncols, dst, b_dst, c_off=0):
            for c0 in range(0, ncols, 512):
                cn = min(512, ncols - c0)
                wb, bwb = wbs[wrot[0] % 2]
                wrot[0] += 1
                S.dma("gpsimd", wb[:, :kcn, :cn], W[:, c_off + c0:c_off + c0 + cn].rearrange("(kc p) c -> p kc c", p=128), writes=[bwb])
                pt, bpt = C.psum(PS_MM)
                for kc in range(kcn):
                    T(lambda e, kc=kc: e.matmul(pt[:DEC, :cn], actT[:, kc, :], wb[:, kc, :cn], start=(kc == 0), stop=(kc == kcn - 1)),
                      [bwb, b_actT], [bpt])
                A(lambda e: e.copy(dst[:, c0:c0 + cn], pt[:DEC, :cn]), [bpt], [b_dst])

        def norm_T(gain):
            load_gain(gain)
            rms_rows(xb[:], hs[:], b_hs, b_xb, stt, b_stt, jk, b_jk)
            to_T(xT, b_xT, xb, b_xb, KC)

        def post_add(gain):
            load_gain(gain)
            rms_rows(mix[:], mix[:], b_mix, b_mix, stt, b_stt, jk, b_jk)
            V(lambda e: e.tensor_tensor(hs[:], hs[:], mix[:], ALU.add), [b_hs, b_mix], [b_hs])

        def conv_ffn(layer):
            norm_T(4 * layer + 2)
            rows_mm(w_up[layer], KC, xT, b_xT, 2 * DFF, ur, b_ur)
            S.dma("sync", ffn_s[layer, :, 2 * DFF:4 * DFF], ur[:], reads=[b_ur], sembuf=b_ur)
            for ci, c0 in enumerate(range(0, 2 * DFF, 512)):
                cn = min(512, 2 * DFF - c0)
                fh, b_fh = fhc[ci % 2]
                for k in range(2):
                    S.dma("sync", fh[:, k, :cn], st_ffn[layer, :, k * 2 * DFF + c0:k * 2 * DFF + c0 + cn], writes=[b_fh])
                for k in range(3):
                    S.dma("sync", fw[:, k, :cn], fcw_r[layer, k:k + 1, c0:c0 + cn].partition_broadcast(DEC), writes=[b_fw])
                S.dma("sync", fb_[:, :cn], fcb_r[layer, :, c0:c0 + cn].partition_broadcast(DEC), writes=[b_fb])
                V(lambda e: e.tensor_tensor(uc[:, c0:c0 + cn], ur[:, c0:c0 + cn], fw[:, 2, :cn], ALU.mult), [b_ur, b_fw], [b_uc])
                for k in range(2):
                    V(lambda e, k=k: e.tensor_tensor(t5[:, :cn], fh[:, k, :cn], fw[:, k, :cn], ALU.mult), [b_fh, b_fw], [b_t5])
                    V(lambda e: e.tensor_tensor(uc[:, c0:c0 + cn], uc[:, c0:c0 + cn], t5[:, :cn], ALU.add), [b_uc, b_t5], [b_uc])
                V(lambda e: e.tensor_tensor(uc[:, c0:c0 + cn], uc[:, c0:c0 + cn], fb_[:, :cn], ALU.add), [b_uc, b_fb], [b_uc])
                S.dma("sync", ffn_s[layer, :, c0:c0 + cn], fh[:, 1, :cn], reads=[b_fh], sembuf=b_fh)
            A(lambda e: e.activation(uc[:, 0:DFF], uc[:, 0:DFF], AF.Gelu_apprx_tanh), [b_uc], [b_uc])
            V(lambda e: e.tensor_tensor(hb[:], uc[:, 0:DFF], uc[:, DFF:2 * DFF], ALU.mult), [b_uc], [b_hb])
            to_T(aT, b_aT, hb, b_hb, DFF // 128)
            rows_mm(w_down[layer], DFF // 128, aT, b_aT, D, mix, b_mix)
            post_add(4 * layer + 3)

        S.dma("sync", hs[:], xs_in, writes=[b_hs])
        S.dma("sync", yc[:], ycs_d, reads=[bycs], writes=[b_yc])
        V(lambda e: e.tensor_copy(ycb[:], yc[:]), [b_yc], [b_ycb])
        to_T(aT, b_aT, ycb, b_ycb, DMIX // 128)
        rows_mm(w_out, DMIX // 128, aT, b_aT, D, mix, b_mix)
        post_add(1)
        conv_ffn(0)
        norm_T(4)
        u2, b_u2 = C.sb("u23", [DEC, 2 * D], F32)
        rows_mm(w_pw1, KC, xT, b_xT, 2 * D, u2, b_u2)
        V(lambda e: e.tensor_tensor(u2[:], u2[:], rowt["bpw1"][0][:], ALU.add), [b_u2, rowt["bpw1"][1]], [b_u2])
        A(lambda e: e.activation(u2[:, D:2 * D], u2[:, D:2 * D], AF.Sigmoid), [b_u2], [b_u2])
        V(lambda e: e.tensor_tensor(u2[:, 0:D], u2[:, 0:D], u2[:, D:2 * D], ALU.mult), [b_u2], [b_u2])
        S.dma("sync", arow_d, u2[:, 0:D], reads=[b_u2], writes=[barow], sembuf=b_u2)
        ah = [C.sb("ah3_%d" % i, [31, D], F32) for i in range(2)]
        dwt, b_dwt = C.sb("dwt3", [31, D], F32)
        selc, b_selc = C.sb("selc3", [31, DEC, DEC], F32)
        S.dma("sync", dwt[:], dww_r, writes=[b_dwt])
        S.dma("sync", selc[:], selc_d.rearrange("s k m -> k s m"), writes=[b_selc])
        pcs = [C.psum(PS_MM) for _ in range((D + 511) // 512)]
        for s_ in range(DEC):
            a_, ba_ = ah[s_ % 2]
            S.dma("sync", a_[0:30, :], st_conf[s_], writes=[ba_])
            S.dma("sync", a_[30:31, :], arow_d[s_:s_ + 1, :], reads=[barow], writes=[ba_])
            S.dma("sync", conf_s[s_], a_[1:31, :], reads=[ba_], sembuf=ba_)
            V(lambda e: e.tensor_tensor(a_[:], a_[:], dwt[:], ALU.mult), [ba_, b_dwt], [ba_])
            for ci, c0 in enumerate(range(0, D, 512)):
                cn = min(512, D - c0)
                pc, bpc = pcs[ci]
                T(lambda e: e.matmul(pc[:DEC, :cn], selc[:, s_, :], a_[:, c0:c0 + cn], start=(s_ == 0), stop=(s_ == DEC - 1)), [b_selc, ba_], [bpc])
        cv, b_cv = C.sb("cv3", [DEC, D], F32)
        for ci, c0 in enumerate(range(0, D, 512)):
            cn = min(512, D - c0)
            pc, bpc = pcs[ci]
            V(lambda e: e.tensor_tensor(cv[:, c0:c0 + cn], pc[:DEC, :cn], rowt["dwb"][0][:, c0:c0 + cn], ALU.add), [bpc, rowt["dwb"][1]], [b_cv])
        V(lambda e: e.reduce_sum(stt[R_, 0:1], cv[:], axis=AX.X), [b_cv], [b_stt])
        V(lambda e: e.tensor_scalar(stt[R_, 1:2], stt[R_, 0:1], -1.0 / D, None, op0=ALU.mult), [b_stt], [b_stt])
        V(lambda e: e.tensor_scalar(cv[:], cv[:], stt[R_, 1:2], None, op0=ALU.add), [b_cv, b_stt], [b_cv])
        A(lambda e: e.activation(jk[:, :D], cv[:], AF.Square, accum_out=stt[R_, 2:3]), [b_cv], [b_jk, b_stt])
        A(lambda e: e.activation(stt[R_, 3:4], stt[R_, 2:3], AF.Sqrt, scale=1.0 / D, bias=EPS), [b_stt], [b_stt])
        V(lambda e: e.reciprocal(stt[R_, 4:5], stt[R_, 3:4]), [b_stt], [b_stt])
        V(lambda e: e.scalar_tensor_tensor(cv[:], cv[:], stt[R_, 4:5], rowt["lng"][0][:], op0=ALU.mult, op1=ALU.mult), [b_cv, b_stt, rowt["lng"][1]], [b_cv])
        V(lambda e: e.tensor_tensor(cv[:], cv[:], rowt["lnb"][0][:], ALU.add), [b_cv, rowt["lnb"][1]], [b_cv])
        A(lambda e: e.activation(xb[:], cv[:], AF.Silu), [b_cv], [b_xb])
        to_T(xT, b_xT, xb, b_xb, KC)
        rows_mm(w_pw2, KC, xT, b_xT, D, mix, b_mix)
        V(lambda e: e.tensor_tensor(mix[:], mix[:], rowt["bpw2"][0][:], ALU.add), [b_mix, rowt["bpw2"][1]], [b_mix])
        post_add(5)
        conv_ffn(1)
        S.dma("sync", y_s, hs[:], reads=[b_hs], sembuf=b_hs)


def _bf16():
    import ml_dtypes
    return ml_dtypes.bfloat16


def host_prep(cfg, inp, b, h):
    bf = _bf16()
    NT, D = cfg.NT, cfg.D
    m = {}
    x = np.zeros((NT * 128, D), np.float32)
    xp = inp["x_prompt"][b]
    if h == 1:
        x[:] = xp
    else:
        x[cfg.HALF:] = xp[:cfg.HALF]
    m["xin"] = x
    m["flag"] = np.full((128, 1), float(h), np.float32)
    m["normg"] = np.ascontiguousarray(inp["norm_g"].reshape(8, D))
    m["w_in"] = inp["ab_w_in"][0]
    NCT = cfg.CD // 128
    m["convwT"] = np.ascontiguousarray(inp["ab_conv_w"][0].T.reshape(NCT, 128, 4).transpose(1, 0, 2))
    m["convbT"] = np.ascontiguousarray(inp["ab_conv_b"][0].reshape(NCT, 128).T)
    m["dt_bias"] = np.ascontiguousarray(inp["ab_dt_bias"][0][None])
    m["a_log"] = np.ascontiguousarray(inp["ab_a_log"][0][None])
    m["d_skip"] = np.ascontiguousarray(inp["ab_d_skip"][0][None])
    m["ssm_norm"] = np.ascontiguousarray(inp["ab_ssm_norm"][0][None])
    pos = np.arange(NT * 128) + (h - 1) * cfg.HALF
    inv = (10000.0 ** (-np.arange(32, dtype=np.float32) / 32)).astype(np.float32)
    ang = pos.astype(np.float32)[:, None] * inv[None, :]
    m["rcos"] = np.cos(ang).astype(np.float32)
    m["rsin"] = np.sin(ang).astype(np.float32)
    m["identb"] = np.eye(128, dtype=np.float32).astype(bf)
    m["identf"] = np.eye(128, dtype=np.float32)
    m["utri"] = np.triu(np.ones((128, 128), np.float32))
    kk = np.arange(128)[:, None]
    ii = np.arange(512)[None, :] % 128
    m["negm"] = np.where(kk > ii, -30000.0, 0.0).astype(np.float32).astype(bf)
    w1 = inp["ab_cmp_w1"][0]
    w1r = w1.reshape(2, 32, 64, 128).transpose(0, 2, 1, 3)
    m["cmp_w1"] = np.ascontiguousarray(np.concatenate([w1r, w1r], 1).reshape(2, 128, 32 * 128))
    peT = inp["ab_cmp_pe"][0].transpose(0, 2, 1)
    m["cmp_peT"] = np.ascontiguousarray(np.concatenate([peT, peT], 1))
    m["cmp_b1T"] = np.ascontiguousarray(inp["ab_cmp_b1"][0].T)
    w2 = inp["ab_cmp_w2"][0]
    w2k = np.zeros((2, 128, 128), np.float32)
    w2k[0, :, 0:64] = w2[0]
    w2k[1, :, 64:128] = w2[0]
    m["cmp_w2k"] = w2k
    m["cmp_w2v"] = np.ascontiguousarray(w2[1])
    b2 = inp["ab_cmp_b2"][0]
    m["cmp_b2k"] = np.ascontiguousarray(np.concatenate([b2[0], b2[0]])[:, None].astype(np.float32))
    m["cmp_b2v"] = np.ascontiguousarray(b2[1][None].astype(np.float32))
    NBT = (cfg.NCMP + 127) // 128
    NSLC, TF0 = cfg.NSLC, cfg.TF0
    i_ = np.arange(NBT * 128)[:, None]
    j_ = np.arange(NSLC)[None, :]
    m["ovm"] = ((i_ * 16 < (j_ + 1) * 64) & (i_ * 16 + 32 > j_ * 64) & (i_ < cfg.NCMP)).astype(np.float32)
    base = (h - 1) * cfg.HALF
    lpq = np.arange(TF0 * 128, NT * 128)
    blk = np.arange(NBT * 128)
    valid_blk = (blk < cfg.NCMP) & (16 * blk + base >= 0)
    mc = ((16 * blk[:, None] + 31) <= lpq[None, :]) & valid_blk[:, None]
    m["maskc"] = np.ascontiguousarray(mc.reshape(NBT, 128, -1).astype(np.float32).astype(bf))
    q_abs = lpq + base
    j_abs = np.arange(NSLC) * 64 + base
    validj = (j_abs[None, :] <= q_abs[:, None]) & (j_abs[None, :] >= 0)
    qblk = q_abs // 64
    jb = j_abs // 64
    forced = (jb[None, :] == 0) | (jb[None, :] == qblk[:, None]) | (jb[None, :] == qblk[:, None] - 1)
    m["fbias"] = np.where(validj, np.where(forced, 1e4, 0.0), -1e30).astype(np.float32)
    m["exm"] = (np.arange(NT * 128)[None, :] // 64 == np.arange(NSLC)[:, None]).astype(np.float32).astype(bf)
    qq = np.arange(128)[None, :]
    m["causT"] = (kk <= qq).astype(np.float32).astype(bf)
    m["bandT"] = (kk > qq).astype(np.float32).astype(bf)
    m["w_out"] = inp["ab_w_out"][0]
    m["w_up"] = inp["ffn_w_up"]
    m["w_down"] = inp["ffn_w_down"]
    NFT = 2 * cfg.DFF // 128
    KC = cfg.KC
    m["ffn_cwT"] = np.ascontiguousarray(inp["ffn_conv_w"].transpose(0, 2, 1).reshape(2, NFT, 128, 3).transpose(0, 2, 1, 3))
    m["ffn_cbT"] = np.ascontiguousarray(inp["ffn_conv_b"].reshape(2, NFT, 128).transpose(0, 2, 1))
    m["w_pw1"] = inp["c_w_pw1"][0]
    m["b_pw1T"] = np.ascontiguousarray(inp["c_b_pw1"][0].reshape(2 * KC, 128).T)
    m["dw_wT"] = np.ascontiguousarray(inp["c_dw_w"][0].T.reshape(KC, 128, 31).transpose(1, 0, 2))
    m["dw_bT"] = np.ascontiguousarray(inp["c_dw_b"][0].reshape(KC, 128).T)
    m["ln_gT"] = np.ascontiguousarray(inp["c_ln_g"][0].reshape(KC, 128).T)
    m["ln_bT"] = np.ascontiguousarray(inp["c_ln_b"][0].reshape(KC, 128).T)
    m["w_pw2"] = inp["c_w_pw2"][0]
    m["b_pw2"] = np.ascontiguousarray(inp["c_b_pw2"][0][None])
    c = 2 * b + h
    DEC = cfg.DEC
    sq = slice(c * DEC, (c + 1) * DEC)
    m["xs_in"] = np.ascontiguousarray(inp["x_sample"][sq, 0])
    m["st_ssm"] = np.ascontiguousarray(inp["state_ssm"][0, sq].reshape(DEC, cfg.DI, 128))
    m["st_sconv"] = np.ascontiguousarray(inp["state_ssm_conv"][0, sq].reshape(DEC, -1))
    m["c_cmp"] = inp["cache_cmp"][0].reshape(-1, 256)
    m["c_slc"] = inp["cache_slc"][0].reshape(-1, 256)
    m["c_win"] = np.ascontiguousarray(inp["cache_win"][0, sq].reshape(DEC, -1, 256))
    m["ptab"] = np.ascontiguousarray(inp["page_table"][sq].astype(np.int32))
    m["st_conf"] = np.ascontiguousarray(inp["state_conf_conv"][0, sq])
    m["st_ffn"] = np.ascontiguousarray(inp["state_ffn_conv"][:, sq].reshape(2, DEC, -1))
    m["convw_r"] = inp["ab_conv_w"][0]
    m["convb_r"] = np.ascontiguousarray(inp["ab_conv_b"][0][None])
    m["fcw_r"] = inp["ffn_conv_w"]
    m["fcb_r"] = np.ascontiguousarray(inp["ffn_conv_b"][:, None, :])
    m["dww_r"] = inp["c_dw_w"][0]
    m["dwb_r"] = np.ascontiguousarray(inp["c_dw_b"][0][None])
    m["lng_r"] = np.ascontiguousarray(inp["c_ln_g"][0][None])
    m["lnb_r"] = np.ascontiguousarray(inp["c_ln_b"][0][None])
    m["bpw1_r"] = np.ascontiguousarray(inp["c_b_pw1"][0][None])
    PAST = cfg.PAST
    angs = np.float32(PAST) * inv
    m["rope_s"] = np.stack([np.cos(angs), np.sin(angs)]).astype(np.float32)
    selc = np.zeros((DEC, 31, DEC), np.float32)
    for s_ in range(DEC):
        selc[s_, :, s_] = 1.0
    m["selc"] = selc
    NCS = PAST // 16 - 1
    NBS = (NCS + 127) // 128
    NSL = PAST // 64
    i_ = np.arange(NBS * 128)[:, None]
    j_ = np.arange(NSL + 1)[None, :]
    m["ovs"] = ((i_ * 16 < (j_ + 1) * 64) & (i_ * 16 + 32 > j_ * 64) & (i_ < NCS)).astype(np.float32)
    jj = np.arange(NSL + 1)
    forced_s = (jj == 0) | (jj == NSL) | (jj == NSL - 1)
    m["fbs"] = np.where(forced_s, 1e4, 0.0).astype(np.float32)[None]
    m["exms"] = (np.arange(PAST)[None, :] // 64 == np.arange(NSL)[:, None]).astype(np.float32).astype(bf)
    m["iota_p"] = np.ascontiguousarray(np.broadcast_to(np.arange(128, dtype=np.float32)[:, None], (128, PAST // 128)))
    m["maskcs"] = np.ascontiguousarray((np.arange(NBS * 128) < NCS).astype(np.float32).reshape(NBS, 128).T)
    return m


OUT_SPECS = None


def assemble(cfg, res, nb):
    D, DI, HS, CD, DFF, DEC = cfg.D, cfg.DI, cfg.HS, cfg.CD, cfg.DFF, cfg.DEC
    f = np.float32
    cat = lambda k, cs: np.concatenate([np.asarray(res[c][k], f) for c in cs], 0)
    y_p = np.stack([cat("y_p", (2 * b, 2 * b + 1)) for b in range(nb)])
    allc = range(2 * nb)
    y_s = cat("y_s", allc)[:, None, :]
    ssm_p = np.stack([np.asarray(res[2 * b + 1]["ssm_p"], f).reshape(HS, 64, 128) for b in range(nb)])[None]
    ssm_s = cat("ssm_s", allc).reshape(-1, HS, 64, 128)[None]
    sconv_p = np.stack([np.asarray(res[2 * b + 1]["sconv_p"], f) for b in range(nb)])[None]
    sconv_s = cat("sconv_s", allc).reshape(-1, 3, CD)[None]
    kvs = cat("kv_s", allc)
    outs = [y_p, y_s, ssm_p, ssm_s, sconv_p, sconv_s]
    for bi, nm in enumerate(("cmp_p", "slc_p")):
        p_ = np.stack([cat(nm, (2 * b, 2 * b + 1)) for b in range(nb)]).reshape(nb, -1, 2, 2, 64)[None]
        s_ = kvs[:, bi * 256:(bi + 1) * 256].reshape(-1, 1, 2, 2, 64)[None]
        outs += [p_, s_]
    win_p = np.stack([np.asarray(res[2 * b + 1]["win_p"], f) for b in range(nb)]).reshape(nb, -1, 2, 2, 64)[None]
    win_s = kvs[:, 512:768].reshape(-1, 1, 2, 2, 64)[None]
    outs += [win_p, win_s]
    conf_p = np.stack([np.asarray(res[2 * b + 1]["conf_p"], f) for b in range(nb)])[None]
    conf_s = cat("conf_s", allc)[None]
    ffn_p = np.stack([np.asarray(res[2 * b + 1]["ffn_p"], f) for b in range(nb)], 1)
    ffn_s = np.concatenate([np.asarray(res[c]["ffn_s"], f) for c in allc], 1).reshape(2, -1, 2, 2 * DFF)
    outs += [conf_p, conf_s, ffn_p, ffn_s]
    return tuple(np.ascontiguousarray(o, dtype=np.float32) for o in outs)


_NC_CACHE = {}


def kernel(**inputs):
    cfg = FULL
    inp = {k: np.asarray(v) for k, v in inputs.items()}
    if "nc" not in _NC_CACHE:
        _NC_CACHE["nc"] = build(cfg)
    nc = _NC_CACHE["nc"]
    maps = [host_prep(cfg, inp, c // 2, c % 2) for c in range(8)]
    res = run_bass_kernel_spmd(nc, maps, core_ids=list(range(8)))
    return assemble(cfg, res.results, 4)
```

```python
import numpy as np
from contextlib import ExitStack
import concourse.bass as bass
import concourse.mybir as mybir
from concourse.bass_utils import run_bass_kernel_spmd

F32 = mybir.dt.float32
BF16 = mybir.dt.bfloat16
I32 = mybir.dt.int32
AF = mybir.ActivationFunctionType
ALU = mybir.AluOpType
AX = mybir.AxisListType
EPS = 1e-6


class Buf:
    __slots__ = ("name", "writer", "readers", "dsem", "dcount")

    def __init__(self, name):
        self.name = name
        self.writer = None
        self.readers = {}
        self.dsem = None
        self.dcount = None


class Sched:
    ENG = ("tensor", "vector", "scalar", "gpsimd", "sync")

    def __init__(self, nc, stack):
        self.nc = nc
        self.stack = stack
        self.eng = {"tensor": nc.tensor, "vector": nc.vector, "scalar": nc.scalar,
                    "gpsimd": nc.gpsimd, "sync": nc.sync}
        self.sem = {}
        self.cnt = {}
        self.semobj = {}
        for e in self.ENG:
            self.sem[e] = stack.enter_context(nc.semaphore("s_" + e))
            self.semobj[e] = self.sem[e]
            self.cnt[e] = 0
        self.waited = {e: {} for e in self.ENG}
        self.ndsem = 0
        self.n_wait = 0
        self.n_ops = 0
        self.dcnt = {}
        self.free_sems = {False: [], True: []}
        self.phase_sems = []

    def _dsem(self, buf, sw=False):
        attr = "dcount" if sw else "dsem"
        if not getattr(buf, attr):
            free = self.free_sems[sw]
            if free:
                name = free.pop(0)
            else:
                name = "d%d" % self.ndsem
                self.ndsem += 1
                self.semobj[name] = self.stack.enter_context(self.nc.semaphore(name))
            self.phase_sems.append((sw, name))
            setattr(buf, attr, name)
        return getattr(buf, attr)

    def recycle(self):
        for sw, name in self.phase_sems:
            self.free_sems[sw].append(name)
        self.phase_sems = []

    def share_sem(self, bufs):
        name = self._dsem(bufs[0])
        for b in bufs[1:]:
            b.dsem = name

    def _wait(self, e, tok):
        if tok is None:
            return
        key, val = tok
        if key in self.dcnt:
            val = self.dcnt[key]
        if self.waited[e].get(key, 0) >= val:
            return
        if key == e:
            if e == "tensor" or val <= self.cnt[e] - 2:
                return
        self.waited[e][key] = val
        self.eng[e].wait_ge(self.semobj[key], val)
        self.n_wait += 1

    def _deps(self, e, reads, writes):
        for b in reads:
            self._wait(e, b.writer)
        for b in writes:
            self._wait(e, b.writer)
            for t in list(b.readers.items()):
                self._wait(e, t)

    def op(self, e, fn, reads=(), writes=()):
        self._deps(e, reads, writes)
        ins = fn(self.eng[e])
        self.cnt[e] += 1
        self.n_ops += 1
        ins.then_inc(self.sem[e], 1)
        tok = (e, self.cnt[e])
        for b in reads:
            b.readers[e] = self.cnt[e]
        for b in writes:
            b.writer = tok
            b.readers = {}
        return ins

    def dma(self, q, out, in_, reads=(), writes=(), sembuf=None, **kw):
        self._deps(q, reads, writes)
        sb = sembuf if sembuf is not None else (writes[0] if writes else reads[0])
        name = self._dsem(sb, sw=(q == "gpsimd"))
        ins = self.eng[q].dma_start(out=out, in_=in_, **kw)
        self.n_ops += 1
        self.dcnt[name] = self.dcnt.get(name, 0) + 16
        ins.then_inc(self.semobj[name], 16)
        tok = (name, self.dcnt[name])
        for b in reads:
            b.readers[name] = self.dcnt[name]
        for b in writes:
            b.writer = tok
            b.readers = {}
        return ins

    def dma_custom(self, q, fn, reads=(), writes=(), sembuf=None):
        self._deps(q, reads, writes)
        sb = sembuf if sembuf is not None else (writes[0] if writes else reads[0])
        name = self._dsem(sb, sw=(q == "gpsimd"))
        ins = fn(self.eng[q])
        self.n_ops += 1
        self.dcnt[name] = self.dcnt.get(name, 0) + 16
        ins.then_inc(self.semobj[name], 16)
        tok = (name, self.dcnt[name])
        for b in reads:
            b.readers[name] = self.dcnt[name]
        for b in writes:
            b.writer = tok
            b.readers = {}
        return ins

    def barrier(self):
        snap = dict(self.cnt)
        dsn = dict(self.dcnt)
        for e in self.ENG:
            for f in self.ENG:
                if snap[f] > 0 and self.waited[e].get(f, 0) < snap[f]:
                    self.waited[e][f] = snap[f]
                    self.eng[e].wait_ge(self.semobj[f], snap[f])
                    self.n_wait += 1
            for name, val in dsn.items():
                self._wait(e, (name, val))

    def finish(self):
        for name, val in self.dcnt.items():
            self._wait("sync", (name, val))
        for e in self.ENG:
            if e != "sync" and self.cnt[e] > 0:
                self._wait("sync", (e, self.cnt[e]))


class Cfg:
    def __init__(self, D=1024, SEQ=4096, NH=16, TOPN=16, WINDOW=512, DFF=2816, DEC_PER_CORE=4,
                 PAST=8192, NPHYS=2560):
        self.D = D
        self.KC = D // 128
        self.SEQ = SEQ
        self.HALF = SEQ // 2
        self.NT = SEQ // 128
        self.NTH = self.NT // 2
        self.TF0 = self.NTH - 1
        self.DI = 2 * D
        self.HS = self.DI // 64
        self.G = 4
        self.HPGS = self.HS // 4
        self.NS = 128
        self.CD = self.DI + 2 * 4 * 128
        self.NH = NH
        self.KV = 2
        self.HPG = NH // 2
        self.DH = 64
        self.TOPN = TOPN
        self.WINDOW = WINDOW
        self.DFF = DFF
        self.DPROJ = self.DI + self.CD + self.HS + NH * 64 + 3 * 256 + 3 * NH
        self.DMIX = self.DI + NH * 64
        self.o_z = 0
        self.o_xbc = self.DI
        self.o_dt = self.DI + self.CD
        self.o_q = self.o_dt + self.HS
        self.o_kv = self.o_q + NH * 64
        self.o_g = self.o_kv + 768
        self.NSLC = SEQ // 64
        self.NCMP = SEQ // 16 - 1
        self.DEC = DEC_PER_CORE
        self.PAST = PAST
        self.NPHYS = NPHYS
        self.groups = []
        t = 0
        while t < self.TF0:
            n = min(5, self.TF0 - t)
            self.groups.append((list(range(t, t + n)), False))
            t += n
        first = True
        while t < self.NT:
            n = min(5 if first else 4, self.NT - t)
            self.groups.append((list(range(t, t + n)), True))
            t += n
            first = False


FULL = Cfg()


class Ctx:
    def __init__(self, nc, st, cfg):
        self.nc, self.st, self.cfg = nc, st, cfg
        self.S = Sched(nc, st)
        self.sb_bytes = 0
        self.ps = []
        for i in range(8):
            t = st.enter_context(nc.psum_tensor("ps%d" % i, [128, 512], F32))
            self.ps.append((t, Buf("ps%d" % i)))
        self.ps_rr = {}

    def sb(self, name, shape, dt=F32):
        t = self.st.enter_context(self.nc.sbuf_tensor("sb_" + name, list(shape), dt))
        n = 1
        for s in shape[1:]:
            n *= s
        self.sb_bytes += n * (4 if dt in (F32, I32) else 2)
        return t, Buf(name)

    def psum(self, pool):
        k = self.ps_rr.get(pool, 0)
        self.ps_rr[pool] = k + 1
        return self.ps[pool[k % len(pool)]]

    def dram(self, name, shape, dt=F32, kind="Internal"):
        return self.nc.dram_tensor(name, list(shape), dt, kind=kind).ap()


PS_MM = (0, 1, 2, 3)
PS_TR = (4, 5)
PS_X = (6, 7)


def build(cfg, dbg=()):
    nc = bass.Bass("TRN2", target_bir_lowering=False)
    st = ExitStack()
    with st:
        C = Ctx(nc, st, cfg)
        _emit(C, cfg, dbg)
    return nc


def _emit(C, cfg, dbg):
    nc, S = C.nc, C.S
    D, KC, NT, DI, HS, CD, NH, HPG = cfg.D, cfg.KC, cfg.NT, cfg.DI, cfg.HS, cfg.CD, cfg.NH, cfg.HPG
    HPGS = cfg.HPGS
    NCT = CD // 128
    NXT = DI // 128
    TF0, NTH = cfg.TF0, cfg.NTH
    GW = HPGS * 64
    NSLC, NBT = cfg.NSLC, (cfg.NCMP + 127) // 128
    NF = NT - TF0
    WT = cfg.WINDOW // 128
    CW = 65 + NSLC
    DMIX, DFF = cfg.DMIX, cfg.DFF
    NFT = 2 * DFF // 128
    NHT = DFF // 128

    def V(fn, r=(), w=()):
        return S.op("vector", fn, r, w)

    def A(fn, r=(), w=()):
        return S.op("scalar", fn, r, w)

    def P(fn, r=(), w=()):
        return S.op("gpsimd", fn, r, w)

    def T(fn, r=(), w=()):
        return S.op("tensor", fn, r, w)

    def bc3(ap2, n):
        return ap2.unsqueeze(2).to_broadcast([128, ap2.shape[1], n])

    din = lambda n, s, dt=F32: nc.dram_tensor(n, list(s), dt, kind="ExternalInput").ap()
    dout = lambda n, s, dt=F32: nc.dram_tensor(n, list(s), dt, kind="ExternalOutput").ap()
    dscr = lambda n, s, dt=F32: nc.dram_tensor(n, list(s), dt, kind="Internal").ap()
    xin = din("xin", [NT * 128, D])
    flag_d = din("flag", [128, 1])
    normg = din("normg", [8, D])
    w_in = din("w_in", [D, cfg.DPROJ])
    convw_d = din("convwT", [128, NCT, 4])
    convb_d = din("convbT", [128, NCT])
    dtb_d = din("dt_bias", [1, HS])
    alog_d = din("a_log", [1, HS])
    dskip_d = din("d_skip", [1, HS])
    ssmn_d = din("ssm_norm", [1, DI])
    cos_d = din("rcos", [NT * 128, 32])
    sin_d = din("rsin", [NT * 128, 32])
    idb_d = din("identb", [128, 128], BF16)
    idf_d = din("identf", [128, 128])
    utri_d = din("utri", [128, 128])
    negm_d = din("negm", [128, 512], BF16)
    cw1_d = din("cmp_w1", [2, 128, 32 * 128])
    cpe_d = din("cmp_peT", [2, 128, 32])
    cb1_d = din("cmp_b1T", [128, 2])
    cw2k_d = din("cmp_w2k", [2, 128, 128])
    cw2v_d = din("cmp_w2v", [128, 64])
    cb2k_d = din("cmp_b2k", [128, 1])
    cb2v_d = din("cmp_b2v", [1, 64])
    ovm_d = din("ovm", [NBT * 128, NSLC])
    maskc_d = din("maskc", [NBT, 128, NF * 128], BF16)
    fbias_d = din("fbias", [NF * 128, NSLC])
    exm_d = din("exm", [NSLC, NT * 128], BF16)
    caus_d = din("causT", [128, 128], BF16)
    band_d = din("bandT", [128, 128], BF16)

    w_out = din("w_out", [DMIX, D])
    w_up = din("w_up", [2, D, 2 * DFF])
    w_down = din("w_down", [2, DFF, D])
    fcw_d = din("ffn_cwT", [2, 128, NFT, 3])
    fcb_d = din("ffn_cbT", [2, 128, NFT])
    w_pw1 = din("w_pw1", [D, 2 * D])
    bpw1_d = din("b_pw1T", [128, 2 * KC])
    dww_d = din("dw_wT", [128, KC, 31])
    dwb_d = din("dw_bT", [128, KC])
    lng_d = din("ln_gT", [128, KC])
    lnb_d = din("ln_bT", [128, KC])
    w_pw2 = din("w_pw2", [D, D])
    bpw2_d = din("b_pw2", [1, D])
    conf_p = dout("conf_p", [30, D])
    ffn_p = dout("ffn_p", [2, 2, 2 * DFF])
    y_p = dout("y_p", [cfg.HALF, D])
    ssm_p = dout("ssm_p", [DI, 128])
    sconv_p = dout("sconv_p", [3, CD])
    cmp_p = dout("cmp_p", [cfg.HALF, 256])
    slc_p = dout("slc_p", [cfg.HALF, 256])
    NWT = min(cfg.WINDOW, cfg.HALF) // 128
    win_p = dout("win_p", [NWT * 128, 256])
    dbg_o = {}
    for name, shape in dbg:
        dbg_o[name] = dout(name, shape)

    XS_d = dscr("XS_d", [NT * 128, DI], BF16)
    BTm_d = dscr("BTm_d", [NT * 128, 512], BF16)
    BT_d = dscr("BT_d", [4, 128, NT * 128], BF16)
    CT_d = dscr("CT_d", [4, 128, NT * 128], BF16)
    dtr_d = dscr("dtr_d", [NT * 128, HS])
    zs_d = dscr("zs_d", [NF * 128, DI], BF16)
    qb_d = dscr("qb_d", [NF * 128, NH * 64], BF16)
    gs_d = dscr("gs_d", [NF * 128, 3 * NH])
    kT_d = dscr("kT_d", [3, 128, NT * 128], BF16)
    vT_d = dscr("vT_d", [128, NT * 128], BF16)
    v_d = dscr("v_d", [2, NT * 128, 128], BF16)
    ycat_d = dscr("ycat_d", [NF * 128, DMIX], BF16)
    bXS_d, bBTm_d, bBT_d, bCT_d, bdtr_d, bzs_d, bqb_d, bgs_d, bkT_d, bvT_d, bv_d, bycat_d = [Buf("scr%d" % i) for i in range(12)]

    identb, b_identb = C.sb("identb", [128, 128], BF16)
    identf, b_identf = C.sb("identf", [128, 128], F32)
    onesf, b_onesf = C.sb("onesf", [128, 128], F32)
    flag, b_flag = C.sb("flag", [128, 1], F32)
    gtab, b_gtab = C.sb("gtab", [128, D], F32)
    bconst = Buf("consts")
    for (dst, src) in ((identb, idb_d), (identf, idf_d), (flag, flag_d)):
        S.dma("sync", dst[:], src, writes=[bconst])
    b_identb = b_identf = b_flag = bconst
    P(lambda e: e.memset(onesf[:], 1.0), [], [b_onesf])

    def load_gain(j):
        S.dma("sync", gtab[:], normg[j:j + 1, :].partition_broadcast(128), writes=[b_gtab])

    class Phase:
        def __enter__(self):
            self.old = C.st
            self.es = ExitStack()
            self.es.__enter__()
            C.st = self.es
            self.base = C.sb_bytes
            return self

        def __exit__(self, *a):
            S.barrier()
            S.recycle()
            print("phase sbuf KB", (C.sb_bytes - self.base) / 1024, "ops", S.n_ops, "sems", S.ndsem)
            C.sb_bytes = self.base
            C.st = self.old
            return self.es.__exit__(*a)

    def transpose_to(dst_fn, src_fn, n, b_src, b_dst, eng="scalar"):
        for c0 in range(0, n, 8):
            cnt = min(8, n - c0)
            pt, bpt = C.psum(PS_TR)
            ptb = pt[:].bitcast(BF16)
            for c in range(cnt):
                T(lambda e, c=c: e.transpose(ptb[:, c * 128:(c + 1) * 128], src_fn(c0 + c), identb[:]),
                  [b_src, b_identb], [bpt])
            src = ptb[:, 0:cnt * 128].rearrange("p (c t) -> p c t", c=cnt)
            if eng == "scalar":
                A(lambda e: e.copy(dst_fn(c0, cnt), src), [bpt], [b_dst])
            else:
                V(lambda e: e.tensor_copy(dst_fn(c0, cnt), src), [bpt], [b_dst])

    GA = 8
    groupsA = []
    t = 0
    while t < TF0:
        n = min(GA, TF0 - t)
        groupsA.append((list(range(t, t + n)), False))
        t += n
    while t < NT:
        n = min(GA, NT - t)
        if NT - t - n == 1:
            n += 1
        groupsA.append((list(range(t, t + n)), True))
        t += n
    GAM = max(len(g[0]) for g in groupsA)
    with Phase():
        convw, b_convw = C.sb("convw", [128, NCT, 4], F32)
        convb, b_convb = C.sb("convb", [128, NCT], F32)
        for (dst, src) in ((convw, convw_d), (convb, convb_d)):
            S.dma("sync", dst[:], src, writes=[bconst])
        b_convw = b_convb = bconst
        xt = [C.sb("xt%d" % i, [128, D], F32) for i in range(2)]
        junk, b_junk = C.sb("junk", [128, D], F32)
        xnb = [C.sb("xnb%d" % i, [128, D], BF16) for i in range(2)]
        stat, b_stat = C.sb("stat", [128, 16], F32)
        xnT, b_xnT = C.sb("xnT", [128, KC, GAM * 128], BF16)
        wbf = [C.sb("wbf%d" % i, [128, 8, 512], BF16) for i in range(3)]
        wrot = [0]
        cb = [C.sb("cb%d" % i, [128, 3 + GAM * 128], F32) for i in range(2)]
        acc = [C.sb("acc%d" % i, [128, GAM * 128], F32) for i in range(2)]
        hist, b_hist = C.sb("hist", [128, NCT, 3], F32)
        xcT = [C.sb("xcT%d" % i, [128, GAM * 128], BF16) for i in range(3)]
        xst = [C.sb("xst%d" % i, [128, GAM, 512], BF16) for i in range(2)]
        rcos, b_rcos = C.sb("rcos", [128, GAM, 32], F32)
        rsin, b_rsin = C.sb("rsin", [128, GAM, 32], F32)
        kvo = [C.sb("kvo%d" % i, [128, 256], F32) for i in range(3)]
        kvb = [C.sb("kvb%d" % i, [128, 256], BF16) for i in range(3)]
        kTs = [C.sb("kTs%d" % i, [128, 256], BF16) for i in range(3)]
        rtmp = [C.sb("rtmp%d" % i, [128, 512], F32) for i in range(2)]
        zst = [C.sb("zst%d" % i, [128, 512], BF16) for i in range(3)]
        dst4 = [C.sb("dst4%d" % i, [128, max(HS, 3 * NH)], F32) for i in range(3)]
        V(lambda e: e.memset(hist[:], 0.0), [], [b_hist])

        def load_w(W, r0, kcn, c0, ncols):
            k = wrot[0] % 3
            wrot[0] += 1
            wb, bwb = wbf[k]
            src = W[r0 * 128:(r0 + kcn) * 128, c0:c0 + ncols].rearrange("(kc p) c -> p kc c", p=128)
            S.dma("gpsimd", wb[:, :kcn, :ncols], src, writes=[bwb])
            return wb, bwb

        def rms_to_T(src_ap, b_src, i, k, dT, b_dT):
            xb, bxb = xnb[k % 2]
            A(lambda e: e.activation(junk[:, :D], src_ap, AF.Square, accum_out=stat[:, 0:1]), [b_src], [b_junk, b_stat])
            A(lambda e: e.activation(stat[:, 1:2], stat[:, 0:1], AF.Sqrt, scale=1.0 / D, bias=EPS), [b_stat], [b_stat])
            V(lambda e: e.reciprocal(stat[:, 2:3], stat[:, 1:2]), [b_stat], [b_stat])
            V(lambda e: e.scalar_tensor_tensor(xb[:], src_ap, stat[:, 2:3], gtab[:], op0=ALU.mult, op1=ALU.mult),
              [b_src, b_stat, b_gtab], [bxb])
            transpose_to(lambda c0, cnt: dT[:, c0:c0 + cnt, i * 128:(i + 1) * 128],
                         lambda c: xb[:, c * 128:(c + 1) * 128], KC, bxb, b_dT)

        def rope(dst, src, i, nh, b_src, b_dst):
            s3 = src.rearrange("p (g d) -> p g d", g=nh)
            d3 = dst.rearrange("p (g d) -> p g d", g=nh)
            x1, x2 = s3[:, :, 0:32], s3[:, :, 32:64]
            o1, o2 = d3[:, :, 0:32], d3[:, :, 32:64]
            cs = rcos[:, i, :].unsqueeze(1).to_broadcast([128, nh, 32])
            sn = rsin[:, i, :].unsqueeze(1).to_broadcast([128, nh, 32])
            ta = rtmp[0][0][:, 0:nh * 32].rearrange("p (g d) -> p g d", g=nh)
            tb = rtmp[1][0][:, 0:nh * 32].rearrange("p (g d) -> p g d", g=nh)
            bta, btb = rtmp[0][1], rtmp[1][1]
            V(lambda e: e.tensor_tensor(ta, x1, cs, ALU.mult), [b_src, b_rcos], [bta])
            V(lambda e: e.tensor_tensor(tb, x2, sn, ALU.mult), [b_src, b_rsin], [btb])
            V(lambda e: e.tensor_tensor(o1, ta, tb, ALU.subtract), [bta, btb], [b_dst])
            V(lambda e: e.tensor_tensor(ta, x2, cs, ALU.mult), [b_src, b_rcos], [bta])
            V(lambda e: e.tensor_tensor(tb, x1, sn, ALU.mult), [b_src, b_rsin], [btb])
            V(lambda e: e.tensor_tensor(o2, ta, tb, ALU.add), [bta, btb], [b_dst])

        load_gain(0)
        rr = [0]
        for gi, (tiles, full) in enumerate(groupsA):
            ng = len(tiles)
            NG = ng * 128
            t0 = tiles[0]
            chunks = [(c, min(512, NG - c)) for c in range(0, NG, 512)]
            for i, t in enumerate(tiles):
                src, bsrc = xt[i % 2][0][:], xt[i % 2][1]
                S.dma("sync", src, xin[t * 128:(t + 1) * 128, :], writes=[bsrc])
                rms_to_T(src, bsrc, i, i, xnT, b_xnT)
            S.dma("sync", rcos[:, :ng, :], cos_d[t0 * 128:(t0 + ng) * 128, :].rearrange("(i p) c -> p i c", p=128), writes=[b_rcos])
            S.dma("sync", rsin[:, :ng, :], sin_d[t0 * 128:(t0 + ng) * 128, :].rearrange("(i p) c -> p i c", p=128), writes=[b_rsin])
            for mb in range(0, NCT, 4):
                nm = min(4, NCT - mb)
                wb, bwb = load_w(w_in, 0, KC, cfg.o_xbc + mb * 128, nm * 128)
                st_, bst_ = xst[(mb // 4) % 2]
                for j in range(nm):
                    m = mb + j
                    cbt, bcb = cb[m % 2]
                    act, bacc = acc[m % 2]
                    for (c0, cn) in chunks:
                        pt, bpt = C.psum(PS_MM)
                        for kc in range(KC):
                            T(lambda e, kc=kc: e.matmul(pt[:, :cn], wb[:, kc, j * 128:(j + 1) * 128], xnT[:, kc, c0:c0 + cn],
                                                        start=(kc == 0), stop=(kc == KC - 1)), [bwb, b_xnT], [bpt])
                        A(lambda e: e.copy(cbt[:, 3 + c0:3 + c0 + cn], pt[:, :cn]), [bpt], [bcb])
                    P(lambda e: e.tensor_copy(cbt[:, 0:3], hist[:, m, :]), [b_hist], [bcb])
                    P(lambda e: e.tensor_copy(hist[:, m, :], cbt[:, NG:NG + 3]), [bcb], [b_hist])
                    V(lambda e: e.tensor_scalar(act[:, :NG], cbt[:, 0:NG], convw[:, m, 0:1], None, op0=ALU.mult),
                      [bcb, b_convw], [bacc])
                    for k in range(1, 4):
                        V(lambda e, k=k: e.scalar_tensor_tensor(act[:, :NG], cbt[:, k:k + NG], convw[:, m, k:k + 1], act[:, :NG],
                                                                op0=ALU.mult, op1=ALU.add), [bcb, b_convw, bacc], [bacc])
                    xc, bxc = xcT[m % 3]
                    A(lambda e: e.activation(xc[:, :NG], act[:, :NG], AF.Silu, bias=convb[:, m:m + 1]), [bacc, b_convb], [bxc])
                    if m < NXT + 4:
                        transpose_to(lambda c0, cnt: st_[:, c0:c0 + cnt, j * 128:(j + 1) * 128],
                                     lambda c: xc[:, c * 128:(c + 1) * 128], ng, bxc, bst_, eng="vector")
                    if m >= NXT:
                        g = (m - NXT) % 4
                        dstd, bd = (BT_d, bBT_d) if m < NXT + 4 else (CT_d, bCT_d)
                        S.dma("sync", dstd[g, :, t0 * 128:t0 * 128 + NG], xc[:, :NG], reads=[bxc], writes=[bd], sembuf=bxc)
                if mb < NXT:
                    S.dma("sync", XS_d[t0 * 128:t0 * 128 + NG, mb * 128:(mb + nm) * 128].rearrange("(i p) c -> p i c", p=128),
                          st_[:, :ng, :nm * 128], reads=[bst_], writes=[bXS_d], sembuf=bst_)
                elif mb < NXT + 4:
                    S.dma("sync", BTm_d[t0 * 128:t0 * 128 + NG, :].rearrange("(i p) c -> p i c", p=128),
                          st_[:, :ng, :512], reads=[bst_], writes=[bBTm_d], sembuf=bst_)
            blocks = [("dt", cfg.o_dt, HS)]
            for br in range(3):
                blocks.append(("kv%d" % br, cfg.o_kv + br * 256, 256))
            if full:
                for c0 in range(0, DI, 512):
                    blocks.append(("z", c0, min(512, DI - c0)))
                for c0 in range(0, NH * 64, 512):
                    blocks.append(("q", cfg.o_q + c0, min(512, NH * 64 - c0)))
                blocks.append(("g", cfg.o_g, 3 * NH))
            for (kind, c0, cn) in blocks:
                wb, bwb = load_w(w_in, 0, KC, c0, cn)
                for i, t in enumerate(tiles):
                    rows = slice(t * 128, (t + 1) * 128)
                    frows = slice((t - TF0) * 128, (t - TF0 + 1) * 128)
                    pt, bpt = C.psum(PS_MM)
                    for kc in range(KC):
                        T(lambda e, kc=kc: e.matmul(pt[:, :cn], xnT[:, kc, i * 128:(i + 1) * 128], wb[:, kc, :cn],
                                                    start=(kc == 0), stop=(kc == KC - 1)), [bwb, b_xnT], [bpt])
                    k3 = rr[0] % 3
                    rr[0] += 1
                    if kind == "dt":
                        d4, bd4 = dst4[k3]
                        A(lambda e: e.copy(d4[:, :HS], pt[:, :cn]), [bpt], [bd4])
                        S.dma("sync", dtr_d[rows, :], d4[:, :HS], reads=[bd4], writes=[bdtr_d], sembuf=bd4)
                    elif kind == "g":
                        d4, bd4 = dst4[k3]
                        A(lambda e: e.activation(d4[:, :3 * NH], pt[:, :cn], AF.Sigmoid), [bpt], [bd4])
                        S.dma("sync", gs_d[frows, :], d4[:, :3 * NH], reads=[bd4], writes=[bgs_d], sembuf=bd4)
                    elif kind == "z":
                        z_, bz_ = zst[k3]
                        A(lambda e: e.activation(z_[:, :cn], pt[:, :cn], AF.Silu), [bpt], [bz_])
                        S.dma("sync", zs_d[frows, c0:c0 + cn], z_[:, :cn], reads=[bz_], writes=[bzs_d], sembuf=bz_)
                    elif kind == "q":
                        z_, bz_ = zst[k3]
                        rope(z_[:, :cn], pt[:, :cn], i, cn // 64, bpt, bz_)
                        qd4 = qb_d[frows, :].rearrange("t (hh g d) -> t hh g d", g=2, d=64)
                        hs0 = (c0 - cfg.o_q) // 64
                        for g in range(2):
                            lo, hi = max(hs0, g * HPG), min(hs0 + cn // 64, (g + 1) * HPG)
                            if hi > lo:
                                S.dma("sync", qd4[:, lo - g * HPG:hi - g * HPG, g, :],
                                      z_[:, (lo - hs0) * 64:(hi - hs0) * 64].rearrange("p (h d) -> p h d", d=64),
                                      reads=[bz_], writes=[bqb_d], sembuf=bz_)
                    else:
                        br = int(kind[2])
                        kv, bkv = kvo[k3]
                        kb, bkb = kvb[k3]
                        kT_, bkT_ = kTs[k3]
                        rope(kv[:, 0:128], pt[:, 0:128], i, 2, bpt, bkv)
                        A(lambda e: e.copy(kv[:, 128:256], pt[:, 128:256]), [bpt], [bkv])
                        dsto = (cmp_p, slc_p, win_p)[br]
                        if br < 2 and t >= NTH:
                            r = (t - NTH) * 128
                            S.dma("sync", dsto[r:r + 128, :], kv[:], reads=[bkv], sembuf=bkv)
                        if br == 2 and t >= NT - NWT:
                            r = (t - (NT - NWT)) * 128
                            S.dma("sync", dsto[r:r + 128, :], kv[:], reads=[bkv], sembuf=bkv)
                        V(lambda e: e.tensor_copy(kb[:], kv[:]), [bkv], [bkb])
                        ntr = 2 if br == 0 else 1
                        pt2, bpt2 = C.psum(PS_TR)
                        ptb2 = pt2[:].bitcast(BF16)
                        for j2 in range(ntr):
                            T(lambda e, j2=j2: e.transpose(ptb2[:, j2 * 128:(j2 + 1) * 128], kb[:, j2 * 128:(j2 + 1) * 128], identb[:]),
                              [bkb, b_identb], [bpt2])
                        A(lambda e: e.copy(kT_[:, :ntr * 128], ptb2[:, :ntr * 128]), [bpt2], [bkT_])
                        S.dma("sync", kT_d[br, :, rows], kT_[:, 0:128], reads=[bkT_], writes=[bkT_d], sembuf=bkT_)
                        if br == 0:
                            S.dma("sync", vT_d[:, rows], kT_[:, 128:256], reads=[bkT_], writes=[bvT_d], sembuf=bkT_)
                        else:
                            S.dma("sync", v_d[br - 1, rows, :], kb[:, 128:256], reads=[bkb], writes=[bv_d], sembuf=bkb)
        sc, b_sc = C.sb("sc", [3, CD], F32)
        for m in range(NCT):
            pt, bpt = C.psum(PS_MM)
            T(lambda e: e.transpose(pt[:3, :128], hist[:, m, :], identf[:]), [b_hist, b_identf], [bpt])
            V(lambda e: e.tensor_copy(sc[:, m * 128:(m + 1) * 128], pt[:3, :128]), [bpt], [b_sc])
        S.dma("sync", sconv_p, sc[:], reads=[b_sc], sembuf=b_sc)

    if getattr(cfg, "stop", None) == "A":
        S.finish()
        return
    with Phase():
        utri, b_utri = C.sb("utri", [128, 128], F32)
        negm, b_negm = C.sb("negm", [128, 512], BF16)
        dtb, b_dtb = C.sb("dtb", [128, HS], F32)
        atab, b_atab = C.sb("atab", [128, HS], F32)
        dskip, b_dskip = C.sb("dskip", [128, HS], F32)
        ssmn, b_ssmn = C.sb("ssmn", [128, DI], F32)
        KTs, b_KTs = C.sb("KTs", [128, NT * 128], BF16)
        Vs, b_Vs = C.sb("Vs", [128, NT, 2, 65], BF16)
        KTw, b_KTw = C.sb("KTw", [128, NT * 128], BF16)
        Vw, b_Vw = C.sb("Vw", [128, NT, 2, 65], BF16)
        hidT, b_hidT = C.sb("hidT", [128, 2, 2, NBT * 128], BF16)
        kcT, b_kcT = C.sb("kcT", [128, NBT * 128], BF16)
        vca, b_vca = C.sb("vca", [128, NBT, 2, CW], BF16)
        cpe, b_cpe = C.sb("cpe", [128, 2, 32], BF16)
        cb1, b_cb1 = C.sb("cmpb1", [128, 2], F32)
        b1e, b_b1e = C.sb("b1e", [128, 2], F32)
        cw2k, b_cw2k = C.sb("cw2k", [128, 2, 128], BF16)
        cw2v, b_cw2v = C.sb("cw2v", [128, 64], BF16)
        cb2k, b_cb2k = C.sb("cb2k", [128, 1], F32)
        cb2v, b_cb2v = C.sb("cb2v", [128, 64], F32)
        ovt, b_ovt = C.sb("ovt", [128, NBT, NSLC], F32)
        exm, b_exm = C.sb("exm", [NSLC, NT * 128], BF16)
        causT, b_causT = C.sb("causT", [128, 128], BF16)
        bandT, b_bandT = C.sb("bandT", [128, 128], BF16)
        onesb, b_onesb = C.sb("onesb", [128, 128], BF16)
        bc2 = Buf("constsM")
        V(lambda e: e.memset(onesb[:], 1.0), [], [bc2])
        for (dst, src) in ((utri, utri_d), (negm, negm_d), (dtb, dtb_d.partition_broadcast(128)),
                           (atab, alog_d.partition_broadcast(128)), (dskip, dskip_d.partition_broadcast(128)),
                           (ssmn, ssmn_d.partition_broadcast(128)), (cb1, cb1_d), (cb2k, cb2k_d),
                           (cb2v, cb2v_d.partition_broadcast(128)), (ovt, ovm_d.rearrange("(b p) j -> p b j", p=128)),
                           (exm, exm_d), (causT, caus_d), (bandT, band_d),
                           (KTs, kT_d[1]), (KTw, kT_d[2])):
            S.dma("sync", dst[:], src, reads=[bkT_d], writes=[bc2])
        for (dst, src) in ((cpe, cpe_d.rearrange("k p l -> p k l")), (cw2k, cw2k_d.rearrange("g c m -> c g m")), (cw2v, cw2v_d)):
            S.dma("gpsimd", dst[:], src, writes=[bc2])
        b_utri = b_negm = b_dtb = b_atab = b_dskip = b_ssmn = b_cb1 = b_cb2k = b_cb2v = b_ovt = b_exm = b_causT = b_bandT = bc2
        b_KTs = b_KTw = b_cpe = b_cw2k = b_cw2v = bc2
        A(lambda e: e.activation(atab[:], atab[:], AF.Exp), [bc2], [bc2])
        V(lambda e: e.tensor_scalar(atab[:], atab[:], -1.0, None, op0=ALU.mult), [bc2], [bc2])
        for br, (Vx, b_Vx) in enumerate(((Vs, b_Vs), (Vw, b_Vw))):
            for g in range(2):
                S.dma("sync", Vx[:, :, g, 0:64], v_d[br, :, g * 64:(g + 1) * 64].rearrange("(t p) d -> p t d", p=128), reads=[bv_d], writes=[b_Vx])
            V(lambda e: e.memset(Vx[:, :, :, 64:65], 1.0), [b_Vx], [b_Vx])
            npre = TF0 + 1
            V(lambda e: e.tensor_scalar(Vx[:, 0:npre, :, :], Vx[:, 0:npre, :, :], flag[:, 0:1], None, op0=ALU.mult), [b_Vx, b_flag], [b_Vx])
        NCMP = cfg.NCMP
        V(lambda e: e.memset(hidT[:], 0.0), [], [b_hidT])
        V(lambda e: e.memset(kcT[:], 0.0), [], [b_kcT])
        V(lambda e: e.memset(vca[:], 0.0), [], [b_vca])
        V(lambda e: e.memset(vca[:, :, :, 64:65], 1.0), [], [b_vca])
        for g in range(2):
            V(lambda e, g=g: e.tensor_copy(vca[:, :, g, 65:CW], ovt[:]), [b_ovt], [b_vca])
        with ExitStack() as es2:
            old = C.st
            C.st = es2
            craw = [C.sb("craw%d" % i, [128, NT * 128], BF16) for i in range(2)]
            w1b = [C.sb("w1b%d" % i, [128, 32, 128], BF16) for i in range(2)]
            S.dma("sync", craw[0][0][:], kT_d[0], reads=[bkT_d], writes=[craw[0][1]])
            S.dma("sync", craw[1][0][:], vT_d, reads=[bvT_d], writes=[craw[1][1]])
            for kvi in range(2):
                w1v, bw1 = w1b[kvi]
                S.dma("gpsimd", w1v[:].rearrange("p l c -> p (l c)"), cw1_d[kvi], writes=[bw1])
                pb1, bpb1 = C.psum(PS_MM)
                for l in range(32):
                    T(lambda e, l=l: e.matmul(pb1[:, 0:1], w1v[0:64, l, :], cpe[0:64, kvi, l:l + 1], start=(l == 0), stop=(l == 31)),
                      [bw1, b_cpe], [bpb1])
                V(lambda e: e.tensor_tensor(b1e[:, kvi:kvi + 1], pb1[:, 0:1], cb1[:, kvi:kvi + 1], ALU.add), [bpb1, b_cb1], [b_b1e])
                cr, bcr = craw[kvi]
                for g in range(2):
                    gsl2 = slice(g * 64, (g + 1) * 64)
                    for b0 in range(0, NCMP, 512):
                        nb = min(512, NCMP - b0)
                        ph, bph = C.psum(PS_MM)
                        for l in range(32):
                            T(lambda e, l=l: e.matmul(ph[:, :nb], w1v[gsl2, l, :], cr[gsl2, 16 * b0 + l:16 * b0 + l + 16 * (nb - 1) + 1:16],
                                                      start=(l == 0), stop=(l == 31)), [bw1, bcr], [bph])
                        A(lambda e: e.activation(hidT[:, kvi, g, b0:b0 + nb], ph[:, :nb], AF.Silu, bias=b1e[:, kvi:kvi + 1]),
                          [bph, b_b1e], [b_hidT])
            for b0 in range(0, NCMP, 512):
                nb = min(512, NCMP - b0)
                pk, bpk = C.psum(PS_MM)
                for g in range(2):
                    T(lambda e, g=g: e.matmul(pk[:, :nb], cw2k[:, g, :], hidT[:, 0, g, b0:b0 + nb], start=(g == 0), stop=(g == 1)),
                      [b_cw2k, b_hidT], [bpk])
                A(lambda e: e.activation(kcT[:, b0:b0 + nb], pk[:, :nb], AF.Identity, bias=cb2k[:, 0:1]), [bpk, b_cb2k], [b_kcT])
            for bt in range(NBT):
                pv2, bpv2 = C.psum(PS_MM)
                for g in range(2):
                    T(lambda e, g=g: e.matmul(pv2[:, g * 64:(g + 1) * 64], hidT[:, 1, g, bt * 128:(bt + 1) * 128], cw2v[:], start=True, stop=True),
                      [b_hidT, b_cw2v], [bpv2])
                V(lambda e: e.tensor_tensor(vca[:, bt, :, 0:64], pv2[:, 0:128].rearrange("p (g d) -> p g d", g=2),
                                            cb2v[:].unsqueeze(1).to_broadcast([128, 2, 64]), ALU.add), [bpv2, b_cb2v], [b_vca])
            S.barrier()
            C.st = old
        XSt = [C.sb("XSt%d" % i, [128, DI], BF16) for i in range(2)]
        BTmt = [C.sb("BTmt%d" % i, [128, 512], BF16) for i in range(2)]
        BTt = [C.sb("BTt%d" % i, [128, 4, 128], BF16) for i in range(2)]
        CTt = [C.sb("CTt%d" % i, [128, 4, 128], BF16) for i in range(2)]
        dtrt = [C.sb("dtrt%d" % i, [128, HS], F32) for i in range(2)]
        zst_ = [C.sb("zst_%d" % i, [128, DI], BF16) for i in range(2)]
        qbt = [C.sb("qbt%d" % i, [128, NH * 64], BF16) for i in range(2)]
        gst = [C.sb("gst%d" % i, [128, 3 * NH], F32) for i in range(2)]
        mct = [C.sb("mct%d" % i, [128, NBT, 128], BF16) for i in range(2)]
        fbt = [C.sb("fbt%d" % i, [128, NSLC], F32) for i in range(2)]
        sm = {}
        for nm in ("dt", "ax", "ex", "dA", "cum", "ncum", "dec", "te", "ec", "coef", "cml"):
            sm[nm] = C.sb("sm_" + nm, [128, HS], F32)
        stat, b_stat = C.sb("statm", [128, 16], F32)
        junk, b_junk = C.sb("junkm", [128, GW], F32)
        Hs, b_Hs = C.sb("Hs", [128, DI], F32)
        Hb, b_Hb = C.sb("Hb", [128, DI], BF16)
        xw, b_xw = C.sb("xw", [128, DI], BF16)
        xdt = [C.sb("xdt%d" % i, [128, GW], BF16) for i in range(2)]
        Rg = [C.sb("Rg%d" % i, [128, HPGS * 128], F32) for i in range(2)]
        LTg = [C.sb("LTg%d" % i, [128, HPGS * 128], F32) for i in range(2)]
        Wg = [C.sb("Wg%d" % i, [128, HPGS * 128], BF16) for i in range(2)]
        yt1 = [C.sb("yt1_%d" % i, [128, GW], F32) for i in range(2)]
        yt2 = [C.sb("yt2_%d" % i, [128, GW], F32) for i in range(2)]
        yz, b_yz = C.sb("yz", [128, DI], F32)
        ycat = [C.sb("ycat%d" % i, [128, DMIX], BF16) for i in range(2)]
        qT, b_qT = C.sb("qT", [128, HPG, 128], BF16)
        Eb = [C.sb("Eb%d" % i, [128, HPG * 128], F32) for i in range(2)]
        PTb = [C.sb("PTb%d" % i, [128, HPG * 128], BF16) for i in range(2)]
        mdiag = [C.sb("mdiag%d" % i, [128, 128], F32) for i in range(2)]
        oacc, b_oacc = C.sb("oacc", [128, NH * 64], F32)
        otmp, b_otmp = C.sb("otmp", [128, HPG * 64], F32)
        nst, b_nst = C.sb("nst", [128, 4, HPG], F32)
        impg, b_impg = C.sb("impg", [128, NSLC], F32)
        wk1, b_wk1 = C.sb("wk1", [128, NSLC], F32)
        mx8, b_mx8 = C.sb("mx8", [128, 8], F32)
        selb, b_selb = C.sb("selb", [128, NSLC], BF16)
        selT, b_selT = C.sb("selT", [NSLC, 128], BF16)
        V(lambda e: e.memset(Hs[:], 0.0), [], [b_Hs])
        V(lambda e: e.memset(Hb[:], 0.0), [], [b_Hb])

        def f2(nm):
            return sm[nm][0][:]

        def b2(nm):
            return sm[nm][1]

        def load_tile(t):
            k = t % 2
            rows = slice(t * 128, (t + 1) * 128)
            S.dma("sync", XSt[k][0][:], XS_d[rows, :], reads=[bXS_d], writes=[XSt[k][1]])
            S.dma("sync", BTmt[k][0][:], BTm_d[rows, :], reads=[bBTm_d], writes=[BTmt[k][1]])
            S.dma("sync", dtrt[k][0][:], dtr_d[rows, :], reads=[bdtr_d], writes=[dtrt[k][1]])
            if t >= TF0:
                fr = slice((t - TF0) * 128, (t - TF0 + 1) * 128)
                S.dma("sync", BTt[k][0][:], BT_d[:, :, rows].rearrange("g n t -> n g t"), reads=[bBT_d], writes=[BTt[k][1]])
                S.dma("sync", CTt[k][0][:], CT_d[:, :, rows].rearrange("g n t -> n g t"), reads=[bCT_d], writes=[CTt[k][1]])
                S.dma("sync", zst_[k][0][:], zs_d[fr, :], reads=[bzs_d], writes=[zst_[k][1]])
                S.dma("sync", qbt[k][0][:], qb_d[fr, :], reads=[bqb_d], writes=[qbt[k][1]])
                S.dma("sync", gst[k][0][:], gs_d[fr, :], reads=[bgs_d], writes=[gst[k][1]])
                S.dma("sync", mct[k][0][:], maskc_d[:, :, fr].rearrange("b p q -> p b q"), writes=[mct[k][1]])
                S.dma("sync", fbt[k][0][:], fbias_d[fr, :], writes=[fbt[k][1]])

        load_tile(0)
        for t in range(NT if getattr(cfg, "stop", None) != "M0" else 0):
            if t + 1 < NT:
                load_tile(t + 1)
            k = t % 2
            full = t >= TF0 and getattr(cfg, "stop", None) != "M1"
            XS, b_XS = XSt[k]
            BTm, b_BTm = BTmt[k]
            BT, b_BT = BTt[k]
            CT, b_CT = CTt[k]
            dtr, b_dtr = dtrt[k]
            zs, b_zs = zst_[k]
            qb, b_qb = qbt[k]
            gs, b_gs = gst[k]
            maskc, b_maskc = mct[k]
            fbias, b_fbias = fbt[k]
            yc, b_yc = ycat[k]
            XS3 = XS[:].rearrange("p (h d) -> p h d", h=HS)
            V(lambda e: e.tensor_tensor(f2("dt"), dtr[:], dtb[:], ALU.add), [b_dtr, b_dtb], [b2("dt")])
            V(lambda e: e.tensor_scalar(f2("ax"), f2("dt"), -1.0, None, op0=ALU.mult), [b2("dt")], [b2("ax")])
            V(lambda e: e.tensor_tensor(f2("ax"), f2("ax"), f2("dt"), ALU.min), [b2("dt"), b2("ax")], [b2("ax")])
            A(lambda e: e.activation(f2("ex"), f2("ax"), AF.Exp), [b2("ax")], [b2("ex")])
            A(lambda e: e.activation(f2("ex"), f2("ex"), AF.Ln, bias=1.0), [b2("ex")], [b2("ex")])
            V(lambda e: e.tensor_scalar(f2("dt"), f2("dt"), 0.0, None, op0=ALU.max), [b2("dt")], [b2("dt")])
            V(lambda e: e.tensor_tensor(f2("dt"), f2("dt"), f2("ex"), ALU.add), [b2("dt"), b2("ex")], [b2("dt")])
            if t <= TF0:
                V(lambda e: e.tensor_scalar(f2("dt"), f2("dt"), flag[:, 0:1], None, op0=ALU.mult), [b2("dt"), b_flag], [b2("dt")])
            V(lambda e: e.tensor_tensor(f2("dA"), f2("dt"), atab[:], ALU.mult), [b2("dt"), b_atab], [b2("dA")])
            pt, bpt = C.psum(PS_MM)
            T(lambda e: e.matmul(pt[:, :HS], utri[:], f2("dA"), start=True, stop=True), [b_utri, b2("dA")], [bpt])
            V(lambda e: e.tensor_copy(f2("cum"), pt[:, :HS]), [bpt], [b2("cum")])
            A(lambda e: e.mul(f2("ncum"), pt[:, :HS], -1.0), [bpt], [b2("ncum")])
            V(lambda e: e.tensor_scalar(f2("cml"), f2("cum"), identf[:, 127:128], None, op0=ALU.mult), [b2("cum"), b_identf], [b2("cml")])
            pt, bpt = C.psum(PS_MM)
            T(lambda e: e.matmul(pt[:, :HS], onesf[:], f2("cml"), start=True, stop=True), [b_onesf, b2("cml")], [bpt])
            A(lambda e: e.activation(f2("dec"), pt[:, :HS], AF.Exp), [bpt], [b2("dec")])
            V(lambda e: e.tensor_tensor(f2("te"), pt[:, :HS], f2("cum"), ALU.subtract), [bpt, b2("cum")], [b2("te")])
            A(lambda e: e.activation(f2("te"), f2("te"), AF.Exp), [b2("te")], [b2("te")])
            A(lambda e: e.activation(f2("ec"), f2("cum"), AF.Exp), [b2("cum")], [b2("ec")])
            V(lambda e: e.tensor_tensor(f2("coef"), f2("dt"), f2("te"), ALU.mult), [b2("dt"), b2("te")], [b2("coef")])
            if full:
                for g in range(4):
                    hsl = slice(g * HPGS, (g + 1) * HPGS)
                    gsl = slice(g * GW, (g + 1) * GW)
                    R, bR = Rg[g % 2]
                    LT, bLT = LTg[g % 2]
                    W, bW = Wg[g % 2]
                    xd, bxd = xdt[g % 2]
                    R3 = R[:].rearrange("p (h i) -> p h i", h=HPGS)
                    V(lambda e: e.tensor_tensor(R3, identf[:].unsqueeze(1).to_broadcast([128, HPGS, 128]),
                                                bc3(sm["cum"][0][:, hsl], 128), ALU.mult), [b_identf, b2("cum")], [bR])
                    for c0 in range(0, HPGS * 128, 512):
                        cn = min(512, HPGS * 128 - c0)
                        pb, bpb = C.psum(PS_MM)
                        T(lambda e: e.matmul(pb[:, :cn], onesf[:], R[:, c0:c0 + cn], start=True, stop=False), [b_onesf, bR], [bpb])
                        T(lambda e: e.matmul(pb[:, :cn], identb[:], negm[:, :cn], start=False, stop=True), [b_identb, b_negm], [bpb])
                        for hh in range(cn // 128):
                            h = g * HPGS + c0 // 128 + hh
                            A(lambda e, hh=hh, h=h: e.activation(LT[:, c0 + hh * 128:c0 + (hh + 1) * 128], pb[:, hh * 128:(hh + 1) * 128],
                                                                 AF.Exp, bias=sm["ncum"][0][:, h:h + 1]), [bpb, b2("ncum")], [bLT])
                    pc, bpc = C.psum(PS_MM)
                    T(lambda e: e.matmul(pc[:, :128], BT[:, g, :], CT[:, g, :], start=True, stop=True), [b_BT, b_CT], [bpc])
                    V(lambda e: e.tensor_tensor(W[:].rearrange("p (h i) -> p h i", h=HPGS), LT[:].rearrange("p (h i) -> p h i", h=HPGS),
                                                pc[:, :128].unsqueeze(1).to_broadcast([128, HPGS, 128]), ALU.mult), [bLT, bpc], [bW])
                    P(lambda e: e.tensor_tensor(xd[:].rearrange("p (h d) -> p h d", h=HPGS), XS3[:, hsl, :],
                                                bc3(sm["dt"][0][:, hsl], 64), ALU.mult), [b_XS, b2("dt")], [bxd])
                    pyd, bpyd = C.ps[6]
                    pyo, bpyo = C.ps[7]
                    for hh in range(HPGS):
                        T(lambda e, hh=hh: e.matmul(pyd[:, hh * 64:(hh + 1) * 64], W[:, hh * 128:(hh + 1) * 128], xd[:, hh * 64:(hh + 1) * 64],
                                                    start=True, stop=True), [bW, bxd], [bpyd])
                    T(lambda e: e.matmul(pyo[:, :GW], CT[:, g, :], Hb[:, gsl], start=True, stop=True), [b_CT, b_Hb], [bpyo])
                    y1, by1 = yt1[g % 2]
                    y2, by2 = yt2[g % 2]
                    V(lambda e: e.tensor_tensor(y1[:].rearrange("p (h d) -> p h d", h=HPGS), pyo[:, :GW].rearrange("p (h d) -> p h d", h=HPGS),
                                                bc3(sm["ec"][0][:, hsl], 64), ALU.mult), [bpyo, b2("ec")], [by1])
                    V(lambda e: e.tensor_tensor(y1[:], y1[:], pyd[:, :GW], ALU.add), [by1, bpyd], [by1])
                    P(lambda e: e.tensor_tensor(y2[:].rearrange("p (h d) -> p h d", h=HPGS), XS3[:, hsl, :],
                                                bc3(dskip[:, hsl], 64), ALU.mult), [b_XS, b_dskip], [by2])
                    P(lambda e: e.tensor_tensor(y2[:], y2[:], y1[:], ALU.add), [by1, by2], [by2])
                    P(lambda e: e.tensor_tensor(yz[:, gsl], y2[:], zs[:, gsl], ALU.mult), [by2, b_zs], [b_yz])
                    A(lambda e: e.activation(junk[:, :GW], yz[:, gsl], AF.Square, accum_out=stat[:, 4 + g:5 + g]), [b_yz], [b_junk, b_stat])
                A(lambda e: e.activation(stat[:, 8:12], stat[:, 4:8], AF.Sqrt, scale=1.0 / GW, bias=EPS), [b_stat], [b_stat])
                V(lambda e: e.reciprocal(stat[:, 12:16], stat[:, 8:12]), [b_stat], [b_stat])
                for g in range(4):
                    gsl = slice(g * GW, (g + 1) * GW)
                    V(lambda e: e.scalar_tensor_tensor(yc[:, gsl], yz[:, gsl], stat[:, 12 + g:13 + g], ssmn[:, gsl],
                                                       op0=ALU.mult, op1=ALU.mult), [b_yz, b_stat, b_ssmn], [b_yc])
                nsa_on = getattr(cfg, "stop", None) != "M2"
                transpose_to(lambda c0, cnt: qT[:, c0:c0 + cnt, :], lambda c: qb[:, c * 128:(c + 1) * 128], HPG, b_qb, b_qT)
                qTf = qT[:].rearrange("p h q -> p (h q)")
                HB = [(a, min(a + 3, HPG)) for a in range(0, HPG, 3)]
                cbanks = (3, 6, 7)
                rot = [0]

                def s_scores(KT, bKT, kcols, g, need_f32):
                    gsl2 = slice(g * 64, (g + 1) * 64)
                    k2 = rot[0]
                    rot[0] += 1
                    E, bE = Eb[k2 % 2] if need_f32 else PTb[k2 % 2]
                    for h0 in range(0, HPG, 4):
                        nh = min(4, HPG - h0)
                        ps_, bps_ = C.psum((0, 1, 2)) if getattr(cfg, "stop2", None) != "pool" else C.psum(PS_TR)
                        T(lambda e: e.matmul(ps_[:, :nh * 128], KT[gsl2, kcols], qTf[gsl2, h0 * 128:(h0 + nh) * 128], start=True, stop=True),
                          [bKT, b_qT], [bps_])
                        if getattr(cfg, "stop2", None) == "noexp":
                            continue
                        if getattr(cfg, "stop2", None) == "last" and t != NT - 1:
                            continue
                        if getattr(cfg, "stop2", None) == "first" and t != TF0:
                            continue
                        if getattr(cfg, "stop2", None) == "second" and t != TF0 + 1:
                            continue
                        if getattr(cfg, "stop2", None) == "notfirst" and t == TF0:
                            continue
                        if getattr(cfg, "stop2", None) == "g0" and g != 0:
                            continue
                        if getattr(cfg, "stop2", None) == "g1" and g != 1:
                            continue
                        if getattr(cfg, "stop2", None) == "exp128":
                            for hq in range(nh):
                                A(lambda e, hq=hq: e.activation(E[:, (h0 + hq) * 128:(h0 + hq + 1) * 128], ps_[:, hq * 128:(hq + 1) * 128], AF.Exp, scale=0.125), [bps_], [bE])
                            continue
                        if getattr(cfg, "stop2", None) == "exp3":
                            A(lambda e: e.copy(E[:, h0 * 128:(h0 + nh) * 128], ps_[:, :nh * 128]), [bps_], [bE])
                            A(lambda e: e.activation(E[:, h0 * 128:(h0 + nh) * 128], E[:, h0 * 128:(h0 + nh) * 128], AF.Exp, scale=0.125), [bE], [bE])
                            continue
                        if getattr(cfg, "stop2", None) == "exp2":
                            V(lambda e: e.tensor_copy(E[:, h0 * 128:(h0 + nh) * 128], ps_[:, :nh * 128]), [bps_], [bE])
                            A(lambda e: e.activation(E[:, h0 * 128:(h0 + nh) * 128], E[:, h0 * 128:(h0 + nh) * 128], AF.Exp, scale=0.125), [bE], [bE])
                            continue
                        if getattr(cfg, "stop2", None) == "copy":
                            V(lambda e: e.tensor_copy(E[:, h0 * 128:(h0 + nh) * 128], ps_[:, :nh * 128]), [bps_], [bE])
                            continue
                        if getattr(cfg, "stop2", None) == "actpsum":
                            A(lambda e: e.activation(E[:, h0 * 128:(h0 + nh) * 128], ps_[:, :nh * 128], AF.Exp, scale=0.125), [bps_], [bE])
                            continue
                        Ef, bEf = Eb[k2 % 2]
                        V(lambda e: e.tensor_copy(Ef[:, h0 * 128:(h0 + nh) * 128], ps_[:, :nh * 128]), [bps_], [bEf])
                        A(lambda e: e.activation(E[:, h0 * 128:(h0 + nh) * 128], Ef[:, h0 * 128:(h0 + nh) * 128], AF.Exp, scale=0.125), [bEf], [bE] if bE is not bEf else [bEf])
                    return E, bE, k2

                def evac(banks, per, gate_idx, g, with_imp):
                    for (pb, bpb, h0, h1) in banks:
                        nh = h1 - h0
                        v = pb[:, :nh * per].rearrange("p (h c) -> p h c", h=nh)
                        V(lambda e: e.tensor_scalar(nst[:, 0, h0:h1], v[:, :, 64], 1e-30, None, op0=ALU.max), [bpb], [b_nst])
                    V(lambda e: e.reciprocal(nst[:, 1, :], nst[:, 0, :]), [b_nst], [b_nst])
                    g3 = gs[:].rearrange("p (h c) -> p h c", c=3)
                    V(lambda e: e.tensor_tensor(nst[:, 2, :], nst[:, 1, :], g3[:, g * HPG:(g + 1) * HPG, gate_idx], ALU.mult), [b_nst, b_gs], [b_nst])
                    for (pb, bpb, h0, h1) in banks:
                        nh = h1 - h0
                        v = pb[:, :nh * per].rearrange("p (h c) -> p h c", h=nh)
                        osl = oacc[:, (g * HPG + h0) * 64:(g * HPG + h1) * 64].rearrange("p (h d) -> p h d", h=nh)
                        if gate_idx == 0:
                            V(lambda e: e.tensor_tensor(osl, v[:, :, 0:64], bc3(nst[:, 2, h0:h1], 64), ALU.mult), [bpb, b_nst], [b_oacc])
                        else:
                            ot = otmp[:, h0 * 64:h1 * 64].rearrange("p (h d) -> p h d", h=nh)
                            V(lambda e: e.tensor_tensor(ot, v[:, :, 0:64], bc3(nst[:, 2, h0:h1], 64), ALU.mult), [bpb, b_nst], [b_otmp])
                            P(lambda e: e.tensor_tensor(osl, osl, ot, ALU.add), [b_otmp, b_oacc], [b_oacc])
                        if with_imp:
                            for hh in range(h0, h1):
                                if hh == 0:
                                    V(lambda e: e.tensor_scalar(impg[:], v[:, hh - h0, 65:CW], nst[:, 1, hh:hh + 1], None, op0=ALU.mult),
                                      [bpb, b_nst], [b_impg])
                                else:
                                    V(lambda e: e.scalar_tensor_tensor(impg[:], v[:, hh - h0, 65:CW], nst[:, 1, hh:hh + 1], impg[:],
                                                                       op0=ALU.mult, op1=ALU.add), [bpb, b_nst, b_impg], [b_impg])

                def pv(banks, PT, bPT, Vop, bV, first, last, per):
                    for (pb, bpb, h0, h1) in banks:
                        for hh in range(h0, h1):
                            T(lambda e, hh=hh: e.matmul(pb[:, (hh - h0) * per:(hh - h0 + 1) * per], PT[:, hh * 128:(hh + 1) * 128], Vop,
                                                        start=(first and hh == h0), stop=(last and hh == h1 - 1)), [bPT, bV], [bpb])

                def masked(E, bE, k2, msk, bmsk):
                    PT, bPT = PTb[(k2 + 1) % 2]
                    V(lambda e: e.tensor_tensor(PT[:].rearrange("p (h q) -> p h q", h=HPG), E[:].rearrange("p (h q) -> p h q", h=HPG),
                                                msk.unsqueeze(1).to_broadcast([128, HPG, 128]), ALU.mult), [bE, bmsk], [bPT])
                    return PT, bPT

                for g in (range(2) if nsa_on else ()):
                    nbt = min(NBT, (8 * t + 6) // 128 + 1)
                    banks = [(C.ps[cbanks[k_]][0], C.ps[cbanks[k_]][1], a, b) for k_, (a, b) in enumerate(HB)]
                    stp = getattr(cfg, "stop", None)
                    if stp == "C1":
                        continue
                    pend = None
                    for bt in range(nbt):
                        E, bE, k2 = s_scores(kcT, b_kcT, slice(bt * 128, (bt + 1) * 128), g, True)
                        PT, bPT = masked(E, bE, k2, maskc[:, bt, :], b_maskc)
                        if pend is not None:
                            pv(banks, *pend)
                        pend = (PT, bPT, vca[:, bt, g, :], b_vca, bt == 0, bt == nbt - 1, CW)
                    pv(banks, *pend)
                    evac(banks, CW, 0, g, True)
                    if getattr(cfg, "stop", None) == "N1":
                        continue
                    V(lambda e: e.tensor_tensor(wk1[:], impg[:], fbias[:], ALU.add), [b_impg, b_fbias], [b_wk1])
                    for r_ in range((cfg.TOPN - 1) // 8):
                        V(lambda e: e.max(out=mx8[:], in_=wk1[:]), [b_wk1], [b_mx8])
                        V(lambda e: e.match_replace(out=wk1[:], in_to_replace=mx8[:], in_values=wk1[:], imm_value=-3.0e38), [b_mx8, b_wk1], [b_wk1])
                    V(lambda e: e.max(out=mx8[:], in_=wk1[:]), [b_wk1], [b_mx8])
                    kth = (cfg.TOPN - 1) % 8
                    V(lambda e: e.tensor_tensor(wk1[:], impg[:], fbias[:], ALU.add), [b_impg, b_fbias, b_mx8], [b_wk1])
                    V(lambda e: e.tensor_scalar(selb[:], wk1[:], mx8[:, kth:kth + 1], None, op0=ALU.is_ge), [b_wk1, b_mx8], [b_selb])
                    pt2, bpt2 = C.psum(PS_TR)
                    ptb2 = pt2[:].bitcast(BF16)
                    T(lambda e: e.transpose(ptb2[:NSLC, 0:128], selb[:], identb[:]), [b_selb, b_identb], [bpt2])
                    A(lambda e: e.copy(selT[:], ptb2[:NSLC, 0:128]), [bpt2], [b_selT])
                    if getattr(cfg, "stop", None) == "N2":
                        continue
                    HB2 = [(a, min(a + 4, HPG)) for a in range(0, HPG, 4)]
                    banks = [(C.ps[6 + k_][0], C.ps[6 + k_][1], a, b) for k_, (a, b) in enumerate(HB2)]
                    pend = None
                    for kt in range(t + 1):
                        kc_ = slice(kt * 128, (kt + 1) * 128)
                        E, bE, k2 = s_scores(KTs, b_KTs, kc_, g, True)
                        pm, bpm = C.psum(PS_TR)
                        T(lambda e: e.matmul(pm[:, :128], exm[:, kc_], selT[:], start=True, stop=True), [b_exm, b_selT], [bpm])
                        if kt == t:
                            md, bmd = mdiag[kt % 2]
                            V(lambda e: e.tensor_tensor(md[:], pm[:, :128], causT[:], ALU.mult), [bpm, b_causT], [bmd])
                            PT, bPT = masked(E, bE, k2, md[:], bmd)
                        else:
                            PT, bPT = masked(E, bE, k2, pm[:, :128], bpm)
                        if pend is not None:
                            pv(banks, *pend)
                        pend = (PT, bPT, Vs[:, kt, g, :], b_Vs, kt == 0, kt == t, 65)
                    pv(banks, *pend)
                    evac(banks, 65, 1, g, False)
                    if getattr(cfg, "stop", None) == "N3":
                        continue
                    kts = [kt for kt in range(t - WT, t + 1) if kt >= 0]
                    pend = None
                    for kt in kts:
                        kc_ = slice(kt * 128, (kt + 1) * 128)
                        E, bE, k2 = s_scores(KTw, b_KTw, kc_, g, True)
                        mk = causT if kt == t else (bandT if kt == t - WT else onesb)
                        PT, bPT = masked(E, bE, k2, mk[:], b_causT)
                        if pend is not None:
                            pv(banks, *pend)
                        pend = (PT, bPT, Vw[:, kt, g, :], b_Vw, kt == kts[0], kt == kts[-1], 65)
                    pv(banks, *pend)
                    evac(banks, 65, 2, g, False)
                if not nsa_on:
                    V(lambda e: e.memset(oacc[:], 0.0), [], [b_oacc])
                V(lambda e: e.tensor_copy(yc[:, DI:], oacc[:]), [b_oacc], [b_yc])
                S.dma("sync", ycat_d[(t - TF0) * 128:(t - TF0 + 1) * 128, :], yc[:], reads=[b_yc], writes=[bycat_d], sembuf=b_yc)
                if "ycat" in dbg_o and t >= NTH:
                    dbs, b_dbs = C.sb("dbs%d" % t, [128, DMIX], F32)
                    V(lambda e: e.tensor_copy(dbs[:], yc[:]), [b_yc], [b_dbs])
                    S.dma("sync", dbg_o["ycat"][(t - NTH) * 128:(t - NTH + 1) * 128, :], dbs[:], reads=[b_dbs], sembuf=b_dbs)
            V(lambda e: e.tensor_tensor(xw[:].rearrange("p (h d) -> p h d", h=HS), XS3, bc3(sm["coef"][0][:], 64), ALU.mult),
              [b_XS, b2("coef")], [b_xw])
            for g in range(4):
                hsl = slice(g * HPGS, (g + 1) * HPGS)
                gsl = slice(g * GW, (g + 1) * GW)
                pst, bpst = C.psum(PS_MM)
                T(lambda e: e.matmul(pst[:, :GW], BTm[:, g * 128:(g + 1) * 128], xw[:, gsl], start=True, stop=True), [b_BTm, b_xw], [bpst])
                H3 = Hs[:, gsl].rearrange("p (h d) -> p h d", h=HPGS)
                V(lambda e: e.tensor_tensor(H3, H3, bc3(sm["dec"][0][:, hsl], 64), ALU.mult), [b_Hs, b2("dec")], [b_Hs])
                V(lambda e: e.tensor_tensor(Hs[:, gsl], Hs[:, gsl], pst[:, :GW], ALU.add), [b_Hs, bpst], [b_Hs])
            P(lambda e: e.tensor_copy(Hb[:], Hs[:]), [b_Hs], [b_Hb])
        stg = [C.sb("stg%d" % i, [128, 128], F32) for i in range(2)]
        for c in range(DI // 128):
            sg, bsg = stg[c % 2]
            pt, bpt = C.psum(PS_MM)
            T(lambda e: e.transpose(pt[:, :128], Hs[:, c * 128:(c + 1) * 128], identf[:]), [b_Hs, b_identf], [bpt])
            V(lambda e: e.tensor_copy(sg[:], pt[:, :128]), [bpt], [bsg])
            S.dma("sync", ssm_p[c * 128:(c + 1) * 128, :], sg[:], reads=[bsg], sembuf=bsg)

    if getattr(cfg, "stop", None) == "M":
        S.finish()
        return
    GO = 5
    groupsO = []
    t = TF0
    while t < NT:
        n = min(GO, NT - t)
        if 0 < NT - t - n < 3:
            n = (NT - t + 1) // 2
        groupsO.append(list(range(t, t + n)))
        t += n
    GOM = max(len(g) for g in groupsO)
    NGM = GOM * 128
    KO = max(DMIX // 128, NHT)
    with Phase():
        hp, b_hp = C.sb("hp", [128, GOM, D], F32)
        xnT, b_xnT = C.sb("xnTo", [128, KC, NGM], BF16)
        arena, b_arena = C.sb("arena", [128, KO, NGM], BF16)
        NSUB = max(1, KO // KC)
        wbo = []
        for i in range(2):
            t_, _b = C.sb("wbo%d" % i, [128, KO, 512], BF16)
            wbo.append((t_, [Buf("wbo%d_%d" % (i, j)) for j in range(max(NSUB, 1))]))
        wrot = [0]
        srot = [0]

        def load_sub(W, c0, ncols):
            k = srot[0] % (2 * NSUB)
            srot[0] += 1
            t_, bl = wbo[k // NSUB]
            j = k % NSUB
            view = t_[:, j * KC:(j + 1) * KC, :]
            S.dma("gpsimd", view[:, :, :ncols], W[:, c0:c0 + ncols].rearrange("(kc p) c -> p kc c", p=128), writes=[bl[j]])
            return view, bl[j]
        junk, b_junk = C.sb("junko", [128, D], F32)
        xnb = [C.sb("xnbo%d" % i, [128, D], BF16) for i in range(2)]
        stat, b_stat = C.sb("stato", [128, 16], F32)
        mixb = [C.sb("mixb%d" % i, [128, D], F32) for i in range(2)]
        yct = [C.sb("yct%d" % i, [128, DMIX], BF16) for i in range(1)]
        ub = [C.sb("ub%d" % i, [128, 2 + NGM], F32) for i in range(2)]
        ua = [C.sb("ua%d" % i, [128, NGM], F32) for i in range(2)]
        gg, b_gg = C.sb("gg", [128, NGM], F32)
        fhist = [C.sb("fhist%d" % l, [128, NFT, 2], F32) for l in range(2)]
        fcw, b_fcw = C.sb("fcw", [128, 2, NFT, 3], F32)
        fcb, b_fcb = C.sb("fcb", [128, 2, NFT], F32)
        chist, b_chist = C.sb("chist", [128, KC, 30], F32)
        ab_ = [C.sb("ab%d" % i, [128, 30 + NGM], F32) for i in range(2)]
        cT, b_cT = C.sb("cT", [128, KC, NGM], F32)
        sq, b_sq = C.sb("sq", [128, NGM], F32)
        mean, b_mean = C.sb("mean", [128, NGM], F32)
        rstd, b_rstd = C.sb("rstd", [128, NGM], F32)
        sg_, b_sg = C.sb("sgl", [128, NGM], F32)
        bpw1, b_bpw1 = C.sb("bpw1", [128, 2 * KC], F32)
        dww, b_dww = C.sb("dww", [128, KC, 31], F32)
        dwb, b_dwb = C.sb("dwb", [128, KC], F32)
        lng, b_lng = C.sb("lng", [128, KC], F32)
        lnb, b_lnb = C.sb("lnb", [128, KC], F32)
        bpw2, b_bpw2 = C.sb("bpw2", [128, D], F32)
        bco = Buf("constsO")
        for (dst, src) in ((fcw, fcw_d.rearrange("l p m k -> p l m k")), (fcb, fcb_d.rearrange("l p m -> p l m")),
                           (bpw1, bpw1_d), (dww, dww_d), (dwb, dwb_d), (lng, lng_d), (lnb, lnb_d),
                           (bpw2, bpw2_d.partition_broadcast(128))):
            S.dma("sync", dst[:], src, writes=[bco])
        b_fcw = b_fcb = b_bpw1 = b_dww = b_dwb = b_lng = b_lnb = b_bpw2 = bco
        for l in range(2):
            V(lambda e, l=l: e.memset(fhist[l][0][:], 0.0), [], [fhist[l][1]])
        V(lambda e: e.memset(chist[:], 0.0), [], [b_chist])

        def load_wo(W, r0, kcn, c0, ncols):
            k = wrot[0] % 2
            wrot[0] += 1
            wb, bl = wbo[k]
            src = W[r0 * 128:(r0 + kcn) * 128, c0:c0 + ncols].rearrange("(kc p) c -> p kc c", p=128)
            S.dma("gpsimd", wb[:, :kcn, :ncols], src, writes=list(bl), sembuf=bl[0])
            return wb, bl

        def rms_T(i, k):
            xb, bxb = xnb[k % 2]
            A(lambda e: e.activation(junk[:, :D], hp[:, i, :], AF.Square, accum_out=stat[:, 0:1]), [b_hp], [b_junk, b_stat])
            A(lambda e: e.activation(stat[:, 1:2], stat[:, 0:1], AF.Sqrt, scale=1.0 / D, bias=EPS), [b_stat], [b_stat])
            V(lambda e: e.reciprocal(stat[:, 2:3], stat[:, 1:2]), [b_stat], [b_stat])
            V(lambda e: e.scalar_tensor_tensor(xb[:], hp[:, i, :], stat[:, 2:3], gtab[:], op0=ALU.mult, op1=ALU.mult),
              [b_hp, b_stat, b_gtab], [bxb])
            transpose_to(lambda c0, cnt: xnT[:, c0:c0 + cnt, i * 128:(i + 1) * 128],
                         lambda c: xb[:, c * 128:(c + 1) * 128], KC, bxb, b_xnT)

        def post_add(i, mx, bmx):
            A(lambda e: e.activation(junk[:, :D], mx[:], AF.Square, accum_out=stat[:, 4:5]), [bmx], [b_junk, b_stat])
            A(lambda e: e.activation(stat[:, 5:6], stat[:, 4:5], AF.Sqrt, scale=1.0 / D, bias=EPS), [b_stat], [b_stat])
            V(lambda e: e.reciprocal(stat[:, 6:7], stat[:, 5:6]), [b_stat], [b_stat])
            V(lambda e: e.scalar_tensor_tensor(mx[:], mx[:], stat[:, 6:7], gtab[:], op0=ALU.mult, op1=ALU.mult),
              [bmx, b_stat, b_gtab], [bmx])
            P(lambda e: e.tensor_tensor(hp[:, i, :], hp[:, i, :], mx[:], ALU.add), [bmx, b_hp], [b_hp])

        def tok_major(W, kcn, actT, b_actT, ng, bias=None):
            wbs = [load_wo(W, 0, kcn, c0, min(512, D - c0)) for c0 in range(0, D, 512)]
            for i in range(ng):
                mx, bmx = mixb[i % 2]
                for ci, c0 in enumerate(range(0, D, 512)):
                    cn = min(512, D - c0)
                    wb, bwb = wbs[ci]
                    pt, bpt = C.psum(PS_MM)
                    for kc in range(kcn):
                        T(lambda e, kc=kc: e.matmul(pt[:, :cn], actT[:, kc, i * 128:(i + 1) * 128], wb[:, kc, :cn],
                                                    start=(kc == 0), stop=(kc == kcn - 1)), list(bwb) + [b_actT], [bpt])
                    if bias is None:
                        A(lambda e: e.copy(mx[:, c0:c0 + cn], pt[:, :cn]), [bpt], [bmx])
                    else:
                        V(lambda e: e.tensor_tensor(mx[:, c0:c0 + cn], pt[:, :cn], bias[:, c0:c0 + cn], ALU.add), [bpt, b_bpw2], [bmx])
                yield i, mx, bmx

        def conv_ffn(layer, gi_, tiles, ng, NG, chunks):
            gsel = 4 * layer + 2
            load_gain(gsel)
            for i in range(ng):
                rms_T(i, i)
            fh, bfh = fhist[layer]
            blk = {}
            for m in range(NHT):
                res = []
                if m % 4 == 0:
                    nm_ = min(4, NHT - m)
                    blk = {0: load_sub(w_up[layer], m * 128, nm_ * 128), 1: load_sub(w_up[layer], (NHT + m) * 128, nm_ * 128)}
                jm = m % 4
                for part in range(2):
                    mt = m + part * NHT
                    wb, bwb = blk[part]
                    u, bu = ub[part]
                    for (c0, cn) in chunks:
                        pt, bpt = C.psum(PS_MM)
                        for kc in range(KC):
                            T(lambda e, kc=kc: e.matmul(pt[:, :cn], wb[:, kc, jm * 128:(jm + 1) * 128], xnT[:, kc, c0:c0 + cn],
                                                        start=(kc == 0), stop=(kc == KC - 1)), [bwb, b_xnT], [bpt])
                        A(lambda e: e.copy(u[:, 2 + c0:2 + c0 + cn], pt[:, :cn]), [bpt], [bu])
                    if gi_ == 0:
                        V(lambda e: e.tensor_scalar(u[:, 2:130], u[:, 2:130], flag[:, 0:1], None, op0=ALU.mult), [bu, b_flag], [bu])
                    P(lambda e: e.tensor_copy(u[:, 0:2], fh[:, mt, :]), [bfh], [bu])
                    P(lambda e: e.tensor_copy(fh[:, mt, :], u[:, NG:NG + 2]), [bu], [bfh])
                    a_, ba_ = ua[part]
                    V(lambda e: e.tensor_scalar(a_[:, :NG], u[:, 0:NG], fcw[:, layer, mt, 0:1], None, op0=ALU.mult), [bu, b_fcw], [ba_])
                    for k in range(1, 3):
                        V(lambda e, k=k: e.scalar_tensor_tensor(a_[:, :NG], u[:, k:k + NG], fcw[:, layer, mt, k:k + 1], a_[:, :NG],
                                                                op0=ALU.mult, op1=ALU.add), [bu, b_fcw, ba_], [ba_])
                    res.append((a_, ba_, mt))
                (ag, bag, mg), (av, bav, mv) = res
                A(lambda e: e.activation(gg[:, :NG], ag[:, :NG], AF.Gelu_apprx_tanh, bias=fcb[:, layer, mg:mg + 1]), [bag, b_fcb], [b_gg])
                V(lambda e: e.scalar_tensor_tensor(arena[:, m, :NG], av[:, :NG], fcb[:, layer, mv:mv + 1], gg[:, :NG],
                                                   op0=ALU.add, op1=ALU.mult), [bav, b_fcb, b_gg], [b_arena])
            load_gain(gsel + 1)
            for i, mx, bmx in tok_major(w_down[layer], NHT, arena, b_arena, ng):
                post_add(i, mx, bmx)

        for gi_, tiles in enumerate(groupsO):
            ng = len(tiles)
            NG = ng * 128
            chunks = [(c, min(512, NG - c)) for c in range(0, NG, 512)]
            for i, t in enumerate(tiles):
                S.dma("sync", hp[:, i, :], xin[t * 128:(t + 1) * 128, :], writes=[b_hp])
                yc, byc = yct[0]
                S.dma("sync", yc[:], ycat_d[(t - TF0) * 128:(t - TF0 + 1) * 128, :], reads=[bycat_d], writes=[byc])
                transpose_to(lambda c0, cnt: arena[:, c0:c0 + cnt, i * 128:(i + 1) * 128],
                             lambda c: yc[:, c * 128:(c + 1) * 128], DMIX // 128, byc, b_arena)
            load_gain(1)
            for i, mx, bmx in tok_major(w_out, DMIX // 128, arena, b_arena, ng):
                post_add(i, mx, bmx)
            conv_ffn(0, gi_, tiles, ng, NG, chunks)
            load_gain(4)
            for i in range(ng):
                rms_T(i, i)
            for m in range(KC):
                if m % 4 == 0:
                    nm_ = min(4, KC - m)
                    blka = load_sub(w_pw1, m * 128, nm_ * 128)
                    blkg = load_sub(w_pw1, (KC + m) * 128, nm_ * 128)
                jm = m % 4
                (wa, bwa), (wg, bwg) = blka, blkg
                ab, bab = ab_[m % 2]
                for (c0, cn) in chunks:
                    pa, bpa = C.psum(PS_MM)
                    pg, bpg = C.psum(PS_MM)
                    for kc in range(KC):
                        T(lambda e, kc=kc: e.matmul(pa[:, :cn], wa[:, kc, jm * 128:(jm + 1) * 128], xnT[:, kc, c0:c0 + cn],
                                                    start=(kc == 0), stop=(kc == KC - 1)), [bwa, b_xnT], [bpa])
                    for kc in range(KC):
                        T(lambda e, kc=kc: e.matmul(pg[:, :cn], wg[:, kc, jm * 128:(jm + 1) * 128], xnT[:, kc, c0:c0 + cn],
                                                    start=(kc == 0), stop=(kc == KC - 1)), [bwg, b_xnT], [bpg])
                    A(lambda e: e.activation(sg_[:, :cn], pg[:, :cn], AF.Sigmoid, bias=bpw1[:, KC + m:KC + m + 1]), [bpg, b_bpw1], [b_sg])
                    V(lambda e: e.scalar_tensor_tensor(ab[:, 30 + c0:30 + c0 + cn], pa[:, :cn], bpw1[:, m:m + 1], sg_[:, :cn],
                                                       op0=ALU.add, op1=ALU.mult), [bpa, b_bpw1, b_sg], [bab])
                if gi_ == 0:
                    V(lambda e: e.tensor_scalar(ab[:, 30:158], ab[:, 30:158], flag[:, 0:1], None, op0=ALU.mult), [bab, b_flag], [bab])
                P(lambda e: e.tensor_copy(ab[:, 0:30], chist[:, m, :]), [b_chist], [bab])
                P(lambda e: e.tensor_copy(chist[:, m, :], ab[:, NG:NG + 30]), [bab], [b_chist])
                V(lambda e: e.tensor_scalar(cT[:, m, :NG], ab[:, 0:NG], dww[:, m, 0:1], None, op0=ALU.mult), [bab, b_dww], [b_cT])
                for k in range(1, 31):
                    V(lambda e, k=k: e.scalar_tensor_tensor(cT[:, m, :NG], ab[:, k:k + NG], dww[:, m, k:k + 1], cT[:, m, :NG],
                                                            op0=ALU.mult, op1=ALU.add), [bab, b_dww, b_cT], [b_cT])
                V(lambda e: e.tensor_scalar(cT[:, m, :NG], cT[:, m, :NG], dwb[:, m:m + 1], None, op0=ALU.add), [b_cT, b_dwb], [b_cT])
            for (c0, cn) in chunks:
                p1, bp1 = C.psum(PS_MM)
                p2, bp2 = C.psum(PS_MM)
                for m in range(KC):
                    T(lambda e, m=m: e.matmul(p1[:, :cn], onesf[:], cT[:, m, c0:c0 + cn], start=(m == 0), stop=(m == KC - 1)), [b_onesf, b_cT], [bp1])
                for m in range(KC):
                    A(lambda e, m=m: e.activation(sq[:, :cn], cT[:, m, c0:c0 + cn], AF.Square), [b_cT], [b_sq])
                    T(lambda e, m=m: e.matmul(p2[:, :cn], onesf[:], sq[:, :cn], start=(m == 0), stop=(m == KC - 1)), [b_onesf, b_sq], [bp2])
                A(lambda e: e.mul(mean[:, c0:c0 + cn], p1[:, :cn], 1.0 / D), [bp1], [b_mean])
                V(lambda e: e.tensor_tensor(sq[:, :cn], mean[:, c0:c0 + cn], mean[:, c0:c0 + cn], ALU.mult), [b_mean, b_sq], [b_sq])
                V(lambda e: e.scalar_tensor_tensor(rstd[:, c0:c0 + cn], p2[:, :cn], 1.0 / D, sq[:, :cn], op0=ALU.mult, op1=ALU.subtract),
                  [bp2, b_sq], [b_rstd])
                A(lambda e: e.activation(rstd[:, c0:c0 + cn], rstd[:, c0:c0 + cn], AF.Sqrt, bias=EPS), [b_rstd], [b_rstd])
                V(lambda e: e.reciprocal(rstd[:, c0:c0 + cn], rstd[:, c0:c0 + cn]), [b_rstd], [b_rstd])
            for m in range(KC):
                V(lambda e: e.tensor_tensor(cT[:, m, :NG], cT[:, m, :NG], mean[:, :NG], ALU.subtract), [b_cT, b_mean], [b_cT])
                V(lambda e: e.tensor_tensor(cT[:, m, :NG], cT[:, m, :NG], rstd[:, :NG], ALU.mult), [b_cT, b_rstd], [b_cT])
                A(lambda e: e.activation(arena[:, m, :NG], cT[:, m, :NG], AF.Silu, bias=lnb[:, m:m + 1], scale=lng[:, m:m + 1]),
                  [b_cT, b_lng, b_lnb], [b_arena])
            load_gain(5)
            for i, mx, bmx in tok_major(w_pw2, KC, arena, b_arena, ng, bias=bpw2):
                post_add(i, mx, bmx)
            conv_ffn(1, gi_, tiles, ng, NG, chunks)
            for i, t in enumerate(tiles):
                if t >= NTH:
                    S.dma("sync", y_p[(t - NTH) * 128:(t - NTH + 1) * 128, :], hp[:, i, :], reads=[b_hp], sembuf=b_hp)
        co, b_co = C.sb("co", [32, D], F32)
        for m in range(KC):
            pt, bpt = C.psum(PS_MM)
            T(lambda e: e.transpose(pt[:30, :128], chist[:, m, :], identf[:]), [b_chist, b_identf], [bpt])
            V(lambda e: e.tensor_copy(co[:30, m * 128:(m + 1) * 128], pt[:30, :128]), [bpt], [b_co])
        S.dma("sync", conf_p, co[:30, :], reads=[b_co], sembuf=b_co)
        fo = [C.sb("fo%d" % i, [2, 1024], F32) for i in range(2)]
        for l in range(2):
            for mb in range(0, NFT, 8):
                nmb = min(8, NFT - mb)
                f_, bf_ = fo[(mb // 8) % 2]
                for m in range(mb, mb + nmb):
                    pt, bpt = C.psum(PS_MM)
                    T(lambda e: e.transpose(pt[:2, :128], fhist[l][0][:, m, :], identf[:]), [fhist[l][1], b_identf], [bpt])
                    V(lambda e: e.tensor_copy(f_[:, (m - mb) * 128:(m - mb + 1) * 128], pt[:2, :128]), [bpt], [bf_])
                S.dma("sync", ffn_p[l, :, mb * 128:(mb + nmb) * 128], f_[:, :nmb * 128], reads=[bf_], sembuf=bf_)

    if getattr(cfg, "stop", None) == "O":
        S.finish()
        return
    _emit_sample(C, cfg, locals())
    S.finish()
    print("ops", S.n_ops, "waits", S.n_wait, "dma sems", S.ndsem)


def _emit_sample(C, cfg, L):
    nc, S = C.nc, C.S
    D, KC, DI, HS, CD, NH, HPG = cfg.D, cfg.KC, cfg.DI, cfg.HS, cfg.CD, cfg.NH, cfg.HPG
    HPGS, DMIX, DFF, DEC, PAST = cfg.HPGS, cfg.DMIX, cfg.DFF, cfg.DEC, cfg.PAST
    GW = HPGS * 64
    NPG = PAST // 128
    NCS = PAST // 16 - 1
    NBS = (NCS + 127) // 128
    NSL = PAST // 64
    CWs = 65 + NSL + 1
    NWT = min(cfg.WINDOW, PAST) // 128
    DPROJ = cfg.DPROJ
    V, A, P, T = L["V"], L["A"], L["P"], L["T"]
    din, dout, dscr = L["din"], L["dout"], L["dscr"]
    identb, identf, onesf, gtab = L["identb"], L["identf"], L["onesf"], L["gtab"]
    b_identb, b_identf, b_onesf, b_gtab = L["b_identb"], L["b_identf"], L["b_onesf"], L["b_gtab"]
    Phase, load_gain, normg = L["Phase"], L["load_gain"], L["normg"]
    w_in, w_out, w_up, w_down, w_pw1, w_pw2 = L["w_in"], L["w_out"], L["w_up"], L["w_down"], L["w_pw1"], L["w_pw2"]

    xs_in = din("xs_in", [DEC, D])
    st_ssm = din("st_ssm", [DEC, DI, 128])
    st_sconv = din("st_sconv", [DEC, 3 * CD])
    c_cmp = din("c_cmp", [cfg.NPHYS * 128, 256])
    c_slc = din("c_slc", [cfg.NPHYS * 128, 256])
    c_win = din("c_win", [DEC, NWT * 128, 256])
    ptab = din("ptab", [DEC, NPG], I32)
    st_conf = din("st_conf", [DEC, 30, D])
    st_ffn = din("st_ffn", [2, DEC, 2 * 2 * DFF])
    convw_r = din("convw_r", [4, CD])
    convb_r = din("convb_r", [1, CD])
    fcw_r = din("fcw_r", [2, 3, 2 * DFF])
    fcb_r = din("fcb_r", [2, 1, 2 * DFF])
    dww_r = din("dww_r", [31, D])
    dwb_r = din("dwb_r", [1, D])
    lng_r = din("lng_r", [1, D])
    lnb_r = din("lnb_r", [1, D])
    bpw1_r = din("bpw1_r", [1, 2 * D])
    bpw2_r = L["bpw2_d"]
    rope_s = din("rope_s", [2, 32])
    selc_d = din("selc", [DEC, 31, DEC])
    ovs_d = din("ovs", [NBS * 128, NSL + 1])
    fbs_d = din("fbs", [1, NSL + 1])
    dtb_d, alog_d, dskip_d, ssmn_d = L["dtb_d"], L["alog_d"], L["dskip_d"], L["ssmn_d"]
    cw1_d, cpe_d, cb1_d, cw2k_d, cw2v_d, cb2k_d, cb2v_d = (L[k] for k in ("cw1_d", "cpe_d", "cb1_d", "cw2k_d", "cw2v_d", "cb2k_d", "cb2v_d"))
    exm_d = L["exm_d"]

    y_s = dout("y_s", [DEC, D])
    ssm_s = dout("ssm_s", [DEC, DI, 128])
    sconv_s = dout("sconv_s", [DEC, 3 * CD])
    kv_s = dout("kv_s", [DEC, 768])
    conf_s = dout("conf_s", [DEC, 30, D])
    ffn_s = dout("ffn_s", [2, DEC, 2 * 2 * DFF])
    ycs_d = dscr("ycs_d", [DEC, DMIX])
    gss_d = dscr("gss_d", [DEC, 3 * NH])
    vn_d = dscr("vn_d", [DEC, 768], BF16)
    arow_d = dscr("arow_d", [DEC, D])
    bycs, bgss, bvn, barow = [Buf("sscr%d" % i) for i in range(4)]
    R_ = slice(0, DEC)

    def transposeF(dst, src, rows, cols, b_src, b_dst):
        pt, bpt = C.psum(PS_MM)
        T(lambda e: e.transpose(pt[:cols, :rows], src, identf[:rows, :rows]), [b_src, b_identf], [bpt])
        V(lambda e: e.tensor_copy(dst, pt[:cols, :rows]), [bpt], [b_dst])

    def rms_rows(dst_bf, src, b_src, b_dst, st_, b_st, jk, b_jk):
        A(lambda e: e.activation(jk[R_, :D], src, AF.Square, accum_out=st_[R_, 0:1]), [b_src], [b_jk, b_st])
        A(lambda e: e.activation(st_[R_, 1:2], st_[R_, 0:1], AF.Sqrt, scale=1.0 / D, bias=EPS), [b_st], [b_st])
        V(lambda e: e.reciprocal(st_[R_, 2:3], st_[R_, 1:2]), [b_st], [b_st])
        V(lambda e: e.scalar_tensor_tensor(dst_bf, src, st_[R_, 2:3], gtab[R_, :], op0=ALU.mult, op1=ALU.mult),
          [b_src, b_st, b_gtab], [b_dst])

    with Phase():
        hs, b_hs = C.sb("hs", [DEC, D], F32)
        stt, b_stt = C.sb("stt", [128, 16], F32)
        jk, b_jk = C.sb("jks", [DEC, max(D, DI)], F32)
        xb, b_xb = C.sb("xbs", [DEC, D], BF16)
        xT, b_xT = C.sb("xTs", [128, KC, DEC], BF16)
        raw, b_raw = C.sb("raws", [DEC, DPROJ], F32)
        wbs = [C.sb("wbs%d" % i, [128, KC, 512], BF16) for i in range(3)]
        hst = [C.sb("hsts%d" % i, [DEC, 3, 512], F32) for i in range(2)]
        cwr = [C.sb("cwr%d" % i, [DEC, 4, 512], F32) for i in range(2)]
        cbr = [C.sb("cbr%d" % i, [DEC, 512], F32) for i in range(2)]
        xc, b_xc = C.sb("xcs", [DEC, CD], F32)
        S.dma("sync", hs[:], xs_in, writes=[b_hs])
        load_gain(0)
        rms_rows(xb[:], hs[:], b_hs, b_xb, stt, b_stt, jk, b_jk)
        for kc in range(KC):
            pt, bpt = C.psum(PS_TR)
            ptb = pt[:].bitcast(BF16)
            T(lambda e: e.transpose(ptb[:, 0:DEC], xb[:, kc * 128:(kc + 1) * 128], identb[:DEC, :DEC]), [b_xb, b_identb], [bpt])
            A(lambda e: e.copy(xT[:, kc, :], ptb[:, 0:DEC]), [bpt], [b_xT])
        for bi, c0 in enumerate(range(0, DPROJ, 512)):
            cn = min(512, DPROJ - c0)
            wb, bwb = wbs[bi % 3]
            S.dma("gpsimd", wb[:, :, :cn], w_in[:, c0:c0 + cn].rearrange("(kc p) c -> p kc c", p=128), writes=[bwb])
            pt, bpt = C.psum(PS_MM)
            for kc in range(KC):
                T(lambda e, kc=kc: e.matmul(pt[:DEC, :cn], xT[:, kc, :], wb[:, kc, :cn], start=(kc == 0), stop=(kc == KC - 1)),
                  [bwb, b_xT], [bpt])
            A(lambda e: e.copy(raw[:, c0:c0 + cn], pt[:DEC, :cn]), [bpt], [b_raw])
        tmpc, b_tmpc = C.sb("tmpc", [DEC, 512], F32)
        for ci, c0 in enumerate(range(0, CD, 512)):
            cn = min(512, CD - c0)
            h_, bh_ = hst[ci % 2]
            w_, bw_ = cwr[ci % 2]
            b_, bb_ = cbr[ci % 2]
            for k in range(3):
                S.dma("sync", h_[:, k, :cn], st_sconv[:, k * CD + c0:k * CD + c0 + cn], writes=[bh_])
            for k in range(4):
                S.dma("sync", w_[:, k, :cn], convw_r[k:k + 1, c0:c0 + cn].partition_broadcast(DEC), writes=[bw_])
            S.dma("sync", b_[:, :cn], convb_r[:, c0:c0 + cn].partition_broadcast(DEC), writes=[bb_])
            xn_ = raw[:, cfg.o_xbc + c0:cfg.o_xbc + c0 + cn]
            V(lambda e: e.tensor_tensor(xc[:, c0:c0 + cn], xn_, w_[:, 3, :cn], ALU.mult), [b_raw, bw_], [b_xc])
            for k in range(3):
                V(lambda e, k=k: e.tensor_tensor(tmpc[:, :cn], h_[:, k, :cn], w_[:, k, :cn], ALU.mult), [bh_, bw_], [b_tmpc])
                V(lambda e: e.tensor_tensor(xc[:, c0:c0 + cn], xc[:, c0:c0 + cn], tmpc[:, :cn], ALU.add), [b_xc, b_tmpc], [b_xc])
            V(lambda e: e.tensor_tensor(xc[:, c0:c0 + cn], xc[:, c0:c0 + cn], b_[:, :cn], ALU.add), [b_xc, bb_], [b_xc])
            for k in range(1, 3):
                S.dma("sync", sconv_s[:, (k - 1) * CD + c0:(k - 1) * CD + c0 + cn], h_[:, k, :cn], reads=[bh_], sembuf=bh_)
        A(lambda e: e.activation(xc[:], xc[:], AF.Silu), [b_xc], [b_xc])
        S.dma("sync", sconv_s[:, 2 * CD:3 * CD], raw[:, cfg.o_xbc:cfg.o_xbc + CD], reads=[b_raw], sembuf=b_raw)
        tb = {}
        for nm, src in (("dtb", dtb_d), ("atab", alog_d), ("dskip", dskip_d)):
            tb[nm] = C.sb("s_" + nm, [DEC, HS], F32)
            S.dma("sync", tb[nm][0][:], src.partition_broadcast(DEC), writes=[tb[nm][1]])
        ssmn, b_ssmn = C.sb("s_ssmn", [DEC, DI], F32)
        S.dma("sync", ssmn[:], ssmn_d.partition_broadcast(DEC), writes=[b_ssmn])
        atab, b_atab = tb["atab"]
        A(lambda e: e.activation(atab[:], atab[:], AF.Exp), [b_atab], [b_atab])
        V(lambda e: e.tensor_scalar(atab[:], atab[:], -1.0, None, op0=ALU.mult), [b_atab], [b_atab])
        sm = {nm: C.sb("ss_" + nm, [DEC, HS], F32) for nm in ("dt", "ax", "ex", "dec")}
        f2 = lambda nm: sm[nm][0][:]
        b2 = lambda nm: sm[nm][1]
        V(lambda e: e.tensor_tensor(f2("dt"), raw[:, cfg.o_dt:cfg.o_dt + HS], tb["dtb"][0][:], ALU.add), [b_raw, tb["dtb"][1]], [b2("dt")])
        V(lambda e: e.tensor_scalar(f2("ax"), f2("dt"), -1.0, None, op0=ALU.mult), [b2("dt")], [b2("ax")])
        V(lambda e: e.tensor_tensor(f2("ax"), f2("ax"), f2("dt"), ALU.min), [b2("dt"), b2("ax")], [b2("ax")])
        A(lambda e: e.activation(f2("ex"), f2("ax"), AF.Exp), [b2("ax")], [b2("ex")])
        A(lambda e: e.activation(f2("ex"), f2("ex"), AF.Ln, bias=1.0), [b2("ex")], [b2("ex")])
        V(lambda e: e.tensor_scalar(f2("dt"), f2("dt"), 0.0, None, op0=ALU.max), [b2("dt")], [b2("dt")])
        V(lambda e: e.tensor_tensor(f2("dt"), f2("dt"), f2("ex"), ALU.add), [b2("dt"), b2("ex")], [b2("dt")])
        V(lambda e: e.tensor_tensor(f2("dec"), f2("dt"), atab[:], ALU.mult), [b2("dt"), b_atab], [b2("dec")])
        A(lambda e: e.activation(f2("dec"), f2("dec"), AF.Exp), [b2("dec")], [b2("dec")])
        xdr, b_xdr = C.sb("xdr", [DEC, DI], F32)
        der, b_der = C.sb("der", [DEC, DI], F32)
        xs3 = xc[:, 0:DI].rearrange("p (h d) -> p h d", h=HS)
        V(lambda e: e.tensor_tensor(xdr[:].rearrange("p (h d) -> p h d", h=HS), xs3, f2("dt").unsqueeze(2).to_broadcast([DEC, HS, 64]), ALU.mult),
          [b_xc, b2("dt")], [b_xdr])
        V(lambda e: e.tensor_copy(der[:].rearrange("p (h d) -> p h d", h=HS), f2("dec").unsqueeze(2).to_broadcast([DEC, HS, 64])), [b2("dec")], [b_der])
        NC16 = DI // 128
        xdc, b_xdc = C.sb("xdc", [128, NC16, DEC], F32)
        dcc, b_dcc = C.sb("dcc", [128, NC16, DEC], F32)
        ycol, b_ycol = C.sb("ycol", [128, NC16, DEC], F32)
        for c in range(NC16):
            transposeF(xdc[:, c, :], xdr[:, c * 128:(c + 1) * 128], DEC, 128, b_xdr, b_xdc)
            transposeF(dcc[:, c, :], der[:, c * 128:(c + 1) * 128], DEC, 128, b_der, b_dcc)
        stb = [C.sb("stb%d" % i, [128, NC16, 128], F32) for i in range(2)]
        Bb, b_Bb = C.sb("Bbs", [128, 512], F32)
        Cb, b_Cb = C.sb("Cbs", [128, 512], F32)
        tmp2, b_tmp2 = C.sb("tmp2s", [128, 128], F32)
        V(lambda e: e.memset(ycol[:], 0.0), [], [b_ycol])
        for s_ in range(DEC):
            st_, bst_ = stb[s_ % 2]
            S.dma("sync", st_[:], st_ssm[s_].rearrange("(c p) n -> p c n", p=128), writes=[bst_])
            selm, b_selm = C.sb("selm%d" % s_, [DEC, 128], F32)
            V(lambda e: e.tensor_copy(selm[:], identf[:DEC, s_:s_ + 1].to_broadcast([DEC, 128])), [b_identf], [b_selm])
            for (dst, bd, o) in ((Bb, b_Bb, DI), (Cb, b_Cb, DI + 512)):
                pt, bpt = C.psum(PS_MM)
                T(lambda e: e.matmul(pt[:, :512], selm[:], xc[:, o:o + 512], start=True, stop=True), [b_selm, b_xc], [bpt])
                V(lambda e: e.tensor_copy(dst[:], pt[:, :512]), [bpt], [bd])
            for c in range(NC16):
                g = (2 * c) // HPGS
                V(lambda e: e.tensor_scalar(st_[:, c, :], st_[:, c, :], dcc[:, c, s_:s_ + 1], None, op0=ALU.mult), [bst_, b_dcc], [bst_])
                V(lambda e: e.scalar_tensor_tensor(st_[:, c, :], Bb[:, g * 128:(g + 1) * 128], xdc[:, c, s_:s_ + 1], st_[:, c, :],
                                                   op0=ALU.mult, op1=ALU.add), [b_Bb, b_xdc, bst_], [bst_])
                V(lambda e: e.tensor_tensor(tmp2[:], st_[:, c, :], Cb[:, g * 128:(g + 1) * 128], ALU.mult), [bst_, b_Cb], [b_tmp2])
                V(lambda e: e.reduce_sum(ycol[:, c, s_:s_ + 1], tmp2[:], axis=AX.X), [b_tmp2], [b_ycol])
            S.dma("sync", ssm_s[s_].rearrange("(c p) n -> p c n", p=128), st_[:], reads=[bst_], sembuf=bst_)
        yr, b_yr = C.sb("yrs", [DEC, DI], F32)
        for c in range(NC16):
            transposeF(yr[:, c * 128:(c + 1) * 128], ycol[:, c, :], 128, DEC, b_ycol, b_yr)
        V(lambda e: e.tensor_tensor(xdr[:].rearrange("p (h d) -> p h d", h=HS), xs3, tb["dskip"][0][:].unsqueeze(2).to_broadcast([DEC, HS, 64]), ALU.mult),
          [b_xc, tb["dskip"][1]], [b_xdr])
        V(lambda e: e.tensor_tensor(yr[:], yr[:], xdr[:], ALU.add), [b_yr, b_xdr], [b_yr])
        A(lambda e: e.activation(der[:], raw[:, 0:DI], AF.Silu), [b_raw], [b_der])
        V(lambda e: e.tensor_tensor(yr[:], yr[:], der[:], ALU.mult), [b_yr, b_der], [b_yr])
        for g in range(4):
            A(lambda e, g=g: e.activation(jk[:, :GW], yr[:, g * GW:(g + 1) * GW], AF.Square, accum_out=stt[R_, 4 + g:5 + g]), [b_yr], [b_jk, b_stt])
        A(lambda e: e.activation(stt[R_, 8:12], stt[R_, 4:8], AF.Sqrt, scale=1.0 / GW, bias=EPS), [b_stt], [b_stt])
        V(lambda e: e.reciprocal(stt[R_, 12:16], stt[R_, 8:12]), [b_stt], [b_stt])
        for g in range(4):
            V(lambda e, g=g: e.scalar_tensor_tensor(yr[:, g * GW:(g + 1) * GW], yr[:, g * GW:(g + 1) * GW], stt[R_, 12 + g:13 + g], ssmn[:, g * GW:(g + 1) * GW],
                                                    op0=ALU.mult, op1=ALU.mult), [b_yr, b_stt, b_ssmn], [b_yr])
        S.dma("sync", ycs_d[:, 0:DI], yr[:], reads=[b_yr], writes=[bycs], sembuf=b_yr)
        rp, b_rp = C.sb("rps", [DEC, 2, 32], F32)
        S.dma("sync", rp[:].rearrange("p a c -> p (a c)"), rope_s.rearrange("a c -> (a c)").unsqueeze(0).partition_broadcast(DEC) if False else rope_s.rearrange("(o a) c -> o (a c)", o=1).partition_broadcast(DEC), writes=[b_rp])
        qk, b_qk = C.sb("qks", [DEC, NH * 64 + 768], F32)
        ta, b_ta = C.sb("tas", [DEC, (NH + 6) * 32], F32)
        tb2, b_tb2 = C.sb("tbs", [DEC, (NH + 6) * 32], F32)

        def rope_rows(dst, src, nh):
            s3 = src.rearrange("p (g d) -> p g d", g=nh)
            d3 = dst.rearrange("p (g d) -> p g d", g=nh)
            cs = rp[:, 0, :].unsqueeze(1).to_broadcast([DEC, nh, 32])
            sn = rp[:, 1, :].unsqueeze(1).to_broadcast([DEC, nh, 32])
            a3 = ta[:, :nh * 32].rearrange("p (g d) -> p g d", g=nh)
            b3 = tb2[:, :nh * 32].rearrange("p (g d) -> p g d", g=nh)
            V(lambda e: e.tensor_tensor(a3, s3[:, :, 0:32], cs, ALU.mult), [b_raw, b_rp], [b_ta])
            V(lambda e: e.tensor_tensor(b3, s3[:, :, 32:64], sn, ALU.mult), [b_raw, b_rp], [b_tb2])
            V(lambda e: e.tensor_tensor(d3[:, :, 0:32], a3, b3, ALU.subtract), [b_ta, b_tb2], [b_qk])
            V(lambda e: e.tensor_tensor(a3, s3[:, :, 32:64], cs, ALU.mult), [b_raw, b_rp], [b_ta])
            V(lambda e: e.tensor_tensor(b3, s3[:, :, 0:32], sn, ALU.mult), [b_raw, b_rp], [b_tb2])
            V(lambda e: e.tensor_tensor(d3[:, :, 32:64], a3, b3, ALU.add), [b_ta, b_tb2], [b_qk])

        rope_rows(qk[:, 0:NH * 64], raw[:, cfg.o_q:cfg.o_q + NH * 64], NH)
        for br in range(3):
            o = cfg.o_kv + br * 256
            rope_rows(qk[:, NH * 64 + br * 256:NH * 64 + br * 256 + 128], raw[:, o:o + 128], 2)
            V(lambda e: e.tensor_copy(qk[:, NH * 64 + br * 256 + 128:NH * 64 + (br + 1) * 256], raw[:, o + 128:o + 256]), [b_raw], [b_qk])
        S.dma("sync", kv_s, qk[:, NH * 64:], reads=[b_qk], sembuf=b_qk)
        kvb, b_kvb = C.sb("kvbs", [DEC, NH * 64 + 768], BF16)
        V(lambda e: e.tensor_copy(kvb[:], qk[:]), [b_qk], [b_kvb])
        S.dma("sync", vn_d, kvb[:, NH * 64:], reads=[b_kvb], writes=[bvn], sembuf=b_kvb)
        gsr, b_gsr = C.sb("gsr", [DEC, 3 * NH], F32)
        A(lambda e: e.activation(gsr[:], raw[:, cfg.o_g:cfg.o_g + 3 * NH], AF.Sigmoid), [b_raw], [b_gsr])
        S.dma("sync", gss_d, gsr[:], reads=[b_gsr], writes=[bgss], sembuf=b_gsr)
        qT_d = dscr("qTs_d", [128, HPG + 3, DEC], BF16)
        bqT = Buf("sscr_qT")
        qTa, b_qTa = C.sb("qTa", [128, HPG + 3, DEC], BF16)
        q4 = kvb[:, 0:NH * 64].rearrange("p (g h d) -> p g h d", g=2, h=HPG)
        qpair, b_qpair = C.sb("qpair", [DEC, 128], BF16)
        for hh in range(HPG + 3):
            if hh < HPG:
                for g in range(2):
                    V(lambda e, g=g: e.tensor_copy(qpair[:, g * 64:(g + 1) * 64], q4[:, g, hh, :]), [b_kvb], [b_qpair])
                src, bsrc = qpair[:], b_qpair
            else:
                o = NH * 64 + (hh - HPG) * 256
                src, bsrc = kvb[:, o:o + 128], b_kvb
            pt, bpt = C.psum(PS_TR)
            ptb = pt[:].bitcast(BF16)
            T(lambda e: e.transpose(ptb[:, 0:DEC], src, identb[:DEC, :DEC]), [bsrc, b_identb], [bpt])
            A(lambda e: e.copy(qTa[:, hh, :], ptb[:, 0:DEC]), [bpt], [b_qTa])
        S.dma("sync", qT_d, qTa[:], reads=[b_qTa], writes=[bqT], sembuf=b_qTa)

    exms_d = din("exms", [NSL, PAST], BF16)
    iota_d = din("iota_p", [128, NPG], F32)
    mcs_d = din("maskcs", [128, NBS])
    with Phase():
        crK, b_crK = C.sb("crK", [128, PAST], BF16)
        crV, b_crV = C.sb("crV", [128, PAST], BF16)
        KTs, b_KTs = C.sb("sKTs", [128, PAST + 128], BF16)
        Vs, b_Vs = C.sb("sVs", [128, NPG + 1, 2, 65], BF16)
        KTw, b_KTw = C.sb("sKTw", [128, (NWT + 1) * 128], BF16)
        Vw, b_Vw = C.sb("sVw", [128, NWT + 1, 2, 65], BF16)
        w1b = [C.sb("sw1b%d" % i, [128, 32, 128], BF16) for i in range(2)]
        hidT, b_hidT = C.sb("shidT", [128, 2, 2, NBS * 128], BF16)
        kcT, b_kcT = C.sb("skcT", [128, NBS * 128], BF16)
        vca, b_vca = C.sb("svca", [128, NBS, 2, CWs], BF16)
        ovt, b_ovt = C.sb("sovt", [128, NBS, NSL + 1], F32)
        exms, b_exms = C.sb("sexms", [NSL, PAST], BF16)
        mcs, b_mcs = C.sb("smcs", [128, NBS], F32)
        cpe, b_cpe = C.sb("scpe", [128, 2, 32], BF16)
        cb1, b_cb1 = C.sb("scb1", [128, 2], F32)
        b1e, b_b1e = C.sb("sb1e", [128, 2], F32)
        cw2k, b_cw2k = C.sb("scw2k", [128, 2, 128], BF16)
        cw2v, b_cw2v = C.sb("scw2v", [128, 64], BF16)
        cb2k, b_cb2k = C.sb("scb2k", [128, 1], F32)
        cb2v, b_cb2v = C.sb("scb2v", [128, 64], F32)
        qTb, b_qTb = C.sb("sqTb", [128, HPG + 3, DEC], BF16)
        fbs, b_fbs = C.sb("sfbs", [1, NSL + 1], F32)
        e0, b_e0 = C.sb("se0", [128, 1], F32)
        ne0, b_ne0 = C.sb("sne0", [128, 1], F32)
        one1, b_one1 = C.sb("sone1", [128, 1], F32)
        iot, b_iot = C.sb("siot", [128, NPG], F32)
        idxf, b_idxf = C.sb("sidxf", [128, NPG], F32)
        idx, b_idx = C.sb("sidx", [128, NPG], I32)
        bc3_ = Buf("constsS")
        for (dst, src) in ((ovt, ovs_d.rearrange("(b p) j -> p b j", p=128)), (exms, exms_d), (mcs, mcs_d), (cb1, cb1_d), (cb2k, cb2k_d),
                           (cb2v, cb2v_d.partition_broadcast(128)), (fbs, fbs_d), (qTb, qT_d)):
            S.dma("sync", dst[:], src, reads=[bqT], writes=[bc3_])
        for (dst, src) in ((cpe, cpe_d.rearrange("k p l -> p k l")), (cw2k, cw2k_d.rearrange("g c m -> c g m")), (cw2v, cw2v_d)):
            S.dma("gpsimd", dst[:], src, writes=[bc3_])
        for kvi in range(2):
            S.dma("gpsimd", w1b[kvi][0][:].rearrange("p l c -> p (l c)"), cw1_d[kvi], writes=[bc3_])
        b_ovt = b_exms = b_mcs = b_cb1 = b_cb2k = b_cb2v = b_fbs = b_qTb = b_cpe = b_cw2k = b_cw2v = bc3_
        V(lambda e: e.tensor_copy(e0[:], identf[:, 0:1]), [b_identf], [b_e0])
        V(lambda e: e.tensor_scalar(ne0[:], identf[:, 0:1], -1.0, 1.0, op0=ALU.mult, op1=ALU.add), [b_identf], [b_ne0])
        V(lambda e: e.memset(one1[:], 1.0), [], [b_one1])
        S.dma("sync", iot[:], iota_d, writes=[b_iot])
        for kvi in range(2):
            w1v = w1b[kvi][0]
            pb1, bpb1 = C.psum(PS_MM)
            for l in range(32):
                T(lambda e, l=l: e.matmul(pb1[:, 0:1], w1v[0:64, l, :], cpe[0:64, kvi, l:l + 1], start=(l == 0), stop=(l == 31)),
                  [bc3_], [bpb1])
            V(lambda e: e.tensor_tensor(b1e[:, kvi:kvi + 1], pb1[:, 0:1], cb1[:, kvi:kvi + 1], ALU.add), [bpb1, b_cb1], [b_b1e])
        V(lambda e: e.memset(vca[:], 0.0), [], [b_vca])
        V(lambda e: e.memset(vca[:, :, :, 64:65], 1.0), [], [b_vca])
        for g in range(2):
            V(lambda e, g=g: e.tensor_copy(vca[:, :, g, 65:CWs], ovt[:]), [b_ovt], [b_vca])
        V(lambda e: e.memset(hidT[:], 0.0), [], [b_hidT])
        V(lambda e: e.memset(kcT[:], 0.0), [], [b_kcT])
        pg = [C.sb("spg%d" % i, [128, 256], F32) for i in range(4)]
        pgb = [C.sb("spgb%d" % i, [128, 256], BF16) for i in range(4)]
        Es, b_Es = C.sb("sEs", [128, HPG], F32)
        PTs = [C.sb("sPTs%d" % i, [128, HPG], BF16) for i in range(2)]
        pmS, b_pmS = C.sb("spmS", [128, 1], F32)
        nst, b_nst = C.sb("snst", [HPG, 4], F32)
        gt, b_gt = C.sb("sgt", [HPG, 3], F32)
        oac = [C.sb("soac%d" % i, [HPG, 64], F32) for i in range(2)]
        otm, b_otm = C.sb("sotm", [HPG, 64], F32)
        impu, b_impu = C.sb("simpu", [HPG, NSL + 1], F32)
        wk1, b_wk1 = C.sb("swk1", [1, NSL + 1], F32)
        wk2, b_wk2 = C.sb("swk2", [1, NSL + 1], F32)
        mx8, b_mx8 = C.sb("smx8", [1, 8], F32)
        selr, b_selr = C.sb("sselr", [1, NSL], F32)
        selTb, b_selTb = C.sb("sselTb", [128, 1], BF16)
        rot = [0]

        def page_in(src_ap, idx_col, k, use_idx=True):
            p_, bp_ = pg[k % 4]
            pb_, bpb_ = pgb[k % 4]
            if use_idx:
                S.dma_custom("gpsimd", lambda e: e.indirect_dma_start(out=p_[:], out_offset=None, in_=src_ap,
                                                                      in_offset=bass.IndirectOffsetOnAxis(ap=idx_col, axis=0)), [b_idx], [bp_])
            else:
                S.dma("sync", p_[:], src_ap, writes=[bp_])
            P(lambda e: e.tensor_copy(pb_[:], p_[:]), [bp_], [bpb_])
            return pb_, bpb_

        def tr_to(dst, b_dst, src, b_src):
            pt, bpt = C.psum(PS_TR)
            ptb = pt[:].bitcast(BF16)
            T(lambda e: e.transpose(ptb[:, 0:128], src, identb[:]), [b_src, b_identb], [bpt])
            A(lambda e: e.copy(dst, ptb[:, 0:128]), [bpt], [b_dst])

        NTM = NPG + 1
        EsA, b_EsA = C.sb("sEsA", [128, NTM * HPG], F32)
        PTA, b_PTA = C.sb("sPTA", [128, NTM * HPG], BF16)
        mskA, b_mskA = C.sb("smskA", [128, NTM], F32)
        mskW, b_mskW = C.sb("smskW", [128, NWT + 1], F32)
        V(lambda e: e.memset(mskW[:], 1.0), [], [b_mskW])
        if NWT * 128 >= cfg.WINDOW:
            V(lambda e: e.tensor_copy(mskW[:, 0:1], ne0[:]), [b_ne0], [b_mskW])
        V(lambda e: e.tensor_copy(mskW[:, NWT:NWT + 1], e0[:]), [b_e0], [b_mskW])

        def branch(KT, bKT, nt, msk, bmsk, g, s_, per, rhs_fn, bVv, pso, bpso):
            gsl = slice(g * 64, (g + 1) * 64)
            for c0 in range(0, nt, 64):
                cnt = min(64, nt - c0)
                ps_, bps_ = C.psum((0, 1, 2))
                for j in range(cnt):
                    kt = c0 + j
                    T(lambda e, j=j, kt=kt: e.matmul(ps_[:, j * HPG:(j + 1) * HPG], KT[gsl, kt * 128:(kt + 1) * 128], qTb[gsl, 0:HPG, s_],
                                                     start=True, stop=True), [bKT, b_qTb], [bps_])
                V(lambda e: e.tensor_copy(EsA[:, c0 * HPG:(c0 + cnt) * HPG], ps_[:, :cnt * HPG]), [bps_], [b_EsA])
            A(lambda e: e.activation(EsA[:, :nt * HPG], EsA[:, :nt * HPG], AF.Exp, scale=0.125), [b_EsA], [b_EsA])
            V(lambda e: e.tensor_tensor(PTA[:, :nt * HPG].rearrange("p (t h) -> p t h", h=HPG), EsA[:, :nt * HPG].rearrange("p (t h) -> p t h", h=HPG),
                                        msk.unsqueeze(2).to_broadcast([128, nt, HPG]), ALU.mult), [b_EsA, bmsk], [b_PTA])
            for kt in range(nt):
                T(lambda e, kt=kt: e.matmul(pso[:HPG, :per], PTA[:, kt * HPG:(kt + 1) * HPG], rhs_fn(kt), start=(kt == 0), stop=(kt == nt - 1)),
                  [b_PTA, bVv], [bpso])

        def evac_s(pso, bpso, gi_, oa, boa, first):
            V(lambda e: e.tensor_scalar(nst[:, 0:1], pso[:HPG, 64:65], 1e-30, None, op0=ALU.max), [bpso], [b_nst])
            V(lambda e: e.reciprocal(nst[:, 1:2], nst[:, 0:1]), [b_nst], [b_nst])
            V(lambda e: e.tensor_tensor(nst[:, 2:3], nst[:, 1:2], gt[:, gi_:gi_ + 1], ALU.mult), [b_nst, b_gt], [b_nst])
            if first:
                V(lambda e: e.tensor_scalar(oa[:], pso[:HPG, 0:64], nst[:, 2:3], None, op0=ALU.mult), [bpso, b_nst], [boa])
            else:
                V(lambda e: e.scalar_tensor_tensor(oa[:], pso[:HPG, 0:64], nst[:, 2:3], oa[:], op0=ALU.mult, op1=ALU.add), [bpso, b_nst, boa], [boa])

        for s_ in range(DEC):
            S.dma("sync", idx[:], ptab[s_:s_ + 1, :].partition_broadcast(128), writes=[b_idx])
            V(lambda e: e.tensor_copy(idxf[:], idx[:]), [b_idx], [b_idxf])
            V(lambda e: e.scalar_tensor_tensor(idxf[:], idxf[:], 128.0, iot[:], op0=ALU.mult, op1=ALU.add), [b_idxf, b_iot], [b_idxf])
            V(lambda e: e.tensor_copy(idx[:], idxf[:]), [b_idxf], [b_idx])
            V(lambda e: e.memset(KTs[:, PAST:PAST + 128], 0.0), [], [b_KTs])
            V(lambda e: e.memset(KTw[:, NWT * 128:(NWT + 1) * 128], 0.0), [], [b_KTw])
            V(lambda e: e.memset(Vs[:, NPG, :, :], 0.0), [], [b_Vs])
            V(lambda e: e.memset(Vw[:, NWT, :, :], 0.0), [], [b_Vw])
            kk = 0
            for j in range(NPG):
                pb_, bpb_ = page_in(c_cmp, idx[:, j:j + 1], kk)
                kk += 1
                tr_to(crK[:, j * 128:(j + 1) * 128], b_crK, pb_[:, 0:128], bpb_)
                tr_to(crV[:, j * 128:(j + 1) * 128], b_crV, pb_[:, 128:256], bpb_)
                pb_, bpb_ = page_in(c_slc, idx[:, j:j + 1], kk)
                kk += 1
                tr_to(KTs[:, j * 128:(j + 1) * 128], b_KTs, pb_[:, 0:128], bpb_)
                V(lambda e: e.tensor_copy(Vs[:, j, :, 0:64], pb_[:, 128:256].rearrange("p (g d) -> p g d", g=2)), [bpb_], [b_Vs])
            V(lambda e: e.memset(Vs[:, 0:NPG, :, 64:65], 1.0), [b_Vs], [b_Vs])
            for j in range(NWT):
                pb_, bpb_ = page_in(c_win[s_, j * 128:(j + 1) * 128, :], None, kk, use_idx=False)
                kk += 1
                tr_to(KTw[:, j * 128:(j + 1) * 128], b_KTw, pb_[:, 0:128], bpb_)
                V(lambda e: e.tensor_copy(Vw[:, j, :, 0:64], pb_[:, 128:256].rearrange("p (g d) -> p g d", g=2)), [bpb_], [b_Vw])
            V(lambda e: e.memset(Vw[:, 0:NWT, :, 64:65], 1.0), [b_Vw], [b_Vw])
            V(lambda e: e.tensor_copy(KTs[:, PAST:PAST + 1], qTb[:, HPG + 1, s_:s_ + 1]), [b_qTb], [b_KTs])
            V(lambda e: e.tensor_copy(KTw[:, NWT * 128:NWT * 128 + 1], qTb[:, HPG + 2, s_:s_ + 1]), [b_qTb], [b_KTw])
            for g in range(2):
                S.dma("sync", Vs[0:1, NPG, g, 0:64], vn_d[s_:s_ + 1, 256 + 128 + g * 64:256 + 128 + (g + 1) * 64], reads=[bvn], writes=[b_Vs])
                S.dma("sync", Vw[0:1, NWT, g, 0:64], vn_d[s_:s_ + 1, 512 + 128 + g * 64:512 + 128 + (g + 1) * 64], reads=[bvn], writes=[b_Vw])
            V(lambda e: e.memset(Vs[0:1, NPG, :, 64:65], 1.0), [b_Vs], [b_Vs])
            V(lambda e: e.memset(Vw[0:1, NWT, :, 64:65], 1.0), [b_Vw], [b_Vw])
            for kvi, (cr, bcr) in enumerate(((crK, b_crK), (crV, b_crV))):
                w1v = w1b[kvi][0]
                for g in range(2):
                    gsl2 = slice(g * 64, (g + 1) * 64)
                    ph, bph = C.psum(PS_MM)
                    for l in range(32):
                        T(lambda e, l=l: e.matmul(ph[:, :NCS], w1v[gsl2, l, :], cr[gsl2, l:l + 16 * (NCS - 1) + 1:16],
                                                  start=(l == 0), stop=(l == 31)), [bc3_, bcr], [bph])
                    A(lambda e: e.activation(hidT[:, kvi, g, 0:NCS], ph[:, :NCS], AF.Silu, bias=b1e[:, kvi:kvi + 1]), [bph, b_b1e], [b_hidT])
            pk, bpk = C.psum(PS_MM)
            for g in range(2):
                T(lambda e, g=g: e.matmul(pk[:, :NCS], cw2k[:, g, :], hidT[:, 0, g, 0:NCS], start=(g == 0), stop=(g == 1)), [b_cw2k, b_hidT], [bpk])
            A(lambda e: e.activation(kcT[:, 0:NCS], pk[:, :NCS], AF.Identity, bias=cb2k[:, 0:1]), [bpk, b_cb2k], [b_kcT])
            for bt in range(NBS):
                pv2, bpv2 = C.psum(PS_MM)
                for g in range(2):
                    T(lambda e, g=g: e.matmul(pv2[:, g * 64:(g + 1) * 64], hidT[:, 1, g, bt * 128:(bt + 1) * 128], cw2v[:], start=True, stop=True),
                      [b_hidT, b_cw2v], [bpv2])
                V(lambda e: e.tensor_tensor(vca[:, bt, :, 0:64], pv2[:, 0:128].rearrange("p (g d) -> p g d", g=2),
                                            cb2v[:].unsqueeze(1).to_broadcast([128, 2, 64]), ALU.add), [bpv2, b_cb2v], [b_vca])
            for g in range(2):
                oa, boa = oac[g]
                S.dma("sync", gt[:], gss_d[s_, g * HPG * 3:(g + 1) * HPG * 3].rearrange("(h c) -> h c", c=3), reads=[bgss], writes=[b_gt])
                pso, bpso = C.ps[3]
                branch(kcT, b_kcT, NBS, mcs[:, 0:NBS], b_mcs, g, s_, CWs, lambda kt: vca[:, kt, g, :], b_vca, pso, bpso)
                evac_s(pso, bpso, 0, oa, boa, True)
                V(lambda e: e.tensor_copy(impu[:], pso[:HPG, 65:CWs]), [bpso], [b_impu])
                pi, bpi = C.psum(PS_MM)
                T(lambda e: e.matmul(pi[:1, :NSL + 1], nst[:, 1:2], impu[:], start=True, stop=True), [b_nst, b_impu], [bpi])
                V(lambda e: e.tensor_tensor(wk1[:], pi[:1, :NSL + 1], fbs[:], ALU.add), [bpi, b_fbs], [b_wk1])
                V(lambda e: e.tensor_copy(wk2[:], wk1[:]), [b_wk1], [b_wk2])
                for r_ in range((cfg.TOPN - 1) // 8):
                    V(lambda e: e.max(out=mx8[:], in_=wk2[:]), [b_wk2], [b_mx8])
                    V(lambda e: e.match_replace(out=wk2[:], in_to_replace=mx8[:], in_values=wk2[:], imm_value=-3.0e38), [b_mx8, b_wk2], [b_wk2])
                V(lambda e: e.max(out=mx8[:], in_=wk2[:]), [b_wk2], [b_mx8])
                kth = (cfg.TOPN - 1) % 8
                V(lambda e: e.tensor_scalar(selr[:], wk1[:, 0:NSL], mx8[:, kth:kth + 1], None, op0=ALU.is_ge), [b_wk1, b_mx8], [b_selr])
                pt_, bpt_ = C.psum(PS_MM)
                T(lambda e: e.transpose(pt_[:NSL, 0:1], selr[:], identf[0:1, 0:1]), [b_selr, b_identf], [bpt_])
                V(lambda e: e.tensor_copy(selTb[:NSL, :], pt_[:NSL, 0:1]), [bpt_], [b_selTb])
                pso, bpso = C.ps[6]

                pm, bpm = C.psum(PS_TR)
                for kt in range(NPG):
                    T(lambda e, kt=kt: e.matmul(pm[:, kt:kt + 1], exms[:, kt * 128:(kt + 1) * 128], selTb[:NSL, :], start=True, stop=True),
                      [b_exms, b_selTb], [bpm])
                V(lambda e: e.tensor_copy(mskA[:, 0:NPG], pm[:, 0:NPG]), [bpm], [b_mskA])
                V(lambda e: e.tensor_copy(mskA[:, NPG:NPG + 1], e0[:]), [b_e0], [b_mskA])
                branch(KTs, b_KTs, NPG + 1, mskA[:, 0:NPG + 1], b_mskA, g, s_, 65, lambda kt: Vs[:, kt, g, :], b_Vs, pso, bpso)
                evac_s(pso, bpso, 1, oa, boa, False)
                pso, bpso = C.ps[7]

                branch(KTw, b_KTw, NWT + 1, mskW[:, 0:NWT + 1], b_mskW, g, s_, 65, lambda kt: Vw[:, kt, g, :], b_Vw, pso, bpso)
                evac_s(pso, bpso, 2, oa, boa, False)
                S.dma("sync", ycs_d[s_, DI + g * HPG * 64:DI + (g + 1) * HPG * 64].rearrange("(h d) -> h d", d=64), oa[:],
                      reads=[boa], writes=[bycs], sembuf=boa)

    with Phase():
        hs, b_hs = C.sb("hs3", [DEC, D], F32)
        stt, b_stt = C.sb("stt3", [128, 16], F32)
        jk, b_jk = C.sb("jk3", [DEC, D], F32)
        xb, b_xb = C.sb("xb3", [DEC, D], BF16)
        xT, b_xT = C.sb("xT3", [128, KC, DEC], BF16)
        KO = max(DMIX // 128, DFF // 128)
        aT, b_aT = C.sb("aT3", [128, KO, DEC], BF16)
        wbs = [C.sb("wb3_%d" % i, [128, KO, 512], BF16) for i in range(2)]
        mix, b_mix = C.sb("mix3", [DEC, D], F32)
        yc, b_yc = C.sb("yc3", [DEC, DMIX], F32)
        ycb, b_ycb = C.sb("ycb3", [DEC, DMIX], BF16)
        ur, b_ur = C.sb("ur3", [DEC, 2 * DFF], F32)
        fhc = [C.sb("fh3_%d" % i, [DEC, 2, 512], F32) for i in range(2)]
        fw, b_fw = C.sb("fw3", [DEC, 3, 512], F32)
        fb_, b_fb = C.sb("fb3", [DEC, 512], F32)
        uc, b_uc = C.sb("uc3", [DEC, 2 * DFF], F32)
        t5, b_t5 = C.sb("t53", [DEC, 512], F32)
        hb, b_hb = C.sb("hb3", [DEC, DFF], BF16)
        rowt = {nm: C.sb("rw_" + nm, [DEC, w_], F32) for nm, w_ in (("bpw1", 2 * D), ("bpw2", D), ("dwb", D), ("lng", D), ("lnb", D))}
        for nm, src in (("bpw1", bpw1_r), ("bpw2", bpw2_r), ("dwb", dwb_r), ("lng", lng_r), ("lnb", lnb_r)):
            S.dma("sync", rowt[nm][0][:], src.partition_broadcast(DEC), writes=[rowt[nm][1]])
        wrot = [0]
        nrot = [0]

        def to_T(dst, b_dst, src_bf, b_src, n):
            for c in range(n):
                pt, bpt = C.psum(PS_TR)
                ptb = pt[:].bitcast(BF16)
                T(lambda e: e.transpose(ptb[:, 0:DEC], src_bf[:, c * 128:(c + 1) * 128], identb[:DEC, :DEC]), [b_src, b_identb], [bpt])
                A(lambda e: e.copy(dst[:, c, :], ptb[:, 0:DEC]), [bpt], [b_dst])

        def rows_mm(W, kcn, actT, b_actT, ncols, dst, b_dst, c_off=0):
            for c0 in range(0, ncols, 512):
                cn = min(512, ncols - c0)
                wb, bwb = wbs[wrot[0] % 2]
                wrot[0] += 1
                S.dma("gpsimd", wb[:, :kcn, :cn], W[:, c_off + c0:c_off + c0 + cn].rearrange("(kc p) c -> p kc c", p=128), writes=[bwb])
                pt, bpt = C.psum(PS_MM)
                for kc in range(kcn):
                    T(lambda e, kc=kc: e.matmul(pt[:DEC, :cn], actT[:, kc, :], wb[:, kc, :cn], start=(kc == 0), stop=(kc == kcn - 1)),
                      [bwb, b_actT], [bpt])
                A(lambda e: e.copy(dst[:, c0:c0 + cn], pt[:DEC, :cn]), [bpt], [b_dst])

        def norm_T(gain):
            load_gain(gain)
            rms_rows(xb[:], hs[:], b_hs, b_xb, stt, b_stt, jk, b_jk)
            to_T(xT, b_xT, xb, b_xb, KC)

        def post_add(gain):
            load_gain(gain)
            rms_rows(mix[:], mix[:], b_mix, b_mix, stt, b_stt, jk, b_jk)
            V(lambda e: e.tensor_tensor(hs[:], hs[:], mix[:], ALU.add), [b_hs, b_mix], [b_hs])

        def conv_ffn(layer):
            norm_T(4 * layer + 2)
            rows_mm(w_up[layer], KC, xT, b_xT, 2 * DFF, ur, b_ur)
            S.dma("sync", ffn_s[layer, :, 2 * DFF:4 * DFF], ur[:], reads=[b_ur], sembuf=b_ur)
            for ci, c0 in enumerate(range(0, 2 * DFF, 512)):
                cn = min(512, 2 * DFF - c0)
                fh, b_fh = fhc[ci % 2]
                for k in range(2):
                    S.dma("sync", fh[:, k, :cn], st_ffn[layer, :, k * 2 * DFF + c0:k * 2 * DFF + c0 + cn], writes=[b_fh])
                for k in range(3):
                    S.dma("sync", fw[:, k, :cn], fcw_r[layer, k:k + 1, c0:c0 + cn].partition_broadcast(DEC), writes=[b_fw])
                S.dma("sync", fb_[:, :cn], fcb_r[layer, :, c0:c0 + cn].partition_broadcast(DEC), writes=[b_fb])
                V(lambda e: e.tensor_tensor(uc[:, c0:c0 + cn], ur[:, c0:c0 + cn], fw[:, 2, :cn], ALU.mult), [b_ur, b_fw], [b_uc])
                for k in range(2):
                    V(lambda e, k=k: e.tensor_tensor(t5[:, :cn], fh[:, k, :cn], fw[:, k, :cn], ALU.mult), [b_fh, b_fw], [b_t5])
                    V(lambda e: e.tensor_tensor(uc[:, c0:c0 + cn], uc[:, c0:c0 + cn], t5[:, :cn], ALU.add), [b_uc, b_t5], [b_uc])
                V(lambda e: e.tensor_tensor(uc[:, c0:c0 + cn], uc[:, c0:c0 + cn], fb_[:, :cn], ALU.add), [b_uc, b_fb], [b_uc])
                S.dma("sync", ffn_s[layer, :, c0:c0 + cn], fh[:, 1, :cn], reads=[b_fh], sembuf=b_fh)
            A(lambda e: e.activation(uc[:, 0:DFF], uc[:, 0:DFF], AF.Gelu_apprx_tanh), [b_uc], [b_uc])
            V(lambda e: e.tensor_tensor(hb[:], uc[:, 0:DFF], uc[:, DFF:2 * DFF], ALU.mult), [b_uc], [b_hb])
            to_T(aT, b_aT, hb, b_hb, DFF // 128)
            rows_mm(w_down[layer], DFF // 128, aT, b_aT, D, mix, b_mix)
            post_add(4 * layer + 3)

        S.dma("sync", hs[:], xs_in, writes=[b_hs])
        S.dma("sync", yc[:], ycs_d, reads=[bycs], writes=[b_yc])
        V(lambda e: e.tensor_copy(ycb[:], yc[:]), [b_yc], [b_ycb])
        to_T(aT, b_aT, ycb, b_ycb, DMIX // 128)
        rows_mm(w_out, DMIX // 128, aT, b_aT, D, mix, b_mix)
        post_add(1)
        conv_ffn(0)
        norm_T(4)
        u2, b_u2 = C.sb("u23", [DEC, 2 * D], F32)
        rows_mm(w_pw1, KC, xT, b_xT, 2 * D, u2, b_u2)
        V(lambda e: e.tensor_tensor(u2[:], u2[:], rowt["bpw1"][0][:], ALU.add), [b_u2, rowt["bpw1"][1]], [b_u2])
        A(lambda e: e.activation(u2[:, D:2 * D], u2[:, D:2 * D], AF.Sigmoid), [b_u2], [b_u2])
        V(lambda e: e.tensor_tensor(u2[:, 0:D], u2[:, 0:D], u2[:, D:2 * D], ALU.mult), [b_u2], [b_u2])
        S.dma("sync", arow_d, u2[:, 0:D], reads=[b_u2], writes=[barow], sembuf=b_u2)
        ah = [C.sb("ah3_%d" % i, [31, D], F32) for i in range(2)]
        dwt, b_dwt = C.sb("dwt3", [31, D], F32)
        selc, b_selc = C.sb("selc3", [31, DEC, DEC], F32)
        S.dma("sync", dwt[:], dww_r, writes=[b_dwt])
        S.dma("sync", selc[:], selc_d.rearrange("s k m -> k s m"), writes=[b_selc])
        pcs = [C.psum(PS_MM) for _ in range((D + 511) // 512)]
        for s_ in range(DEC):
            a_, ba_ = ah[s_ % 2]
            S.dma("sync", a_[0:30, :], st_conf[s_], writes=[ba_])
            S.dma("sync", a_[30:31, :], arow_d[s_:s_ + 1, :], reads=[barow], writes=[ba_])
            S.dma("sync", conf_s[s_], a_[1:31, :], reads=[ba_], sembuf=ba_)
            V(lambda e: e.tensor_tensor(a_[:], a_[:], dwt[:], ALU.mult), [ba_, b_dwt], [ba_])
            for ci, c0 in enumerate(range(0, D, 512)):
                cn = min(512, D - c0)
                pc, bpc = pcs[ci]
                T(lambda e: e.matmul(pc[:DEC, :cn], selc[:, s_, :], a_[:, c0:c0 + cn], start=(s_ == 0), stop=(s_ == DEC - 1)), [b_selc, ba_], [bpc])
        cv, b_cv = C.sb("cv3", [DEC, D], F32)
        for ci, c0 in enumerate(range(0, D, 512)):
            cn = min(512, D - c0)
            pc, bpc = pcs[ci]
            V(lambda e: e.tensor_tensor(cv[:, c0:c0 + cn], pc[:DEC, :cn], rowt["dwb"][0][:, c0:c0 + cn], ALU.add), [bpc, rowt["dwb"][1]], [b_cv])
        V(lambda e: e.reduce_sum(stt[R_, 0:1], cv[:], axis=AX.X), [b_cv], [b_stt])
        V(lambda e: e.tensor_scalar(stt[R_, 1:2], stt[R_, 0:1], -1.0 / D, None, op0=ALU.mult), [b_stt], [b_stt])
        V(lambda e: e.tensor_scalar(cv[:], cv[:], stt[R_, 1:2], None, op0=ALU.add), [b_cv, b_stt], [b_cv])
        A(lambda e: e.activation(jk[:, :D], cv[:], AF.Square, accum_out=stt[R_, 2:3]), [b_cv], [b_jk, b_stt])
        A(lambda e: e.activation(stt[R_, 3:4], stt[R_, 2:3], AF.Sqrt, scale=1.0 / D, bias=EPS), [b_stt], [b_stt])
        V(lambda e: e.reciprocal(stt[R_, 4:5], stt[R_, 3:4]), [b_stt], [b_stt])
        V(lambda e: e.scalar_tensor_tensor(cv[:], cv[:], stt[R_, 4:5], rowt["lng"][0][:], op0=ALU.mult, op1=ALU.mult), [b_cv, b_stt, rowt["lng"][1]], [b_cv])
        V(lambda e: e.tensor_tensor(cv[:], cv[:], rowt["lnb"][0][:], ALU.add), [b_cv, rowt["lnb"][1]], [b_cv])
        A(lambda e: e.activation(xb[:], cv[:], AF.Silu), [b_cv], [b_xb])
        to_T(xT, b_xT, xb, b_xb, KC)
        rows_mm(w_pw2, KC, xT, b_xT, D, mix, b_mix)
        V(lambda e: e.tensor_tensor(mix[:], mix[:], rowt["bpw2"][0][:], ALU.add), [b_mix, rowt["bpw2"][1]], [b_mix])
        post_add(5)
        conv_ffn(1)
        S.dma("sync", y_s, hs[:], reads=[b_hs], sembuf=b_hs)


def _bf16():
    import ml_dtypes
    return ml_dtypes.bfloat16


def host_prep(cfg, inp, b, h):
    bf = _bf16()
    NT, D = cfg.NT, cfg.D
    m = {}
    x = np.zeros((NT * 128, D), np.float32)
    xp = inp["x_prompt"][b]
    if h == 1:
        x[:] = xp
    else:
        x[cfg.HALF:] = xp[:cfg.HALF]
    m["xin"] = x
    m["flag"] = np.full((128, 1), float(h), np.float32)
    m["normg"] = np.ascontiguousarray(inp["norm_g"].reshape(8, D))
    m["w_in"] = inp["ab_w_in"][0]
    NCT = cfg.CD // 128
    m["convwT"] = np.ascontiguousarray(inp["ab_conv_w"][0].T.reshape(NCT, 128, 4).transpose(1, 0, 2))
    m["convbT"] = np.ascontiguousarray(inp["ab_conv_b"][0].reshape(NCT, 128).T)
    m["dt_bias"] = np.ascontiguousarray(inp["ab_dt_bias"][0][None])
    m["a_log"] = np.ascontiguousarray(inp["ab_a_log"][0][None])
    m["d_skip"] = np.ascontiguousarray(inp["ab_d_skip"][0][None])
    m["ssm_norm"] = np.ascontiguousarray(inp["ab_ssm_norm"][0][None])
    pos = np.arange(NT * 128) + (h - 1) * cfg.HALF
    inv = (10000.0 ** (-np.arange(32, dtype=np.float32) / 32)).astype(np.float32)
    ang = pos.astype(np.float32)[:, None] * inv[None, :]
    m["rcos"] = np.cos(ang).astype(np.float32)
    m["rsin"] = np.sin(ang).astype(np.float32)
    m["identb"] = np.eye(128, dtype=np.float32).astype(bf)
    m["identf"] = np.eye(128, dtype=np.float32)
    m["utri"] = np.triu(np.ones((128, 128), np.float32))
    kk = np.arange(128)[:, None]
    ii = np.arange(512)[None, :] % 128
    m["negm"] = np.where(kk > ii, -30000.0, 0.0).astype(np.float32).astype(bf)
    w1 = inp["ab_cmp_w1"][0]
    w1r = w1.reshape(2, 32, 64, 128).transpose(0, 2, 1, 3)
    m["cmp_w1"] = np.ascontiguousarray(np.concatenate([w1r, w1r], 1).reshape(2, 128, 32 * 128))
    peT = inp["ab_cmp_pe"][0].transpose(0, 2, 1)
    m["cmp_peT"] = np.ascontiguousarray(np.concatenate([peT, peT], 1))
    m["cmp_b1T"] = np.ascontiguousarray(inp["ab_cmp_b1"][0].T)
    w2 = inp["ab_cmp_w2"][0]
    w2k = np.zeros((2, 128, 128), np.float32)
    w2k[0, :, 0:64] = w2[0]
    w2k[1, :, 64:128] = w2[0]
    m["cmp_w2k"] = w2k
    m["cmp_w2v"] = np.ascontiguousarray(w2[1])
    b2 = inp["ab_cmp_b2"][0]
    m["cmp_b2k"] = np.ascontiguousarray(np.concatenate([b2[0], b2[0]])[:, None].astype(np.float32))
    m["cmp_b2v"] = np.ascontiguousarray(b2[1][None].astype(np.float32))
    NBT = (cfg.NCMP + 127) // 128
    NSLC, TF0 = cfg.NSLC, cfg.TF0
    i_ = np.arange(NBT * 128)[:, None]
    j_ = np.arange(NSLC)[None, :]
    m["ovm"] = ((i_ * 16 < (j_ + 1) * 64) & (i_ * 16 + 32 > j_ * 64) & (i_ < cfg.NCMP)).astype(np.float32)
    base = (h - 1) * cfg.HALF
    lpq = np.arange(TF0 * 128, NT * 128)
    blk = np.arange(NBT * 128)
    valid_blk = (blk < cfg.NCMP) & (16 * blk + base >= 0)
    mc = ((16 * blk[:, None] + 31) <= lpq[None, :]) & valid_blk[:, None]
    m["maskc"] = np.ascontiguousarray(mc.reshape(NBT, 128, -1).astype(np.float32).astype(bf))
    q_abs = lpq + base
    j_abs = np.arange(NSLC) * 64 + base
    validj = (j_abs[None, :] <= q_abs[:, None]) & (j_abs[None, :] >= 0)
    qblk = q_abs // 64
    jb = j_abs // 64
    forced = (jb[None, :] == 0) | (jb[None, :] == qblk[:, None]) | (jb[None, :] == qblk[:, None] - 1)
    m["fbias"] = np.where(validj, np.where(forced, 1e4, 0.0), -1e30).astype(np.float32)
    m["exm"] = (np.arange(NT * 128)[None, :] // 64 == np.arange(NSLC)[:, None]).astype(np.float32).astype(bf)
    qq = np.arange(128)[None, :]
    m["causT"] = (kk <= qq).astype(np.float32).astype(bf)
    m["bandT"] = (kk > qq).astype(np.float32).astype(bf)
    m["w_out"] = inp["ab_w_out"][0]
    m["w_up"] = inp["ffn_w_up"]
    m["w_down"] = inp["ffn_w_down"]
    NFT = 2 * cfg.DFF // 128
    KC = cfg.KC
    m["ffn_cwT"] = np.ascontiguousarray(inp["ffn_conv_w"].transpose(0, 2, 1).reshape(2, NFT, 128, 3).transpose(0, 2, 1, 3))
    m["ffn_cbT"] = np.ascontiguousarray(inp["ffn_conv_b"].reshape(2, NFT, 128).transpose(0, 2, 1))
    m["w_pw1"] = inp["c_w_pw1"][0]
    m["b_pw1T"] = np.ascontiguousarray(inp["c_b_pw1"][0].reshape(2 * KC, 128).T)
    m["dw_wT"] = np.ascontiguousarray(inp["c_dw_w"][0].T.reshape(KC, 128, 31).transpose(1, 0, 2))
    m["dw_bT"] = np.ascontiguousarray(inp["c_dw_b"][0].reshape(KC, 128).T)
    m["ln_gT"] = np.ascontiguousarray(inp["c_ln_g"][0].reshape(KC, 128).T)
    m["ln_bT"] = np.ascontiguousarray(inp["c_ln_b"][0].reshape(KC, 128).T)
    m["w_pw2"] = inp["c_w_pw2"][0]
    m["b_pw2"] = np.ascontiguousarray(inp["c_b_pw2"][0][None])
    c = 2 * b + h
    DEC = cfg.DEC
    sq = slice(c * DEC, (c + 1) * DEC)
    m["xs_in"] = np.ascontiguousarray(inp["x_sample"][sq, 0])
    m["st_ssm"] = np.ascontiguousarray(inp["state_ssm"][0, sq].reshape(DEC, cfg.DI, 128))
    m["st_sconv"] = np.ascontiguousarray(inp["state_ssm_conv"][0, sq].reshape(DEC, -1))
    m["c_cmp"] = inp["cache_cmp"][0].reshape(-1, 256)
    m["c_slc"] = inp["cache_slc"][0].reshape(-1, 256)
    m["c_win"] = np.ascontiguousarray(inp["cache_win"][0, sq].reshape(DEC, -1, 256))
    m["ptab"] = np.ascontiguousarray(inp["page_table"][sq].astype(np.int32))
    m["st_conf"] = np.ascontiguousarray(inp["state_conf_conv"][0, sq])
    m["st_ffn"] = np.ascontiguousarray(inp["state_ffn_conv"][:, sq].reshape(2, DEC, -1))
    m["convw_r"] = inp["ab_conv_w"][0]
    m["convb_r"] = np.ascontiguousarray(inp["ab_conv_b"][0][None])
    m["fcw_r"] = inp["ffn_conv_w"]
    m["fcb_r"] = np.ascontiguousarray(inp["ffn_conv_b"][:, None, :])
    m["dww_r"] = inp["c_dw_w"][0]
    m["dwb_r"] = np.ascontiguousarray(inp["c_dw_b"][0][None])
    m["lng_r"] = np.ascontiguousarray(inp["c_ln_g"][0][None])
    m["lnb_r"] = np.ascontiguousarray(inp["c_ln_b"][0][None])
    m["bpw1_r"] = np.ascontiguousarray(inp["c_b_pw1"][0][None])
    PAST = cfg.PAST
    angs = np.float32(PAST) * inv
    m["rope_s"] = np.stack([np.cos(angs), np.sin(angs)]).astype(np.float32)
    selc = np.zeros((DEC, 31, DEC), np.float32)
    for s_ in range(DEC):
        selc[s_, :, s_] = 1.0
    m["selc"] = selc
    NCS = PAST // 16 - 1
    NBS = (NCS + 127) // 128
    NSL = PAST // 64
    i_ = np.arange(NBS * 128)[:, None]
    j_ = np.arange(NSL + 1)[None, :]
    m["ovs"] = ((i_ * 16 < (j_ + 1) * 64) & (i_ * 16 + 32 > j_ * 64) & (i_ < NCS)).astype(np.float32)
    jj = np.arange(NSL + 1)
    forced_s = (jj == 0) | (jj == NSL) | (jj == NSL - 1)
    m["fbs"] = np.where(forced_s, 1e4, 0.0).astype(np.float32)[None]
    m["exms"] = (np.arange(PAST)[None, :] // 64 == np.arange(NSL)[:, None]).astype(np.float32).astype(bf)
    m["iota_p"] = np.ascontiguousarray(np.broadcast_to(np.arange(128, dtype=np.float32)[:, None], (128, PAST // 128)))
    m["maskcs"] = np.ascontiguousarray((np.arange(NBS * 128) < NCS).astype(np.float32).reshape(NBS, 128).T)
    return m


OUT_SPECS = None


def assemble(cfg, res, nb):
    D, DI, HS, CD, DFF, DEC = cfg.D, cfg.DI, cfg.HS, cfg.CD, cfg.DFF, cfg.DEC
    f = np.float32
    cat = lambda k, cs: np.concatenate([np.asarray(res[c][k], f) for c in cs], 0)
    y_p = np.stack([cat("y_p", (2 * b, 2 * b + 1)) for b in range(nb)])
    allc = range(2 * nb)
    y_s = cat("y_s", allc)[:, None, :]
    ssm_p = np.stack([np.asarray(res[2 * b + 1]["ssm_p"], f).reshape(HS, 64, 128) for b in range(nb)])[None]
    ssm_s = cat("ssm_s", allc).reshape(-1, HS, 64, 128)[None]
    sconv_p = np.stack([np.asarray(res[2 * b + 1]["sconv_p"], f) for b in range(nb)])[None]
    sconv_s = cat("sconv_s", allc).reshape(-1, 3, CD)[None]
    kvs = cat("kv_s", allc)
    outs = [y_p, y_s, ssm_p, ssm_s, sconv_p, sconv_s]
    for bi, nm in enumerate(("cmp_p", "slc_p")):
        p_ = np.stack([cat(nm, (2 * b, 2 * b + 1)) for b in range(nb)]).reshape(nb, -1, 2, 2, 64)[None]
        s_ = kvs[:, bi * 256:(bi + 1) * 256].reshape(-1, 1, 2, 2, 64)[None]
        outs += [p_, s_]
    win_p = np.stack([np.asarray(res[2 * b + 1]["win_p"], f) for b in range(nb)]).reshape(nb, -1, 2, 2, 64)[None]
    win_s = kvs[:, 512:768].reshape(-1, 1, 2, 2, 64)[None]
    outs += [win_p, win_s]
    conf_p = np.stack([np.asarray(res[2 * b + 1]["conf_p"], f) for b in range(nb)])[None]
    conf_s = cat("conf_s", allc)[None]
    ffn_p = np.stack([np.asarray(res[2 * b + 1]["ffn_p"], f) for b in range(nb)], 1)
    ffn_s = np.concatenate([np.asarray(res[c]["ffn_s"], f) for c in allc], 1).reshape(2, -1, 2, 2 * DFF)
    outs += [conf_p, conf_s, ffn_p, ffn_s]
    return tuple(np.ascontiguousarray(o, dtype=np.float32) for o in outs)


_NC_CACHE = {}


def kernel(**inputs):
    cfg = FULL
    inp = {k: np.asarray(v) for k, v in inputs.items()}
    if "nc" not in _NC_CACHE:
        _NC_CACHE["nc"] = build(cfg)
    nc = _NC_CACHE["nc"]
    maps = [host_prep(cfg, inp, c // 2, c % 2) for c in range(8)]
    res = run_bass_kernel_spmd(nc, maps, core_ids=list(range(8)))
    return assemble(cfg, res.results, 4)
```

```python
import numpy as np
from contextlib import ExitStack
import concourse.bass as bass
import concourse.mybir as mybir
from concourse.bass_utils import run_bass_kernel_spmd

F32 = mybir.dt.float32
BF16 = mybir.dt.bfloat16
I32 = mybir.dt.int32
AF = mybir.ActivationFunctionType
ALU = mybir.AluOpType
AX = mybir.AxisListType
EPS = 1e-6


class Buf:
    __slots__ = ("name", "writer", "readers", "dsem", "dcount")

    def __init__(self, name):
        self.name = name
        self.writer = None
        self.readers = {}
        self.dsem = None
        self.dcount = None


class Sched:
    ENG = ("tensor", "vector", "scalar", "gpsimd", "sync")

    def __init__(self, nc, stack):
        self.nc = nc
        self.stack = stack
        self.eng = {"tensor": nc.tensor, "vector": nc.vector, "scalar": nc.scalar,
                    "gpsimd": nc.gpsimd, "sync": nc.sync}
        self.sem = {}
        self.cnt = {}
        self.semobj = {}
        for e in self.ENG:
            self.sem[e] = stack.enter_context(nc.semaphore("s_" + e))
            self.semobj[e] = self.sem[e]
            self.cnt[e] = 0
        self.waited = {e: {} for e in self.ENG}
        self.ndsem = 0
        self.n_wait = 0
        self.n_ops = 0
        self.dcnt = {}
        self.free_sems = {False: [], True: []}
        self.phase_sems = []

    def _dsem(self, buf, sw=False):
        attr = "dcount" if sw else "dsem"
        if not getattr(buf, attr):
            free = self.free_sems[sw]
            if free:
                name = free.pop(0)
            else:
                name = "d%d" % self.ndsem
                self.ndsem += 1
                self.semobj[name] = self.stack.enter_context(self.nc.semaphore(name))
            self.phase_sems.append((sw, name))
            setattr(buf, attr, name)
        return getattr(buf, attr)

    def recycle(self):
        for sw, name in self.phase_sems:
            self.free_sems[sw].append(name)
        self.phase_sems = []

    def share_sem(self, bufs):
        name = self._dsem(bufs[0])
        for b in bufs[1:]:
            b.dsem = name

    def _wait(self, e, tok):
        if tok is None:
            return
        key, val = tok
        if key in self.dcnt:
            val = self.dcnt[key]
        if self.waited[e].get(key, 0) >= val:
            return
        if key == e:
            if e == "tensor" or val <= self.cnt[e] - 2:
                return
        self.waited[e][key] = val
        self.eng[e].wait_ge(self.semobj[key], val)
        self.n_wait += 1

    def _deps(self, e, reads, writes):
        for b in reads:
            self._wait(e, b.writer)
        for b in writes:
            self._wait(e, b.writer)
            for t in list(b.readers.items()):
                self._wait(e, t)

    def op(self, e, fn, reads=(), writes=()):
        self._deps(e, reads, writes)
        ins = fn(self.eng[e])
        self.cnt[e] += 1
        self.n_ops += 1
        ins.then_inc(self.sem[e], 1)
        tok = (e, self.cnt[e])
        for b in reads:
            b.readers[e] = self.cnt[e]
        for b in writes:
            b.writer = tok
            b.readers = {}
        return ins

    def dma(self, q, out, in_, reads=(), writes=(), sembuf=None, **kw):
        self._deps(q, reads, writes)
        sb = sembuf if sembuf is not None else (writes[0] if writes else reads[0])
        name = self._dsem(sb, sw=(q == "gpsimd"))
        ins = self.eng[q].dma_start(out=out, in_=in_, **kw)
        self.n_ops += 1
        self.dcnt[name] = self.dcnt.get(name, 0) + 16
        ins.then_inc(self.semobj[name], 16)
        tok = (name, self.dcnt[name])
        for b in reads:
            b.readers[name] = self.dcnt[name]
        for b in writes:
            b.writer = tok
            b.readers = {}
        return ins

    def dma_custom(self, q, fn, reads=(), writes=(), sembuf=None):
        self._deps(q, reads, writes)
        sb = sembuf if sembuf is not None else (writes[0] if writes else reads[0])
        name = self._dsem(sb, sw=(q == "gpsimd"))
        ins = fn(self.eng[q])
        self.n_ops += 1
        self.dcnt[name] = self.dcnt.get(name, 0) + 16
        ins.then_inc(self.semobj[name], 16)
        tok = (name, self.dcnt[name])
        for b in reads:
            b.readers[name] = self.dcnt[name]
        for b in writes:
            b.writer = tok
            b.readers = {}
        return ins

    def barrier(self):
        snap = dict(self.cnt)
        dsn = dict(self.dcnt)
        for e in self.ENG:
            for f in self.ENG:
                if snap[f] > 0 and self.waited[e].get(f, 0) < snap[f]:
                    self.waited[e][f] = snap[f]
                    self.eng[e].wait_ge(self.semobj[f], snap[f])
                    self.n_wait += 1
            for name, val in dsn.items():
                self._wait(e, (name, val))

    def finish(self):
        for name, val in self.dcnt.items():
            self._wait("sync", (name, val))
        for e in self.ENG:
            if e != "sync" and self.cnt[e] > 0:
                self._wait("sync", (e, self.cnt[e]))


class Cfg:
    def __init__(self, D=1024, SEQ=4096, NH=16, TOPN=16, WINDOW=512, DFF=2816, DEC_PER_CORE=4,
                 PAST=8192, NPHYS=2560):
        self.D = D
        self.KC = D // 128
        self.SEQ = SEQ
        self.HALF = SEQ // 2
        self.NT = SEQ // 128
        self.NTH = self.NT // 2
        self.TF0 = self.NTH - 1
        self.DI = 2 * D
        self.HS = self.DI // 64
        self.G = 4
        self.HPGS = self.HS // 4
        self.NS = 128
        self.CD = self.DI + 2 * 4 * 128
        self.NH = NH
        self.KV = 2
        self.HPG = NH // 2
        self.DH = 64
        self.TOPN = TOPN
        self.WINDOW = WINDOW
        self.DFF = DFF
        self.DPROJ = self.DI + self.CD + self.HS + NH * 64 + 3 * 256 + 3 * NH
        self.DMIX = self.DI + NH * 64
        self.o_z = 0
        self.o_xbc = self.DI
        self.o_dt = self.DI + self.CD
        self.o_q = self.o_dt + self.HS
        self.o_kv = self.o_q + NH * 64
        self.o_g = self.o_kv + 768
        self.NSLC = SEQ // 64
        self.NCMP = SEQ // 16 - 1
        self.DEC = DEC_PER_CORE
        self.PAST = PAST
        self.NPHYS = NPHYS
        self.groups = []
        t = 0
        while t < self.TF0:
            n = min(5, self.TF0 - t)
            self.groups.append((list(range(t, t + n)), False))
            t += n
        first = True
        while t < self.NT:
            n = min(5 if first else 4, self.NT - t)
            self.groups.append((list(range(t, t + n)), True))
            t += n
            first = False


FULL = Cfg()


class Ctx:
    def __init__(self, nc, st, cfg):
        self.nc, self.st, self.cfg = nc, st, cfg
        self.S = Sched(nc, st)
        self.sb_bytes = 0
        self.ps = []
        for i in range(8):
            t = st.enter_context(nc.psum_tensor("ps%d" % i, [128, 512], F32))
            self.ps.append((t, Buf("ps%d" % i)))
        self.ps_rr = {}

    def sb(self, name, shape, dt=F32):
        t = self.st.enter_context(self.nc.sbuf_tensor("sb_" + name, list(shape), dt))
        n = 1
        for s in shape[1:]:
            n *= s
        self.sb_bytes += n * (4 if dt in (F32, I32) else 2)
        return t, Buf(name)

    def psum(self, pool):
        k = self.ps_rr.get(pool, 0)
        self.ps_rr[pool] = k + 1
        return self.ps[pool[k % len(pool)]]

    def dram(self, name, shape, dt=F32, kind="Internal"):
        return self.nc.dram_tensor(name, list(shape), dt, kind=kind).ap()


PS_MM = (0, 1, 2, 3)
PS_TR = (4, 5)
PS_X = (6, 7)


def build(cfg, dbg=()):
    nc = bass.Bass("TRN2", target_bir_lowering=False)
    st = ExitStack()
    with st:
        C = Ctx(nc, st, cfg)
        _emit(C, cfg, dbg)
    return nc


def _emit(C, cfg, dbg):
    nc, S = C.nc, C.S
    D, KC, NT, DI, HS, CD, NH, HPG = cfg.D, cfg.KC, cfg.NT, cfg.DI, cfg.HS, cfg.CD, cfg.NH, cfg.HPG
    HPGS = cfg.HPGS
    NCT = CD // 128
    NXT = DI // 128
    TF0, NTH = cfg.TF0, cfg.NTH
    GW = HPGS * 64
    NSLC, NBT = cfg.NSLC, (cfg.NCMP + 127) // 128
    NF = NT - TF0
    WT = cfg.WINDOW // 128
    CW = 65 + NSLC
    DMIX, DFF = cfg.DMIX, cfg.DFF
    NFT = 2 * DFF // 128
    NHT = DFF // 128

    def V(fn, r=(), w=()):
        return S.op("vector", fn, r, w)

    def A(fn, r=(), w=()):
        return S.op("scalar", fn, r, w)

    def P(fn, r=(), w=()):
        return S.op("gpsimd", fn, r, w)

    def T(fn, r=(), w=()):
        return S.op("tensor", fn, r, w)

    def bc3(ap2, n):
        return ap2.unsqueeze(2).to_broadcast([128, ap2.shape[1], n])

    din = lambda n, s, dt=F32: nc.dram_tensor(n, list(s), dt, kind="ExternalInput").ap()
    dout = lambda n, s, dt=F32: nc.dram_tensor(n, list(s), dt, kind="ExternalOutput").ap()
    dscr = lambda n, s, dt=F32: nc.dram_tensor(n, list(s), dt, kind="Internal").ap()
    xin = din("xin", [NT * 128, D])
    flag_d = din("flag", [128, 1])
    normg = din("normg", [8, D])
    w_in = din("w_in", [D, cfg.DPROJ])
    convw_d = din("convwT", [128, NCT, 4])
    convb_d = din("convbT", [128, NCT])
    dtb_d = din("dt_bias", [1, HS])
    alog_d = din("a_log", [1, HS])
    dskip_d = din("d_skip", [1, HS])
    ssmn_d = din("ssm_norm", [1, DI])
    cos_d = din("rcos", [NT * 128, 32])
    sin_d = din("rsin", [NT * 128, 32])
    idb_d = din("identb", [128, 128], BF16)
    idf_d = din("identf", [128, 128])
    utri_d = din("utri", [128, 128])
    negm_d = din("negm", [128, 512], BF16)
    cw1_d = din("cmp_w1", [2, 128, 32 * 128])
    cpe_d = din("cmp_peT", [2, 128, 32])
    cb1_d = din("cmp_b1T", [128, 2])
    cw2k_d = din("cmp_w2k", [2, 128, 128])
    cw2v_d = din("cmp_w2v", [128, 64])
    cb2k_d = din("cmp_b2k", [128, 1])
    cb2v_d = din("cmp_b2v", [1, 64])
    ovm_d = din("ovm", [NBT * 128, NSLC])
    maskc_d = din("maskc", [NBT, 128, NF * 128], BF16)
    fbias_d = din("fbias", [NF * 128, NSLC])
    exm_d = din("exm", [NSLC, NT * 128], BF16)
    caus_d = din("causT", [128, 128], BF16)
    band_d = din("bandT", [128, 128], BF16)

    w_out = din("w_out", [DMIX, D])
    w_up = din("w_up", [2, D, 2 * DFF])
    w_down = din("w_down", [2, DFF, D])
    fcw_d = din("ffn_cwT", [2, 128, NFT, 3])
    fcb_d = din("ffn_cbT", [2, 128, NFT])
    w_pw1 = din("w_pw1", [D, 2 * D])
    bpw1_d = din("b_pw1T", [128, 2 * KC])
    dww_d = din("dw_wT", [128, KC, 31])
    dwb_d = din("dw_bT", [128, KC])
    lng_d = din("ln_gT", [128, KC])
    lnb_d = din("ln_bT", [128, KC])
    w_pw2 = din("w_pw2", [D, D])
    bpw2_d = din("b_pw2", [1, D])
    conf_p = dout("conf_p", [30, D])
    ffn_p = dout("ffn_p", [2, 2, 2 * DFF])
    y_p = dout("y_p", [cfg.HALF, D])
    ssm_p = dout("ssm_p", [DI, 128])
    sconv_p = dout("sconv_p", [3, CD])
    cmp_p = dout("cmp_p", [cfg.HALF, 256])
    slc_p = dout("slc_p", [cfg.HALF, 256])
    NWT = min(cfg.WINDOW, cfg.HALF) // 128
    win_p = dout("win_p", [NWT * 128, 256])
    dbg_o = {}
    for name, shape in dbg:
        dbg_o[name] = dout(name, shape)

    XS_d = dscr("XS_d", [NT * 128, DI], BF16)
    BTm_d = dscr("BTm_d", [NT * 128, 512], BF16)
    BT_d = dscr("BT_d", [4, 128, NT * 128], BF16)
    CT_d = dscr("CT_d", [4, 128, NT * 128], BF16)
    dtr_d = dscr("dtr_d", [NT * 128, HS])
    zs_d = dscr("zs_d", [NF * 128, DI], BF16)
    qb_d = dscr("qb_d", [NF * 128, NH * 64], BF16)
    gs_d = dscr("gs_d", [NF * 128, 3 * NH])
    kT_d = dscr("kT_d", [3, 128, NT * 128], BF16)
    vT_d = dscr("vT_d", [128, NT * 128], BF16)
    v_d = dscr("v_d", [2, NT * 128, 128], BF16)
    ycat_d = dscr("ycat_d", [NF * 128, DMIX], BF16)
    bXS_d, bBTm_d, bBT_d, bCT_d, bdtr_d, bzs_d, bqb_d, bgs_d, bkT_d, bvT_d, bv_d, bycat_d = [Buf("scr%d" % i) for i in range(12)]

    identb, b_identb = C.sb("identb", [128, 128], BF16)
    identf, b_identf = C.sb("identf", [128, 128], F32)
    onesf, b_onesf = C.sb("onesf", [128, 128], F32)
    flag, b_flag = C.sb("flag", [128, 1], F32)
    gtab, b_gtab = C.sb("gtab", [128, D], F32)
    bconst = Buf("consts")
    for (dst, src) in ((identb, idb_d), (identf, idf_d), (flag, flag_d)):
        S.dma("sync", dst[:], src, writes=[bconst])
    b_identb = b_identf = b_flag = bconst
    P(lambda e: e.memset(onesf[:], 1.0), [], [b_onesf])

    def load_gain(j):
        S.dma("sync", gtab[:], normg[j:j + 1, :].partition_broadcast(128), writes=[b_gtab])

    class Phase:
        def __enter__(self):
            self.old = C.st
            self.es = ExitStack()
            self.es.__enter__()
            C.st = self.es
            self.base = C.sb_bytes
            return self

        def __exit__(self, *a):
            S.barrier()
            S.recycle()
            print("phase sbuf KB", (C.sb_bytes - self.base) / 1024, "ops", S.n_ops, "sems", S.ndsem)
            C.sb_bytes = self.base
            C.st = self.old
            return self.es.__exit__(*a)

    def transpose_to(dst_fn, src_fn, n, b_src, b_dst, eng="scalar"):
        for c0 in range(0, n, 8):
            cnt = min(8, n - c0)
            pt, bpt = C.psum(PS_TR)
            ptb = pt[:].bitcast(BF16)
            for c in range(cnt):
                T(lambda e, c=c: e.transpose(ptb[:, c * 128:(c + 1) * 128], src_fn(c0 + c), identb[:]),
                  [b_src, b_identb], [bpt])
            src = ptb[:, 0:cnt * 128].rearrange("p (c t) -> p c t", c=cnt)
            if eng == "scalar":
                A(lambda e: e.copy(dst_fn(c0, cnt), src), [bpt], [b_dst])
            else:
                V(lambda e: e.tensor_copy(dst_fn(c0, cnt), src), [bpt], [b_dst])

    GA = 8
    groupsA = []
    t = 0
    while t < TF0:
        n = min(GA, TF0 - t)
        groupsA.append((list(range(t, t + n)), False))
        t += n
    while t < NT:
        n = min(GA, NT - t)
        if NT - t - n == 1:
            n += 1
        groupsA.append((list(range(t, t + n)), True))
        t += n
    GAM = max(len(g[0]) for g in groupsA)
    with Phase():
        convw, b_convw = C.sb("convw", [128, NCT, 4], F32)
        convb, b_convb = C.sb("convb", [128, NCT], F32)
        for (dst, src) in ((convw, convw_d), (convb, convb_d)):
            S.dma("sync", dst[:], src, writes=[bconst])
        b_convw = b_convb = bconst
        xt = [C.sb("xt%d" % i, [128, D], F32) for i in range(2)]
        junk, b_junk = C.sb("junk", [128, D], F32)
        xnb = [C.sb("xnb%d" % i, [128, D], BF16) for i in range(2)]
        stat, b_stat = C.sb("stat", [128, 16], F32)
        xnT, b_xnT = C.sb("xnT", [128, KC, GAM * 128], BF16)
        wbf = [C.sb("wbf%d" % i, [128, 8, 512], BF16) for i in range(3)]
        wrot = [0]
        cb = [C.sb("cb%d" % i, [128, 3 + GAM * 128], F32) for i in range(2)]
        acc = [C.sb("acc%d" % i, [128, GAM * 128], F32) for i in range(2)]
        hist, b_hist = C.sb("hist", [128, NCT, 3], F32)
        xcT = [C.sb("xcT%d" % i, [128, GAM * 128], BF16) for i in range(3)]
        xst = [C.sb("xst%d" % i, [128, GAM, 512], BF16) for i in range(2)]
        rcos, b_rcos = C.sb("rcos", [128, GAM, 32], F32)
        rsin, b_rsin = C.sb("rsin", [128, GAM, 32], F32)
        kvo = [C.sb("kvo%d" % i, [128, 256], F32) for i in range(3)]
        kvb = [C.sb("kvb%d" % i, [128, 256], BF16) for i in range(3)]
        kTs = [C.sb("kTs%d" % i, [128, 256], BF16) for i in range(3)]
        rtmp = [C.sb("rtmp%d" % i, [128, 512], F32) for i in range(2)]
        zst = [C.sb("zst%d" % i, [128, 512], BF16) for i in range(3)]
        dst4 = [C.sb("dst4%d" % i, [128, max(HS, 3 * NH)], F32) for i in range(3)]
        V(lambda e: e.memset(hist[:], 0.0), [], [b_hist])

        def load_w(W, r0, kcn, c0, ncols):
            k = wrot[0] % 3
            wrot[0] += 1
            wb, bwb = wbf[k]
            src = W[r0 * 128:(r0 + kcn) * 128, c0:c0 + ncols].rearrange("(kc p) c -> p kc c", p=128)
            S.dma("gpsimd", wb[:, :kcn, :ncols], src, writes=[bwb])
            return wb, bwb

        def rms_to_T(src_ap, b_src, i, k, dT, b_dT):
            xb, bxb = xnb[k % 2]
            A(lambda e: e.activation(junk[:, :D], src_ap, AF.Square, accum_out=stat[:, 0:1]), [b_src], [b_junk, b_stat])
            A(lambda e: e.activation(stat[:, 1:2], stat[:, 0:1], AF.Sqrt, scale=1.0 / D, bias=EPS), [b_stat], [b_stat])
            V(lambda e: e.reciprocal(stat[:, 2:3], stat[:, 1:2]), [b_stat], [b_stat])
            V(lambda e: e.scalar_tensor_tensor(xb[:], src_ap, stat[:, 2:3], gtab[:], op0=ALU.mult, op1=ALU.mult),
              [b_src, b_stat, b_gtab], [bxb])
            transpose_to(lambda c0, cnt: dT[:, c0:c0 + cnt, i * 128:(i + 1) * 128],
                         lambda c: xb[:, c * 128:(c + 1) * 128], KC, bxb, b_dT)

        def rope(dst, src, i, nh, b_src, b_dst):
            s3 = src.rearrange("p (g d) -> p g d", g=nh)
            d3 = dst.rearrange("p (g d) -> p g d", g=nh)
            x1, x2 = s3[:, :, 0:32], s3[:, :, 32:64]
            o1, o2 = d3[:, :, 0:32], d3[:, :, 32:64]
            cs = rcos[:, i, :].unsqueeze(1).to_broadcast([128, nh, 32])
            sn = rsin[:, i, :].unsqueeze(1).to_broadcast([128, nh, 32])
            ta = rtmp[0][0][:, 0:nh * 32].rearrange("p (g d) -> p g d", g=nh)
            tb = rtmp[1][0][:, 0:nh * 32].rearrange("p (g d) -> p g d", g=nh)
            bta, btb = rtmp[0][1], rtmp[1][1]
            V(lambda e: e.tensor_tensor(ta, x1, cs, ALU.mult), [b_src, b_rcos], [bta])
            V(lambda e: e.tensor_tensor(tb, x2, sn, ALU.mult), [b_src, b_rsin], [btb])
            V(lambda e: e.tensor_tensor(o1, ta, tb, ALU.subtract), [bta, btb], [b_dst])
            V(lambda e: e.tensor_tensor(ta, x2, cs, ALU.mult), [b_src, b_rcos], [bta])
            V(lambda e: e.tensor_tensor(tb, x1, sn, ALU.mult), [b_src, b_rsin], [btb])
            V(lambda e: e.tensor_tensor(o2, ta, tb, ALU.add), [bta, btb], [b_dst])

        load_gain(0)
        rr = [0]
        for gi, (tiles, full) in enumerate(groupsA):
            ng = len(tiles)
            NG = ng * 128
            t0 = tiles[0]
            chunks = [(c, min(512, NG - c)) for c in range(0, NG, 512)]
            for i, t in enumerate(tiles):
                src, bsrc = xt[i % 2][0][:], xt[i % 2][1]
                S.dma("sync", src, xin[t * 128:(t + 1) * 128, :], writes=[bsrc])
                rms_to_T(src, bsrc, i, i, xnT, b_xnT)
            S.dma("sync", rcos[:, :ng, :], cos_d[t0 * 128:(t0 + ng) * 128, :].rearrange("(i p) c -> p i c", p=128), writes=[b_rcos])
            S.dma("sync", rsin[:, :ng, :], sin_d[t0 * 128:(t0 + ng) * 128, :].rearrange("(i p) c -> p i c", p=128), writes=[b_rsin])
            for mb in range(0, NCT, 4):
                nm = min(4, NCT - mb)
                wb, bwb = load_w(w_in, 0, KC, cfg.o_xbc + mb * 128, nm * 128)
                st_, bst_ = xst[(mb // 4) % 2]
                for j in range(nm):
                    m = mb + j
                    cbt, bcb = cb[m % 2]
                    act, bacc = acc[m % 2]
                    for (c0, cn) in chunks:
                        pt, bpt = C.psum(PS_MM)
                        for kc in range(KC):
                            T(lambda e, kc=kc: e.matmul(pt[:, :cn], wb[:, kc, j * 128:(j + 1) * 128], xnT[:, kc, c0:c0 + cn],
                                                        start=(kc == 0), stop=(kc == KC - 1)), [bwb, b_xnT], [bpt])
                        A(lambda e: e.copy(cbt[:, 3 + c0:3 + c0 + cn], pt[:, :cn]), [bpt], [bcb])
                    P(lambda e: e.tensor_copy(cbt[:, 0:3], hist[:, m, :]), [b_hist], [bcb])
                    P(lambda e: e.tensor_copy(hist[:, m, :], cbt[:, NG:NG + 3]), [bcb], [b_hist])
                    V(lambda e: e.tensor_scalar(act[:, :NG], cbt[:, 0:NG], convw[:, m, 0:1], None, op0=ALU.mult),
                      [bcb, b_convw], [bacc])
                    for k in range(1, 4):
                        V(lambda e, k=k: e.scalar_tensor_tensor(act[:, :NG], cbt[:, k:k + NG], convw[:, m, k:k + 1], act[:, :NG],
                                                                op0=ALU.mult, op1=ALU.add), [bcb, b_convw, bacc], [bacc])
                    xc, bxc = xcT[m % 3]
                    A(lambda e: e.activation(xc[:, :NG], act[:, :NG], AF.Silu, bias=convb[:, m:m + 1]), [bacc, b_convb], [bxc])
                    if m < NXT + 4:
                        transpose_to(lambda c0, cnt: st_[:, c0:c0 + cnt, j * 128:(j + 1) * 128],
                                     lambda c: xc[:, c * 128:(c + 1) * 128], ng, bxc, bst_, eng="vector")
                    if m >= NXT:
                        g = (m - NXT) % 4
                        dstd, bd = (BT_d, bBT_d) if m < NXT + 4 else (CT_d, bCT_d)
                        S.dma("sync", dstd[g, :, t0 * 128:t0 * 128 + NG], xc[:, :NG], reads=[bxc], writes=[bd], sembuf=bxc)
                if mb < NXT:
                    S.dma("sync", XS_d[t0 * 128:t0 * 128 + NG, mb * 128:(mb + nm) * 128].rearrange("(i p) c -> p i c", p=128),
                          st_[:, :ng, :nm * 128], reads=[bst_], writes=[bXS_d], sembuf=bst_)
                elif mb < NXT + 4:
                    S.dma("sync", BTm_d[t0 * 128:t0 * 128 + NG, :].rearrange("(i p) c -> p i c", p=128),
                          st_[:, :ng, :512], reads=[bst_], writes=[bBTm_d], sembuf=bst_)
            blocks = [("dt", cfg.o_dt, HS)]
            for br in range(3):
                blocks.append(("kv%d" % br, cfg.o_kv + br * 256, 256))
            if full:
                for c0 in range(0, DI, 512):
                    blocks.append(("z", c0, min(512, DI - c0)))
                for c0 in range(0, NH * 64, 512):
                    blocks.append(("q", cfg.o_q + c0, min(512, NH * 64 - c0)))
                blocks.append(("g", cfg.o_g, 3 * NH))
            for (kind, c0, cn) in blocks:
                wb, bwb = load_w(w_in, 0, KC, c0, cn)
                for i, t in enumerate(tiles):
                    rows = slice(t * 128, (t + 1) * 128)
                    frows = slice((t - TF0) * 128, (t - TF0 + 1) * 128)
                    pt, bpt = C.psum(PS_MM)
                    for kc in range(KC):
                        T(lambda e, kc=kc: e.matmul(pt[:, :cn], xnT[:, kc, i * 128:(i + 1) * 128], wb[:, kc, :cn],
                                                    start=(kc == 0), stop=(kc == KC - 1)), [bwb, b_xnT], [bpt])
                    k3 = rr[0] % 3
                    rr[0] += 1
                    if kind == "dt":
                        d4, bd4 = dst4[k3]
                        A(lambda e: e.copy(d4[:, :HS], pt[:, :cn]), [bpt], [bd4])
                        S.dma("sync", dtr_d[rows, :], d4[:, :HS], reads=[bd4], writes=[bdtr_d], sembuf=bd4)
                    elif kind == "g":
                        d4, bd4 = dst4[k3]
                        A(lambda e: e.activation(d4[:, :3 * NH], pt[:, :cn], AF.Sigmoid), [bpt], [bd4])
                        S.dma("sync", gs_d[frows, :], d4[:, :3 * NH], reads=[bd4], writes=[bgs_d], sembuf=bd4)
                    elif kind == "z":
                        z_, bz_ = zst[k3]
                        A(lambda e: e.activation(z_[:, :cn], pt[:, :cn], AF.Silu), [bpt], [bz_])
                        S.dma("sync", zs_d[frows, c0:c0 + cn], z_[:, :cn], reads=[bz_], writes=[bzs_d], sembuf=bz_)
                    elif kind == "q":
                        z_, bz_ = zst[k3]
                        rope(z_[:, :cn], pt[:, :cn], i, cn // 64, bpt, bz_)
                        qd4 = qb_d[frows, :].rearrange("t (hh g d) -> t hh g d", g=2, d=64)
                        hs0 = (c0 - cfg.o_q) // 64
                        for g in range(2):
                            lo, hi = max(hs0, g * HPG), min(hs0 + cn // 64, (g + 1) * HPG)
                            if hi > lo:
                                S.dma("sync", qd4[:, lo - g * HPG:hi - g * HPG, g, :],
                                      z_[:, (lo - hs0) * 64:(hi - hs0) * 64].rearrange("p (h d) -> p h d", d=64),
                                      reads=[bz_], writes=[bqb_d], sembuf=bz_)
                    else:
                        br = int(kind[2])
                        kv, bkv = kvo[k3]
                        kb, bkb = kvb[k3]
                        kT_, bkT_ = kTs[k3]
                        rope(kv[:, 0:128], pt[:, 0:128], i, 2, bpt, bkv)
                        A(lambda e: e.copy(kv[:, 128:256], pt[:, 128:256]), [bpt], [bkv])
                        dsto = (cmp_p, slc_p, win_p)[br]
                        if br < 2 and t >= NTH:
                            r = (t - NTH) * 128
                            S.dma("sync", dsto[r:r + 128, :], kv[:], reads=[bkv], sembuf=bkv)
                        if br == 2 and t >= NT - NWT:
                            r = (t - (NT - NWT)) * 128
                            S.dma("sync", dsto[r:r + 128, :], kv[:], reads=[bkv], sembuf=bkv)
                        V(lambda e: e.tensor_copy(kb[:], kv[:]), [bkv], [bkb])
                        ntr = 2 if br == 0 else 1
                        pt2, bpt2 = C.psum(PS_TR)
                        ptb2 = pt2[:].bitcast(BF16)
                        for j2 in range(ntr):
                            T(lambda e, j2=j2: e.transpose(ptb2[:, j2 * 128:(j2 + 1) * 128], kb[:, j2 * 128:(j2 + 1) * 128], identb[:]),
                              [bkb, b_identb], [bpt2])
                        A(lambda e: e.copy(kT_[:, :ntr * 128], ptb2[:, :ntr * 128]), [bpt2], [bkT_])
                        S.dma("sync", kT_d[br, :, rows], kT_[:, 0:128], reads=[bkT_], writes=[bkT_d], sembuf=bkT_)
                        if br == 0:
                            S.dma("sync", vT_d[:, rows], kT_[:, 128:256], reads=[bkT_], writes=[bvT_d], sembuf=bkT_)
                        else:
                            S.dma("sync", v_d[br - 1, rows, :], kb[:, 128:256], reads=[bkb], writes=[bv_d], sembuf=bkb)
        sc, b_sc = C.sb("sc", [3, CD], F32)
        for m in range(NCT):
            pt, bpt = C.psum(PS_MM)
            T(lambda e: e.transpose(pt[:3, :128], hist[:, m, :], identf[:]), [b_hist, b_identf], [bpt])
            V(lambda e: e.tensor_copy(sc[:, m * 128:(m + 1) * 128], pt[:3, :128]), [bpt], [b_sc])
        S.dma("sync", sconv_p, sc[:], reads=[b_sc], sembuf=b_sc)

    if getattr(cfg, "stop", None) == "A":
        S.finish()
        return
    with Phase():
        utri, b_utri = C.sb("utri", [128, 128], F32)
        negm, b_negm = C.sb("negm", [128, 512], BF16)
        dtb, b_dtb = C.sb("dtb", [128, HS], F32)
        atab, b_atab = C.sb("atab", [128, HS], F32)
        dskip, b_dskip = C.sb("dskip", [128, HS], F32)
        ssmn, b_ssmn = C.sb("ssmn", [128, DI], F32)
        KTs, b_KTs = C.sb("KTs", [128, NT * 128], BF16)
        Vs, b_Vs = C.sb("Vs", [128, NT, 2, 65], BF16)
        KTw, b_KTw = C.sb("KTw", [128, NT * 128], BF16)
        Vw, b_Vw = C.sb("Vw", [128, NT, 2, 65], BF16)
        hidT, b_hidT = C.sb("hidT", [128, 2, 2, NBT * 128], BF16)
        kcT, b_kcT = C.sb("kcT", [128, NBT * 128], BF16)
        vca, b_vca = C.sb("vca", [128, NBT, 2, CW], BF16)
        cpe, b_cpe = C.sb("cpe", [128, 2, 32], BF16)
        cb1, b_cb1 = C.sb("cmpb1", [128, 2], F32)
        b1e, b_b1e = C.sb("b1e", [128, 2], F32)
        cw2k, b_cw2k = C.sb("cw2k", [128, 2, 128], BF16)
        cw2v, b_cw2v = C.sb("cw2v", [128, 64], BF16)
        cb2k, b_cb2k = C.sb("cb2k", [128, 1], F32)
        cb2v, b_cb2v = C.sb("cb2v", [128, 64], F32)
        ovt, b_ovt = C.sb("ovt", [128, NBT, NSLC], F32)
        exm, b_exm = C.sb("exm", [NSLC, NT * 128], BF16)
        causT, b_causT = C.sb("causT", [128, 128], BF16)
        bandT, b_bandT = C.sb("bandT", [128, 128], BF16)
        onesb, b_onesb = C.sb("onesb", [128, 128], BF16)
        bc2 = Buf("constsM")
        V(lambda e: e.memset(onesb[:], 1.0), [], [bc2])
        for (dst, src) in ((utri, utri_d), (negm, negm_d), (dtb, dtb_d.partition_broadcast(128)),
                           (atab, alog_d.partition_broadcast(128)), (dskip, dskip_d.partition_broadcast(128)),
                           (ssmn, ssmn_d.partition_broadcast(128)), (cb1, cb1_d), (cb2k, cb2k_d),
                           (cb2v, cb2v_d.partition_broadcast(128)), (ovt, ovm_d.rearrange("(b p) j -> p b j", p=128)),
                           (exm, exm_d), (causT, caus_d), (bandT, band_d),
                           (KTs, kT_d[1]), (KTw, kT_d[2])):
            S.dma("sync", dst[:], src, reads=[bkT_d], writes=[bc2])
        for (dst, src) in ((cpe, cpe_d.rearrange("k p l -> p k l")), (cw2k, cw2k_d.rearrange("g c m -> c g m")), (cw2v, cw2v_d)):
            S.dma("gpsimd", dst[:], src, writes=[bc2])
        b_utri = b_negm = b_dtb = b_atab = b_dskip = b_ssmn = b_cb1 = b_cb2k = b_cb2v = b_ovt = b_exm = b_causT = b_bandT = bc2
        b_KTs = b_KTw = b_cpe = b_cw2k = b_cw2v = bc2
        A(lambda e: e.activation(atab[:], atab[:], AF.Exp), [bc2], [bc2])
        V(lambda e: e.tensor_scalar(atab[:], atab[:], -1.0, None, op0=ALU.mult), [bc2], [bc2])
        for br, (Vx, b_Vx) in enumerate(((Vs, b_Vs), (Vw, b_Vw))):
            for g in range(2):
                S.dma("sync", Vx[:, :, g, 0:64], v_d[br, :, g * 64:(g + 1) * 64].rearrange("(t p) d -> p t d", p=128), reads=[bv_d], writes=[b_Vx])
            V(lambda e: e.memset(Vx[:, :, :, 64:65], 1.0), [b_Vx], [b_Vx])
            npre = TF0 + 1
            V(lambda e: e.tensor_scalar(Vx[:, 0:npre, :, :], Vx[:, 0:npre, :, :], flag[:, 0:1], None, op0=ALU.mult), [b_Vx, b_flag], [b_Vx])
        NCMP = cfg.NCMP
        V(lambda e: e.memset(hidT[:], 0.0), [], [b_hidT])
        V(lambda e: e.memset(kcT[:], 0.0), [], [b_kcT])
        V(lambda e: e.memset(vca[:], 0.0), [], [b_vca])
        V(lambda e: e.memset(vca[:, :, :, 64:65], 1.0), [], [b_vca])
        for g in range(2):
            V(lambda e, g=g: e.tensor_copy(vca[:, :, g, 65:CW], ovt[:]), [b_ovt], [b_vca])
        with ExitStack() as es2:
            old = C.st
            C.st = es2
            craw = [C.sb("craw%d" % i, [128, NT * 128], BF16) for i in range(2)]
            w1b = [C.sb("w1b%d" % i, [128, 32, 128], BF16) for i in range(2)]
            S.dma("sync", craw[0][0][:], kT_d[0], reads=[bkT_d], writes=[craw[0][1]])
            S.dma("sync", craw[1][0][:], vT_d, reads=[bvT_d], writes=[craw[1][1]])
            for kvi in range(2):
                w1v, bw1 = w1b[kvi]
                S.dma("gpsimd", w1v[:].rearrange("p l c -> p (l c)"), cw1_d[kvi], writes=[bw1])
                pb1, bpb1 = C.psum(PS_MM)
                for l in range(32):
                    T(lambda e, l=l: e.matmul(pb1[:, 0:1], w1v[0:64, l, :], cpe[0:64, kvi, l:l + 1], start=(l == 0), stop=(l == 31)),
                      [bw1, b_cpe], [bpb1])
                V(lambda e: e.tensor_tensor(b1e[:, kvi:kvi + 1], pb1[:, 0:1], cb1[:, kvi:kvi + 1], ALU.add), [bpb1, b_cb1], [b_b1e])
                cr, bcr = craw[kvi]
                for g in range(2):
                    gsl2 = slice(g * 64, (g + 1) * 64)
                    for b0 in range(0, NCMP, 512):
                        nb = min(512, NCMP - b0)
                        ph, bph = C.psum(PS_MM)
                        for l in range(32):
                            T(lambda e, l=l: e.matmul(ph[:, :nb], w1v[gsl2, l, :], cr[gsl2, 16 * b0 + l:16 * b0 + l + 16 * (nb - 1) + 1:16],
                                                      start=(l == 0), stop=(l == 31)), [bw1, bcr], [bph])
                        A(lambda e: e.activation(hidT[:, kvi, g, b0:b0 + nb], ph[:, :nb], AF.Silu, bias=b1e[:, kvi:kvi + 1]),
                          [bph, b_b1e], [b_hidT])
            for b0 in range(0, NCMP, 512):
                nb = min(512, NCMP - b0)
                pk, bpk = C.psum(PS_MM)
                for g in range(2):
                    T(lambda e, g=g: e.matmul(pk[:, :nb], cw2k[:, g, :], hidT[:, 0, g, b0:b0 + nb], start=(g == 0), stop=(g == 1)),
                      [b_cw2k, b_hidT], [bpk])
                A(lambda e: e.activation(kcT[:, b0:b0 + nb], pk[:, :nb], AF.Identity, bias=cb2k[:, 0:1]), [bpk, b_cb2k], [b_kcT])
            for bt in range(NBT):
                pv2, bpv2 = C.psum(PS_MM)
                for g in range(2):
                    T(lambda e, g=g: e.matmul(pv2[:, g * 64:(g + 1) * 64], hidT[:, 1, g, bt * 128:(bt + 1) * 128], cw2v[:], start=True, stop=True),
                      [b_hidT, b_cw2v], [bpv2])
                V(lambda e: e.tensor_tensor(vca[:, bt, :, 0:64], pv2[:, 0:128].rearrange("p (g d) -> p g d", g=2),
                                            cb2v[:].unsqueeze(1).to_broadcast([128, 2, 64]), ALU.add), [bpv2, b_cb2v], [b_vca])
            S.barrier()
            C.st = old
        XSt = [C.sb("XSt%d" % i, [128, DI], BF16) for i in range(2)]
        BTmt = [C.sb("BTmt%d" % i, [128, 512], BF16) for i in range(2)]
        BTt = [C.sb("BTt%d" % i, [128, 4, 128], BF16) for i in range(2)]
        CTt = [C.sb("CTt%d" % i, [128, 4, 128], BF16) for i in range(2)]
        dtrt = [C.sb("dtrt%d" % i, [128, HS], F32) for i in range(2)]
        zst_ = [C.sb("zst_%d" % i, [128, DI], BF16) for i in range(2)]
        qbt = [C.sb("qbt%d" % i, [128, NH * 64], BF16) for i in range(2)]
        gst = [C.sb("gst%d" % i, [128, 3 * NH], F32) for i in range(2)]
        mct = [C.sb("mct%d" % i, [128, NBT, 128], BF16) for i in range(2)]
        fbt = [C.sb("fbt%d" % i, [128, NSLC], F32) for i in range(2)]
        sm = {}
        for nm in ("dt", "ax", "ex", "dA", "cum", "ncum", "dec", "te", "ec", "coef", "cml"):
            sm[nm] = C.sb("sm_" + nm, [128, HS], F32)
        stat, b_stat = C.sb("statm", [128, 16], F32)
        junk, b_junk = C.sb("junkm", [128, GW], F32)
        Hs, b_Hs = C.sb("Hs", [128, DI], F32)
        Hb, b_Hb = C.sb("Hb", [128, DI], BF16)
        xw, b_xw = C.sb("xw", [128, DI], BF16)
        xdt = [C.sb("xdt%d" % i, [128, GW], BF16) for i in range(2)]
        Rg = [C.sb("Rg%d" % i, [128, HPGS * 128], F32) for i in range(2)]
        LTg = [C.sb("LTg%d" % i, [128, HPGS * 128], F32) for i in range(2)]
        Wg = [C.sb("Wg%d" % i, [128, HPGS * 128], BF16) for i in range(2)]
        yt1 = [C.sb("yt1_%d" % i, [128, GW], F32) for i in range(2)]
        yt2 = [C.sb("yt2_%d" % i, [128, GW], F32) for i in range(2)]
        yz, b_yz = C.sb("yz", [128, DI], F32)
        ycat = [C.sb("ycat%d" % i, [128, DMIX], BF16) for i in range(2)]
        qT, b_qT = C.sb("qT", [128, HPG, 128], BF16)
        Eb = [C.sb("Eb%d" % i, [128, HPG * 128], F32) for i in range(2)]
        PTb = [C.sb("PTb%d" % i, [128, HPG * 128], BF16) for i in range(2)]
        mdiag = [C.sb("mdiag%d" % i, [128, 128], F32) for i in range(2)]
        oacc, b_oacc = C.sb("oacc", [128, NH * 64], F32)
        otmp, b_otmp = C.sb("otmp", [128, HPG * 64], F32)
        nst, b_nst = C.sb("nst", [128, 4, HPG], F32)
        impg, b_impg = C.sb("impg", [128, NSLC], F32)
        wk1, b_wk1 = C.sb("wk1", [128, NSLC], F32)
        mx8, b_mx8 = C.sb("mx8", [128, 8], F32)
        selb, b_selb = C.sb("selb", [128, NSLC], BF16)
        selT, b_selT = C.sb("selT", [NSLC, 128], BF16)
        V(lambda e: e.memset(Hs[:], 0.0), [], [b_Hs])
        V(lambda e: e.memset(Hb[:], 0.0), [], [b_Hb])

        def f2(nm):
            return sm[nm][0][:]

        def b2(nm):
            return sm[nm][1]

        def load_tile(t):
            k = t % 2
            rows = slice(t * 128, (t + 1) * 128)
            S.dma("sync", XSt[k][0][:], XS_d[rows, :], reads=[bXS_d], writes=[XSt[k][1]])
            S.dma("sync", BTmt[k][0][:], BTm_d[rows, :], reads=[bBTm_d], writes=[BTmt[k][1]])
            S.dma("sync", dtrt[k][0][:], dtr_d[rows, :], reads=[bdtr_d], writes=[dtrt[k][1]])
            if t >= TF0:
                fr = slice((t - TF0) * 128, (t - TF0 + 1) * 128)
                S.dma("sync", BTt[k][0][:], BT_d[:, :, rows].rearrange("g n t -> n g t"), reads=[bBT_d], writes=[BTt[k][1]])
                S.dma("sync", CTt[k][0][:], CT_d[:, :, rows].rearrange("g n t -> n g t"), reads=[bCT_d], writes=[CTt[k][1]])
                S.dma("sync", zst_[k][0][:], zs_d[fr, :], reads=[bzs_d], writes=[zst_[k][1]])
                S.dma("sync", qbt[k][0][:], qb_d[fr, :], reads=[bqb_d], writes=[qbt[k][1]])
                S.dma("sync", gst[k][0][:], gs_d[fr, :], reads=[bgs_d], writes=[gst[k][1]])
                S.dma("sync", mct[k][0][:], maskc_d[:, :, fr].rearrange("b p q -> p b q"), writes=[mct[k][1]])
                S.dma("sync", fbt[k][0][:], fbias_d[fr, :], writes=[fbt[k][1]])

        load_tile(0)
        for t in range(NT if getattr(cfg, "stop", None) != "M0" else 0):
            if t + 1 < NT:
                load_tile(t + 1)
            k = t % 2
            full = t >= TF0 and getattr(cfg, "stop", None) != "M1"
            XS, b_XS = XSt[k]
            BTm, b_BTm = BTmt[k]
            BT, b_BT = BTt[k]
            CT, b_CT = CTt[k]
            dtr, b_dtr = dtrt[k]
            zs, b_zs = zst_[k]
            qb, b_qb = qbt[k]
            gs, b_gs = gst[k]
            maskc, b_maskc = mct[k]
            fbias, b_fbias = fbt[k]
            yc, b_yc = ycat[k]
            XS3 = XS[:].rearrange("p (h d) -> p h d", h=HS)
            V(lambda e: e.tensor_tensor(f2("dt"), dtr[:], dtb[:], ALU.add), [b_dtr, b_dtb], [b2("dt")])
            V(lambda e: e.tensor_scalar(f2("ax"), f2("dt"), -1.0, None, op0=ALU.mult), [b2("dt")], [b2("ax")])
            V(lambda e: e.tensor_tensor(f2("ax"), f2("ax"), f2("dt"), ALU.min), [b2("dt"), b2("ax")], [b2("ax")])
            A(lambda e: e.activation(f2("ex"), f2("ax"), AF.Exp), [b2("ax")], [b2("ex")])
            A(lambda e: e.activation(f2("ex"), f2("ex"), AF.Ln, bias=1.0), [b2("ex")], [b2("ex")])
            V(lambda e: e.tensor_scalar(f2("dt"), f2("dt"), 0.0, None, op0=ALU.max), [b2("dt")], [b2("dt")])
            V(lambda e: e.tensor_tensor(f2("dt"), f2("dt"), f2("ex"), ALU.add), [b2("dt"), b2("ex")], [b2("dt")])
            if t <= TF0:
                V(lambda e: e.tensor_scalar(f2("dt"), f2("dt"), flag[:, 0:1], None, op0=ALU.mult), [b2("dt"), b_flag], [b2("dt")])
            V(lambda e: e.tensor_tensor(f2("dA"), f2("dt"), atab[:], ALU.mult), [b2("dt"), b_atab], [b2("dA")])
            pt, bpt = C.psum(PS_MM)
            T(lambda e: e.matmul(pt[:, :HS], utri[:], f2("dA"), start=True, stop=True), [b_utri, b2("dA")], [bpt])
            V(lambda e: e.tensor_copy(f2("cum"), pt[:, :HS]), [bpt], [b2("cum")])
            A(lambda e: e.mul(f2("ncum"), pt[:, :HS], -1.0), [bpt], [b2("ncum")])
            V(lambda e: e.tensor_scalar(f2("cml"), f2("cum"), identf[:, 127:128], None, op0=ALU.mult), [b2("cum"), b_identf], [b2("cml")])
            pt, bpt = C.psum(PS_MM)
            T(lambda e: e.matmul(pt[:, :HS], onesf[:], f2("cml"), start=True, stop=True), [b_onesf, b2("cml")], [bpt])
            A(lambda e: e.activation(f2("dec"), pt[:, :HS], AF.Exp), [bpt], [b2("dec")])
            V(lambda e: e.tensor_tensor(f2("te"), pt[:, :HS], f2("cum"), ALU.subtract), [bpt, b2("cum")], [b2("te")])
            A(lambda e: e.activation(f2("te"), f2("te"), AF.Exp), [b2("te")], [b2("te")])
            A(lambda e: e.activation(f2("ec"), f2("cum"), AF.Exp), [b2("cum")], [b2("ec")])
            V(lambda e: e.tensor_tensor(f2("coef"), f2("dt"), f2("te"), ALU.mult), [b2("dt"), b2("te")], [b2("coef")])
            if full:
                for g in range(4):
                    hsl = slice(g * HPGS, (g + 1) * HPGS)
                    gsl = slice(g * GW, (g + 1) * GW)
                    R, bR = Rg[g % 2]
                    LT, bLT = LTg[g % 2]
                    W, bW = Wg[g % 2]
                    xd, bxd = xdt[g % 2]
                    R3 = R[:].rearrange("p (h i) -> p h i", h=HPGS)
                    V(lambda e: e.tensor_tensor(R3, identf[:].unsqueeze(1).to_broadcast([128, HPGS, 128]),
                                                bc3(sm["cum"][0][:, hsl], 128), ALU.mult), [b_identf, b2("cum")], [bR])
                    for c0 in range(0, HPGS * 128, 512):
                        cn = min(512, HPGS * 128 - c0)
                        pb, bpb = C.psum(PS_MM)
                        T(lambda e: e.matmul(pb[:, :cn], onesf[:], R[:, c0:c0 + cn], start=True, stop=False), [b_onesf, bR], [bpb])
                        T(lambda e: e.matmul(pb[:, :cn], identb[:], negm[:, :cn], start=False, stop=True), [b_identb, b_negm], [bpb])
                        for hh in range(cn // 128):
                            h = g * HPGS + c0 // 128 + hh
                            A(lambda e, hh=hh, h=h: e.activation(LT[:, c0 + hh * 128:c0 + (hh + 1) * 128], pb[:, hh * 128:(hh + 1) * 128],
                                                                 AF.Exp, bias=sm["ncum"][0][:, h:h + 1]), [bpb, b2("ncum")], [bLT])
                    pc, bpc = C.psum(PS_MM)
                    T(lambda e: e.matmul(pc[:, :128], BT[:, g, :], CT[:, g, :], start=True, stop=True), [b_BT, b_CT], [bpc])
                    V(lambda e: e.tensor_tensor(W[:].rearrange("p (h i) -> p h i", h=HPGS), LT[:].rearrange("p (h i) -> p h i", h=HPGS),
                                                pc[:, :128].unsqueeze(1).to_broadcast([128, HPGS, 128]), ALU.mult), [bLT, bpc], [bW])
                    P(lambda e: e.tensor_tensor(xd[:].rearrange("p (h d) -> p h d", h=HPGS), XS3[:, hsl, :],
                                                bc3(sm["dt"][0][:, hsl], 64), ALU.mult), [b_XS, b2("dt")], [bxd])
                    pyd, bpyd = C.ps[6]
                    pyo, bpyo = C.ps[7]
                    for hh in range(HPGS):
                        T(lambda e, hh=hh: e.matmul(pyd[:, hh * 64:(hh + 1) * 64], W[:, hh * 128:(hh + 1) * 128], xd[:, hh * 64:(hh + 1) * 64],
                                                    start=True, stop=True), [bW, bxd], [bpyd])
                    T(lambda e: e.matmul(pyo[:, :GW], CT[:, g, :], Hb[:, gsl], start=True, stop=True), [b_CT, b_Hb], [bpyo])
                    y1, by1 = yt1[g % 2]
                    y2, by2 = yt2[g % 2]
                    V(lambda e: e.tensor_tensor(y1[:].rearrange("p (h d) -> p h d", h=HPGS), pyo[:, :GW].rearrange("p (h d) -> p h d", h=HPGS),
                                                bc3(sm["ec"][0][:, hsl], 64), ALU.mult), [bpyo, b2("ec")], [by1])
                    V(lambda e: e.tensor_tensor(y1[:], y1[:], pyd[:, :GW], ALU.add), [by1, bpyd], [by1])
                    P(lambda e: e.tensor_tensor(y2[:].rearrange("p (h d) -> p h d", h=HPGS), XS3[:, hsl, :],
                                                bc3(dskip[:, hsl], 64), ALU.mult), [b_XS, b_dskip], [by2])
                    P(lambda e: e.tensor_tensor(y2[:], y2[:], y1[:], ALU.add), [by1, by2], [by2])
                    P(lambda e: e.tensor_tensor(yz[:, gsl], y2[:], zs[:, gsl], ALU.mult), [by2, b_zs], [b_yz])
                    A(lambda e: e.activation(junk[:, :GW], yz[:, gsl], AF.Square, accum_out=stat[:, 4 + g:5 + g]), [b_yz], [b_junk, b_stat])
                A(lambda e: e.activation(stat[:, 8:12], stat[:, 4:8], AF.Sqrt, scale=1.0 / GW, bias=EPS), [b_stat], [b_stat])
                V(lambda e: e.reciprocal(stat[:, 12:16], stat[:, 8:12]), [b_stat], [b_stat])
                for g in range(4):
                    gsl = slice(g * GW, (g + 1) * GW)
                    V(lambda e: e.scalar_tensor_tensor(yc[:, gsl], yz[:, gsl], stat[:, 12 + g:13 + g], ssmn[:, gsl],
                                                       op0=ALU.mult, op1=ALU.mult), [b_yz, b_stat, b_ssmn], [b_yc])
                nsa_on = getattr(cfg, "stop", None) != "M2"
                transpose_to(lambda c0, cnt: qT[:, c0:c0 + cnt, :], lambda c: qb[:, c * 128:(c + 1) * 128], HPG, b_qb, b_qT)
                qTf = qT[:].rearrange("p h q -> p (h q)")
                HB = [(a, min(a + 3, HPG)) for a in range(0, HPG, 3)]
                cbanks = (3, 6, 7)
                rot = [0]

                def s_scores(KT, bKT, kcols, g, need_f32):
                    gsl2 = slice(g * 64, (g + 1) * 64)
                    k2 = rot[0]
                    rot[0] += 1
                    E, bE = Eb[k2 % 2] if need_f32 else PTb[k2 % 2]
                    for h0 in range(0, HPG, 4):
                        nh = min(4, HPG - h0)
                        ps_, bps_ = C.psum((0, 1, 2)) if getattr(cfg, "stop2", None) != "pool" else C.psum(PS_TR)
                        T(lambda e: e.matmul(ps_[:, :nh * 128], KT[gsl2, kcols], qTf[gsl2, h0 * 128:(h0 + nh) * 128], start=True, stop=True),
                          [bKT, b_qT], [bps_])
                        if getattr(cfg, "stop2", None) == "noexp":
                            continue
                        if getattr(cfg, "stop2", None) == "last" and t != NT - 1:
                            continue
                        if getattr(cfg, "stop2", None) == "first" and t != TF0:
                            continue
                        if getattr(cfg, "stop2", None) == "second" and t != TF0 + 1:
                            continue
                        if getattr(cfg, "stop2", None) == "notfirst" and t == TF0:
                            continue
                        if getattr(cfg, "stop2", None) == "g0" and g != 0:
                            continue
                        if getattr(cfg, "stop2", None) == "g1" and g != 1:
                            continue
                        if getattr(cfg, "stop2", None) == "exp128":
                            for hq in range(nh):
                                A(lambda e, hq=hq: e.activation(E[:, (h0 + hq) * 128:(h0 + hq + 1) * 128], ps_[:, hq * 128:(hq + 1) * 128], AF.Exp, scale=0.125), [bps_], [bE])
                            continue
                        if getattr(cfg, "stop2", None) == "exp3":
                            A(lambda e: e.copy(E[:, h0 * 128:(h0 + nh) * 128], ps_[:, :nh * 128]), [bps_], [bE])
                            A(lambda e: e.activation(E[:, h0 * 128:(h0 + nh) * 128], E[:, h0 * 128:(h0 + nh) * 128], AF.Exp, scale=0.125), [bE], [bE])
                            continue
                        if getattr(cfg, "stop2", None) == "exp2":
                            V(lambda e: e.tensor_copy(E[:, h0 * 128:(h0 + nh) * 128], ps_[:, :nh * 128]), [bps_], [bE])
                            A(lambda e: e.activation(E[:, h0 * 128:(h0 + nh) * 128], E[:, h0 * 128:(h0 + nh) * 128], AF.Exp, scale=0.125), [bE], [bE])
                            continue
                        if getattr(cfg, "stop2", None) == "copy":
                            V(lambda e: e.tensor_copy(E[:, h0 * 128:(h0 + nh) * 128], ps_[:, :nh * 128]), [bps_], [bE])
                            continue
                        if getattr(cfg, "stop2", None) == "actpsum":
                            A(lambda e: e.activation(E[:, h0 * 128:(h0 + nh) * 128], ps_[:, :nh * 128], AF.Exp, scale=0.125), [bps_], [bE])
                            continue
                        Ef, bEf = Eb[k2 % 2]
                        V(lambda e: e.tensor_copy(Ef[:, h0 * 128:(h0 + nh) * 128], ps_[:, :nh * 128]), [bps_], [bEf])
                        A(lambda e: e.activation(E[:, h0 * 128:(h0 + nh) * 128], Ef[:, h0 * 128:(h0 + nh) * 128], AF.Exp, scale=0.125), [bEf], [bE] if bE is not bEf else [bEf])
                    return E, bE, k2

                def evac(banks, per, gate_idx, g, with_imp):
                    for (pb, bpb, h0, h1) in banks:
                        nh = h1 - h0
                        v = pb[:, :nh * per].rearrange("p (h c) -> p h c", h=nh)
                        V(lambda e: e.tensor_scalar(nst[:, 0, h0:h1], v[:, :, 64], 1e-30, None, op0=ALU.max), [bpb], [b_nst])
                    V(lambda e: e.reciprocal(nst[:, 1, :], nst[:, 0, :]), [b_nst], [b_nst])
                    g3 = gs[:].rearrange("p (h c) -> p h c", c=3)
                    V(lambda e: e.tensor_tensor(nst[:, 2, :], nst[:, 1, :], g3[:, g * HPG:(g + 1) * HPG, gate_idx], ALU.mult), [b_nst, b_gs], [b_nst])
                    for (pb, bpb, h0, h1) in banks:
                        nh = h1 - h0
                        v = pb[:, :nh * per].rearrange("p (h c) -> p h c", h=nh)
                        osl = oacc[:, (g * HPG + h0) * 64:(g * HPG + h1) * 64].rearrange("p (h d) -> p h d", h=nh)
                        if gate_idx == 0:
                            V(lambda e: e.tensor_tensor(osl, v[:, :, 0:64], bc3(nst[:, 2, h0:h1], 64), ALU.mult), [bpb, b_nst], [b_oacc])
                        else:
                            ot = otmp[:, h0 * 64:h1 * 64].rearrange("p (h d) -> p h d", h=nh)
                            V(lambda e: e.tensor_tensor(ot, v[:, :, 0:64], bc3(nst[:, 2, h0:h1], 64), ALU.mult), [bpb, b_nst], [b_otmp])
                            P(lambda e: e.tensor_tensor(osl, osl, ot, ALU.add), [b_otmp, b_oacc], [b_oacc])
                        if with_imp:
                            for hh in range(h0, h1):
                                if hh == 0:
                                    V(lambda e: e.tensor_scalar(impg[:], v[:, hh - h0, 65:CW], nst[:, 1, hh:hh + 1], None, op0=ALU.mult),
                                      [bpb, b_nst], [b_impg])
                                else:
                                    V(lambda e: e.scalar_tensor_tensor(impg[:], v[:, hh - h0, 65:CW], nst[:, 1, hh:hh + 1], impg[:],
                                                                       op0=ALU.mult, op1=ALU.add), [bpb, b_nst, b_impg], [b_impg])

                def pv(banks, PT, bPT, Vop, bV, first, last, per):
                    for (pb, bpb, h0, h1) in banks:
                        for hh in range(h0, h1):
                            T(lambda e, hh=hh: e.matmul(pb[:, (hh - h0) * per:(hh - h0 + 1) * per], PT[:, hh * 128:(hh + 1) * 128], Vop,
                                                        start=(first and hh == h0), stop=(last and hh == h1 - 1)), [bPT, bV], [bpb])

                def masked(E, bE, k2, msk, bmsk):
                    PT, bPT = PTb[(k2 + 1) % 2]
                    V(lambda e: e.tensor_tensor(PT[:].rearrange("p (h q) -> p h q", h=HPG), E[:].rearrange("p (h q) -> p h q", h=HPG),
                                                msk.unsqueeze(1).to_broadcast([128, HPG, 128]), ALU.mult), [bE, bmsk], [bPT])
                    return PT, bPT

                for g in (range(2) if nsa_on else ()):
                    nbt = min(NBT, (8 * t + 6) // 128 + 1)
                    banks = [(C.ps[cbanks[k_]][0], C.ps[cbanks[k_]][1], a, b) for k_, (a, b) in enumerate(HB)]
                    stp = getattr(cfg, "stop", None)
                    if stp == "C1":
                        continue
                    pend = None
                    nxt = s_scores(kcT, b_kcT, slice(0, 128), g, True)
                    for bt in range(nbt):
                        E, bE, k2 = nxt
                        if bt + 1 < nbt:
                            nxt = s_scores(kcT, b_kcT, slice((bt + 1) * 128, (bt + 2) * 128), g, True)
                        PT, bPT = masked(E, bE, k2, maskc[:, bt, :], b_maskc)
                        if pend is not None:
                            pv(banks, *pend)
                        pend = (PT, bPT, vca[:, bt, g, :], b_vca, bt == 0, bt == nbt - 1, CW)
                    pv(banks, *pend)
                    evac(banks, CW, 0, g, True)
                    if getattr(cfg, "stop", None) == "N1":
                        continue
                    V(lambda e: e.tensor_tensor(wk1[:], impg[:], fbias[:], ALU.add), [b_impg, b_fbias], [b_wk1])
                    for r_ in range((cfg.TOPN - 1) // 8):
                        V(lambda e: e.max(out=mx8[:], in_=wk1[:]), [b_wk1], [b_mx8])
                        V(lambda e: e.match_replace(out=wk1[:], in_to_replace=mx8[:], in_values=wk1[:], imm_value=-3.0e38), [b_mx8, b_wk1], [b_wk1])
                    V(lambda e: e.max(out=mx8[:], in_=wk1[:]), [b_wk1], [b_mx8])
                    kth = (cfg.TOPN - 1) % 8
                    V(lambda e: e.tensor_tensor(wk1[:], impg[:], fbias[:], ALU.add), [b_impg, b_fbias, b_mx8], [b_wk1])
                    V(lambda e: e.tensor_scalar(selb[:], wk1[:], mx8[:, kth:kth + 1], None, op0=ALU.is_ge), [b_wk1, b_mx8], [b_selb])
                    pt2, bpt2 = C.psum(PS_TR)
                    ptb2 = pt2[:].bitcast(BF16)
                    T(lambda e: e.transpose(ptb2[:NSLC, 0:128], selb[:], identb[:]), [b_selb, b_identb], [bpt2])
                    A(lambda e: e.copy(selT[:], ptb2[:NSLC, 0:128]), [bpt2], [b_selT])
                    if getattr(cfg, "stop", None) == "N2":
                        continue
                    HB2 = [(a, min(a + 4, HPG)) for a in range(0, HPG, 4)]
                    banks = [(C.ps[6 + k_][0], C.ps[6 + k_][1], a, b) for k_, (a, b) in enumerate(HB2)]
                    pend = None
                    nxt = s_scores(KTs, b_KTs, slice(0, 128), g, True)
                    for kt in range(t + 1):
                        kc_ = slice(kt * 128, (kt + 1) * 128)
                        E, bE, k2 = nxt
                        if kt + 1 <= t:
                            nxt = s_scores(KTs, b_KTs, slice((kt + 1) * 128, (kt + 2) * 128), g, True)
                        pm, bpm = C.psum(PS_TR)
                        T(lambda e: e.matmul(pm[:, :128], exm[:, kc_], selT[:], start=True, stop=True), [b_exm, b_selT], [bpm])
                        if kt == t:
                            md, bmd = mdiag[kt % 2]
                            V(lambda e: e.tensor_tensor(md[:], pm[:, :128], causT[:], ALU.mult), [bpm, b_causT], [bmd])
                            PT, bPT = masked(E, bE, k2, md[:], bmd)
                        else:
                            PT, bPT = masked(E, bE, k2, pm[:, :128], bpm)
                        if pend is not None:
                            pv(banks, *pend)
                        pend = (PT, bPT, Vs[:, kt, g, :], b_Vs, kt == 0, kt == t, 65)
                    pv(banks, *pend)
                    evac(banks, 65, 1, g, False)
                    if getattr(cfg, "stop", None) == "N3":
                        continue
                    kts = [kt for kt in range(t - WT, t + 1) if kt >= 0]
                    pend = None
                    nxt = s_scores(KTw, b_KTw, slice(kts[0] * 128, (kts[0] + 1) * 128), g, True)
                    for kt in kts:
                        E, bE, k2 = nxt
                        if kt < kts[-1]:
                            nxt = s_scores(KTw, b_KTw, slice((kt + 1) * 128, (kt + 2) * 128), g, True)
                        mk = causT if kt == t else (bandT if kt == t - WT else onesb)
                        PT, bPT = masked(E, bE, k2, mk[:], b_causT)
                        if pend is not None:
                            pv(banks, *pend)
                        pend = (PT, bPT, Vw[:, kt, g, :], b_Vw, kt == kts[0], kt == kts[-1], 65)
                    pv(banks, *pend)
                    evac(banks, 65, 2, g, False)
                if not nsa_on:
                    V(lambda e: e.memset(oacc[:], 0.0), [], [b_oacc])
                V(lambda e: e.tensor_copy(yc[:, DI:], oacc[:]), [b_oacc], [b_yc])
                S.dma("sync", ycat_d[(t - TF0) * 128:(t - TF0 + 1) * 128, :], yc[:], reads=[b_yc], writes=[bycat_d], sembuf=b_yc)
                if "ycat" in dbg_o and t >= NTH:
                    dbs, b_dbs = C.sb("dbs%d" % t, [128, DMIX], F32)
                    V(lambda e: e.tensor_copy(dbs[:], yc[:]), [b_yc], [b_dbs])
                    S.dma("sync", dbg_o["ycat"][(t - NTH) * 128:(t - NTH + 1) * 128, :], dbs[:], reads=[b_dbs], sembuf=b_dbs)
            V(lambda e: e.tensor_tensor(xw[:].rearrange("p (h d) -> p h d", h=HS), XS3, bc3(sm["coef"][0][:], 64), ALU.mult),
              [b_XS, b2("coef")], [b_xw])
            for g in range(4):
                hsl = slice(g * HPGS, (g + 1) * HPGS)
                gsl = slice(g * GW, (g + 1) * GW)
                pst, bpst = C.psum(PS_MM)
                T(lambda e: e.matmul(pst[:, :GW], BTm[:, g * 128:(g + 1) * 128], xw[:, gsl], start=True, stop=True), [b_BTm, b_xw], [bpst])
                H3 = Hs[:, gsl].rearrange("p (h d) -> p h d", h=HPGS)
                V(lambda e: e.tensor_tensor(H3, H3, bc3(sm["dec"][0][:, hsl], 64), ALU.mult), [b_Hs, b2("dec")], [b_Hs])
                V(lambda e: e.tensor_tensor(Hs[:, gsl], Hs[:, gsl], pst[:, :GW], ALU.add), [b_Hs, bpst], [b_Hs])
            P(lambda e: e.tensor_copy(Hb[:], Hs[:]), [b_Hs], [b_Hb])
        stg = [C.sb("stg%d" % i, [128, 128], F32) for i in range(2)]
        for c in range(DI // 128):
            sg, bsg = stg[c % 2]
            pt, bpt = C.psum(PS_MM)
            T(lambda e: e.transpose(pt[:, :128], Hs[:, c * 128:(c + 1) * 128], identf[:]), [b_Hs, b_identf], [bpt])
            V(lambda e: e.tensor_copy(sg[:], pt[:, :128]), [bpt], [bsg])
            S.dma("sync", ssm_p[c * 128:(c + 1) * 128, :], sg[:], reads=[bsg], sembuf=bsg)

    if getattr(cfg, "stop", None) == "M":
        S.finish()
        return
    GO = 5
    groupsO = []
    t = TF0
    while t < NT:
        n = min(GO, NT - t)
        if 0 < NT - t - n < 3:
            n = (NT - t + 1) // 2
        groupsO.append(list(range(t, t + n)))
        t += n
    GOM = max(len(g) for g in groupsO)
    NGM = GOM * 128
    KO = max(DMIX // 128, NHT)
    with Phase():
        hp, b_hp = C.sb("hp", [128, GOM, D], F32)
        xnT, b_xnT = C.sb("xnTo", [128, KC, NGM], BF16)
        arena, b_arena = C.sb("arena", [128, KO, NGM], BF16)
        NSUB = max(1, KO // KC)
        wbo = []
        for i in range(2):
            t_, _b = C.sb("wbo%d" % i, [128, KO, 512], BF16)
            wbo.append((t_, [Buf("wbo%d_%d" % (i, j)) for j in range(max(NSUB, 1))]))
        wrot = [0]
        srot = [0]

        def load_sub(W, c0, ncols):
            k = srot[0] % (2 * NSUB)
            srot[0] += 1
            t_, bl = wbo[k // NSUB]
            j = k % NSUB
            view = t_[:, j * KC:(j + 1) * KC, :]
            S.dma("gpsimd", view[:, :, :ncols], W[:, c0:c0 + ncols].rearrange("(kc p) c -> p kc c", p=128), writes=[bl[j]])
            return view, bl[j]
        junk, b_junk = C.sb("junko", [128, D], F32)
        xnb = [C.sb("xnbo%d" % i, [128, D], BF16) for i in range(2)]
        stat, b_stat = C.sb("stato", [128, 16], F32)
        mixb = [C.sb("mixb%d" % i, [128, D], F32) for i in range(2)]
        yct = [C.sb("yct%d" % i, [128, DMIX], BF16) for i in range(1)]
        ub = [C.sb("ub%d" % i, [128, 2 + NGM], F32) for i in range(2)]
        ua = [C.sb("ua%d" % i, [128, NGM], F32) for i in range(2)]
        gg, b_gg = C.sb("gg", [128, NGM], F32)
        fhist = [C.sb("fhist%d" % l, [128, NFT, 2], F32) for l in range(2)]
        fcw, b_fcw = C.sb("fcw", [128, 2, NFT, 3], F32)
        fcb, b_fcb = C.sb("fcb", [128, 2, NFT], F32)
        chist, b_chist = C.sb("chist", [128, KC, 30], F32)
        ab_ = [C.sb("ab%d" % i, [128, 30 + NGM], F32) for i in range(2)]
        cT, b_cT = C.sb("cT", [128, KC, NGM], F32)
        sq, b_sq = C.sb("sq", [128, NGM], F32)
        mean, b_mean = C.sb("mean", [128, NGM], F32)
        rstd, b_rstd = C.sb("rstd", [128, NGM], F32)
        sg_, b_sg = C.sb("sgl", [128, NGM], F32)
        bpw1, b_bpw1 = C.sb("bpw1", [128, 2 * KC], F32)
        dww, b_dww = C.sb("dww", [128, KC, 31], F32)
        dwb, b_dwb = C.sb("dwb", [128, KC], F32)
        lng, b_lng = C.sb("lng", [128, KC], F32)
        lnb, b_lnb = C.sb("lnb", [128, KC], F32)
        bpw2, b_bpw2 = C.sb("bpw2", [128, D], F32)
        bco = Buf("constsO")
        for (dst, src) in ((fcw, fcw_d.rearrange("l p m k -> p l m k")), (fcb, fcb_d.rearrange("l p m -> p l m")),
                           (bpw1, bpw1_d), (dww, dww_d), (dwb, dwb_d), (lng, lng_d), (lnb, lnb_d),
                           (bpw2, bpw2_d.partition_broadcast(128))):
            S.dma("sync", dst[:], src, writes=[bco])
        b_fcw = b_fcb = b_bpw1 = b_dww = b_dwb = b_lng = b_lnb = b_bpw2 = bco
        for l in range(2):
            V(lambda e, l=l: e.memset(fhist[l][0][:], 0.0), [], [fhist[l][1]])
        V(lambda e: e.memset(chist[:], 0.0), [], [b_chist])

        def load_wo(W, r0, kcn, c0, ncols):
            k = wrot[0] % 2
            wrot[0] += 1
            wb, bl = wbo[k]
            src = W[r0 * 128:(r0 + kcn) * 128, c0:c0 + ncols].rearrange("(kc p) c -> p kc c", p=128)
            S.dma("gpsimd", wb[:, :kcn, :ncols], src, writes=list(bl), sembuf=bl[0])
            return wb, bl

        def rms_T(i, k):
            xb, bxb = xnb[k % 2]
            A(lambda e: e.activation(junk[:, :D], hp[:, i, :], AF.Square, accum_out=stat[:, 0:1]), [b_hp], [b_junk, b_stat])
            A(lambda e: e.activation(stat[:, 1:2], stat[:, 0:1], AF.Sqrt, scale=1.0 / D, bias=EPS), [b_stat], [b_stat])
            V(lambda e: e.reciprocal(stat[:, 2:3], stat[:, 1:2]), [b_stat], [b_stat])
            V(lambda e: e.scalar_tensor_tensor(xb[:], hp[:, i, :], stat[:, 2:3], gtab[:], op0=ALU.mult, op1=ALU.mult),
              [b_hp, b_stat, b_gtab], [bxb])
            transpose_to(lambda c0, cnt: xnT[:, c0:c0 + cnt, i * 128:(i + 1) * 128],
                         lambda c: xb[:, c * 128:(c + 1) * 128], KC, bxb, b_xnT)

        def post_add(i, mx, bmx):
            A(lambda e: e.activation(junk[:, :D], mx[:], AF.Square, accum_out=stat[:, 4:5]), [bmx], [b_junk, b_stat])
            A(lambda e: e.activation(stat[:, 5:6], stat[:, 4:5], AF.Sqrt, scale=1.0 / D, bias=EPS), [b_stat], [b_stat])
            V(lambda e: e.reciprocal(stat[:, 6:7], stat[:, 5:6]), [b_stat], [b_stat])
            V(lambda e: e.scalar_tensor_tensor(mx[:], mx[:], stat[:, 6:7], gtab[:], op0=ALU.mult, op1=ALU.mult),
              [bmx, b_stat, b_gtab], [bmx])
            P(lambda e: e.tensor_tensor(hp[:, i, :], hp[:, i, :], mx[:], ALU.add), [bmx, b_hp], [b_hp])

        def tok_major(W, kcn, actT, b_actT, ng, bias=None):
            wbs = [load_wo(W, 0, kcn, c0, min(512, D - c0)) for c0 in range(0, D, 512)]
            for i in range(ng):
                mx, bmx = mixb[i % 2]
                for ci, c0 in enumerate(range(0, D, 512)):
                    cn = min(512, D - c0)
                    wb, bwb = wbs[ci]
                    pt, bpt = C.psum(PS_MM)
                    for kc in range(kcn):
                        T(lambda e, kc=kc: e.matmul(pt[:, :cn], actT[:, kc, i * 128:(i + 1) * 128], wb[:, kc, :cn],
                                                    start=(kc == 0), stop=(kc == kcn - 1)), list(bwb) + [b_actT], [bpt])
                    if bias is None:
                        A(lambda e: e.copy(mx[:, c0:c0 + cn], pt[:, :cn]), [bpt], [bmx])
                    else:
                        V(lambda e: e.tensor_tensor(mx[:, c0:c0 + cn], pt[:, :cn], bias[:, c0:c0 + cn], ALU.add), [bpt, b_bpw2], [bmx])
                yield i, mx, bmx

        def conv_ffn(layer, gi_, tiles, ng, NG, chunks):
            gsel = 4 * layer + 2
            load_gain(gsel)
            for i in range(ng):
                rms_T(i, i)
            fh, bfh = fhist[layer]
            blk = {}
            for m in range(NHT):
                res = []
                if m % 4 == 0:
                    nm_ = min(4, NHT - m)
                    blk = {0: load_sub(w_up[layer], m * 128, nm_ * 128), 1: load_sub(w_up[layer], (NHT + m) * 128, nm_ * 128)}
                jm = m % 4
                for part in range(2):
                    mt = m + part * NHT
                    wb, bwb = blk[part]
                    u, bu = ub[part]
                    for (c0, cn) in chunks:
                        pt, bpt = C.psum(PS_MM)
                        for kc in range(KC):
                            T(lambda e, kc=kc: e.matmul(pt[:, :cn], wb[:, kc, jm * 128:(jm + 1) * 128], xnT[:, kc, c0:c0 + cn],
                                                        start=(kc == 0), stop=(kc == KC - 1)), [bwb, b_xnT], [bpt])
                        A(lambda e: e.copy(u[:, 2 + c0:2 + c0 + cn], pt[:, :cn]), [bpt], [bu])
                    if gi_ == 0:
                        V(lambda e: e.tensor_scalar(u[:, 2:130], u[:, 2:130], flag[:, 0:1], None, op0=ALU.mult), [bu, b_flag], [bu])
                    P(lambda e: e.tensor_copy(u[:, 0:2], fh[:, mt, :]), [bfh], [bu])
                    P(lambda e: e.tensor_copy(fh[:, mt, :], u[:, NG:NG + 2]), [bu], [bfh])
                    a_, ba_ = ua[part]
                    V(lambda e: e.tensor_scalar(a_[:, :NG], u[:, 0:NG], fcw[:, layer, mt, 0:1], None, op0=ALU.mult), [bu, b_fcw], [ba_])
                    for k in range(1, 3):
                        V(lambda e, k=k: e.scalar_tensor_tensor(a_[:, :NG], u[:, k:k + NG], fcw[:, layer, mt, k:k + 1], a_[:, :NG],
                                                                op0=ALU.mult, op1=ALU.add), [bu, b_fcw, ba_], [ba_])
                    res.append((a_, ba_, mt))
                (ag, bag, mg), (av, bav, mv) = res
                A(lambda e: e.activation(gg[:, :NG], ag[:, :NG], AF.Gelu_apprx_tanh, bias=fcb[:, layer, mg:mg + 1]), [bag, b_fcb], [b_gg])
                V(lambda e: e.scalar_tensor_tensor(arena[:, m, :NG], av[:, :NG], fcb[:, layer, mv:mv + 1], gg[:, :NG],
                                                   op0=ALU.add, op1=ALU.mult), [bav, b_fcb, b_gg], [b_arena])
            load_gain(gsel + 1)
            for i, mx, bmx in tok_major(w_down[layer], NHT, arena, b_arena, ng):
                post_add(i, mx, bmx)

        for gi_, tiles in enumerate(groupsO):
            ng = len(tiles)
            NG = ng * 128
            chunks = [(c, min(512, NG - c)) for c in range(0, NG, 512)]
            for i, t in enumerate(tiles):
                S.dma("sync", hp[:, i, :], xin[t * 128:(t + 1) * 128, :], writes=[b_hp])
                yc, byc = yct[0]
                S.dma("sync", yc[:], ycat_d[(t - TF0) * 128:(t - TF0 + 1) * 128, :], reads=[bycat_d], writes=[byc])
                transpose_to(lambda c0, cnt: arena[:, c0:c0 + cnt, i * 128:(i + 1) * 128],
                             lambda c: yc[:, c * 128:(c + 1) * 128], DMIX // 128, byc, b_arena)
            load_gain(1)
            for i, mx, bmx in tok_major(w_out, DMIX // 128, arena, b_arena, ng):
                post_add(i, mx, bmx)
            conv_ffn(0, gi_, tiles, ng, NG, chunks)
            load_gain(4)
            for i in range(ng):
                rms_T(i, i)
            for m in range(KC):
                if m % 4 == 0:
                    nm_ = min(4, KC - m)
                    blka = load_sub(w_pw1, m * 128, nm_ * 128)
                    blkg = load_sub(w_pw1, (KC + m) * 128, nm_ * 128)
                jm = m % 4
                (wa, bwa), (wg, bwg) = blka, blkg
                ab, bab = ab_[m % 2]
                for (c0, cn) in chunks:
                    pa, bpa = C.psum(PS_MM)
                    pg, bpg = C.psum(PS_MM)
                    for kc in range(KC):
                        T(lambda e, kc=kc: e.matmul(pa[:, :cn], wa[:, kc, jm * 128:(jm + 1) * 128], xnT[:, kc, c0:c0 + cn],
                                                    start=(kc == 0), stop=(kc == KC - 1)), [bwa, b_xnT], [bpa])
                    for kc in range(KC):
                        T(lambda e, kc=kc: e.matmul(pg[:, :cn], wg[:, kc, jm * 128:(jm + 1) * 128], xnT[:, kc, c0:c0 + cn],
                                                    start=(kc == 0), stop=(kc == KC - 1)), [bwg, b_xnT], [bpg])
                    A(lambda e: e.activation(sg_[:, :cn], pg[:, :cn], AF.Sigmoid, bias=bpw1[:, KC + m:KC + m + 1]), [bpg, b_bpw1], [b_sg])
                    V(lambda e: e.scalar_tensor_tensor(ab[:, 30 + c0:30 + c0 + cn], pa[:, :cn], bpw1[:, m:m + 1], sg_[:, :cn],
                                                       op0=ALU.add, op1=ALU.mult), [bpa, b_bpw1, b_sg], [bab])
                if gi_ == 0:
                    V(lambda e: e.tensor_scalar(ab[:, 30:158], ab[:, 30:158], flag[:, 0:1], None, op0=ALU.mult), [bab, b_flag], [bab])
                P(lambda e: e.tensor_copy(ab[:, 0:30], chist[:, m, :]), [b_chist], [bab])
                P(lambda e: e.tensor_copy(chist[:, m, :], ab[:, NG:NG + 30]), [bab], [b_chist])
                V(lambda e: e.tensor_scalar(cT[:, m, :NG], ab[:, 0:NG], dww[:, m, 0:1], None, op0=ALU.mult), [bab, b_dww], [b_cT])
                for k in range(1, 31):
                    V(lambda e, k=k: e.scalar_tensor_tensor(cT[:, m, :NG], ab[:, k:k + NG], dww[:, m, k:k + 1], cT[:, m, :NG],
                                                            op0=ALU.mult, op1=ALU.add), [bab, b_dww, b_cT], [b_cT])
                V(lambda e: e.tensor_scalar(cT[:, m, :NG], cT[:, m, :NG], dwb[:, m:m + 1], None, op0=ALU.add), [b_cT, b_dwb], [b_cT])
            for (c0, cn) in chunks:
                p1, bp1 = C.psum(PS_MM)
                p2, bp2 = C.psum(PS_MM)
                for m in range(KC):
                    T(lambda e, m=m: e.matmul(p1[:, :cn], onesf[:], cT[:, m, c0:c0 + cn], start=(m == 0), stop=(m == KC - 1)), [b_onesf, b_cT], [bp1])
                for m in range(KC):
                    A(lambda e, m=m: e.activation(sq[:, :cn], cT[:, m, c0:c0 + cn], AF.Square), [b_cT], [b_sq])
                    T(lambda e, m=m: e.matmul(p2[:, :cn], onesf[:], sq[:, :cn], start=(m == 0), stop=(m == KC - 1)), [b_onesf, b_sq], [bp2])
                A(lambda e: e.mul(mean[:, c0:c0 + cn], p1[:, :cn], 1.0 / D), [bp1], [b_mean])
                V(lambda e: e.tensor_tensor(sq[:, :cn], mean[:, c0:c0 + cn], mean[:, c0:c0 + cn], ALU.mult), [b_mean, b_sq], [b_sq])
                V(lambda e: e.scalar_tensor_tensor(rstd[:, c0:c0 + cn], p2[:, :cn], 1.0 / D, sq[:, :cn], op0=ALU.mult, op1=ALU.subtract),
                  [bp2, b_sq], [b_rstd])
                A(lambda e: e.activation(rstd[:, c0:c0 + cn], rstd[:, c0:c0 + cn], AF.Sqrt, bias=EPS), [b_rstd], [b_rstd])
                V(lambda e: e.reciprocal(rstd[:, c0:c0 + cn], rstd[:, c0:c0 + cn]), [b_rstd], [b_rstd])
            for m in range(KC):
                V(lambda e: e.tensor_tensor(cT[:, m, :NG], cT[:, m, :NG], mean[:, :NG], ALU.subtract), [b_cT, b_mean], [b_cT])
                V(lambda e: e.tensor_tensor(cT[:, m, :NG], cT[:, m, :NG], rstd[:, :NG], ALU.mult), [b_cT, b_rstd], [b_cT])
                A(lambda e: e.activation(arena[:, m, :NG], cT[:, m, :NG], AF.Silu, bias=lnb[:, m:m + 1], scale=lng[:, m:m + 1]),
                  [b_cT, b_lng, b_lnb], [b_arena])
            load_gain(5)
            for i, mx, bmx in tok_major(w_pw2, KC, arena, b_arena, ng, bias=bpw2):
                post_add(i, mx, bmx)
            conv_ffn(1, gi_, tiles, ng, NG, chunks)
            for i, t in enumerate(tiles):
                if t >= NTH:
                    S.dma("sync", y_p[(t - NTH) * 128:(t - NTH + 1) * 128, :], hp[:, i, :], reads=[b_hp], sembuf=b_hp)
        co, b_co = C.sb("co", [32, D], F32)
        for m in range(KC):
            pt, bpt = C.psum(PS_MM)
            T(lambda e: e.transpose(pt[:30, :128], chist[:, m, :], identf[:]), [b_chist, b_identf], [bpt])
            V(lambda e: e.tensor_copy(co[:30, m * 128:(m + 1) * 128], pt[:30, :128]), [bpt], [b_co])
        S.dma("sync", conf_p, co[:30, :], reads=[b_co], sembuf=b_co)
        fo = [C.sb("fo%d" % i, [2, 1024], F32) for i in range(2)]
        for l in range(2):
            for mb in range(0, NFT, 8):
                nmb = min(8, NFT - mb)
                f_, bf_ = fo[(mb // 8) % 2]
                for m in range(mb, mb + nmb):
                    pt, bpt = C.psum(PS_MM)
                    T(lambda e: e.transpose(pt[:2, :128], fhist[l][0][:, m, :], identf[:]), [fhist[l][1], b_identf], [bpt])
                    V(lambda e: e.tensor_copy(f_[:, (m - mb) * 128:(m - mb + 1) * 128], pt[:2, :128]), [bpt], [bf_])
                S.dma("sync", ffn_p[l, :, mb * 128:(mb + nmb) * 128], f_[:, :nmb * 128], reads=[bf_], sembuf=bf_)

    if getattr(cfg, "stop", None) == "O":
        S.finish()
        return
    _emit_sample(C, cfg, locals())
    S.finish()
    print("ops", S.n_ops, "waits", S.n_wait, "dma sems", S.ndsem)


def _emit_sample(C, cfg, L):
    nc, S = C.nc, C.S
    D, KC, DI, HS, CD, NH, HPG = cfg.D, cfg.KC, cfg.DI, cfg.HS, cfg.CD, cfg.NH, cfg.HPG
    HPGS, DMIX, DFF, DEC, PAST = cfg.HPGS, cfg.DMIX, cfg.DFF, cfg.DEC, cfg.PAST
    GW = HPGS * 64
    NPG = PAST // 128
    NCS = PAST // 16 - 1
    NBS = (NCS + 127) // 128
    NSL = PAST // 64
    CWs = 65 + NSL + 1
    NWT = min(cfg.WINDOW, PAST) // 128
    DPROJ = cfg.DPROJ
    V, A, P, T = L["V"], L["A"], L["P"], L["T"]
    din, dout, dscr = L["din"], L["dout"], L["dscr"]
    identb, identf, onesf, gtab = L["identb"], L["identf"], L["onesf"], L["gtab"]
    b_identb, b_identf, b_onesf, b_gtab = L["b_identb"], L["b_identf"], L["b_onesf"], L["b_gtab"]
    Phase, load_gain, normg = L["Phase"], L["load_gain"], L["normg"]
    w_in, w_out, w_up, w_down, w_pw1, w_pw2 = L["w_in"], L["w_out"], L["w_up"], L["w_down"], L["w_pw1"], L["w_pw2"]

    xs_in = din("xs_in", [DEC, D])
    st_ssm = din("st_ssm", [DEC, DI, 128])
    st_sconv = din("st_sconv", [DEC, 3 * CD])
    c_cmp = din("c_cmp", [cfg.NPHYS * 128, 256])
    c_slc = din("c_slc", [cfg.NPHYS * 128, 256])
    c_win = din("c_win", [DEC, NWT * 128, 256])
    ptab = din("ptab", [DEC, NPG], I32)
    st_conf = din("st_conf", [DEC, 30, D])
    st_ffn = din("st_ffn", [2, DEC, 2 * 2 * DFF])
    convw_r = din("convw_r", [4, CD])
    convb_r = din("convb_r", [1, CD])
    fcw_r = din("fcw_r", [2, 3, 2 * DFF])
    fcb_r = din("fcb_r", [2, 1, 2 * DFF])
    dww_r = din("dww_r", [31, D])
    dwb_r = din("dwb_r", [1, D])
    lng_r = din("lng_r", [1, D])
    lnb_r = din("lnb_r", [1, D])
    bpw1_r = din("bpw1_r", [1, 2 * D])
    bpw2_r = L["bpw2_d"]
    rope_s = din("rope_s", [2, 32])
    selc_d = din("selc", [DEC, 31, DEC])
    ovs_d = din("ovs", [NBS * 128, NSL + 1])
    fbs_d = din("fbs", [1, NSL + 1])
    dtb_d, alog_d, dskip_d, ssmn_d = L["dtb_d"], L["alog_d"], L["dskip_d"], L["ssmn_d"]
    cw1_d, cpe_d, cb1_d, cw2k_d, cw2v_d, cb2k_d, cb2v_d = (L[k] for k in ("cw1_d", "cpe_d", "cb1_d", "cw2k_d", "cw2v_d", "cb2k_d", "cb2v_d"))
    exm_d = L["exm_d"]

    y_s = dout("y_s", [DEC, D])
    ssm_s = dout("ssm_s", [DEC, DI, 128])
    sconv_s = dout("sconv_s", [DEC, 3 * CD])
    kv_s = dout("kv_s", [DEC, 768])
    conf_s = dout("conf_s", [DEC, 30, D])
    ffn_s = dout("ffn_s", [2, DEC, 2 * 2 * DFF])
    ycs_d = dscr("ycs_d", [DEC, DMIX])
    gss_d = dscr("gss_d", [DEC, 3 * NH])
    vn_d = dscr("vn_d", [DEC, 768], BF16)
    arow_d = dscr("arow_d", [DEC, D])
    bycs, bgss, bvn, barow = [Buf("sscr%d" % i) for i in range(4)]
    R_ = slice(0, DEC)

    def transposeF(dst, src, rows, cols, b_src, b_dst):
        pt, bpt = C.psum(PS_MM)
        T(lambda e: e.transpose(pt[:cols, :rows], src, identf[:rows, :rows]), [b_src, b_identf], [bpt])
        V(lambda e: e.tensor_copy(dst, pt[:cols, :rows]), [bpt], [b_dst])

    def rms_rows(dst_bf, src, b_src, b_dst, st_, b_st, jk, b_jk):
        A(lambda e: e.activation(jk[R_, :D], src, AF.Square, accum_out=st_[R_, 0:1]), [b_src], [b_jk, b_st])
        A(lambda e: e.activation(st_[R_, 1:2], st_[R_, 0:1], AF.Sqrt, scale=1.0 / D, bias=EPS), [b_st], [b_st])
        V(lambda e: e.reciprocal(st_[R_, 2:3], st_[R_, 1:2]), [b_st], [b_st])
        V(lambda e: e.scalar_tensor_tensor(dst_bf, src, st_[R_, 2:3], gtab[R_, :], op0=ALU.mult, op1=ALU.mult),
          [b_src, b_st, b_gtab], [b_dst])

    with Phase():
        hs, b_hs = C.sb("hs", [DEC, D], F32)
        stt, b_stt = C.sb("stt", [128, 16], F32)
        jk, b_jk = C.sb("jks", [DEC, max(D, DI)], F32)
        xb, b_xb = C.sb("xbs", [DEC, D], BF16)
        xT, b_xT = C.sb("xTs", [128, KC, DEC], BF16)
        raw, b_raw = C.sb("raws", [DEC, DPROJ], F32)
        wbs = [C.sb("wbs%d" % i, [128, KC, 512], BF16) for i in range(3)]
        hst = [C.sb("hsts%d" % i, [DEC, 3, 512], F32) for i in range(2)]
        cwr = [C.sb("cwr%d" % i, [DEC, 4, 512], F32) for i in range(2)]
        cbr = [C.sb("cbr%d" % i, [DEC, 512], F32) for i in range(2)]
        xc, b_xc = C.sb("xcs", [DEC, CD], F32)
        S.dma("sync", hs[:], xs_in, writes=[b_hs])
        load_gain(0)
        rms_rows(xb[:], hs[:], b_hs, b_xb, stt, b_stt, jk, b_jk)
        for kc in range(KC):
            pt, bpt = C.psum(PS_TR)
            ptb = pt[:].bitcast(BF16)
            T(lambda e: e.transpose(ptb[:, 0:DEC], xb[:, kc * 128:(kc + 1) * 128], identb[:DEC, :DEC]), [b_xb, b_identb], [bpt])
            A(lambda e: e.copy(xT[:, kc, :], ptb[:, 0:DEC]), [bpt], [b_xT])
        for bi, c0 in enumerate(range(0, DPROJ, 512)):
            cn = min(512, DPROJ - c0)
            wb, bwb = wbs[bi % 3]
            S.dma("gpsimd", wb[:, :, :cn], w_in[:, c0:c0 + cn].rearrange("(kc p) c -> p kc c", p=128), writes=[bwb])
            pt, bpt = C.psum(PS_MM)
            for kc in range(KC):
                T(lambda e, kc=kc: e.matmul(pt[:DEC, :cn], xT[:, kc, :], wb[:, kc, :cn], start=(kc == 0), stop=(kc == KC - 1)),
                  [bwb, b_xT], [bpt])
            A(lambda e: e.copy(raw[:, c0:c0 + cn], pt[:DEC, :cn]), [bpt], [b_raw])
        tmpc, b_tmpc = C.sb("tmpc", [DEC, 512], F32)
        for ci, c0 in enumerate(range(0, CD, 512)):
            cn = min(512, CD - c0)
            h_, bh_ = hst[ci % 2]
            w_, bw_ = cwr[ci % 2]
            b_, bb_ = cbr[ci % 2]
            for k in range(3):
                S.dma("sync", h_[:, k, :cn], st_sconv[:, k * CD + c0:k * CD + c0 + cn], writes=[bh_])
            for k in range(4):
                S.dma("sync", w_[:, k, :cn], convw_r[k:k + 1, c0:c0 + cn].partition_broadcast(DEC), writes=[bw_])
            S.dma("sync", b_[:, :cn], convb_r[:, c0:c0 + cn].partition_broadcast(DEC), writes=[bb_])
            xn_ = raw[:, cfg.o_xbc + c0:cfg.o_xbc + c0 + cn]
            V(lambda e: e.tensor_tensor(xc[:, c0:c0 + cn], xn_, w_[:, 3, :cn], ALU.mult), [b_raw, bw_], [b_xc])
            for k in range(3):
                V(lambda e, k=k: e.tensor_tensor(tmpc[:, :cn], h_[:, k, :cn], w_[:, k, :cn], ALU.mult), [bh_, bw_], [b_tmpc])
                V(lambda e: e.tensor_tensor(xc[:, c0:c0 + cn], xc[:, c0:c0 + cn], tmpc[:, :cn], ALU.add), [b_xc, b_tmpc], [b_xc])
            V(lambda e: e.tensor_tensor(xc[:, c0:c0 + cn], xc[:, c0:c0 + cn], b_[:, :cn], ALU.add), [b_xc, bb_], [b_xc])
            for k in range(1, 3):
                S.dma("sync", sconv_s[:, (k - 1) * CD + c0:(k - 1) * CD + c0 + cn], h_[:, k, :cn], reads=[bh_], sembuf=bh_)
        A(lambda e: e.activation(xc[:], xc[:], AF.Silu), [b_xc], [b_xc])
        S.dma("sync", sconv_s[:, 2 * CD:3 * CD], raw[:, cfg.o_xbc:cfg.o_xbc + CD], reads=[b_raw], sembuf=b_raw)
        tb = {}
        for nm, src in (("dtb", dtb_d), ("atab", alog_d), ("dskip", dskip_d)):
            tb[nm] = C.sb("s_" + nm, [DEC, HS], F32)
            S.dma("sync", tb[nm][0][:], src.partition_broadcast(DEC), writes=[tb[nm][1]])
        ssmn, b_ssmn = C.sb("s_ssmn", [DEC, DI], F32)
        S.dma("sync", ssmn[:], ssmn_d.partition_broadcast(DEC), writes=[b_ssmn])
        atab, b_atab = tb["atab"]
        A(lambda e: e.activation(atab[:], atab[:], AF.Exp), [b_atab], [b_atab])
        V(lambda e: e.tensor_scalar(atab[:], atab[:], -1.0, None, op0=ALU.mult), [b_atab], [b_atab])
        sm = {nm: C.sb("ss_" + nm, [DEC, HS], F32) for nm in ("dt", "ax", "ex", "dec")}
        f2 = lambda nm: sm[nm][0][:]
        b2 = lambda nm: sm[nm][1]
        V(lambda e: e.tensor_tensor(f2("dt"), raw[:, cfg.o_dt:cfg.o_dt + HS], tb["dtb"][0][:], ALU.add), [b_raw, tb["dtb"][1]], [b2("dt")])
        V(lambda e: e.tensor_scalar(f2("ax"), f2("dt"), -1.0, None, op0=ALU.mult), [b2("dt")], [b2("ax")])
        V(lambda e: e.tensor_tensor(f2("ax"), f2("ax"), f2("dt"), ALU.min), [b2("dt"), b2("ax")], [b2("ax")])
        A(lambda e: e.activation(f2("ex"), f2("ax"), AF.Exp), [b2("ax")], [b2("ex")])
        A(lambda e: e.activation(f2("ex"), f2("ex"), AF.Ln, bias=1.0), [b2("ex")], [b2("ex")])
        V(lambda e: e.tensor_scalar(f2("dt"), f2("dt"), 0.0, None, op0=ALU.max), [b2("dt")], [b2("dt")])
        V(lambda e: e.tensor_tensor(f2("dt"), f2("dt"), f2("ex"), ALU.add), [b2("dt"), b2("ex")], [b2("dt")])
        V(lambda e: e.tensor_tensor(f2("dec"), f2("dt"), atab[:], ALU.mult), [b2("dt"), b_atab], [b2("dec")])
        A(lambda e: e.activation(f2("dec"), f2("dec"), AF.Exp), [b2("dec")], [b2("dec")])
        xdr, b_xdr = C.sb("xdr", [DEC, DI], F32)
        der, b_der = C.sb("der", [DEC, DI], F32)
        xs3 = xc[:, 0:DI].rearrange("p (h d) -> p h d", h=HS)
        V(lambda e: e.tensor_tensor(xdr[:].rearrange("p (h d) -> p h d", h=HS), xs3, f2("dt").unsqueeze(2).to_broadcast([DEC, HS, 64]), ALU.mult),
          [b_xc, b2("dt")], [b_xdr])
        V(lambda e: e.tensor_copy(der[:].rearrange("p (h d) -> p h d", h=HS), f2("dec").unsqueeze(2).to_broadcast([DEC, HS, 64])), [b2("dec")], [b_der])
        NC16 = DI // 128
        xdc, b_xdc = C.sb("xdc", [128, NC16, DEC], F32)
        dcc, b_dcc = C.sb("dcc", [128, NC16, DEC], F32)
        ycol, b_ycol = C.sb("ycol", [128, NC16, DEC], F32)
        for c in range(NC16):
            transposeF(xdc[:, c, :], xdr[:, c * 128:(c + 1) * 128], DEC, 128, b_xdr, b_xdc)
            transposeF(dcc[:, c, :], der[:, c * 128:(c + 1) * 128], DEC, 128, b_der, b_dcc)
        stb = [C.sb("stb%d" % i, [128, NC16, 128], F32) for i in range(2)]
        Bb, b_Bb = C.sb("Bbs", [128, 512], F32)
        Cb, b_Cb = C.sb("Cbs", [128, 512], F32)
        tmp2, b_tmp2 = C.sb("tmp2s", [128, 128], F32)
        V(lambda e: e.memset(ycol[:], 0.0), [], [b_ycol])
        for s_ in range(DEC):
            st_, bst_ = stb[s_ % 2]
            S.dma("sync", st_[:], st_ssm[s_].rearrange("(c p) n -> p c n", p=128), writes=[bst_])
            selm, b_selm = C.sb("selm%d" % s_, [DEC, 128], F32)
            V(lambda e: e.tensor_copy(selm[:], identf[:DEC, s_:s_ + 1].to_broadcast([DEC, 128])), [b_identf], [b_selm])
            for (dst, bd, o) in ((Bb, b_Bb, DI), (Cb, b_Cb, DI + 512)):
                pt, bpt = C.psum(PS_MM)
                T(lambda e: e.matmul(pt[:, :512], selm[:], xc[:, o:o + 512], start=True, stop=True), [b_selm, b_xc], [bpt])
                V(lambda e: e.tensor_copy(dst[:], pt[:, :512]), [bpt], [bd])
            for c in range(NC16):
                g = (2 * c) // HPGS
                V(lambda e: e.tensor_scalar(st_[:, c, :], st_[:, c, :], dcc[:, c, s_:s_ + 1], None, op0=ALU.mult), [bst_, b_dcc], [bst_])
                V(lambda e: e.scalar_tensor_tensor(st_[:, c, :], Bb[:, g * 128:(g + 1) * 128], xdc[:, c, s_:s_ + 1], st_[:, c, :],
                                                   op0=ALU.mult, op1=ALU.add), [b_Bb, b_xdc, bst_], [bst_])
                V(lambda e: e.tensor_tensor(tmp2[:], st_[:, c, :], Cb[:, g * 128:(g + 1) * 128], ALU.mult), [bst_, b_Cb], [b_tmp2])
                V(lambda e: e.reduce_sum(ycol[:, c, s_:s_ + 1], tmp2[:], axis=AX.X), [b_tmp2], [b_ycol])
            S.dma("sync", ssm_s[s_].rearrange("(c p) n -> p c n", p=128), st_[:], reads=[bst_], sembuf=bst_)
        yr, b_yr = C.sb("yrs", [DEC, DI], F32)
        for c in range(NC16):
            transposeF(yr[:, c * 128:(c + 1) * 128], ycol[:, c, :], 128, DEC, b_ycol, b_yr)
        V(lambda e: e.tensor_tensor(xdr[:].rearrange("p (h d) -> p h d", h=HS), xs3, tb["dskip"][0][:].unsqueeze(2).to_broadcast([DEC, HS, 64]), ALU.mult),
          [b_xc, tb["dskip"][1]], [b_xdr])
        V(lambda e: e.tensor_tensor(yr[:], yr[:], xdr[:], ALU.add), [b_yr, b_xdr], [b_yr])
        A(lambda e: e.activation(der[:], raw[:, 0:DI], AF.Silu), [b_raw], [b_der])
        V(lambda e: e.tensor_tensor(yr[:], yr[:], der[:], ALU.mult), [b_yr, b_der], [b_yr])
        for g in range(4):
            A(lambda e, g=g: e.activation(jk[:, :GW], yr[:, g * GW:(g + 1) * GW], AF.Square, accum_out=stt[R_, 4 + g:5 + g]), [b_yr], [b_jk, b_stt])
        A(lambda e: e.activation(stt[R_, 8:12], stt[R_, 4:8], AF.Sqrt, scale=1.0 / GW, bias=EPS), [b_stt], [b_stt])
        V(lambda e: e.reciprocal(stt[R_, 12:16], stt[R_, 8:12]), [b_stt], [b_stt])
        for g in range(4):
            V(lambda e, g=g: e.scalar_tensor_tensor(yr[:, g * GW:(g + 1) * GW], yr[:, g * GW:(g + 1) * GW], stt[R_, 12 + g:13 + g], ssmn[:, g * GW:(g + 1) * GW],
                                                    op0=ALU.mult, op1=ALU.mult), [b_yr, b_stt, b_ssmn], [b_yr])
        S.dma("sync", ycs_d[:, 0:DI], yr[:], reads=[b_yr], writes=[bycs], sembuf=b_yr)
        rp, b_rp = C.sb("rps", [DEC, 2, 32], F32)
        S.dma("sync", rp[:].rearrange("p a c -> p (a c)"), rope_s.rearrange("a c -> (a c)").unsqueeze(0).partition_broadcast(DEC) if False else rope_s.rearrange("(o a) c -> o (a c)", o=1).partition_broadcast(DEC), writes=[b_rp])
        qk, b_qk = C.sb("qks", [DEC, NH * 64 + 768], F32)
        ta, b_ta = C.sb("tas", [DEC, (NH + 6) * 32], F32)
        tb2, b_tb2 = C.sb("tbs", [DEC, (NH + 6) * 32], F32)

        def rope_rows(dst, src, nh):
            s3 = src.rearrange("p (g d) -> p g d", g=nh)
            d3 = dst.rearrange("p (g d) -> p g d", g=nh)
            cs = rp[:, 0, :].unsqueeze(1).to_broadcast([DEC, nh, 32])
            sn = rp[:, 1, :].unsqueeze(1).to_broadcast([DEC, nh, 32])
            a3 = ta[:, :nh * 32].rearrange("p (g d) -> p g d", g=nh)
            b3 = tb2[:, :nh * 32].rearrange("p (g d) -> p g d", g=nh)
            V(lambda e: e.tensor_tensor(a3, s3[:, :, 0:32], cs, ALU.mult), [b_raw, b_rp], [b_ta])
            V(lambda e: e.tensor_tensor(b3, s3[:, :, 32:64], sn, ALU.mult), [b_raw, b_rp], [b_tb2])
            V(lambda e: e.tensor_tensor(d3[:, :, 0:32], a3, b3, ALU.subtract), [b_ta, b_tb2], [b_qk])
            V(lambda e: e.tensor_tensor(a3, s3[:, :, 32:64], cs, ALU.mult), [b_raw, b_rp], [b_ta])
            V(lambda e: e.tensor_tensor(b3, s3[:, :, 0:32], sn, ALU.mult), [b_raw, b_rp], [b_tb2])
            V(lambda e: e.tensor_tensor(d3[:, :, 32:64], a3, b3, ALU.add), [b_ta, b_tb2], [b_qk])

        rope_rows(qk[:, 0:NH * 64], raw[:, cfg.o_q:cfg.o_q + NH * 64], NH)
        for br in range(3):
            o = cfg.o_kv + br * 256
            rope_rows(qk[:, NH * 64 + br * 256:NH * 64 + br * 256 + 128], raw[:, o:o + 128], 2)
            V(lambda e: e.tensor_copy(qk[:, NH * 64 + br * 256 + 128:NH * 64 + (br + 1) * 256], raw[:, o + 128:o + 256]), [b_raw], [b_qk])
        S.dma("sync", kv_s, qk[:, NH * 64:], reads=[b_qk], sembuf=b_qk)
        kvb, b_kvb = C.sb("kvbs", [DEC, NH * 64 + 768], BF16)
        V(lambda e: e.tensor_copy(kvb[:], qk[:]), [b_qk], [b_kvb])
        S.dma("sync", vn_d, kvb[:, NH * 64:], reads=[b_kvb], writes=[bvn], sembuf=b_kvb)
        gsr, b_gsr = C.sb("gsr", [DEC, 3 * NH], F32)
        A(lambda e: e.activation(gsr[:], raw[:, cfg.o_g:cfg.o_g + 3 * NH], AF.Sigmoid), [b_raw], [b_gsr])
        S.dma("sync", gss_d, gsr[:], reads=[b_gsr], writes=[bgss], sembuf=b_gsr)
        qT_d = dscr("qTs_d", [128, HPG + 3, DEC], BF16)
        bqT = Buf("sscr_qT")
        qTa, b_qTa = C.sb("qTa", [128, HPG + 3, DEC], BF16)
        q4 = kvb[:, 0:NH * 64].rearrange("p (g h d) -> p g h d", g=2, h=HPG)
        qpair, b_qpair = C.sb("qpair", [DEC, 128], BF16)
        for hh in range(HPG + 3):
            if hh < HPG:
                for g in range(2):
                    V(lambda e, g=g: e.tensor_copy(qpair[:, g * 64:(g + 1) * 64], q4[:, g, hh, :]), [b_kvb], [b_qpair])
                src, bsrc = qpair[:], b_qpair
            else:
                o = NH * 64 + (hh - HPG) * 256
                src, bsrc = kvb[:, o:o + 128], b_kvb
            pt, bpt = C.psum(PS_TR)
            ptb = pt[:].bitcast(BF16)
            T(lambda e: e.transpose(ptb[:, 0:DEC], src, identb[:DEC, :DEC]), [bsrc, b_identb], [bpt])
            A(lambda e: e.copy(qTa[:, hh, :], ptb[:, 0:DEC]), [bpt], [b_qTa])
        S.dma("sync", qT_d, qTa[:], reads=[b_qTa], writes=[bqT], sembuf=b_qTa)

    exms_d = din("exms", [NSL, PAST], BF16)
    iota_d = din("iota_p", [128, NPG], F32)
    mcs_d = din("maskcs", [128, NBS])
    with Phase():
        crK, b_crK = C.sb("crK", [128, PAST], BF16)
        crV, b_crV = C.sb("crV", [128, PAST], BF16)
        KTs, b_KTs = C.sb("sKTs", [128, PAST + 128], BF16)
        Vs, b_Vs = C.sb("sVs", [128, NPG + 1, 2, 65], BF16)
        KTw, b_KTw = C.sb("sKTw", [128, (NWT + 1) * 128], BF16)
        Vw, b_Vw = C.sb("sVw", [128, NWT + 1, 2, 65], BF16)
        w1b = [C.sb("sw1b%d" % i, [128, 32, 128], BF16) for i in range(2)]
        hidT, b_hidT = C.sb("shidT", [128, 2, 2, NBS * 128], BF16)
        kcT, b_kcT = C.sb("skcT", [128, NBS * 128], BF16)
        vca, b_vca = C.sb("svca", [128, NBS, 2, CWs], BF16)
        ovt, b_ovt = C.sb("sovt", [128, NBS, NSL + 1], F32)
        exms, b_exms = C.sb("sexms", [NSL, PAST], BF16)
        mcs, b_mcs = C.sb("smcs", [128, NBS], F32)
        cpe, b_cpe = C.sb("scpe", [128, 2, 32], BF16)
        cb1, b_cb1 = C.sb("scb1", [128, 2], F32)
        b1e, b_b1e = C.sb("sb1e", [128, 2], F32)
        cw2k, b_cw2k = C.sb("scw2k", [128, 2, 128], BF16)
        cw2v, b_cw2v = C.sb("scw2v", [128, 64], BF16)
        cb2k, b_cb2k = C.sb("scb2k", [128, 1], F32)
        cb2v, b_cb2v = C.sb("scb2v", [128, 64], F32)
        qTb, b_qTb = C.sb("sqTb", [128, HPG + 3, DEC], BF16)
        fbs, b_fbs = C.sb("sfbs", [1, NSL + 1], F32)
        e0, b_e0 = C.sb("se0", [128, 1], F32)
        ne0, b_ne0 = C.sb("sne0", [128, 1], F32)
        one1, b_one1 = C.sb("sone1", [128, 1], F32)
        iot, b_iot = C.sb("siot", [128, NPG], F32)
        idxf, b_idxf = C.sb("sidxf", [128, NPG], F32)
        idx, b_idx = C.sb("sidx", [128, NPG], I32)
        bc3_ = Buf("constsS")
        for (dst, src) in ((ovt, ovs_d.rearrange("(b p) j -> p b j", p=128)), (exms, exms_d), (mcs, mcs_d), (cb1, cb1_d), (cb2k, cb2k_d),
                           (cb2v, cb2v_d.partition_broadcast(128)), (fbs, fbs_d), (qTb, qT_d)):
            S.dma("sync", dst[:], src, reads=[bqT], writes=[bc3_])
        for (dst, src) in ((cpe, cpe_d.rearrange("k p l -> p k l")), (cw2k, cw2k_d.rearrange("g c m -> c g m")), (cw2v, cw2v_d)):
            S.dma("gpsimd", dst[:], src, writes=[bc3_])
        for kvi in range(2):
            S.dma("gpsimd", w1b[kvi][0][:].rearrange("p l c -> p (l c)"), cw1_d[kvi], writes=[bc3_])
        b_ovt = b_exms = b_mcs = b_cb1 = b_cb2k = b_cb2v = b_fbs = b_qTb = b_cpe = b_cw2k = b_cw2v = bc3_
        V(lambda e: e.tensor_copy(e0[:], identf[:, 0:1]), [b_identf], [b_e0])
        V(lambda e: e.tensor_scalar(ne0[:], identf[:, 0:1], -1.0, 1.0, op0=ALU.mult, op1=ALU.add), [b_identf], [b_ne0])
        V(lambda e: e.memset(one1[:], 1.0), [], [b_one1])
        S.dma("sync", iot[:], iota_d, writes=[b_iot])
        for kvi in range(2):
            w1v = w1b[kvi][0]
            pb1, bpb1 = C.psum(PS_MM)
            for l in range(32):
                T(lambda e, l=l: e.matmul(pb1[:, 0:1], w1v[0:64, l, :], cpe[0:64, kvi, l:l + 1], start=(l == 0), stop=(l == 31)),
                  [bc3_], [bpb1])
            V(lambda e: e.tensor_tensor(b1e[:, kvi:kvi + 1], pb1[:, 0:1], cb1[:, kvi:kvi + 1], ALU.add), [bpb1, b_cb1], [b_b1e])
        V(lambda e: e.memset(vca[:], 0.0), [], [b_vca])
        V(lambda e: e.memset(vca[:, :, :, 64:65], 1.0), [], [b_vca])
        for g in range(2):
            V(lambda e, g=g: e.tensor_copy(vca[:, :, g, 65:CWs], ovt[:]), [b_ovt], [b_vca])
        V(lambda e: e.memset(hidT[:], 0.0), [], [b_hidT])
        V(lambda e: e.memset(kcT[:], 0.0), [], [b_kcT])
        pg = [C.sb("spg%d" % i, [128, 256], F32) for i in range(4)]
        pgb = [C.sb("spgb%d" % i, [128, 256], BF16) for i in range(4)]
        Es, b_Es = C.sb("sEs", [128, HPG], F32)
        PTs = [C.sb("sPTs%d" % i, [128, HPG], BF16) for i in range(2)]
        pmS, b_pmS = C.sb("spmS", [128, 1], F32)
        nst, b_nst = C.sb("snst", [HPG, 4], F32)
        gt, b_gt = C.sb("sgt", [HPG, 3], F32)
        oac = [C.sb("soac%d" % i, [HPG, 64], F32) for i in range(2)]
        otm, b_otm = C.sb("sotm", [HPG, 64], F32)
        impu, b_impu = C.sb("simpu", [HPG, NSL + 1], F32)
        wk1, b_wk1 = C.sb("swk1", [1, NSL + 1], F32)
        wk2, b_wk2 = C.sb("swk2", [1, NSL + 1], F32)
        mx8, b_mx8 = C.sb("smx8", [1, 8], F32)
        selr, b_selr = C.sb("sselr", [1, NSL], F32)
        selTb, b_selTb = C.sb("sselTb", [128, 1], BF16)
        rot = [0]

        def page_in(src_ap, idx_col, k, use_idx=True):
            pb_, bpb_ = pgb[k % 4]
            if use_idx:
                S.dma_custom("gpsimd", lambda e: e.indirect_dma_start(out=pb_[:], out_offset=None, in_=src_ap,
                                                                      in_offset=bass.IndirectOffsetOnAxis(ap=idx_col, axis=0)), [b_idx], [bpb_])
            else:
                S.dma("gpsimd", pb_[:], src_ap, writes=[bpb_])
            return pb_, bpb_

        def tr_to(dst, b_dst, src, b_src):
            pt, bpt = C.psum(PS_TR)
            ptb = pt[:].bitcast(BF16)
            T(lambda e: e.transpose(ptb[:, 0:128], src, identb[:]), [b_src, b_identb], [bpt])
            A(lambda e: e.copy(dst, ptb[:, 0:128]), [bpt], [b_dst])

        NTM = NPG + 1
        EsA, b_EsA = C.sb("sEsA", [128, NTM * HPG], F32)
        PTA, b_PTA = C.sb("sPTA", [128, NTM * HPG], BF16)
        mskA, b_mskA = C.sb("smskA", [128, NTM], F32)
        mskW, b_mskW = C.sb("smskW", [128, NWT + 1], F32)
        V(lambda e: e.memset(mskW[:], 1.0), [], [b_mskW])
        if NWT * 128 >= cfg.WINDOW:
            V(lambda e: e.tensor_copy(mskW[:, 0:1], ne0[:]), [b_ne0], [b_mskW])
        V(lambda e: e.tensor_copy(mskW[:, NWT:NWT + 1], e0[:]), [b_e0], [b_mskW])

        def branch(KT, bKT, nt, msk, bmsk, g, s_, per, rhs_fn, bVv, pso, bpso):
            gsl = slice(g * 64, (g + 1) * 64)
            for c0 in range(0, nt, 64):
                cnt = min(64, nt - c0)
                ps_, bps_ = C.psum((0, 1, 2))
                for j in range(cnt):
                    kt = c0 + j
                    T(lambda e, j=j, kt=kt: e.matmul(ps_[:, j * HPG:(j + 1) * HPG], KT[gsl, kt * 128:(kt + 1) * 128], qTb[gsl, 0:HPG, s_],
                                                     start=True, stop=True), [bKT, b_qTb], [bps_])
                V(lambda e: e.tensor_copy(EsA[:, c0 * HPG:(c0 + cnt) * HPG], ps_[:, :cnt * HPG]), [bps_], [b_EsA])
            A(lambda e: e.activation(EsA[:, :nt * HPG], EsA[:, :nt * HPG], AF.Exp, scale=0.125), [b_EsA], [b_EsA])
            V(lambda e: e.tensor_tensor(PTA[:, :nt * HPG].rearrange("p (t h) -> p t h", h=HPG), EsA[:, :nt * HPG].rearrange("p (t h) -> p t h", h=HPG),
                                        msk.unsqueeze(2).to_broadcast([128, nt, HPG]), ALU.mult), [b_EsA, bmsk], [b_PTA])
            for kt in range(nt):
                T(lambda e, kt=kt: e.matmul(pso[:HPG, :per], PTA[:, kt * HPG:(kt + 1) * HPG], rhs_fn(kt), start=(kt == 0), stop=(kt == nt - 1)),
                  [b_PTA, bVv], [bpso])

        def evac_s(pso, bpso, gi_, oa, boa, first):
            V(lambda e: e.tensor_scalar(nst[:, 0:1], pso[:HPG, 64:65], 1e-30, None, op0=ALU.max), [bpso], [b_nst])
            V(lambda e: e.reciprocal(nst[:, 1:2], nst[:, 0:1]), [b_nst], [b_nst])
            V(lambda e: e.tensor_tensor(nst[:, 2:3], nst[:, 1:2], gt[:, gi_:gi_ + 1], ALU.mult), [b_nst, b_gt], [b_nst])
            if first:
                V(lambda e: e.tensor_scalar(oa[:], pso[:HPG, 0:64], nst[:, 2:3], None, op0=ALU.mult), [bpso, b_nst], [boa])
            else:
                V(lambda e: e.scalar_tensor_tensor(oa[:], pso[:HPG, 0:64], nst[:, 2:3], oa[:], op0=ALU.mult, op1=ALU.add), [bpso, b_nst, boa], [boa])

        for s_ in range(DEC):
            S.dma("sync", idx[:], ptab[s_:s_ + 1, :].partition_broadcast(128), writes=[b_idx])
            V(lambda e: e.tensor_copy(idxf[:], idx[:]), [b_idx], [b_idxf])
            V(lambda e: e.scalar_tensor_tensor(idxf[:], idxf[:], 128.0, iot[:], op0=ALU.mult, op1=ALU.add), [b_idxf, b_iot], [b_idxf])
            V(lambda e: e.tensor_copy(idx[:], idxf[:]), [b_idxf], [b_idx])
            V(lambda e: e.memset(KTs[:, PAST:PAST + 128], 0.0), [], [b_KTs])
            V(lambda e: e.memset(KTw[:, NWT * 128:(NWT + 1) * 128], 0.0), [], [b_KTw])
            V(lambda e: e.memset(Vs[:, NPG, :, :], 0.0), [], [b_Vs])
            V(lambda e: e.memset(Vw[:, NWT, :, :], 0.0), [], [b_Vw])
            kk = 0
            for j in range(NPG):
                pb_, bpb_ = page_in(c_cmp, idx[:, j:j + 1], kk)
                kk += 1
                tr_to(crK[:, j * 128:(j + 1) * 128], b_crK, pb_[:, 0:128], bpb_)
                tr_to(crV[:, j * 128:(j + 1) * 128], b_crV, pb_[:, 128:256], bpb_)
                pb_, bpb_ = page_in(c_slc, idx[:, j:j + 1], kk)
                kk += 1
                tr_to(KTs[:, j * 128:(j + 1) * 128], b_KTs, pb_[:, 0:128], bpb_)
                V(lambda e: e.tensor_copy(Vs[:, j, :, 0:64], pb_[:, 128:256].rearrange("p (g d) -> p g d", g=2)), [bpb_], [b_Vs])
            V(lambda e: e.memset(Vs[:, 0:NPG, :, 64:65], 1.0), [b_Vs], [b_Vs])
            for j in range(NWT):
                pb_, bpb_ = page_in(c_win[s_, j * 128:(j + 1) * 128, :], None, kk, use_idx=False)
                kk += 1
                tr_to(KTw[:, j * 128:(j + 1) * 128], b_KTw, pb_[:, 0:128], bpb_)
                V(lambda e: e.tensor_copy(Vw[:, j, :, 0:64], pb_[:, 128:256].rearrange("p (g d) -> p g d", g=2)), [bpb_], [b_Vw])
            V(lambda e: e.memset(Vw[:, 0:NWT, :, 64:65], 1.0), [b_Vw], [b_Vw])
            V(lambda e: e.tensor_copy(KTs[:, PAST:PAST + 1], qTb[:, HPG + 1, s_:s_ + 1]), [b_qTb], [b_KTs])
            V(lambda e: e.tensor_copy(KTw[:, NWT * 128:NWT * 128 + 1], qTb[:, HPG + 2, s_:s_ + 1]), [b_qTb], [b_KTw])
            for g in range(2):
                S.dma("sync", Vs[0:1, NPG, g, 0:64], vn_d[s_:s_ + 1, 256 + 128 + g * 64:256 + 128 + (g + 1) * 64], reads=[bvn], writes=[b_Vs])
                S.dma("sync", Vw[0:1, NWT, g, 0:64], vn_d[s_:s_ + 1, 512 + 128 + g * 64:512 + 128 + (g + 1) * 64], reads=[bvn], writes=[b_Vw])
            V(lambda e: e.memset(Vs[0:1, NPG, :, 64:65], 1.0), [b_Vs], [b_Vs])
            V(lambda e: e.memset(Vw[0:1, NWT, :, 64:65], 1.0), [b_Vw], [b_Vw])
            for kvi, (cr, bcr) in enumerate(((crK, b_crK), (crV, b_crV))):
                w1v = w1b[kvi][0]
                for g in range(2):
                    gsl2 = slice(g * 64, (g + 1) * 64)
                    ph, bph = C.psum(PS_MM)
                    for l in range(32):
                        T(lambda e, l=l: e.matmul(ph[:, :NCS], w1v[gsl2, l, :], cr[gsl2, l:l + 16 * (NCS - 1) + 1:16],
                                                  start=(l == 0), stop=(l == 31)), [bc3_, bcr], [bph])
                    A(lambda e: e.activation(hidT[:, kvi, g, 0:NCS], ph[:, :NCS], AF.Silu, bias=b1e[:, kvi:kvi + 1]), [bph, b_b1e], [b_hidT])
            pk, bpk = C.psum(PS_MM)
            for g in range(2):
                T(lambda e, g=g: e.matmul(pk[:, :NCS], cw2k[:, g, :], hidT[:, 0, g, 0:NCS], start=(g == 0), stop=(g == 1)), [b_cw2k, b_hidT], [bpk])
            A(lambda e: e.activation(kcT[:, 0:NCS], pk[:, :NCS], AF.Identity, bias=cb2k[:, 0:1]), [bpk, b_cb2k], [b_kcT])
            for bt in range(NBS):
                pv2, bpv2 = C.psum(PS_MM)
                for g in range(2):
                    T(lambda e, g=g: e.matmul(pv2[:, g * 64:(g + 1) * 64], hidT[:, 1, g, bt * 128:(bt + 1) * 128], cw2v[:], start=True, stop=True),
                      [b_hidT, b_cw2v], [bpv2])
                V(lambda e: e.tensor_tensor(vca[:, bt, :, 0:64], pv2[:, 0:128].rearrange("p (g d) -> p g d", g=2),
                                            cb2v[:].unsqueeze(1).to_broadcast([128, 2, 64]), ALU.add), [bpv2, b_cb2v], [b_vca])
            for g in range(2):
                oa, boa = oac[g]
                S.dma("sync", gt[:], gss_d[s_, g * HPG * 3:(g + 1) * HPG * 3].rearrange("(h c) -> h c", c=3), reads=[bgss], writes=[b_gt])
                pso, bpso = C.ps[3]
                branch(kcT, b_kcT, NBS, mcs[:, 0:NBS], b_mcs, g, s_, CWs, lambda kt: vca[:, kt, g, :], b_vca, pso, bpso)
                evac_s(pso, bpso, 0, oa, boa, True)
                V(lambda e: e.tensor_copy(impu[:], pso[:HPG, 65:CWs]), [bpso], [b_impu])
                pi, bpi = C.psum(PS_MM)
                T(lambda e: e.matmul(pi[:1, :NSL + 1], nst[:, 1:2], impu[:], start=True, stop=True), [b_nst, b_impu], [bpi])
                V(lambda e: e.tensor_tensor(wk1[:], pi[:1, :NSL + 1], fbs[:], ALU.add), [bpi, b_fbs], [b_wk1])
                V(lambda e: e.tensor_copy(wk2[:], wk1[:]), [b_wk1], [b_wk2])
                for r_ in range((cfg.TOPN - 1) // 8):
                    V(lambda e: e.max(out=mx8[:], in_=wk2[:]), [b_wk2], [b_mx8])
                    V(lambda e: e.match_replace(out=wk2[:], in_to_replace=mx8[:], in_values=wk2[:], imm_value=-3.0e38), [b_mx8, b_wk2], [b_wk2])
                V(lambda e: e.max(out=mx8[:], in_=wk2[:]), [b_wk2], [b_mx8])
                kth = (cfg.TOPN - 1) % 8
                V(lambda e: e.tensor_scalar(selr[:], wk1[:, 0:NSL], mx8[:, kth:kth + 1], None, op0=ALU.is_ge), [b_wk1, b_mx8], [b_selr])
                pt_, bpt_ = C.psum(PS_MM)
                T(lambda e: e.transpose(pt_[:NSL, 0:1], selr[:], identf[0:1, 0:1]), [b_selr, b_identf], [bpt_])
                V(lambda e: e.tensor_copy(selTb[:NSL, :], pt_[:NSL, 0:1]), [bpt_], [b_selTb])
                pso, bpso = C.ps[6]

                pm, bpm = C.psum(PS_TR)
                for kt in range(NPG):
                    T(lambda e, kt=kt: e.matmul(pm[:, kt:kt + 1], exms[:, kt * 128:(kt + 1) * 128], selTb[:NSL, :], start=True, stop=True),
                      [b_exms, b_selTb], [bpm])
                V(lambda e: e.tensor_copy(mskA[:, 0:NPG], pm[:, 0:NPG]), [bpm], [b_mskA])
                V(lambda e: e.tensor_copy(mskA[:, NPG:NPG + 1], e0[:]), [b_e0], [b_mskA])
                branch(KTs, b_KTs, NPG + 1, mskA[:, 0:NPG + 1], b_mskA, g, s_, 65, lambda kt: Vs[:, kt, g, :], b_Vs, pso, bpso)
                evac_s(pso, bpso, 1, oa, boa, False)
                pso, bpso = C.ps[7]

                branch(KTw, b_KTw, NWT + 1, mskW[:, 0:NWT + 1], b_mskW, g, s_, 65, lambda kt: Vw[:, kt, g, :], b_Vw, pso, bpso)
                evac_s(pso, bpso, 2, oa, boa, False)
                S.dma("sync", ycs_d[s_, DI + g * HPG * 64:DI + (g + 1) * HPG * 64].rearrange("(h d) -> h d", d=64), oa[:],
                      reads=[boa], writes=[bycs], sembuf=boa)

    with Phase():
        hs, b_hs = C.sb("hs3", [DEC, D], F32)
        stt, b_stt = C.sb("stt3", [128, 16], F32)
        jk, b_jk = C.sb("jk3", [DEC, D], F32)
        xb, b_xb = C.sb("xb3", [DEC, D], BF16)
        xT, b_xT = C.sb("xT3", [128, KC, DEC], BF16)
        KO = max(DMIX // 128, DFF // 128)
        aT, b_aT = C.sb("aT3", [128, KO, DEC], BF16)
        wbs = [C.sb("wb3_%d" % i, [128, KO, 512], BF16) for i in range(2)]
        mix, b_mix = C.sb("mix3", [DEC, D], F32)
        yc, b_yc = C.sb("yc3", [DEC, DMIX], F32)
        ycb, b_ycb = C.sb("ycb3", [DEC, DMIX], BF16)
        ur, b_ur = C.sb("ur3", [DEC, 2 * DFF], F32)
        fhc = [C.sb("fh3_%d" % i, [DEC, 2, 512], F32) for i in range(2)]
        fw, b_fw = C.sb("fw3", [DEC, 3, 512], F32)
        fb_, b_fb = C.sb("fb3", [DEC, 512], F32)
        uc, b_uc = C.sb("uc3", [DEC, 2 * DFF], F32)
        t5, b_t5 = C.sb("t53", [DEC, 512], F32)
        hb, b_hb = C.sb("hb3", [DEC, DFF], BF16)
        rowt = {nm: C.sb("rw_" + nm, [DEC, w_], F32) for nm, w_ in (("bpw1", 2 * D), ("bpw2", D), ("dwb", D), ("lng", D), ("lnb", D))}
        for nm, src in (("bpw1", bpw1_r), ("bpw2", bpw2_r), ("dwb", dwb_r), ("lng", lng_r), ("lnb", lnb_r)):
            S.dma("sync", rowt[nm][0][:], src.partition_broadcast(DEC), writes=[rowt[nm][1]])
        wrot = [0]
        nrot = [0]

        def to_T(dst, b_dst, src_bf, b_src, n):
            for c in range(n):
                pt, bpt = C.psum(PS_TR)
                ptb = pt[:].bitcast(BF16)
                T(lambda e: e.transpose(ptb[:, 0:DEC], src_bf[:, c * 128:(c + 1) * 128], identb[:DEC, :DEC]), [b_src, b_identb], [bpt])
                A(lambda e: e.copy(dst[:, c, :], ptb[:, 0:DEC]), [bpt], [b_dst])

        def rows_mm(W, kcn, actT, b_actT, ncols, dst, b_dst, c_off=0):
            for c0 in range(0, ncols, 512):
                cn = min(512, ncols - c0)
                wb, bwb = wbs[wrot[0] % 2]
                wrot[0] += 1
                S.dma("gpsimd", wb[:, :kcn, :cn], W[:, c_off + c0:c_off + c0 + cn].rearrange("(kc p) c -> p kc c", p=128), writes=[bwb])
                pt, bpt = C.psum(PS_MM)
                for kc in range(kcn):
                    T(lambda e, kc=kc: e.matmul(pt[:DEC, :cn], actT[:, kc, :], wb[:, kc, :cn], start=(kc == 0), stop=(kc == kcn - 1)),
                      [bwb, b_actT], [bpt])
                A(lambda e: e.copy(dst[:, c0:c0 + cn], pt[:DEC, :cn]), [bpt], [b_dst])

        def norm_T(gain):
            load_gain(gain)
            rms_rows(xb[:], hs[:], b_hs, b_xb, stt, b_stt, jk, b_jk)
            to_T(xT, b_xT, xb, b_xb, KC)

        def post_add(gain):
            load_gain(gain)
            rms_rows(mix[:], mix[:], b_mix, b_mix, stt, b_stt, jk, b_jk)
            V(lambda e: e.tensor_tensor(hs[:], hs[:], mix[:], ALU.add), [b_hs, b_mix], [b_hs])

        def conv_ffn(layer):
            norm_T(4 * layer + 2)
            rows_mm(w_up[layer], KC, xT, b_xT, 2 * DFF, ur, b_ur)
            S.dma("sync", ffn_s[layer, :, 2 * DFF:4 * DFF], ur[:], reads=[b_ur], sembuf=b_ur)
            for ci, c0 in enumerate(range(0, 2 * DFF, 512)):
                cn = min(512, 2 * DFF - c0)
                fh, b_fh = fhc[ci % 2]
                for k in range(2):
                    S.dma("sync", fh[:, k, :cn], st_ffn[layer, :, k * 2 * DFF + c0:k * 2 * DFF + c0 + cn], writes=[b_fh])
                for k in range(3):
                    S.dma("sync", fw[:, k, :cn], fcw_r[layer, k:k + 1, c0:c0 + cn].partition_broadcast(DEC), writes=[b_fw])
                S.dma("sync", fb_[:, :cn], fcb_r[layer, :, c0:c0 + cn].partition_broadcast(DEC), writes=[b_fb])
                V(lambda e: e.tensor_tensor(uc[:, c0:c0 + cn], ur[:, c0:c0 + cn], fw[:, 2, :cn], ALU.mult), [b_ur, b_fw], [b_uc])
                for k in range(2):
                    V(lambda e, k=k: e.tensor_tensor(t5[:, :cn], fh[:, k, :cn], fw[:, k, :cn], ALU.mult), [b_fh, b_fw], [b_t5])
                    V(lambda e: e.tensor_tensor(uc[:, c0:c0 + cn], uc[:, c0:c0 + cn], t5[:, :cn], ALU.add), [b_uc, b_t5], [b_uc])
                V(lambda e: e.tensor_tensor(uc[:, c0:c0 + cn], uc[:, c0:c0 + cn], fb_[:, :cn], ALU.add), [b_uc, b_fb], [b_uc])
                S.dma("sync", ffn_s[layer, :, c0:c0 + cn], fh[:, 1, :cn], reads=[b_fh], sembuf=b_fh)
            A(lambda e: e.activation(uc[:, 0:DFF], uc[:, 0:DFF], AF.Gelu_apprx_tanh), [b_uc], [b_uc])
            V(lambda e: e.tensor_tensor(hb[:], uc[:, 0:DFF], uc[:, DFF:2 * DFF], ALU.mult), [b_uc], [b_hb])
            to_T(aT, b_aT, hb, b_hb, DFF // 128)
            rows_mm(w_down[layer], DFF // 128, aT, b_aT, D, mix, b_mix)
            post_add(4 * layer + 3)

        S.dma("sync", hs[:], xs_in, writes=[b_hs])
        S.dma("sync", yc[:], ycs_d, reads=[bycs], writes=[b_yc])
        V(lambda e: e.tensor_copy(ycb[:], yc[:]), [b_yc], [b_ycb])
        to_T(aT, b_aT, ycb, b_ycb, DMIX // 128)
        rows_mm(w_out, DMIX // 128, aT, b_aT, D, mix, b_mix)
        post_add(1)
        conv_ffn(0)
        norm_T(4)
        u2, b_u2 = C.sb("u23", [DEC, 2 * D], F32)
        rows_mm(w_pw1, KC, xT, b_xT, 2 * D, u2, b_u2)
        V(lambda e: e.tensor_tensor(u2[:], u2[:], rowt["bpw1"][0][:], ALU.add), [b_u2, rowt["bpw1"][1]], [b_u2])
        A(lambda e: e.activation(u2[:, D:2 * D], u2[:, D:2 * D], AF.Sigmoid), [b_u2], [b_u2])
        V(lambda e: e.tensor_tensor(u2[:, 0:D], u2[:, 0:D], u2[:, D:2 * D], ALU.mult), [b_u2], [b_u2])
        S.dma("sync", arow_d, u2[:, 0:D], reads=[b_u2], writes=[barow], sembuf=b_u2)
        ah = [C.sb("ah3_%d" % i, [31, D], F32) for i in range(2)]
        dwt, b_dwt = C.sb("dwt3", [31, D], F32)
        selc, b_selc = C.sb("selc3", [31, DEC, DEC], F32)
        S.dma("sync", dwt[:], dww_r, writes=[b_dwt])
        S.dma("sync", selc[:], selc_d.rearrange("s k m -> k s m"), writes=[b_selc])
        pcs = [C.psum(PS_MM) for _ in range((D + 511) // 512)]
        for s_ in range(DEC):
            a_, ba_ = ah[s_ % 2]
            S.dma("sync", a_[0:30, :], st_conf[s_], writes=[ba_])
            S.dma("sync", a_[30:31, :], arow_d[s_:s_ + 1, :], reads=[barow], writes=[ba_])
            S.dma("sync", conf_s[s_], a_[1:31, :], reads=[ba_], sembuf=ba_)
            V(lambda e: e.tensor_tensor(a_[:], a_[:], dwt[:], ALU.mult), [ba_, b_dwt], [ba_])
            for ci, c0 in enumerate(range(0, D, 512)):
                cn = min(512, D - c0)
                pc, bpc = pcs[ci]
                T(lambda e: e.matmul(pc[:DEC, :cn], selc[:, s_, :], a_[:, c0:c0 + cn], start=(s_ == 0), stop=(s_ == DEC - 1)), [b_selc, ba_], [bpc])
        cv, b_cv = C.sb("cv3", [DEC, D], F32)
        for ci, c0 in enumerate(range(0, D, 512)):
            cn = min(512, D - c0)
            pc, bpc = pcs[ci]
            V(lambda e: e.tensor_tensor(cv[:, c0:c0 + cn], pc[:DEC, :cn], rowt["dwb"][0][:, c0:c0 + cn], ALU.add), [bpc, rowt["dwb"][1]], [b_cv])
        V(lambda e: e.reduce_sum(stt[R_, 0:1], cv[:], axis=AX.X), [b_cv], [b_stt])
        V(lambda e: e.tensor_scalar(stt[R_, 1:2], stt[R_, 0:1], -1.0 / D, None, op0=ALU.mult), [b_stt], [b_stt])
        V(lambda e: e.tensor_scalar(cv[:], cv[:], stt[R_, 1:2], None, op0=ALU.add), [b_cv, b_stt], [b_cv])
        A(lambda e: e.activation(jk[:, :D], cv[:], AF.Square, accum_out=stt[R_, 2:3]), [b_cv], [b_jk, b_stt])
        A(lambda e: e.activation(stt[R_, 3:4], stt[R_, 2:3], AF.Sqrt, scale=1.0 / D, bias=EPS), [b_stt], [b_stt])
        V(lambda e: e.reciprocal(stt[R_, 4:5], stt[R_, 3:4]), [b_stt], [b_stt])
        V(lambda e: e.scalar_tensor_tensor(cv[:], cv[:], stt[R_, 4:5], rowt["lng"][0][:], op0=ALU.mult, op1=ALU.mult), [b_cv, b_stt, rowt["lng"][1]], [b_cv])
        V(lambda e: e.tensor_tensor(cv[:], cv[:], rowt["lnb"][0][:], ALU.add), [b_cv, rowt["lnb"][1]], [b_cv])
        A(lambda e: e.activation(xb[:], cv[:], AF.Silu), [b_cv], [b_xb])
        to_T(xT, b_xT, xb, b_xb, KC)
        rows_mm(w_pw2, KC, xT, b_xT, D, mix, b_mix)
        V(lambda e: e.tensor_tensor(mix[:], mix[:], rowt["bpw2"][0][:], ALU.add), [b_mix, rowt["bpw2"][1]], [b_mix])
        post_add(5)
        conv_ffn(1)
        S.dma("sync", y_s, hs[:], reads=[b_hs], sembuf=b_hs)


def _bf16():
    import ml_dtypes
    return ml_dtypes.bfloat16


def host_prep(cfg, inp, b, h):
    bf = _bf16()
    NT, D = cfg.NT, cfg.D
    m = {}
    x = np.zeros((NT * 128, D), np.float32)
    xp = inp["x_prompt"][b]
    if h == 1:
        x[:] = xp
    else:
        x[cfg.HALF:] = xp[:cfg.HALF]
    m["xin"] = x
    m["flag"] = np.full((128, 1), float(h), np.float32)
    m["normg"] = np.ascontiguousarray(inp["norm_g"].reshape(8, D))
    m["w_in"] = inp["ab_w_in"][0]
    NCT = cfg.CD // 128
    m["convwT"] = np.ascontiguousarray(inp["ab_conv_w"][0].T.reshape(NCT, 128, 4).transpose(1, 0, 2))
    m["convbT"] = np.ascontiguousarray(inp["ab_conv_b"][0].reshape(NCT, 128).T)
    m["dt_bias"] = np.ascontiguousarray(inp["ab_dt_bias"][0][None])
    m["a_log"] = np.ascontiguousarray(inp["ab_a_log"][0][None])
    m["d_skip"] = np.ascontiguousarray(inp["ab_d_skip"][0][None])
    m["ssm_norm"] = np.ascontiguousarray(inp["ab_ssm_norm"][0][None])
    pos = np.arange(NT * 128) + (h - 1) * cfg.HALF
    inv = (10000.0 ** (-np.arange(32, dtype=np.float32) / 32)).astype(np.float32)
    ang = pos.astype(np.float32)[:, None] * inv[None, :]
    m["rcos"] = np.cos(ang).astype(np.float32)
    m["rsin"] = np.sin(ang).astype(np.float32)
    m["identb"] = np.eye(128, dtype=np.float32).astype(bf)
    m["identf"] = np.eye(128, dtype=np.float32)
    m["utri"] = np.triu(np.ones((128, 128), np.float32))
    kk = np.arange(128)[:, None]
    ii = np.arange(512)[None, :] % 128
    m["negm"] = np.where(kk > ii, -30000.0, 0.0).astype(np.float32).astype(bf)
    w1 = inp["ab_cmp_w1"][0]
    w1r = w1.reshape(2, 32, 64, 128).transpose(0, 2, 1, 3)
    m["cmp_w1"] = np.ascontiguousarray(np.concatenate([w1r, w1r], 1).reshape(2, 128, 32 * 128))
    peT = inp["ab_cmp_pe"][0].transpose(0, 2, 1)
    m["cmp_peT"] = np.ascontiguousarray(np.concatenate([peT, peT], 1))
    m["cmp_b1T"] = np.ascontiguousarray(inp["ab_cmp_b1"][0].T)
    w2 = inp["ab_cmp_w2"][0]
    w2k = np.zeros((2, 128, 128), np.float32)
    w2k[0, :, 0:64] = w2[0]
    w2k[1, :, 64:128] = w2[0]
    m["cmp_w2k"] = w2k
    m["cmp_w2v"] = np.ascontiguousarray(w2[1])
    b2 = inp["ab_cmp_b2"][0]
    m["cmp_b2k"] = np.ascontiguousarray(np.concatenate([b2[0], b2[0]])[:, None].astype(np.float32))
    m["cmp_b2v"] = np.ascontiguousarray(b2[1][None].astype(np.float32))
    NBT = (cfg.NCMP + 127) // 128
    NSLC, TF0 = cfg.NSLC, cfg.TF0
    i_ = np.arange(NBT * 128)[:, None]
    j_ = np.arange(NSLC)[None, :]
    m["ovm"] = ((i_ * 16 < (j_ + 1) * 64) & (i_ * 16 + 32 > j_ * 64) & (i_ < cfg.NCMP)).astype(np.float32)
    base = (h - 1) * cfg.HALF
    lpq = np.arange(TF0 * 128, NT * 128)
    blk = np.arange(NBT * 128)
    valid_blk = (blk < cfg.NCMP) & (16 * blk + base >= 0)
    mc = ((16 * blk[:, None] + 31) <= lpq[None, :]) & valid_blk[:, None]
    m["maskc"] = np.ascontiguousarray(mc.reshape(NBT, 128, -1).astype(np.float32).astype(bf))
    q_abs = lpq + base
    j_abs = np.arange(NSLC) * 64 + base
    validj = (j_abs[None, :] <= q_abs[:, None]) & (j_abs[None, :] >= 0)
    qblk = q_abs // 64
    jb = j_abs // 64
    forced = (jb[None, :] == 0) | (jb[None, :] == qblk[:, None]) | (jb[None, :] == qblk[:, None] - 1)
    m["fbias"] = np.where(validj, np.where(forced, 1e4, 0.0), -1e30).astype(np.float32)
    m["exm"] = (np.arange(NT * 128)[None, :] // 64 == np.arange(NSLC)[:, None]).astype(np.float32).astype(bf)
    qq = np.arange(128)[None, :]
    m["causT"] = (kk <= qq).astype(np.float32).astype(bf)
    m["bandT"] = (kk > qq).astype(np.float32).astype(bf)
    m["w_out"] = inp["ab_w_out"][0]
    m["w_up"] = inp["ffn_w_up"]
    m["w_down"] = inp["ffn_w_down"]
    NFT = 2 * cfg.DFF // 128
    KC = cfg.KC
    m["ffn_cwT"] = np.ascontiguousarray(inp["ffn_conv_w"].transpose(0, 2, 1).reshape(2, NFT, 128, 3).transpose(0, 2, 1, 3))
    m["ffn_cbT"] = np.ascontiguousarray(inp["ffn_conv_b"].reshape(2, NFT, 128).transpose(0, 2, 1))
    m["w_pw1"] = inp["c_w_pw1"][0]
    m["b_pw1T"] = np.ascontiguousarray(inp["c_b_pw1"][0].reshape(2 * KC, 128).T)
    m["dw_wT"] = np.ascontiguousarray(inp["c_dw_w"][0].T.reshape(KC, 128, 31).transpose(1, 0, 2))
    m["dw_bT"] = np.ascontiguousarray(inp["c_dw_b"][0].reshape(KC, 128).T)
    m["ln_gT"] = np.ascontiguousarray(inp["c_ln_g"][0].reshape(KC, 128).T)
    m["ln_bT"] = np.ascontiguousarray(inp["c_ln_b"][0].reshape(KC, 128).T)
    m["w_pw2"] = inp["c_w_pw2"][0]
    m["b_pw2"] = np.ascontiguousarray(inp["c_b_pw2"][0][None])
    c = 2 * b + h
    DEC = cfg.DEC
    sq = slice(c * DEC, (c + 1) * DEC)
    m["xs_in"] = np.ascontiguousarray(inp["x_sample"][sq, 0])
    m["st_ssm"] = np.ascontiguousarray(inp["state_ssm"][0, sq].reshape(DEC, cfg.DI, 128))
    m["st_sconv"] = np.ascontiguousarray(inp["state_ssm_conv"][0, sq].reshape(DEC, -1))
    m["c_cmp"] = inp["cache_cmp"][0].reshape(-1, 256)
    m["c_slc"] = inp["cache_slc"][0].reshape(-1, 256)
    m["c_win"] = np.ascontiguousarray(inp["cache_win"][0, sq].reshape(DEC, -1, 256))
    m["ptab"] = np.ascontiguousarray(inp["page_table"][sq].astype(np.int32))
    m["st_conf"] = np.ascontiguousarray(inp["state_conf_conv"][0, sq])
    m["st_ffn"] = np.ascontiguousarray(inp["state_ffn_conv"][:, sq].reshape(2, DEC, -1))
    m["convw_r"] = inp["ab_conv_w"][0]
    m["convb_r"] = np.ascontiguousarray(inp["ab_conv_b"][0][None])
    m["fcw_r"] = inp["ffn_conv_w"]
    m["fcb_r"] = np.ascontiguousarray(inp["ffn_conv_b"][:, None, :])
    m["dww_r"] = inp["c_dw_w"][0]
    m["dwb_r"] = np.ascontiguousarray(inp["c_dw_b"][0][None])
    m["lng_r"] = np.ascontiguousarray(inp["c_ln_g"][0][None])
    m["lnb_r"] = np.ascontiguousarray(inp["c_ln_b"][0][None])
    m["bpw1_r"] = np.ascontiguousarray(inp["c_b_pw1"][0][None])
    PAST = cfg.PAST
    angs = np.float32(PAST) * inv
    m["rope_s"] = np.stack([np.cos(angs), np.sin(angs)]).astype(np.float32)
    selc = np.zeros((DEC, 31, DEC), np.float32)
    for s_ in range(DEC):
        selc[s_, :, s_] = 1.0
    m["selc"] = selc
    NCS = PAST // 16 - 1
    NBS = (NCS + 127) // 128
    NSL = PAST // 64
    i_ = np.arange(NBS * 128)[:, None]
    j_ = np.arange(NSL + 1)[None, :]
    m["ovs"] = ((i_ * 16 < (j_ + 1) * 64) & (i_ * 16 + 32 > j_ * 64) & (i_ < NCS)).astype(np.float32)
    jj = np.arange(NSL + 1)
    forced_s = (jj == 0) | (jj == NSL) | (jj == NSL - 1)
    m["fbs"] = np.where(forced_s, 1e4, 0.0).astype(np.float32)[None]
    m["exms"] = (np.arange(PAST)[None, :] // 64 == np.arange(NSL)[:, None]).astype(np.float32).astype(bf)
    m["iota_p"] = np.ascontiguousarray(np.broadcast_to(np.arange(128, dtype=np.float32)[:, None], (128, PAST // 128)))
    m["maskcs"] = np.ascontiguousarray((np.arange(NBS * 128) < NCS).astype(np.float32).reshape(NBS, 128).T)
    return m


OUT_SPECS = None


def assemble(cfg, res, nb):
    D, DI, HS, CD, DFF, DEC = cfg.D, cfg.DI, cfg.HS, cfg.CD, cfg.DFF, cfg.DEC
    f = np.float32
    cat = lambda k, cs: np.concatenate([np.asarray(res[c][k], f) for c in cs], 0)
    y_p = np.stack([cat("y_p", (2 * b, 2 * b + 1)) for b in range(nb)])
    allc = range(2 * nb)
    y_s = cat("y_s", allc)[:, None, :]
    ssm_p = np.stack([np.asarray(res[2 * b + 1]["ssm_p"], f).reshape(HS, 64, 128) for b in range(nb)])[None]
    ssm_s = cat("ssm_s", allc).reshape(-1, HS, 64, 128)[None]
    sconv_p = np.stack([np.asarray(res[2 * b + 1]["sconv_p"], f) for b in range(nb)])[None]
    sconv_s = cat("sconv_s", allc).reshape(-1, 3, CD)[None]
    kvs = cat("kv_s", allc)
    outs = [y_p, y_s, ssm_p, ssm_s, sconv_p, sconv_s]
    for bi, nm in enumerate(("cmp_p", "slc_p")):
        p_ = np.stack([cat(nm, (2 * b, 2 * b + 1)) for b in range(nb)]).reshape(nb, -1, 2, 2, 64)[None]
        s_ = kvs[:, bi * 256:(bi + 1) * 256].reshape(-1, 1, 2, 2, 64)[None]
        outs += [p_, s_]
    win_p = np.stack([np.asarray(res[2 * b + 1]["win_p"], f) for b in range(nb)]).reshape(nb, -1, 2, 2, 64)[None]
    win_s = kvs[:, 512:768].reshape(-1, 1, 2, 2, 64)[None]
    outs += [win_p, win_s]
    conf_p = np.stack([np.asarray(res[2 * b + 1]["conf_p"], f) for b in range(nb)])[None]
    conf_s = cat("conf_s", allc)[None]
    ffn_p = np.stack([np.asarray(res[2 * b + 1]["ffn_p"], f) for b in range(nb)], 1)
    ffn_s = np.concatenate([np.asarray(res[c]["ffn_s"], f) for c in allc], 1).reshape(2, -1, 2, 2 * DFF)
    outs += [conf_p, conf_s, ffn_p, ffn_s]
    return tuple(np.ascontiguousarray(o, dtype=np.float32) for o in outs)


_NC_CACHE = {}


def kernel(**inputs):
    cfg = FULL
    inp = {k: np.asarray(v) for k, v in inputs.items()}
    if "nc" not in _NC_CACHE:
        _NC_CACHE["nc"] = build(cfg)
    nc = _NC_CACHE["nc"]
    maps = [host_prep(cfg, inp, c // 2, c % 2) for c in range(8)]
    res = run_bass_kernel_spmd(nc, maps, core_ids=list(range(8)))
    return assemble(cfg, res.results, 4)
```
